# Optimizing a Trainium2 kernel written in Bass

```python
import jax
import jax.numpy as jnp
from jax import lax
import numpy as np

D_MODEL = 1024
BATCH = 4
SEQ = 4096
DEPTH = 4

GRID_W = 64
CTX_LEN = 256
M_HEADS = 4
M_HEAD_DIM = 256
M_WIDTH = M_HEADS * M_HEAD_DIM
M_CHUNK = 64
M_CONV = 3
A_HEADS = 8
A_NOPE = 128
A_ROPE = 64
A_VDIM = 128
A_QRANK = 384
A_KVRANK = 256
A_QBLOCK = 128
AXIS_ROT = A_ROPE // 2
ROPE_THETA = 10000.0
D_FF = 2816
N_BRANCH = 2
N_MOD = 9
EPS = 1e-6

IN_GROUPS = (('m_q', M_WIDTH), ('m_k', M_WIDTH), ('m_v', M_WIDTH), ('m_o', M_WIDTH), ('m_gate', 4 * M_HEADS), ('a_cq', A_QRANK), ('a_ckv', A_KVRANK), ('a_kr', A_ROPE), ('br_gate', N_BRANCH * D_MODEL))
IN_WIDTH = 4 * M_WIDTH + 4 * M_HEADS + A_QRANK + A_KVRANK + A_ROPE + N_BRANCH * D_MODEL
ALL_GROUPS = ('m_q', 'm_k', 'm_v', 'm_o', 'm_gate', 'a_cq', 'a_ckv', 'a_kr', 'br_gate')
CTX_KV_GROUPS = ('m_k', 'm_v', 'm_gate', 'a_ckv', 'a_kr')

kernel_name = 'hybrid_mlstm_mla_macaron_dit'


def _rmsnorm(x, g):
    xf = x.astype(jnp.float32)
    y = xf * lax.rsqrt(jnp.mean(xf * xf, axis=-1, keepdims=True) + EPS)
    return (y * g.astype(jnp.float32)).astype(x.dtype)


def _modulate(h, shift, scale):
    return h * (1 + scale) + shift


def _swiglu(h, w_up, w_dn):
    a, b = jnp.split(h @ w_up, 2, axis=-1)
    return (jax.nn.silu(a) * b) @ w_dn


def _short_conv(x, w):
    pad = M_CONV // 2
    T = x.shape[1]
    xp = jnp.pad(x, ((0, 0), (pad, pad), (0, 0)))
    return sum(xp[:, i:i + T] * w[i] for i in range(M_CONV))


def _project(h, w_in, names):
    sizes = dict(IN_GROUPS)
    offs, o = {}, 0
    for name, n in IN_GROUPS:
        offs[name] = o
        o += n
    w = w_in if names == ALL_GROUPS else jnp.concatenate([w_in[:, offs[k]:offs[k] + sizes[k]] for k in names], axis=1)
    z = h @ w
    cuts = np.cumsum([sizes[k] for k in names])[:-1].tolist()
    return dict(zip(names, jnp.split(z, cuts, axis=-1)))


def _rope_tables(row, col):
    inv = ROPE_THETA ** (-jnp.arange(0, AXIS_ROT, 2, dtype=jnp.float32) / AXIS_ROT)
    ang = jnp.concatenate([row[:, None] * inv, col[:, None] * inv], axis=-1)
    return jnp.cos(ang), jnp.sin(ang)


def _rope(x, cos, sin):
    xr = x.astype(jnp.float32).reshape(*x.shape[:-1], A_ROPE // 2, 2)
    x1, x2 = xr[..., 0], xr[..., 1]
    c, s = cos[:, None, :], sin[:, None, :]
    out = jnp.stack([x1 * c - x2 * s, x1 * s + x2 * c], axis=-1)
    return out.reshape(x.shape).astype(x.dtype)


def _mlstm_kvg(z, p):
    B, T = z['m_k'].shape[:2]
    k = jax.nn.silu(_short_conv(z['m_k'], p['w_conv'][:, M_WIDTH:])).reshape(B, T, M_HEADS, M_HEAD_DIM)
    v = z['m_v'].reshape(B, T, M_HEADS, M_HEAD_DIM)
    g = (z['m_gate'] + p['b_gate']).reshape(B, T, 2, 2, M_HEADS)
    return k, v, g


def _mlstm_q(z, p):
    B, T = z['m_q'].shape[:2]
    q = jax.nn.silu(_short_conv(z['m_q'], p['w_conv'][:, :M_WIDTH])) * (M_HEAD_DIM ** -0.5)
    return q.reshape(B, T, M_HEADS, M_HEAD_DIM)


def _mlstm_scan(k, v, ig, fg, state, q=None):
    B, T, H, dh = k.shape
    nc = T // M_CHUNK

    def chunks(a):
        a = a.astype(jnp.float32).reshape(B, nc, M_CHUNK, H, *a.shape[3:])
        return jnp.moveaxis(a, (1, 3), (0, 2))

    xs = (chunks(k), chunks(v), chunks(ig), chunks(jax.nn.log_sigmoid(fg.astype(jnp.float32))))
    if q is not None:
        xs = xs + (chunks(q),)
    tri = jnp.tril(jnp.ones((M_CHUNK, M_CHUNK), dtype=bool))

    def body(carry, xc):
        C, n, m = carry
        kc, vc, ic, lc = xc[:4]
        b = jnp.cumsum(lc, axis=-1)
        bL = b[..., -1]
        end_log = bL[..., None] - b + ic
        m_new = jnp.maximum(bL + m, jnp.max(end_log, axis=-1))
        a_state = jnp.exp(bL + m - m_new)
        w_end = jnp.exp(end_log - m_new[..., None])
        C_new = a_state[..., None, None] * C + jnp.einsum('bhs,bhsk,bhsv->bhkv', w_end, kc, vc)
        n_new = a_state[..., None] * n + jnp.einsum('bhs,bhsk->bhk', w_end, kc)
        new = (C_new, n_new, m_new)
        if q is None:
            return new, None
        qc = xc[4]
        dlog = jnp.where(tri, b[..., :, None] - b[..., None, :] + ic[..., None, :], -jnp.inf)
        inter = b + m[..., None]
        mj = jnp.maximum(inter, jnp.max(dlog, axis=-1))
        s = jnp.einsum('bhjd,bhsd->bhjs', qc, kc) * jnp.exp(dlog - mj[..., None])
        e_inter = jnp.exp(inter - mj)
        num = s @ vc + e_inter[..., None] * (qc @ C)
        den = jnp.sum(s, axis=-1) + e_inter * jnp.einsum('bhjk,bhk->bhj', qc, n)
        h = num / jnp.maximum(jnp.abs(den), jnp.exp(-mj))[..., None]
        return new, h

    state, hs = lax.scan(body, state, xs)
    if q is None:
        return state, None
    return state, jnp.moveaxis(hs, (0, 2), (1, 3)).reshape(B, T, H, dh)


def _identity(a):
    return a


def _flip(a):
    return a[:, ::-1]


def _mlstm_out(h, z, p):
    B, T = h.shape[:2]
    hn = _rmsnorm(h, p['g_mh'].reshape(M_HEADS, M_HEAD_DIM)).astype(z['m_o'].dtype).reshape(B, T, M_WIDTH)
    return jax.nn.sigmoid(z['m_o']) * hn


def _mla_kv(z, p, cos, sin):
    B, T = z['a_ckv'].shape[:2]
    kv = (_rmsnorm(z['a_ckv'], p['g_kva']) @ p['w_ukv']).reshape(B, T, A_HEADS, A_NOPE + A_VDIM)
    k_nope, v = kv[..., :A_NOPE], kv[..., A_NOPE:]
    k_rope = z['a_kr'][:, :, None, :]
    if cos is not None:
        k_rope = _rope(k_rope, cos, sin)
    k = jnp.concatenate([k_nope, jnp.broadcast_to(k_rope, (B, T, A_HEADS, A_ROPE))], axis=-1)
    return k, v


def _mla_q(z, p, cos, sin):
    B, T = z['a_cq'].shape[:2]
    q = (_rmsnorm(z['a_cq'], p['g_qa']) @ p['w_uq']).reshape(B, T, A_HEADS, A_NOPE + A_ROPE)
    q_nope, q_rope = q[..., :A_NOPE], q[..., A_NOPE:]
    if cos is not None:
        q_rope = _rope(q_rope, cos, sin)
    return jnp.concatenate([q_nope, q_rope], axis=-1)


def _attend(q, k, v):
    B, T, H, dqk = q.shape
    blk = min(A_QBLOCK, T)
    nb = T // blk
    scale = dqk ** -0.5
    qb = jnp.moveaxis(q.reshape(B, nb, blk, H, dqk), 1, 0)

    def one(qi):
        s = jnp.einsum('bqhd,bkhd->bhqk', qi, k).astype(jnp.float32) * scale
        pr = jax.nn.softmax(s, axis=-1).astype(v.dtype)
        return jnp.einsum('bhqk,bkhd->bqhd', pr, v)

    o = lax.map(one, qb)
    return jnp.moveaxis(o, 0, 1).reshape(B, T, H * v.shape[-1])


def _merge(hm, ha, z, p):
    g_m, g_a = jnp.split(jax.nn.sigmoid(z['br_gate']), N_BRANCH, axis=-1)
    return (g_m * (hm @ p['w_bm']) + g_a * (ha @ p['w_ba'])) @ p['w_out']


def _token_mix(hc, hx, p, cos, sin, ctx_out):
    zc = _project(hc, p['w_in'], ALL_GROUPS if ctx_out else CTX_KV_GROUPS)
    zx = _project(hx, p['w_in'], ALL_GROUPS)
    B = hx.shape[0]
    kc, vc, gc = _mlstm_kvg(zc, p)
    kx, vx, gx = _mlstm_kvg(zx, p)
    qc = _mlstm_q(zc, p) if ctx_out else None
    qx = _mlstm_q(zx, p)
    state0 = (jnp.zeros((B, M_HEADS, M_HEAD_DIM, M_HEAD_DIM), jnp.float32), jnp.zeros((B, M_HEADS, M_HEAD_DIM), jnp.float32), jnp.zeros((B, M_HEADS), jnp.float32))
    h_ctx, h_lat = 0.0, 0.0
    for d in range(2):
        f = _identity if d == 0 else _flip
        st_c, hcd = _mlstm_scan(f(kc), f(vc), f(gc[:, :, d, 0]), f(gc[:, :, d, 1]), state0, None if qc is None else f(qc))
        _, hxd = _mlstm_scan(f(kx), f(vx), f(gx[:, :, d, 0]), f(gx[:, :, d, 1]), st_c, f(qx))
        h_lat = h_lat + f(hxd)
        if ctx_out:
            h_ctx = h_ctx + f(hcd)
    k_c, v_c = _mla_kv(zc, p, None, None)
    k_x, v_x = _mla_kv(zx, p, cos, sin)
    a_lat = _attend(_mla_q(zx, p, cos, sin), jnp.concatenate([k_c, k_x], axis=1), jnp.concatenate([v_c, v_x], axis=1))
    y_lat = _merge(_mlstm_out(h_lat, zx, p), a_lat, zx, p)
    if not ctx_out:
        return None, y_lat
    a_ctx = _attend(_mla_q(zc, p, None, None), k_c, v_c)
    y_ctx = _merge(_mlstm_out(h_ctx, zc, p), a_ctx, zc, p)
    return y_ctx, y_lat


def setup_inputs(seed: int = 0) -> dict:
    key = jax.random.key(seed)
    ks = jax.random.split(key, 32)
    L, D = DEPTH, D_MODEL

    def nrm(k, shape, s):
        return s * jax.random.normal(k, shape, jnp.float32)

    def gain(k, shape):
        return 1.0 + 0.05 * jax.random.normal(k, shape, jnp.float32)

    ib = nrm(ks[20], (L, 2, M_HEADS), 0.1)
    fb = 3.0 + 3.0 * jax.random.uniform(ks[21], (L, 2, M_HEADS), jnp.float32)
    b_gate = jnp.stack([ib, fb], axis=2).reshape(L, 4 * M_HEADS)
    return {
        'x': nrm(ks[0], (BATCH, SEQ, D), 1.0),
        'c': nrm(ks[1], (BATCH, D), 1.0),
        'ctx': nrm(ks[2], (BATCH, CTX_LEN, D), 1.0),
        'c_ctx': nrm(ks[3], (D,), 1.0),
        'w_ada': nrm(ks[4], (L, D, N_MOD * D), 0.5 * D ** -0.5),
        'b_ada': nrm(ks[5], (L, N_MOD * D), 0.02),
        'g_n1': gain(ks[6], (L, D)),
        'g_n2': gain(ks[7], (L, D)),
        'g_n3': gain(ks[8], (L, D)),
        'w_ff1_up': nrm(ks[9], (L, D, 2 * D_FF), D ** -0.5),
        'w_ff1_dn': nrm(ks[10], (L, D_FF, D), D_FF ** -0.5),
        'w_ff2_up': nrm(ks[11], (L, D, 2 * D_FF), D ** -0.5),
        'w_ff2_dn': nrm(ks[12], (L, D_FF, D), D_FF ** -0.5),
        'w_in': nrm(ks[13], (L, D, IN_WIDTH), D ** -0.5),
        'b_gate': b_gate,
        'w_conv': nrm(ks[14], (L, M_CONV, 2 * M_WIDTH), M_CONV ** -0.5),
        'g_mh': gain(ks[15], (L, M_WIDTH)),
        'g_qa': gain(ks[16], (L, A_QRANK)),
        'g_kva': gain(ks[17], (L, A_KVRANK)),
        'w_uq': nrm(ks[18], (L, A_QRANK, A_HEADS * (A_NOPE + A_ROPE)), A_QRANK ** -0.5),
        'w_ukv': nrm(ks[19], (L, A_KVRANK, A_HEADS * (A_NOPE + A_VDIM)), A_KVRANK ** -0.5),
        'w_bm': nrm(ks[22], (L, M_WIDTH, D), M_WIDTH ** -0.5),
        'w_ba': nrm(ks[23], (L, A_HEADS * A_VDIM, D), (A_HEADS * A_VDIM) ** -0.5),
        'w_out': nrm(ks[24], (L, D, D), D ** -0.5),
        'g_final': gain(ks[25], (D,)),
    }


def reference(x, c, ctx, c_ctx, w_ada, b_ada, g_n1, g_n2, g_n3, w_ff1_up, w_ff1_dn, w_ff2_up, w_ff2_dn, w_in, b_gate, w_conv, g_mh, g_qa, g_kva, w_uq, w_ukv, w_bm, w_ba, w_out, g_final):
    S = x.shape[1]
    ROWS = S // GRID_W
    row = jnp.repeat(jnp.arange(ROWS, dtype=jnp.float32), GRID_W)
    col = jnp.tile(jnp.arange(GRID_W, dtype=jnp.float32), ROWS)
    cos, sin = _rope_tables(row, col)
    for l in range(DEPTH):
        last = l == DEPTH - 1
        p = {'w_in': w_in[l], 'b_gate': b_gate[l], 'w_conv': w_conv[l], 'g_mh': g_mh[l], 'g_qa': g_qa[l], 'g_kva': g_kva[l], 'w_uq': w_uq[l], 'w_ukv': w_ukv[l], 'w_bm': w_bm[l], 'w_ba': w_ba[l], 'w_out': w_out[l]}
        mx = [m[:, None, :] for m in jnp.split(jax.nn.silu(c) @ w_ada[l] + b_ada[l], N_MOD, axis=-1)]
        mc = jnp.split(jax.nn.silu(c_ctx) @ w_ada[l] + b_ada[l], N_MOD, axis=-1)
        x = x + 0.5 * mx[2] * _swiglu(_modulate(_rmsnorm(x, g_n1[l]), mx[0], mx[1]), w_ff1_up[l], w_ff1_dn[l])
        ctx = ctx + 0.5 * mc[2] * _swiglu(_modulate(_rmsnorm(ctx, g_n1[l]), mc[0], mc[1]), w_ff1_up[l], w_ff1_dn[l])
        y_ctx, y_lat = _token_mix(_modulate(_rmsnorm(ctx, g_n2[l]), mc[3], mc[4]), _modulate(_rmsnorm(x, g_n2[l]), mx[3], mx[4]), p, cos, sin, not last)
        x = x + mx[5] * y_lat
        x = x + 0.5 * mx[8] * _swiglu(_modulate(_rmsnorm(x, g_n3[l]), mx[6], mx[7]), w_ff2_up[l], w_ff2_dn[l])
        if not last:
            ctx = ctx + mc[5] * y_ctx
            ctx = ctx + 0.5 * mc[8] * _swiglu(_modulate(_rmsnorm(ctx, g_n3[l]), mc[6], mc[7]), w_ff2_up[l], w_ff2_dn[l])
    return _rmsnorm(x, g_final)
```

```python
import numpy as np
from contextlib import ExitStack
import concourse.bass as bass
import concourse.mybir as mybir
from concourse.bass_utils import run_bass_kernel_spmd

F32 = mybir.dt.float32
BF16 = mybir.dt.bfloat16
AF = mybir.ActivationFunctionType
ALU = mybir.AluOpType

D = 1024
KC = 8
DFF = 2816
NF = 22
CT = 256
EPS = 1e-6
NWF = 50


class SemObj:
    _n = 0

    def __init__(self, h):
        self.h = h
        self.count = 0
        SemObj._n += 1
        self.id = SemObj._n


class Res:
    def __init__(self, name=""):
        self.lw = {}
        self.rd = {}
        self.name = name


class Tl:
    def __init__(self, t, name=""):
        self.t = t
        self.r = Res(name)


class EngW:
    def __init__(self, name, eng, so):
        self.name = name
        self.eng = eng
        self.so = so
        self.waited = {}


class Sched:
    def __init__(self, nc, es):
        self.nc = nc

        def mk(name):
            return SemObj(es.enter_context(nc.semaphore(name)))

        self.pe = EngW("pe", nc.tensor, mk("s_pe"))
        self.act = EngW("act", nc.scalar, mk("s_act"))
        self.dve = EngW("dve", nc.vector, mk("s_dve"))
        self.pool = EngW("pool", nc.gpsimd, mk("s_pool"))
        self.sp = EngW("sp", nc.sync, mk("s_sp"))
        self.engs = [self.pe, self.act, self.dve, self.pool, self.sp]
        self.dq = {"sp": [mk(f"dsp{i}") for i in range(12)], "pool": [mk(f"dpl{i}") for i in range(12)]}
        self.dqn = {"sp": 0, "pool": 0}
        self.banks = []
        self.bank_i = 0
        self.held = set()
        self.ninst = 0

    def _wait(self, E, tick):
        so, v = tick
        if E.waited.get(so.id, 0) >= v:
            return
        E.eng.wait_ge(so.h, v)
        E.waited[so.id] = v
        self.ninst += 1

    def _deps(self, reads, writes, own):
        deps = {}

        def add(t, raw):
            so, v = t
            if so is own and not raw:
                return
            if deps.get(so.id, (None, 0))[1] < v:
                deps[so.id] = (so, v)

        for r in reads:
            for t in r.lw.values():
                add(t, True)
        for w in writes:
            for t in w.lw.values():
                add(t, False)
            for t in w.rd.values():
                add(t, False)
        return deps

    def _commit(self, tick, reads, writes):
        so, v = tick
        for w in writes:
            o = w.lw.get(so.id)
            if o is None or o[1] < v:
                w.lw[so.id] = tick
        for r in reads:
            o = r.rd.get(so.id)
            if o is None or o[1] < v:
                r.rd[so.id] = tick

    def op(self, E, fn, reads=(), writes=(), inc=True):
        reads = [x.r if isinstance(x, Tl) else x for x in reads]
        writes = [x.r if isinstance(x, Tl) else x for x in writes]
        for t in self._deps(reads, writes, E.so).values():
            self._wait(E, t)
        ins = fn()
        self.ninst += 1
        if inc:
            ins.then_inc(E.so.h, 1)
            E.so.count += 1
            tick = (E.so, E.so.count)
        else:
            tick = (E.so, E.so.count + 1)
        self._commit(tick, reads, writes)
        return ins

    def dma(self, q, out, in_, reads=(), writes=(), **kw):
        reads = [x.r if isinstance(x, Tl) else x for x in reads]
        writes = [x.r if isinstance(x, Tl) else x for x in writes]
        E = self.sp if q == "sp" else self.pool
        pool = self.dq[q]
        so = pool[self.dqn[q] % len(pool)]
        self.dqn[q] += 1
        if so.count > 0:
            self._wait(E, (so, so.count))
        for t in self._deps(reads, writes, None).values():
            self._wait(E, t)
        E.eng.dma_start(out=out, in_=in_, **kw).then_inc(so.h, 16)
        self.ninst += 1
        so.count += 16
        self._commit((so, so.count), reads, writes)

    def barrier(self):
        ticks = []
        for E in self.engs:
            if E.so.count > 0:
                ticks.append((E.so, E.so.count))
        for q in self.dq.values():
            for so in q:
                if so.count > 0:
                    ticks.append((so, so.count))
        for E in self.engs:
            for t in ticks:
                if t[0] is not E.so:
                    self._wait(E, t)

    def bank(self, hold=False):
        for _ in range(16):
            i = self.bank_i % 8
            self.bank_i += 1
            if i not in self.held:
                if hold:
                    self.held.add(i)
                return self.banks[i]
        raise RuntimeError("no psum bank")

    def release(self, b):
        self.held.discard(self.banks.index(b))


def build(L, S, dbg=False):
    T = CT + S
    NCH = T // 64
    NTB = T // 128
    nc = bass.Bass("TRN2", target_bir_lowering=False)
    es = ExitStack()

    def din(name, shape, dt=F32):
        return nc.dram_tensor(name, list(shape), dt, kind="ExternalInput").ap()

    scratch_kind = "ExternalOutput" if dbg else "Internal"

    def dscr(name, shape, dt):
        return nc.dram_tensor(name, list(shape), dt, kind=scratch_kind).ap()

    xT_in = din("xT", [KC, 128, T])
    scT_in = din("scT", [128, KC, 2])
    wada_in = din("wada", [L, 72, 128, KC, 128])
    bada_in = din("bada", [128, L, 72])
    gn_in = din("gn", [128, L, 3, KC])
    gfin_in = din("gfin", [128, KC])
    wup_in = [din(f"wup{i}", [L, NF, 128, KC, 2, 128]) for i in range(2)]
    wdn_in = [din(f"wdn{i}", [L, 128, NF, D]) for i in range(2)]
    wf_in = din("wf", [L, NWF, 128, KC, 128])
    wv_in = din("wv", [L, 128, KC, D])
    convw_in = din("convw", [128, L, 16, 3])
    bg_in = din("bg", [128, L, 4])
    gmh_in = din("gmh", [128, L, KC])
    gqa_in = din("gqa", [128, L, 3])
    gkva_in = din("gkva", [128, L, 2])
    wuq_in = din("wuq", [L, 128, 3, 16, 128])
    wukvk_in = din("wukvk", [L, 128, 2, 8, 128])
    wukvv_in = din("wukvv", [L, 128, 2, D])
    wbm_in = din("wbm", [L, 128, KC, D])
    wba_in = din("wba", [L, 128, KC, D])
    wout_in = din("wout", [L, 128, KC, D])
    rope_in = din("rope", [64, 2, T])
    ident_in = din("ident", [128, 128])
    sel_in = din("sel", [128, 4, 128])
    tri_in = din("tri", [64, 2, 64])
    outT = nc.dram_tensor("outT", [KC, 128, S], F32, kind="ExternalOutput").ap()

    XT = dscr("XT", [KC, 128, T], F32)
    QT = dscr("QT", [KC, 128, T], BF16)
    KT = dscr("KT", [KC, 128, T], BF16)
    SOT = dscr("SOT", [KC, 128, T], BF16)
    GMT = dscr("GMT", [KC, 128, T], BF16)
    GAT = dscr("GAT", [KC, 128, T], BF16)
    KTOK = dscr("KTOK", [T, D], BF16)
    VTOK = dscr("VTOK", [T, D], BF16)
    GROW = dscr("GROW", [4, 128, T], F32)
    HF = dscr("HF", [T, D], F32)
    HB = dscr("HB", [T, D], F32)
    QN = dscr("QN", [8, 128, T], BF16)
    QR = dscr("QR", [8, 64, T], BF16)
    KN = dscr("KN", [8, 128, T], BF16)
    KR = dscr("KR", [64, T], BF16)
    VA = dscr("VA", [T, D], BF16)
    HAT = dscr("HAT", [8, 128, T], BF16)

    sch = Sched(nc, es)
    pe, act, dve, pool, sp = sch.pe, sch.act, sch.dve, sch.pool, sch.sp

    for i in range(8):
        sch.banks.append(Tl(es.enter_context(nc.psum_tensor(f"bank{i}", [128, 512], F32)), f"bank{i}"))

    sbn = [0]

    def sb(stack, name, shape, dt):
        sbn[0] += 1
        nm = f"s{sbn[0]}_{name}"
        return Tl(stack.enter_context(nc.sbuf_tensor(nm, list(shape), dt)), nm)

    tiles = [(0, CT, 1)] + [(CT + 512 * i, 512, 0) for i in range(S // 512)]
    xres = [Res(f"XT{i}") for i in range(len(tiles))]
    r_xin = Res("xin")
    RQT, RKT, RSOT, RGMT, RGAT, RKTOK, RVTOK, RGROW, RHF, RHB, RQN, RQR, RKN, RKR, RVA, RHAT, ROUT = [Res() for _ in range(17)]

    ident_f = sb(es, "ident_f", [128, 128], F32)
    ident_b = sb(es, "ident_b", [128, 128], BF16)
    ones_b = sb(es, "ones_b", [128, 128], BF16)
    sel_f = sb(es, "sel_f", [128, 4, 128], F32)
    tri_f = sb(es, "tri_f", [64, 2, 64], F32)
    MOD = sb(es, "MOD", [128, L, 72, 2], F32)
    GS = sb(es, "GS", [128, L, 3, KC, 2], F32)
    GT = sb(es, "GT", [128, L, 3, KC, 2], F32)
    gn_s = sb(es, "gn_s", [128, L, 3, KC], F32)
    gfin_s = sb(es, "gfin_s", [128, KC], F32)
    convw_s = sb(es, "convw_s", [128, L, 16, 3], F32)
    bg_s = sb(es, "bg_s", [128, L, 4], F32)
    gmh_s = sb(es, "gmh_s", [128, L, KC], F32)
    gqa_s = sb(es, "gqa_s", [128, L, 3], F32)
    gkva_s = sb(es, "gkva_s", [128, L, 2], F32)
    bada_s = sb(es, "bada_s", [128, L, 72], F32)
    ones_f = sb(es, "ones_f", [128, 1], F32)

    V = nc.vector
    A = nc.scalar
    P = nc.tensor

    def load_const(tl, src, q="sp"):
        sch.dma(q, tl.t[:], src, writes=[tl])

    load_const(ident_f, ident_in[:, :])
    load_const(ident_b, ident_in[:, :], "pool")
    load_const(sel_f, sel_in[:, :, :])
    load_const(tri_f, tri_in[:, :, :])
    load_const(gn_s, gn_in[:, :, :, :])
    load_const(gfin_s, gfin_in[:, :])
    load_const(convw_s, convw_in[:, :, :, :])
    load_const(bg_s, bg_in[:, :, :])
    load_const(gmh_s, gmh_in[:, :, :])
    load_const(gqa_s, gqa_in[:, :, :])
    load_const(gkva_s, gkva_in[:, :, :])
    load_const(bada_s, bada_in[:, :, :])
    sch.op(dve, lambda: V.memset(ones_b.t[:], 1.0), writes=[ones_b])
    sch.op(dve, lambda: V.memset(ones_f.t[:], 1.0), writes=[ones_f])

    with ExitStack() as ps:
        sc_f = sb(ps, "sc_f", [128, KC, 2], F32)
        sc_b = sb(ps, "sc_b", [128, KC, 2], BF16)
        wa = [sb(ps, f"wa{i}", [128, 8, KC, 128], BF16) for i in range(2)]
        load_const(sc_f, scT_in[:, :, :])
        sch.op(act, lambda: A.activation(out=sc_b.t[:], in_=sc_f.t[:], func=AF.Silu), reads=[sc_f], writes=[sc_b])
        it = 0
        for l in range(L):
            for mg in range(9):
                w = wa[it % 2]
                it += 1
                sch.dma("pool", w.t[:], wada_in[l, mg * 8:(mg + 1) * 8].rearrange("m p k c -> p m k c"),
                        writes=[w], max_dma_last_dim=4096)
                bk = sch.bank()
                for m in range(8):
                    for k in range(KC):
                        sch.op(pe, lambda m=m, k=k: P.matmul(bk.t[:, 2 * m:2 * m + 2], w.t[:, m, k, :], sc_b.t[:, k, :],
                                                             start=(k == 0), stop=(k == KC - 1)),
                               reads=[w, sc_b], writes=[bk], inc=(k == KC - 1))
                sch.op(dve, lambda: V.tensor_tensor(
                    out=MOD.t[:, l, mg * 8:(mg + 1) * 8, :],
                    in0=bk.t[:, 0:16].rearrange("p (m j) -> p m j", j=2),
                    in1=bada_s.t[:, l, mg * 8:(mg + 1) * 8].unsqueeze(2).to_broadcast([128, 8, 2]),
                    op=ALU.add), reads=[bk, bada_s], writes=[MOD])
        for l in range(L):
            for i in range(3):
                coef = 1.0 if i == 1 else 0.5
                sc = MOD.t[:, l, (3 * i + 1) * 8:(3 * i + 2) * 8, :]
                gt = MOD.t[:, l, (3 * i + 2) * 8:(3 * i + 3) * 8, :]
                sch.op(dve, lambda: V.scalar_tensor_tensor(
                    out=GS.t[:, l, i, :, :], in0=sc, scalar=1.0,
                    in1=gn_s.t[:, l, i, :].unsqueeze(2).to_broadcast([128, KC, 2]),
                    op0=ALU.add, op1=ALU.mult), reads=[MOD, gn_s], writes=[GS])
                sch.op(dve, lambda: V.tensor_scalar(out=GT.t[:, l, i, :, :], in0=gt, scalar1=coef, scalar2=None,
                                                    op0=ALU.mult), reads=[MOD], writes=[GT])
        sch.barrier()

    def SH(l, i, k, j):
        return MOD.t[:, l, 3 * i * 8 + k, j:j + 1]

    def rms_rstd(stack_tiles, src_ap_fn, nchunks, n, inv_dim, src_res):
        sq, rt = stack_tiles
        for c in range(nchunks):
            sch.op(act, lambda c=c: A.activation(out=sq.t[:, c, :n], in_=src_ap_fn(c), func=AF.Square),
                   reads=[src_res], writes=[sq])
        bk = sch.bank()
        for c in range(nchunks):
            sch.op(pe, lambda c=c: P.matmul(bk.t[:, :n], ones_b.t[:], sq.t[:, c, :n], start=(c == 0), stop=(c == nchunks - 1)),
                   reads=[ones_b, sq], writes=[bk], inc=(c == nchunks - 1))
        sch.op(act, lambda: A.activation(out=rt.t[:, :n], in_=bk.t[:, :n], func=AF.Sqrt, scale=inv_dim, bias=eps_s.t[:, 0:1]),
               reads=[bk, eps_s], writes=[rt])
        sch.op(dve, lambda: V.reciprocal(out=rt.t[:, :n], in_=rt.t[:, :n]), reads=[rt], writes=[rt])
        return rt

    eps_s = sb(es, "eps_s", [128, 1], F32)
    sch.op(dve, lambda: V.memset(eps_s.t[:], EPS), writes=[eps_s])

    def norm_mod(xt, n, l, i, j, sq, rt, tmp, hout, hcol0=0):
        rms_rstd((sq, rt), lambda c: xt.t[:, c, :n], KC, n, 1.0 / D, xt)
        sch.op(dve, lambda: V.tensor_tensor(out=tmp.t[:, :, :n], in0=xt.t[:, :, :n],
                                            in1=rt.t[:, :n].unsqueeze(1).to_broadcast([128, KC, n]), op=ALU.mult),
               reads=[xt, rt], writes=[tmp])
        for k in range(KC):
            sch.op(act, lambda k=k: A.activation(out=hout.t[:, k, hcol0:hcol0 + n], in_=tmp.t[:, k, :n], func=AF.Identity,
                                                 scale=GS.t[:, l, i, k, j:j + 1], bias=SH(l, i, k, j)),
                   reads=[tmp, GS, MOD], writes=[hout])

    def ffn_phase(l, which, first):
        i = 0 if which == 0 else 2
        with ExitStack() as ps:
            wdn = sb(ps, "wdn", [128, NF, D], BF16)
            wup = [sb(ps, f"wupb{q}", [128, KC, 2, 128], BF16) for q in range(3)]
            xts = [sb(ps, f"xt{q}", [128, KC, 512], F32) for q in range(2)]
            h = sb(ps, "h", [128, KC, 512], BF16)
            sq = sb(ps, "sq", [128, KC, 512], BF16)
            rt = sb(ps, "rt", [128, 512], F32)
            tmp = sb(ps, "tmp", [128, KC, 512], F32)
            h2 = sb(ps, "h2", [128, NF, 512], BF16)
            sil = [sb(ps, f"sil{q}", [128, 512], BF16) for q in range(2)]
            for q in range(2):
                sch.dma("pool", wdn.t[:, q * 11:(q + 1) * 11, :], wdn_in[which // 2][l, :, q * 11:(q + 1) * 11, :],
                        writes=[wdn], max_dma_last_dim=4096)

            def load(ti):
                t0, n, j = tiles[ti]
                xt = xts[ti % 2]
                if first:
                    sch.dma("sp", xt.t[:, :, :n], xT_in[:, :, t0:t0 + n].rearrange("k p t -> p k t"), reads=[r_xin], writes=[xt])
                else:
                    sch.dma("sp", xt.t[:, :, :n], XT[:, :, t0:t0 + n].rearrange("k p t -> p k t"), reads=[xres[ti]], writes=[xt])

            load(0)
            wi = 0
            for ti, (t0, n, j) in enumerate(tiles):
                if ti + 1 < len(tiles):
                    load(ti + 1)
                xt = xts[ti % 2]
                norm_mod(xt, n, l, i, j, sq, rt, tmp, h)
                for f in range(NF):
                    w = wup[wi % 3]
                    wi += 1
                    sch.dma("pool", w.t[:], wup_in[which // 2][l, f], writes=[w], max_dma_last_dim=4096)
                    pa = sch.bank()
                    pb = sch.bank()
                    for k in range(KC):
                        sch.op(pe, lambda k=k: P.matmul(pa.t[:, :n], w.t[:, k, 0, :], h.t[:, k, :n], start=(k == 0), stop=(k == KC - 1)),
                               reads=[w, h], writes=[pa], inc=(k == KC - 1))
                    for k in range(KC):
                        sch.op(pe, lambda k=k: P.matmul(pb.t[:, :n], w.t[:, k, 1, :], h.t[:, k, :n], start=(k == 0), stop=(k == KC - 1)),
                               reads=[w, h], writes=[pb], inc=(k == KC - 1))
                    s_ = sil[f % 2]
                    sch.op(act, lambda: A.activation(out=s_.t[:, :n], in_=pa.t[:, :n], func=AF.Silu), reads=[pa], writes=[s_])
                    sch.op(dve, lambda: V.tensor_tensor(out=h2.t[:, f, :n], in0=pb.t[:, :n], in1=s_.t[:, :n], op=ALU.mult),
                           reads=[pb, s_], writes=[h2])
                for m in range(KC):
                    py = sch.bank()
                    for f in range(NF):
                        sch.op(pe, lambda f=f: P.matmul(py.t[:, :n], wdn.t[:, f, m * 128:(m + 1) * 128], h2.t[:, f, :n],
                                                        start=(f == 0), stop=(f == NF - 1)),
                               reads=[wdn, h2], writes=[py], inc=(f == NF - 1))
                    sch.op(dve, lambda: V.scalar_tensor_tensor(out=xt.t[:, m, :n], in0=py.t[:, :n], scalar=GT.t[:, l, i, m, j:j + 1],
                                                               in1=xt.t[:, m, :n], op0=ALU.mult, op1=ALU.add),
                           reads=[py, GT, xt], writes=[xt])
                sch.dma("sp", XT[:, :, t0:t0 + n].rearrange("k p t -> p k t"), xt.t[:, :, :n], reads=[xt], writes=[xres[ti]])
            sch.barrier()

    def inproj_phase(l):
        with ExitStack() as ps:
            hall = sb(ps, "hall", [128, KC, T], BF16)
            with ExitStack() as p1:
                xts = [sb(p1, f"ixt{q}", [128, KC, 512], F32) for q in range(2)]
                sq = sb(p1, "isq", [128, KC, 512], BF16)
                rt = sb(p1, "irt", [128, 512], F32)
                tmp = sb(p1, "itmp", [128, KC, 512], F32)

                def load(ti):
                    t0, n, j = tiles[ti]
                    sch.dma("sp", xts[ti % 2].t[:, :, :n], XT[:, :, t0:t0 + n].rearrange("k p t -> p k t"),
                            reads=[xres[ti]], writes=[xts[ti % 2]])

                load(0)
                for ti, (t0, n, j) in enumerate(tiles):
                    if ti + 1 < len(tiles):
                        load(ti + 1)
                    norm_mod(xts[ti % 2], n, l, 1, j, sq, rt, tmp, hall, hcol0=t0)
                sch.barrier()

            with ExitStack() as p2:
                wfb = [sb(p2, f"wfb{q}", [128, KC, 128], BF16) for q in range(3)]
                zs = [sb(p2, f"zs{q}", [128, T], F32) for q in range(2)]
                ys = sb(p2, "ys", [128, T], F32)
                zb = [sb(p2, f"zb{q}", [128, T], BF16) for q in range(2)]
                ktk = sb(p2, "ktk", [128, NTB, 128], BF16)
                wi = 0

                def chunk_mm(ci, evac):
                    nonlocal wi
                    w = wfb[wi % 3]
                    wi += 1
                    sch.dma("pool", w.t[:], wf_in[l, ci], writes=[w], max_dma_last_dim=4096)
                    for (t0, n, j) in tiles:
                        bk = sch.bank()
                        for k in range(KC):
                            sch.op(pe, lambda k=k: P.matmul(bk.t[:, :n], w.t[:, k, :], hall.t[:, k, t0:t0 + n],
                                                            start=(k == 0), stop=(k == KC - 1)),
                                   reads=[w, hall], writes=[bk], inc=(k == KC - 1))
                        evac(bk, t0, n)

                segs = [(0, CT), (CT, T)]
                for ci in range(16):
                    z = zs[ci % 2]
                    o = zb[ci % 2]

                    def ev(bk, t0, n, z=z):
                        sch.op(act, lambda: A.copy(out=z.t[:, t0:t0 + n], in_=bk.t[:, :n]), reads=[bk], writes=[z])

                    chunk_mm(ci, ev)
                    w0 = convw_s.t[:, l, ci, 0:1]
                    w1 = convw_s.t[:, l, ci, 1:2]
                    w2 = convw_s.t[:, l, ci, 2:3]
                    for (a, b) in segs:
                        sch.op(dve, lambda: V.tensor_scalar(out=ys.t[:, a:b], in0=z.t[:, a:b], scalar1=w1, scalar2=None, op0=ALU.mult),
                               reads=[z, convw_s], writes=[ys])
                        sch.op(dve, lambda: V.scalar_tensor_tensor(out=ys.t[:, a + 1:b], in0=z.t[:, a:b - 1], scalar=w0,
                                                                   in1=ys.t[:, a + 1:b], op0=ALU.mult, op1=ALU.add),
                               reads=[z, ys, convw_s], writes=[ys])
                        sch.op(dve, lambda: V.scalar_tensor_tensor(out=ys.t[:, a:b - 1], in0=z.t[:, a + 1:b], scalar=w2,
                                                                   in1=ys.t[:, a:b - 1], op0=ALU.mult, op1=ALU.add),
                               reads=[z, ys, convw_s], writes=[ys])
                    if ci < 8:
                        sch.op(act, lambda: A.activation(out=z.t[:, :], in_=ys.t[:, :], func=AF.Silu), reads=[ys], writes=[z])
                        sch.op(dve, lambda: V.tensor_scalar(out=o.t[:, :], in0=z.t[:, :], scalar1=0.0625, scalar2=None, op0=ALU.mult),
                               reads=[z], writes=[o])
                        sch.dma("sp", QT[ci], o.t[:, :], reads=[o], writes=[RQT])
                    else:
                        m = ci - 8
                        sch.op(act, lambda: A.activation(out=o.t[:, :], in_=ys.t[:, :], func=AF.Silu), reads=[ys], writes=[o])
                        sch.dma("sp", KT[m], o.t[:, :], reads=[o], writes=[RKT])
                        for g in range(0, NTB, 8):
                            nb = min(8, NTB - g)
                            bk = sch.bank()
                            bv = bk.t[:, :].bitcast(BF16)
                            for q in range(nb):
                                tb = g + q
                                sch.op(pe, lambda q=q, tb=tb: P.transpose(bv[:, q * 128:(q + 1) * 128], o.t[:, tb * 128:(tb + 1) * 128], ident_b.t[:]),
                                       reads=[o, ident_b], writes=[bk], inc=(q == nb - 1))
                            sch.op(act, lambda: A.copy(out=ktk.t[:, g:g + nb, :], in_=bv[:, 0:nb * 128].rearrange("p (q f) -> p q f", f=128)),
                                   reads=[bk], writes=[ktk])
                        sch.dma("sp", KTOK[:, m * 128:(m + 1) * 128].rearrange("(tb p) f -> p tb f", p=128), ktk.t[:, :, :],
                                reads=[ktk], writes=[RKTOK])
                for ci in range(16, 40):
                    o = zb[ci % 2]

                    def ev(bk, t0, n, o=o):
                        sch.op(act, lambda: A.activation(out=o.t[:, t0:t0 + n], in_=bk.t[:, :n], func=AF.Sigmoid), reads=[bk], writes=[o])

                    chunk_mm(ci, ev)
                    if ci < 24:
                        sch.dma("sp", SOT[ci - 16], o.t[:, :], reads=[o], writes=[RSOT])
                    elif ci < 32:
                        sch.dma("sp", GMT[ci - 24], o.t[:, :], reads=[o], writes=[RGMT])
                    else:
                        sch.dma("sp", GAT[ci - 32], o.t[:, :], reads=[o], writes=[RGAT])
                for kind in range(4):
                    ci = 45 + kind
                    z = zs[ci % 2]

                    def ev(bk, t0, n, z=z, kind=kind):
                        sch.op(act, lambda: A.activation(out=z.t[:, t0:t0 + n], in_=bk.t[:, :n], func=AF.Identity,
                                                         bias=bg_s.t[:, l, kind:kind + 1], scale=1.0),
                               reads=[bk, bg_s], writes=[z])

                    chunk_mm(ci, ev)
                    sch.dma("sp", GROW[kind], z.t[:, :], reads=[z], writes=[RGROW])
                sch.barrier()

            with ExitStack() as p3:
                wv = sb(p3, "wv", [128, KC, D], BF16)
                vt = [sb(p3, f"vt{q}", [128, D], BF16) for q in range(2)]
                sch.dma("pool", wv.t[:], wv_in[l], writes=[wv], max_dma_last_dim=4096)
                for tb in range(NTB):
                    v_ = vt[tb % 2]
                    for hf in range(2):
                        bk = sch.bank()
                        for k in range(KC):
                            sch.op(pe, lambda k=k: P.matmul(bk.t[:, :], hall.t[:, k, tb * 128:(tb + 1) * 128], wv.t[:, k, hf * 512:(hf + 1) * 512],
                                                            start=(k == 0), stop=(k == KC - 1)),
                                   reads=[wv, hall], writes=[bk], inc=(k == KC - 1))
                        if hf == 0:
                            sch.op(act, lambda: A.copy(out=v_.t[:, 0:512], in_=bk.t[:, :]), reads=[bk], writes=[v_])
                        else:
                            sch.op(dve, lambda: V.tensor_copy(out=v_.t[:, 512:1024], in_=bk.t[:, :]), reads=[bk], writes=[v_])
                    sch.dma("sp", VTOK[tb * 128:(tb + 1) * 128, :], v_.t[:, :], reads=[v_], writes=[RVTOK])
                sch.barrier()

            with ExitStack() as p4:
                wl = sb(p4, "wl", [128, 6, KC, 128], BF16)
                wuq = sb(p4, "wuq", [128, 3, 16, 128], BF16)
                wkk = sb(p4, "wkk", [128, 2, 8, 128], BF16)
                wkv = sb(p4, "wkv", [128, 2, D], BF16)
                cqf = sb(p4, "cqf", [128, 3, 512], F32)
                sq = sb(p4, "lsq", [128, 3, 512], BF16)
                rt = sb(p4, "lrt", [128, 512], F32)
                cqn = sb(p4, "cqn", [128, 3, 512], BF16)
                ckn = sb(p4, "ckn", [128, 2, 512], BF16)
                rp = [sb(p4, f"rp{q}", [64, 2, 512], F32) for q in range(2)]
                r1 = sb(p4, "r1", [64, 512], F32)
                r2 = sb(p4, "r2", [64, 512], F32)
                ob = [sb(p4, f"ob{q}", [128, 512], BF16) for q in range(4)]
                vb = [sb(p4, f"vb{q}", [128, D], BF16) for q in range(2)]
                for c in range(5):
                    sch.dma("pool", wl.t[:, c], wf_in[l, 40 + c], writes=[wl], max_dma_last_dim=4096)
                sch.dma("pool", wl.t[:, 5], wf_in[l, 49], writes=[wl], max_dma_last_dim=4096)
                sch.dma("pool", wuq.t[:], wuq_in[l], writes=[wuq], max_dma_last_dim=4096)
                sch.dma("pool", wkk.t[:], wukvk_in[l], writes=[wkk], max_dma_last_dim=4096)
                sch.dma("pool", wkv.t[:], wukvv_in[l], writes=[wkv], max_dma_last_dim=4096)
                oi = 0

                def rope_out(pa, pb, n, rpt, dst_ap, dres):
                    nonlocal oi
                    o = ob[oi % 4]
                    oi += 1
                    sch.op(dve, lambda: V.tensor_tensor(out=r1.t[:, :n], in0=pa.t[0:64, :n], in1=rpt.t[:, 0, :n], op=ALU.mult),
                           reads=[pa, rpt], writes=[r1])
                    sch.op(dve, lambda: V.tensor_tensor(out=r2.t[:, :n], in0=pb.t[0:64, :n], in1=rpt.t[:, 1, :n], op=ALU.mult),
                           reads=[pb, rpt], writes=[r2])
                    sch.op(dve, lambda: V.tensor_tensor(out=o.t[0:64, :n], in0=r1.t[:, :n], in1=r2.t[:, :n], op=ALU.add),
                           reads=[r1, r2], writes=[o])
                    sch.dma("sp", dst_ap, o.t[0:64, :n], reads=[o], writes=[dres])

                for ti, (t0, n, j) in enumerate(tiles):
                    rpt = rp[ti % 2]
                    sch.dma("sp", rpt.t[:, :, :n], rope_in[:, :, t0:t0 + n], writes=[rpt])
                    for (c0, ncn, gsrc, dst, inv) in ((0, 3, gqa_s, cqn, 1.0 / 384), (3, 2, gkva_s, ckn, 1.0 / 256)):
                        for c in range(ncn):
                            bk = sch.bank()
                            for k in range(KC):
                                sch.op(pe, lambda k=k: P.matmul(bk.t[:, :n], wl.t[:, c0 + c, k, :], hall.t[:, k, t0:t0 + n],
                                                                start=(k == 0), stop=(k == KC - 1)),
                                       reads=[wl, hall], writes=[bk], inc=(k == KC - 1))
                            sch.op(act, lambda: A.copy(out=cqf.t[:, c, :n], in_=bk.t[:, :n]), reads=[bk], writes=[cqf])
                        rms_rstd((sq, rt), lambda c: cqf.t[:, c, :n], ncn, n, inv, cqf)
                        for c in range(ncn):
                            sch.op(dve, lambda c=c: V.scalar_tensor_tensor(out=dst.t[:, c, :n], in0=cqf.t[:, c, :n], scalar=gsrc.t[:, l, c:c + 1],
                                                                           in1=rt.t[:, :n], op0=ALU.mult, op1=ALU.mult),
                                   reads=[cqf, gsrc, rt], writes=[dst])
                    pa = sch.bank()
                    pb = sch.bank()
                    for k in range(KC):
                        sch.op(pe, lambda k=k: P.matmul(pa.t[0:64, :n], wl.t[:, 5, k, 0:64], hall.t[:, k, t0:t0 + n], start=(k == 0), stop=(k == KC - 1)),
                               reads=[wl, hall], writes=[pa], inc=(k == KC - 1))
                    for k in range(KC):
                        sch.op(pe, lambda k=k: P.matmul(pb.t[0:64, :n], wl.t[:, 5, k, 64:128], hall.t[:, k, t0:t0 + n], start=(k == 0), stop=(k == KC - 1)),
                               reads=[wl, hall], writes=[pb], inc=(k == KC - 1))
                    rope_out(pa, pb, n, rpt, KR[:, t0:t0 + n], RKR)
                    for hd in range(8):
                        bk = sch.bank()
                        for c in range(3):
                            sch.op(pe, lambda c=c: P.matmul(bk.t[:, :n], wuq.t[:, c, 2 * hd, :], cqn.t[:, c, :n], start=(c == 0), stop=(c == 2)),
                                   reads=[wuq, cqn], writes=[bk], inc=(c == 2))
                        o = ob[oi % 4]
                        oi += 1
                        sch.op(act, lambda: A.copy(out=o.t[:, :n], in_=bk.t[:, :n]), reads=[bk], writes=[o])
                        sch.dma("sp", QN[hd, :, t0:t0 + n], o.t[:, :n], reads=[o], writes=[RQN])
                        pa = sch.bank()
                        pb = sch.bank()
                        for c in range(3):
                            sch.op(pe, lambda c=c: P.matmul(pa.t[0:64, :n], wuq.t[:, c, 2 * hd + 1, 0:64], cqn.t[:, c, :n], start=(c == 0), stop=(c == 2)),
                                   reads=[wuq, cqn], writes=[pa], inc=(c == 2))
                        for c in range(3):
                            sch.op(pe, lambda c=c: P.matmul(pb.t[0:64, :n], wuq.t[:, c, 2 * hd + 1, 64:128], cqn.t[:, c, :n], start=(c == 0), stop=(c == 2)),
                                   reads=[wuq, cqn], writes=[pb], inc=(c == 2))
                        rope_out(pa, pb, n, rpt, QR[hd, :, t0:t0 + n], RQR)
                        bk = sch.bank()
                        for c in range(2):
                            sch.op(pe, lambda c=c: P.matmul(bk.t[:, :n], wkk.t[:, c, hd, :], ckn.t[:, c, :n], start=(c == 0), stop=(c == 1)),
                                   reads=[wkk, ckn], writes=[bk], inc=(c == 1))
                        o = ob[oi % 4]
                        oi += 1
                        sch.op(dve, lambda: V.tensor_copy(out=o.t[:, :n], in_=bk.t[:, :n]), reads=[bk], writes=[o])
                        sch.dma("sp", KN[hd, :, t0:t0 + n], o.t[:, :n], reads=[o], writes=[RKN])
                    for b in range(n // 128):
                        v_ = vb[b % 2]
                        for hf in range(2):
                            bk = sch.bank()
                            for c in range(2):
                                sch.op(pe, lambda c=c: P.matmul(bk.t[:, :], ckn.t[:, c, b * 128:(b + 1) * 128], wkv.t[:, c, hf * 512:(hf + 1) * 512],
                                                                start=(c == 0), stop=(c == 1)),
                                       reads=[wkv, ckn], writes=[bk], inc=(c == 1))
                            sch.op(act, lambda: A.copy(out=v_.t[:, hf * 512:(hf + 1) * 512], in_=bk.t[:, :]), reads=[bk], writes=[v_])
                        sch.dma("sp", VA[t0 + b * 128:t0 + (b + 1) * 128, :], v_.t[:, :], reads=[v_], writes=[RVA])
                sch.barrier()
            sch.barrier()

    def mlstm_phase(l):
        with ExitStack() as ps:
            RS = [sb(ps, f"RS{d}", [64, T], F32) for d in range(2)]
            ABC = sb(ps, "ABC", [128, 2, 4, NCH], F32)
            with ExitStack() as pg:
                W1 = sb(pg, "W1", [64, T], F32)
                NB = sb(pg, "NB", [64, T], F32)
                U = sb(pg, "U", [64, T], F32)
                G = sb(pg, "G", [64, T], F32)
                GE = sb(pg, "GE", [128, NCH], F32)
                GP = sb(pg, "GP", [128, NCH], F32)
                AA = sb(pg, "AA", [128, NCH], F32)
                TM = sb(pg, "TM", [64, T], F32)
                sch.op(dve, lambda: V.memset(AA.t[:], 0.0), writes=[AA])
                for d in range(2):
                    sch.dma("sp", W1.t[:, :], GROW[2 * d + 1, 0:64, :], reads=[RGROW], writes=[W1])
                    sch.dma("sp", U.t[:, :], GROW[2 * d, 0:64, :], reads=[RGROW], writes=[U])
                    sch.op(act, lambda: A.activation(out=W1.t[:, :], in_=W1.t[:, :], func=AF.Exp, scale=-1.0), reads=[W1], writes=[W1])
                    sch.op(act, lambda: A.activation(out=W1.t[:, :], in_=W1.t[:, :], func=AF.Ln, bias=ones_f.t[0:64, 0:1], scale=1.0),
                           reads=[W1, ones_f], writes=[W1])
                    if d == 0:
                        scans = [(slice(0, T), None)]
                    else:
                        scans = [(slice(CT - 1, None, -1), None), (slice(T - 1, CT - 1, -1), 0)]
                    for (sl, init_col) in scans:
                        nn = len(range(T)[sl])
                        init = 0.0 if init_col is None else NB.t[:, init_col:init_col + 1]
                        sch.op(dve, lambda: V.tensor_tensor_scan(out=NB.t[:, sl], data0=ones_f.t[0:64, 0:1].to_broadcast([64, nn]),
                                                                 data1=W1.t[:, sl], initial=init, op0=ALU.mult, op1=ALU.add),
                               reads=[W1, ones_f, NB], writes=[NB])
                    sch.op(dve, lambda: V.tensor_tensor(out=U.t[:, :], in0=U.t[:, :], in1=NB.t[:, :], op=ALU.add), reads=[U, NB], writes=[U])
                    for (sl, init_col) in scans:
                        init = 0.0 if init_col is None else G.t[:, init_col:init_col + 1]
                        sch.op(dve, lambda: V.tensor_tensor_scan(out=G.t[:, sl], data0=U.t[:, sl], data1=U.t[:, sl], initial=init,
                                                                 op0=ALU.max, op1=ALU.max),
                               reads=[U, G], writes=[G])
                    Gv = G.t[:, :].rearrange("p (c s) -> p c s", s=64)
                    if d == 0:
                        sch.op(dve, lambda: V.tensor_copy(out=GE.t[0:64, :], in_=Gv[:, :, 63]), reads=[G], writes=[GE])
                        sch.op(dve, lambda: V.memset(GP.t[0:64, 0:1], 0.0), writes=[GP])
                        sch.op(dve, lambda: V.tensor_copy(out=GP.t[0:64, 1:NCH], in_=GE.t[0:64, 0:NCH - 1]), reads=[GE], writes=[GP])
                    else:
                        nck = CT // 64
                        sch.op(dve, lambda: V.tensor_copy(out=GE.t[0:64, :], in_=Gv[:, :, 0]), reads=[G], writes=[GE])
                        sch.op(dve, lambda: V.memset(GP.t[0:64, nck - 1:nck], 0.0), writes=[GP])
                        sch.op(dve, lambda: V.tensor_copy(out=GP.t[0:64, 0:nck - 1], in_=GE.t[0:64, 1:nck]), reads=[GE], writes=[GP])
                        sch.op(dve, lambda: V.tensor_copy(out=GP.t[0:64, NCH - 1:NCH], in_=GE.t[0:64, 0:1]), reads=[GE], writes=[GP])
                        sch.op(dve, lambda: V.tensor_copy(out=GP.t[0:64, nck:NCH - 1], in_=GE.t[0:64, nck + 1:NCH]), reads=[GE], writes=[GP])
                    GEb = GE.t[0:64, :].unsqueeze(2).to_broadcast([64, NCH, 64])
                    TMv = TM.t[:, :].rearrange("p (c s) -> p c s", s=64)
                    sch.op(dve, lambda: V.tensor_tensor(out=TMv[0:32], in0=U.t[0:32, :].rearrange("p (c s) -> p c s", s=64), in1=GEb[0:32], op=ALU.subtract),
                           reads=[U, GE], writes=[TM])
                    sch.op(dve, lambda: V.tensor_tensor(out=TMv[32:64], in0=NB.t[32:64, :].rearrange("p (c s) -> p c s", s=64),
                                                        in1=GE.t[32:64, :].unsqueeze(2).to_broadcast([32, NCH, 64]), op=ALU.subtract),
                           reads=[NB, GE], writes=[TM])
                    sch.op(act, lambda: A.activation(out=RS[d].t[:, :], in_=TM.t[:, :], func=AF.Exp), reads=[TM], writes=[RS[d]])
                    sch.op(dve, lambda: V.tensor_tensor(out=AA.t[0:64, :], in0=GP.t[0:64, :], in1=GE.t[0:64, :], op=ALU.subtract),
                           reads=[GP, GE], writes=[AA])
                    sch.op(act, lambda: A.activation(out=AA.t[0:64, :], in_=AA.t[0:64, :], func=AF.Exp), reads=[AA], writes=[AA])
                    for hd in range(4):
                        bk = sch.bank()
                        sch.op(pe, lambda: P.matmul(bk.t[:, 0:NCH], sel_f.t[:, hd, :], AA.t[:, :], start=True, stop=True),
                               reads=[sel_f, AA], writes=[bk])
                        sch.op(act, lambda: A.copy(out=ABC.t[:, d, hd, :], in_=bk.t[:, 0:NCH]), reads=[bk], writes=[ABC])
                sch.barrier()

            with ExitStack() as ph:
                qT = sb(ph, "qT", [128, 2, T], BF16)
                kT = sb(ph, "kT", [128, 2, T], BF16)
                ktok = sb(ph, "ktok", [64, NCH, 256], BF16)
                vext = sb(ph, "vext", [64, NCH, 258], BF16)
                cols = [sb(ph, f"cols{q}", [64, 64], F32) for q in range(2)]
                Cf = sb(ph, "Cf", [128, 2, 258], F32)
                Cb = sb(ph, "Cb", [128, 2, 258], BF16)
                sTm = [sb(ph, f"sTm{q}", [64, 64], BF16) for q in range(2)]
                t1 = [sb(ph, f"t1{q}", [64, 258], F32) for q in range(2)]
                nd = [sb(ph, f"nd{q}", [64, 258], F32) for q in range(2)]
                dn = [sb(ph, f"dn{q}", [64, 2], F32) for q in range(2)]
                ho = [sb(ph, f"ho{q}", [64, 256], F32) for q in range(2)]
                kw = [sb(ph, f"kw{q}", [64, 256], BF16) for q in range(2)]
                sch.op(dve, lambda: V.memset(vext.t[:, :, 256:258], 1.0), writes=[vext])
                order = [list(range(NCH)), list(range(CT // 64 - 1, -1, -1)) + list(range(NCH - 1, CT // 64 - 1, -1))]
                step = 0
                for hd in range(4):
                    sch.dma("sp", qT.t[:, :, :], QT[2 * hd:2 * hd + 2].rearrange("k p t -> p k t"), reads=[RQT], writes=[qT])
                    sch.dma("sp", kT.t[:, :, :], KT[2 * hd:2 * hd + 2].rearrange("k p t -> p k t"), reads=[RKT], writes=[kT])
                    sch.dma("sp", ktok.t[:, :, :], KTOK[:, hd * 256:(hd + 1) * 256].rearrange("(c p) f -> p c f", p=64), reads=[RKTOK], writes=[ktok])
                    sch.dma("sp", vext.t[:, :, 0:256], VTOK[:, hd * 256:(hd + 1) * 256].rearrange("(c p) f -> p c f", p=64), reads=[RVTOK], writes=[vext])
                    for d in range(2):
                        HD = HF if d == 0 else HB
                        RH = RHF if d == 0 else RHB
                        sch.op(dve, lambda: V.memset(Cf.t[:], 0.0), writes=[Cf])
                        sch.op(dve, lambda: V.memset(Cb.t[:], 0.0), writes=[Cb])
                        for c in order[d]:
                            q = step % 2
                            step += 1
                            cs = slice(c * 64, (c + 1) * 64)
                            bk = sch.bank()
                            sch.op(pe, lambda: P.transpose(bk.t[0:64, 0:64], RS[d].t[:, cs], ident_f.t[0:64, 0:64]),
                                   reads=[RS[d], ident_f], writes=[bk])
                            cl = cols[q]
                            sch.op(act, lambda: A.copy(out=cl.t[:, :], in_=bk.t[0:64, 0:64]), reads=[bk], writes=[cl])
                            WEc = cl.t[:, hd:hd + 1]
                            EMc = cl.t[:, 32 + hd:32 + hd + 1]
                            ps_ = sch.bank()
                            for kc in range(2):
                                sch.op(pe, lambda kc=kc: P.matmul(ps_.t[0:64, 0:64], kT.t[:, kc, cs], qT.t[:, kc, cs], start=(kc == 0), stop=(kc == 1)),
                                       reads=[kT, qT], writes=[ps_], inc=(kc == 1))
                            sm = sTm[q]
                            sch.op(dve, lambda: V.scalar_tensor_tensor(out=sm.t[:, :], in0=ps_.t[0:64, 0:64], scalar=WEc, in1=tri_f.t[:, d, :],
                                                                       op0=ALU.mult, op1=ALU.mult),
                                   reads=[ps_, cl, tri_f], writes=[sm])
                            pi = sch.bank()
                            for kc in range(2):
                                sch.op(pe, lambda kc=kc: P.matmul(pi.t[0:64, 0:258], qT.t[:, kc, cs], Cb.t[:, kc, :], start=(kc == 0), stop=(kc == 1)),
                                       reads=[qT, Cb], writes=[pi], inc=(kc == 1))
                            pn = sch.bank()
                            sch.op(pe, lambda: P.matmul(pn.t[0:64, 0:258], sm.t[:, :], vext.t[:, c, :], start=True, stop=True),
                                   reads=[sm, vext], writes=[pn])
                            t_ = t1[q]
                            sch.op(act, lambda: A.activation(out=t_.t[:, :], in_=pi.t[0:64, 0:258], func=AF.Copy, scale=ABC.t[0:64, d, hd, c:c + 1]),
                                   reads=[pi, ABC], writes=[t_])
                            n_ = nd[q]
                            sch.op(dve, lambda: V.tensor_tensor(out=n_.t[:, :], in0=pn.t[0:64, 0:258], in1=t_.t[:, :], op=ALU.add),
                                   reads=[pn, t_], writes=[n_])
                            d_ = dn[q]
                            sch.op(act, lambda: A.activation(out=d_.t[:, 0:1], in_=n_.t[:, 256:257], func=AF.Abs), reads=[n_], writes=[d_])
                            sch.op(dve, lambda: V.tensor_tensor(out=d_.t[:, 0:1], in0=d_.t[:, 0:1], in1=EMc, op=ALU.max),
                                   reads=[d_, cl], writes=[d_])
                            sch.op(dve, lambda: V.reciprocal(out=d_.t[:, 1:2], in_=d_.t[:, 0:1]), reads=[d_], writes=[d_])
                            h_ = ho[q]
                            sch.op(act, lambda: A.activation(out=h_.t[:, :], in_=n_.t[:, 0:256], func=AF.Copy, scale=d_.t[:, 1:2]),
                                   reads=[n_, d_], writes=[h_])
                            sch.dma("sp", HD[c * 64:(c + 1) * 64, hd * 256:(hd + 1) * 256], h_.t[:, :], reads=[h_], writes=[RH])
                            k_ = kw[q]
                            sch.op(act, lambda: A.activation(out=k_.t[:, :], in_=ktok.t[:, c, :], func=AF.Copy, scale=WEc),
                                   reads=[ktok, cl], writes=[k_])
                            for kc in range(2):
                                pu = sch.bank()
                                sch.op(pe, lambda kc=kc: P.matmul(pu.t[:, 0:258], k_.t[:, kc * 128:(kc + 1) * 128], vext.t[:, c, :], start=True, stop=True),
                                       reads=[k_, vext], writes=[pu])
                                sch.op(dve, lambda kc=kc: V.scalar_tensor_tensor(out=Cf.t[:, kc, :], in0=Cf.t[:, kc, :], scalar=ABC.t[:, d, hd, c:c + 1],
                                                                                 in1=pu.t[:, 0:258], op0=ALU.mult, op1=ALU.add),
                                       reads=[Cf, ABC, pu], writes=[Cf])
                                sch.op(act, lambda kc=kc: A.copy(out=Cb.t[:, kc, :], in_=Cf.t[:, kc, :]), reads=[Cf], writes=[Cb])
                sch.barrier()
            sch.barrier()

    def attn_phase(l):
        sc = float(192 ** -0.5)
        with ExitStack() as ps:
            krT = sb(ps, "krT", [64, T], BF16)
            knT = sb(ps, "knT", [128, T], BF16)
            vh = sb(ps, "vh", [128, NTB, 128], BF16)
            qn = [sb(ps, f"qn{q}", [128, 512], BF16) for q in range(2)]
            qr = [sb(ps, f"qr{q}", [64, 512], BF16) for q in range(2)]
            pt = [sb(ps, f"pt{q}", [128, 512], BF16) for q in range(3)]
            rd = sb(ps, "rd", [128, 512], F32)
            oo = [sb(ps, f"oo{q}", [128, 512], BF16) for q in range(2)]
            sch.dma("sp", krT.t[:, :], KR[:, :], reads=[RKR], writes=[krT])
            it = 0
            pti = 0
            for hd in range(8):
                sch.dma("sp", knT.t[:, :], KN[hd], reads=[RKN], writes=[knT])
                sch.dma("sp", vh.t[:, :, :], VA[:, hd * 128:(hd + 1) * 128].rearrange("(tb p) f -> p tb f", p=128), reads=[RVA], writes=[vh])
                for ti, (t0, n, j) in enumerate(tiles):
                    q_n = qn[it % 2]
                    q_r = qr[it % 2]
                    o_ = oo[it % 2]
                    it += 1
                    sch.dma("sp", q_n.t[:, :n], QN[hd, :, t0:t0 + n], reads=[RQN], writes=[q_n])
                    sch.dma("sp", q_r.t[:, :n], QR[hd, :, t0:t0 + n], reads=[RQR], writes=[q_r])
                    kbs = list(range(CT // 128)) if j == 1 else list(range(NTB))
                    po = sch.bank(hold=True)
                    pd = sch.bank(hold=True)

                    def scores(kb):
                        b = sch.bank()
                        sch.op(pe, lambda: P.matmul(b.t[:, :n], knT.t[:, kb * 128:(kb + 1) * 128], q_n.t[:, :n], start=True, stop=False),
                               reads=[knT, q_n], writes=[b], inc=False)
                        sch.op(pe, lambda: P.matmul(b.t[:, :n], krT.t[:, kb * 128:(kb + 1) * 128], q_r.t[:, :n], start=False, stop=True),
                               reads=[krT, q_r], writes=[b])
                        return b

                    nxt = scores(kbs[0])
                    for ii, kb in enumerate(kbs):
                        b = nxt
                        p_ = pt[pti % 3]
                        pti += 1
                        sch.op(act, lambda: A.activation(out=p_.t[:, :n], in_=b.t[:, :n], func=AF.Exp, scale=sc), reads=[b], writes=[p_])
                        if ii + 1 < len(kbs):
                            nxt = scores(kbs[ii + 1])
                        last = ii == len(kbs) - 1
                        sch.op(pe, lambda: P.matmul(po.t[:, :n], vh.t[:, kb, :], p_.t[:, :n], start=(ii == 0), stop=last),
                               reads=[vh, p_], writes=[po], inc=last)
                        sch.op(pe, lambda: P.matmul(pd.t[:, :n], ones_b.t[:, :], p_.t[:, :n], start=(ii == 0), stop=last),
                               reads=[ones_b, p_], writes=[pd], inc=True)
                    sch.op(dve, lambda: V.reciprocal(out=rd.t[:, :n], in_=pd.t[:, :n]), reads=[pd], writes=[rd])
                    sch.op(dve, lambda: V.tensor_tensor(out=o_.t[:, :n], in0=po.t[:, :n], in1=rd.t[:, :n], op=ALU.mult), reads=[po, rd], writes=[o_])
                    sch.release(po)
                    sch.release(pd)
                    sch.dma("sp", HAT[hd, :, t0:t0 + n], o_.t[:, :n], reads=[o_], writes=[RHAT])
            sch.barrier()

    def merge_phase(l):
        with ExitStack() as ps:
            wbm = sb(ps, "wbm", [128, KC, D], BF16)
            wba = sb(ps, "wba", [128, KC, D], BF16)
            wout = sb(ps, "wout", [128, KC, D], BF16)
            sch.dma("pool", wbm.t[:], wbm_in[l], writes=[wbm], max_dma_last_dim=4096)
            sch.dma("pool", wba.t[:], wba_in[l], writes=[wba], max_dma_last_dim=4096)
            sch.dma("pool", wout.t[:], wout_in[l], writes=[wout], max_dma_last_dim=4096)
            hf = sb(ps, "hf", [128, 4, D], F32)
            hb = sb(ps, "hb", [128, 4, D], F32)
            hn = sb(ps, "hn", [128, 4, D], BF16)
            junk = sb(ps, "junk", [128, 256], BF16)
            ssq = sb(ps, "ssq", [128, 16], F32)
            so = sb(ps, "so", [128, KC, 512], BF16)
            gm = sb(ps, "gm", [128, KC, 512], BF16)
            ga = sb(ps, "ga", [128, KC, 512], BF16)
            hat = sb(ps, "hat", [128, KC, 512], BF16)
            xt = sb(ps, "mxt", [128, KC, 512], F32)
            hmT = sb(ps, "hmT", [128, KC, 512], BF16)
            tm = sb(ps, "tm", [128, KC, 512], F32)
            t2 = sb(ps, "t2", [128, 512], F32)
            tb_ = sb(ps, "tb", [128, KC, 512], BF16)
            for ti, (t0, n, j) in enumerate(tiles):
                nb = n // 128
                sch.dma("sp", hf.t[:, 0:nb, :], HF[t0:t0 + n, :].rearrange("(b p) f -> p b f", p=128), reads=[RHF], writes=[hf])
                sch.dma("sp", hb.t[:, 0:nb, :], HB[t0:t0 + n, :].rearrange("(b p) f -> p b f", p=128), reads=[RHB], writes=[hb])
                sch.dma("sp", so.t[:, :, :n], SOT[:, :, t0:t0 + n].rearrange("k p t -> p k t"), reads=[RSOT], writes=[so])
                sch.dma("sp", gm.t[:, :, :n], GMT[:, :, t0:t0 + n].rearrange("k p t -> p k t"), reads=[RGMT], writes=[gm])
                sch.dma("sp", ga.t[:, :, :n], GAT[:, :, t0:t0 + n].rearrange("k p t -> p k t"), reads=[RGAT], writes=[ga])
                sch.dma("sp", hat.t[:, :, :n], HAT[:, :, t0:t0 + n].rearrange("k p t -> p k t"), reads=[RHAT], writes=[hat])
                sch.dma("sp", xt.t[:, :, :n], XT[:, :, t0:t0 + n].rearrange("k p t -> p k t"), reads=[xres[ti]], writes=[xt])
                sch.op(dve, lambda: V.tensor_tensor(out=hf.t[:, 0:nb, :], in0=hf.t[:, 0:nb, :], in1=hb.t[:, 0:nb, :], op=ALU.add),
                       reads=[hf, hb], writes=[hf])
                for b in range(nb):
                    for hd in range(4):
                        sch.op(act, lambda b=b, hd=hd: A.activation(out=junk.t[:, :], in_=hf.t[:, b, hd * 256:(hd + 1) * 256], func=AF.Square,
                                                                    accum_out=ssq.t[:, b * 4 + hd:b * 4 + hd + 1]),
                               reads=[hf], writes=[junk, ssq])
                sch.op(act, lambda: A.activation(out=ssq.t[:, 0:4 * nb], in_=ssq.t[:, 0:4 * nb], func=AF.Sqrt, scale=1.0 / 256, bias=eps_s.t[:, 0:1]),
                       reads=[ssq, eps_s], writes=[ssq])
                sch.op(dve, lambda: V.reciprocal(out=ssq.t[:, 0:4 * nb], in_=ssq.t[:, 0:4 * nb]), reads=[ssq], writes=[ssq])
                for b in range(nb):
                    for hd in range(4):
                        sch.op(dve, lambda b=b, hd=hd: V.tensor_scalar(out=hn.t[:, b, hd * 256:(hd + 1) * 256], in0=hf.t[:, b, hd * 256:(hd + 1) * 256],
                                                                       scalar1=ssq.t[:, b * 4 + hd:b * 4 + hd + 1], scalar2=None, op0=ALU.mult),
                               reads=[hf, ssq], writes=[hn])
                for m in range(KC):
                    bk = sch.bank()
                    bv = bk.t[:, :].bitcast(BF16)
                    for b in range(nb):
                        sch.op(pe, lambda b=b: P.transpose(bv[:, b * 128:(b + 1) * 128], hn.t[:, b, m * 128:(m + 1) * 128], ident_b.t[:]),
                               reads=[hn, ident_b], writes=[bk], inc=(b == nb - 1))
                    sch.op(dve, lambda: V.scalar_tensor_tensor(out=hmT.t[:, m, :n], in0=bv[:, 0:n], scalar=gmh_s.t[:, l, m:m + 1], in1=so.t[:, m, :n],
                                                               op0=ALU.mult, op1=ALU.mult),
                           reads=[bk, gmh_s, so], writes=[hmT])
                for m in range(KC):
                    bk = sch.bank()
                    for k in range(KC):
                        sch.op(pe, lambda k=k: P.matmul(bk.t[:, :n], wbm.t[:, k, m * 128:(m + 1) * 128], hmT.t[:, k, :n], start=(k == 0), stop=(k == KC - 1)),
                               reads=[wbm, hmT], writes=[bk], inc=(k == KC - 1))
                    sch.op(dve, lambda: V.tensor_tensor(out=tm.t[:, m, :n], in0=bk.t[:, :n], in1=gm.t[:, m, :n], op=ALU.mult),
                           reads=[bk, gm], writes=[tm])
                    bk2 = sch.bank()
                    for k in range(KC):
                        sch.op(pe, lambda k=k: P.matmul(bk2.t[:, :n], wba.t[:, k, m * 128:(m + 1) * 128], hat.t[:, k, :n], start=(k == 0), stop=(k == KC - 1)),
                               reads=[wba, hat], writes=[bk2], inc=(k == KC - 1))
                    sch.op(dve, lambda: V.tensor_tensor(out=t2.t[:, :n], in0=bk2.t[:, :n], in1=ga.t[:, m, :n], op=ALU.mult),
                           reads=[bk2, ga], writes=[t2])
                    sch.op(dve, lambda: V.tensor_tensor(out=tb_.t[:, m, :n], in0=tm.t[:, m, :n], in1=t2.t[:, :n], op=ALU.add),
                           reads=[tm, t2], writes=[tb_])
                for m in range(KC):
                    bk = sch.bank()
                    for k in range(KC):
                        sch.op(pe, lambda k=k: P.matmul(bk.t[:, :n], wout.t[:, k, m * 128:(m + 1) * 128], tb_.t[:, k, :n], start=(k == 0), stop=(k == KC - 1)),
                               reads=[wout, tb_], writes=[bk], inc=(k == KC - 1))
                    sch.op(dve, lambda: V.scalar_tensor_tensor(out=xt.t[:, m, :n], in0=bk.t[:, :n], scalar=GT.t[:, l, 1, m, j:j + 1], in1=xt.t[:, m, :n],
                                                               op0=ALU.mult, op1=ALU.add),
                           reads=[bk, GT, xt], writes=[xt])
                sch.dma("sp", XT[:, :, t0:t0 + n].rearrange("k p t -> p k t"), xt.t[:, :, :n], reads=[xt], writes=[xres[ti]])
            sch.barrier()

    def final_phase():
        with ExitStack() as ps:
            xts = [sb(ps, f"fxt{q}", [128, KC, 512], F32) for q in range(2)]
            sq = sb(ps, "fsq", [128, KC, 512], BF16)
            rt = sb(ps, "frt", [128, 512], F32)
            ot = [sb(ps, f"fot{q}", [128, KC, 512], F32) for q in range(2)]
            for ti, (t0, n, j) in enumerate(tiles):
                if j == 1:
                    continue
                xt = xts[ti % 2]
                o = ot[ti % 2]
                sch.dma("sp", xt.t[:, :, :n], XT[:, :, t0:t0 + n].rearrange("k p t -> p k t"), reads=[xres[ti]], writes=[xt])
                rms_rstd((sq, rt), lambda c: xt.t[:, c, :n], KC, n, 1.0 / D, xt)
                for k in range(KC):
                    sch.op(dve, lambda k=k: V.scalar_tensor_tensor(out=o.t[:, k, :n], in0=xt.t[:, k, :n], scalar=gfin_s.t[:, k:k + 1], in1=rt.t[:, :n],
                                                                   op0=ALU.mult, op1=ALU.mult),
                           reads=[xt, gfin_s, rt], writes=[o])
                sch.dma("sp", outT[:, :, t0 - CT:t0 - CT + n].rearrange("k p t -> p k t"), o.t[:, :, :n], reads=[o], writes=[ROUT])
            sch.barrier()

    for l in range(L):
        ffn_phase(l, 0, first=(l == 0))
        inproj_phase(l)
        mlstm_phase(l)
        attn_phase(l)
        merge_phase(l)
        ffn_phase(l, 2, first=False)
    final_phase()
    es.close()
    build.ninst = sch.ninst
    return nc


def _prep_shared(inp, L, S):
    f = np.float32
    T = CT + S
    out = {}
    w_ada = np.asarray(inp["w_ada"], f)[:L]
    out["wada"] = np.ascontiguousarray(w_ada.reshape(L, KC, 128, 72, 128).transpose(0, 3, 2, 1, 4))
    out["bada"] = np.ascontiguousarray(np.asarray(inp["b_ada"], f)[:L].reshape(L, 72, 128).transpose(2, 0, 1))
    gn = np.stack([np.asarray(inp[k], f)[:L] for k in ("g_n1", "g_n2", "g_n3")], axis=1)
    out["gn"] = np.ascontiguousarray(gn.reshape(L, 3, KC, 128).transpose(3, 0, 1, 2))
    out["gfin"] = np.ascontiguousarray(np.asarray(inp["g_final"], f).reshape(KC, 128).T)
    for i, (ku, kd) in enumerate((("w_ff1_up", "w_ff1_dn"), ("w_ff2_up", "w_ff2_dn"))):
        wu = np.asarray(inp[ku], f)[:L]
        wu = wu.reshape(L, KC, 128, 2, NF, 128)
        out[f"wup{i}"] = np.ascontiguousarray(wu.transpose(0, 4, 2, 1, 3, 5))
        wd = np.asarray(inp[kd], f)[:L].reshape(L, NF, 128, D)
        out[f"wdn{i}"] = np.ascontiguousarray(wd.transpose(0, 2, 1, 3))
    w_in = np.asarray(inp["w_in"], f)[:L]
    o = 0
    offs = {}
    for name, n in (("m_q", 1024), ("m_k", 1024), ("m_v", 1024), ("m_o", 1024), ("m_gate", 16), ("a_cq", 384), ("a_ckv", 256), ("a_kr", 64), ("br_gate", 2048)):
        offs[name] = (o, n)
        o += n

    def grp(name):
        a, n = offs[name]
        return w_in[:, :, a:a + n]

    deint = np.concatenate([np.arange(0, 64, 2), np.arange(1, 64, 2)])
    swp = np.concatenate([deint[32:], deint[:32]])
    gates = grp("m_gate")
    grep = np.zeros((L, D, 4, 128), f)
    for d in range(2):
        for ki in range(2):
            for hd in range(4):
                for qd in range(2):
                    grep[:, :, d * 2 + ki, 32 * qd + hd] = gates[:, :, d * 8 + ki * 4 + hd]
    kr = grp("a_kr")
    kr2 = np.concatenate([kr[:, :, deint], kr[:, :, swp]], axis=2)
    cols = np.concatenate([grp("m_q"), grp("m_k"), grp("m_o"), grp("br_gate"), grp("a_cq"), grp("a_ckv"), grep.reshape(L, D, 512), kr2], axis=2)
    assert cols.shape[2] == NWF * 128
    out["wf"] = np.ascontiguousarray(cols.reshape(L, KC, 128, NWF, 128).transpose(0, 3, 2, 1, 4))
    out["wv"] = np.ascontiguousarray(grp("m_v").reshape(L, KC, 128, D).transpose(0, 2, 1, 3))
    wc = np.asarray(inp["w_conv"], f)[:L]
    out["convw"] = np.ascontiguousarray(wc.reshape(L, 3, 16, 128).transpose(3, 0, 2, 1))
    bgl = np.asarray(inp["b_gate"], f)[:L]
    bg = np.zeros((128, L, 4), f)
    for d in range(2):
        for ki in range(2):
            for hd in range(4):
                for qd in range(2):
                    bg[32 * qd + hd, :, d * 2 + ki] = bgl[:, d * 8 + ki * 4 + hd]
    out["bg"] = bg
    out["gmh"] = np.ascontiguousarray(np.asarray(inp["g_mh"], f)[:L].reshape(L, KC, 128).transpose(2, 0, 1))
    out["gqa"] = np.ascontiguousarray(np.asarray(inp["g_qa"], f)[:L].reshape(L, 3, 128).transpose(2, 0, 1))
    out["gkva"] = np.ascontiguousarray(np.asarray(inp["g_kva"], f)[:L].reshape(L, 2, 128).transpose(2, 0, 1))
    wuq = np.asarray(inp["w_uq"], f)[:L].reshape(L, 384, 8, 192)
    chunks = []
    for hd in range(8):
        chunks.append(wuq[:, :, hd, 0:128])
        r = wuq[:, :, hd, 128:192]
        chunks.append(np.concatenate([r[:, :, deint], r[:, :, swp]], axis=2))
    wuq2 = np.stack(chunks, axis=2)
    out["wuq"] = np.ascontiguousarray(wuq2.reshape(L, 3, 128, 16, 128).transpose(0, 2, 1, 3, 4))
    wukv = np.asarray(inp["w_ukv"], f)[:L].reshape(L, 256, 8, 256)
    out["wukvk"] = np.ascontiguousarray(wukv[:, :, :, 0:128].reshape(L, 2, 128, 8, 128).transpose(0, 2, 1, 3, 4))
    out["wukvv"] = np.ascontiguousarray(wukv[:, :, :, 128:256].reshape(L, 2, 128, 1024).transpose(0, 2, 1, 3))
    for k, kk in (("w_bm", "wbm"), ("w_ba", "wba"), ("w_out", "wout")):
        out[kk] = np.ascontiguousarray(np.asarray(inp[k], f)[:L].reshape(L, KC, 128, D).transpose(0, 2, 1, 3))
    inv = (10000.0 ** (-np.arange(0, 32, 2, dtype=np.float32) / 32)).astype(f)
    t = np.arange(S)
    row = (t // 64).astype(f)
    col = (t % 64).astype(f)
    ang = np.concatenate([row[:, None] * inv, col[:, None] * inv], axis=-1)
    cos = np.cos(ang).astype(f).T
    sin = np.sin(ang).astype(f).T
    rope = np.zeros((64, 2, T), f)
    rope[:, 0, :CT] = 1.0
    rope[0:32, 0, CT:] = cos
    rope[32:64, 0, CT:] = cos
    rope[0:32, 1, CT:] = -sin
    rope[32:64, 1, CT:] = sin
    out["rope"] = rope
    out["ident"] = np.eye(128, dtype=f)
    sel = np.zeros((128, 4, 128), f)
    for hd in range(4):
        sel[hd, hd, :] = 1.0
    out["sel"] = sel
    tri = np.zeros((64, 2, 64), f)
    s_ = np.arange(64)[:, None]
    j_ = np.arange(64)[None, :]
    tri[:, 0, :] = (s_ <= j_)
    tri[:, 1, :] = (s_ >= j_)
    out["tri"] = tri
    return out


def _run(inp, L, S, dbg=False):
    f = np.float32
    shared = _prep_shared(inp, L, S)
    x = np.asarray(inp["x"], f)
    ctx = np.asarray(inp["ctx"], f)
    c = np.asarray(inp["c"], f)
    cc = np.asarray(inp["c_ctx"], f)
    B = x.shape[0]
    T = CT + S
    in_maps = []
    for core in range(8):
        b = core % B
        cat = np.concatenate([ctx[b], x[b]], axis=0)
        m = dict(shared)
        m["xT"] = np.ascontiguousarray(cat.T.reshape(KC, 128, T))
        scT = np.stack([c[b].reshape(KC, 128).T, cc.reshape(KC, 128).T], axis=-1)
        m["scT"] = np.ascontiguousarray(scT)
        in_maps.append(m)
    nc = build(L, S, dbg)
    res = run_bass_kernel_spmd(nc, in_maps, core_ids=list(range(8)))
    outs = []
    for b in range(B):
        o = res.results[b]["outT"]
        outs.append(np.ascontiguousarray(o.reshape(D, S).T))
    out = np.stack(outs, axis=0).astype(f)
    if dbg:
        return out, res
    return out


def kernel(**inputs):
    return _run(inputs, 4, 4096)
```

```python
import numpy as np
from contextlib import ExitStack
import concourse.bass as bass
import concourse.mybir as mybir
from concourse.bass_utils import run_bass_kernel_spmd

F32 = mybir.dt.float32
BF16 = mybir.dt.bfloat16
AF = mybir.ActivationFunctionType
ALU = mybir.AluOpType

D = 1024
KC = 8
DFF = 2816
NF = 22
CT = 256
EPS = 1e-6
NWF = 50


class SemObj:
    _n = 0

    def __init__(self, h):
        self.h = h
        self.count = 0
        SemObj._n += 1
        self.id = SemObj._n


class Res:
    def __init__(self, name=""):
        self.lw = {}
        self.rd = {}
        self.name = name


class Tl:
    def __init__(self, t, name=""):
        self.t = t
        self.r = Res(name)


class EngW:
    def __init__(self, name, eng, so):
        self.name = name
        self.eng = eng
        self.so = so
        self.waited = {}


class Sched:
    def __init__(self, nc, es):
        self.nc = nc

        def mk(name):
            return SemObj(es.enter_context(nc.semaphore(name)))

        self.pe = EngW("pe", nc.tensor, mk("s_pe"))
        self.act = EngW("act", nc.scalar, mk("s_act"))
        self.dve = EngW("dve", nc.vector, mk("s_dve"))
        self.pool = EngW("pool", nc.gpsimd, mk("s_pool"))
        self.sp = EngW("sp", nc.sync, mk("s_sp"))
        self.engs = [self.pe, self.act, self.dve, self.pool, self.sp]
        self.dq = {"sp": [mk(f"dsp{i}") for i in range(12)], "pool": [mk(f"dpl{i}") for i in range(12)]}
        self.dqn = {"sp": 0, "pool": 0}
        self.banks = []
        self.bank_i = 0
        self.held = set()
        self.ninst = 0

    def _wait(self, E, tick):
        so, v = tick
        if E.waited.get(so.id, 0) >= v:
            return
        E.eng.wait_ge(so.h, v)
        E.waited[so.id] = v
        self.ninst += 1

    def _deps(self, reads, writes, own):
        deps = {}

        def add(t, raw):
            so, v = t
            if so is own and not raw:
                return
            if deps.get(so.id, (None, 0))[1] < v:
                deps[so.id] = (so, v)

        for r in reads:
            for t in r.lw.values():
                add(t, True)
        for w in writes:
            for t in w.lw.values():
                add(t, False)
            for t in w.rd.values():
                add(t, False)
        return deps

    def _commit(self, tick, reads, writes):
        so, v = tick
        for w in writes:
            o = w.lw.get(so.id)
            if o is None or o[1] < v:
                w.lw[so.id] = tick
        for r in reads:
            o = r.rd.get(so.id)
            if o is None or o[1] < v:
                r.rd[so.id] = tick

    def op(self, E, fn, reads=(), writes=(), inc=True):
        reads = [x.r if isinstance(x, Tl) else x for x in reads]
        writes = [x.r if isinstance(x, Tl) else x for x in writes]
        for t in self._deps(reads, writes, E.so).values():
            self._wait(E, t)
        ins = fn()
        self.ninst += 1
        if inc:
            ins.then_inc(E.so.h, 1)
            E.so.count += 1
            tick = (E.so, E.so.count)
        else:
            tick = (E.so, E.so.count + 1)
        self._commit(tick, reads, writes)
        return ins

    def dma(self, q, out, in_, reads=(), writes=(), **kw):
        reads = [x.r if isinstance(x, Tl) else x for x in reads]
        writes = [x.r if isinstance(x, Tl) else x for x in writes]
        E = self.sp if q == "sp" else self.pool
        pool = self.dq[q]
        so = pool[self.dqn[q] % len(pool)]
        self.dqn[q] += 1
        if so.count > 0:
            self._wait(E, (so, so.count))
        for t in self._deps(reads, writes, None).values():
            self._wait(E, t)
        E.eng.dma_start(out=out, in_=in_, **kw).then_inc(so.h, 16)
        self.ninst += 1
        so.count += 16
        self._commit((so, so.count), reads, writes)

    def barrier(self):
        ticks = []
        for E in self.engs:
            if E.so.count > 0:
                ticks.append((E.so, E.so.count))
        for q in self.dq.values():
            for so in q:
                if so.count > 0:
                    ticks.append((so, so.count))
        for E in self.engs:
            for t in ticks:
                if t[0] is not E.so:
                    self._wait(E, t)

    def bank(self, hold=False):
        for _ in range(16):
            i = self.bank_i % 8
            self.bank_i += 1
            if i not in self.held:
                if hold:
                    self.held.add(i)
                return self.banks[i]
        raise RuntimeError("no psum bank")

    def release(self, b):
        self.held.discard(self.banks.index(b))


def build(L, S, dbg=False):
    T = CT + S
    NCH = T // 64
    NTB = T // 128
    nc = bass.Bass("TRN2", target_bir_lowering=False)
    es = ExitStack()

    def din(name, shape, dt=F32):
        return nc.dram_tensor(name, list(shape), dt, kind="ExternalInput").ap()

    scratch_kind = "ExternalOutput" if dbg else "Internal"

    def dscr(name, shape, dt):
        return nc.dram_tensor(name, list(shape), dt, kind=scratch_kind).ap()

    xT_in = din("xT", [KC, 128, T])
    scT_in = din("scT", [128, KC, 2])
    wada_in = din("wada", [L, 72, 128, KC, 128])
    bada_in = din("bada", [128, L, 72])
    gn_in = din("gn", [128, L, 3, KC])
    gfin_in = din("gfin", [128, KC])
    wup_in = [din(f"wup{i}", [L, NF, 128, KC, 2, 128]) for i in range(2)]
    wdn_in = [din(f"wdn{i}", [L, 128, NF, D]) for i in range(2)]
    wf_in = din("wf", [L, NWF, 128, KC, 128])
    wv_in = din("wv", [L, 128, KC, D])
    convw_in = din("convw", [128, L, 16, 3])
    bg_in = din("bg", [128, L, 4])
    gmh_in = din("gmh", [128, L, KC])
    gqa_in = din("gqa", [128, L, 3])
    gkva_in = din("gkva", [128, L, 2])
    wuq_in = din("wuq", [L, 128, 3, 16, 128])
    wukvk_in = din("wukvk", [L, 128, 2, 8, 128])
    wukvv_in = din("wukvv", [L, 128, 2, D])
    wbm_in = din("wbm", [L, 128, KC, D])
    wba_in = din("wba", [L, 128, KC, D])
    wout_in = din("wout", [L, 128, KC, D])
    rope_in = din("rope", [64, 2, T])
    ident_in = din("ident", [128, 128])
    sel_in = din("sel", [128, 4, 128])
    tri_in = din("tri", [64, 2, 64])
    outT = nc.dram_tensor("outT", [KC, 128, S], F32, kind="ExternalOutput").ap()

    XT = dscr("XT", [KC, 128, T], F32)
    QT = dscr("QT", [KC, 128, T], BF16)
    KT = dscr("KT", [KC, 128, T], BF16)
    SOT = dscr("SOT", [KC, 128, T], BF16)
    GMT = dscr("GMT", [KC, 128, T], BF16)
    GAT = dscr("GAT", [KC, 128, T], BF16)
    KTOK = dscr("KTOK", [T, D], BF16)
    VTOK = dscr("VTOK", [T, D], BF16)
    GROW = dscr("GROW", [4, 128, T], F32)
    HF = dscr("HF", [T, D], F32)
    HB = dscr("HB", [T, D], F32)
    QN = dscr("QN", [8, 128, T], BF16)
    QR = dscr("QR", [8, 64, T], BF16)
    KN = dscr("KN", [8, 128, T], BF16)
    KR = dscr("KR", [64, T], BF16)
    VA = dscr("VA", [T, D], BF16)
    HAT = dscr("HAT", [8, 128, T], BF16)

    sch = Sched(nc, es)
    pe, act, dve, pool, sp = sch.pe, sch.act, sch.dve, sch.pool, sch.sp

    for i in range(8):
        sch.banks.append(Tl(es.enter_context(nc.psum_tensor(f"bank{i}", [128, 512], F32)), f"bank{i}"))

    sbn = [0]

    def sb(stack, name, shape, dt):
        sbn[0] += 1
        nm = f"s{sbn[0]}_{name}"
        return Tl(stack.enter_context(nc.sbuf_tensor(nm, list(shape), dt)), nm)

    tiles = [(0, CT, 1)] + [(CT + 512 * i, 512, 0) for i in range(S // 512)]
    xres = [Res(f"XT{i}") for i in range(len(tiles))]
    r_xin = Res("xin")
    RQT, RKT, RSOT, RGMT, RGAT, RKTOK, RVTOK, RGROW, RHF, RHB, RQN, RQR, RKN, RKR, RVA, RHAT, ROUT = [Res() for _ in range(17)]

    ident_f = sb(es, "ident_f", [128, 128], F32)
    ident_b = sb(es, "ident_b", [128, 128], BF16)
    ones_b = sb(es, "ones_b", [128, 128], BF16)
    sel_f = sb(es, "sel_f", [128, 4, 128], F32)
    tri_f = sb(es, "tri_f", [64, 2, 64], F32)
    MOD = sb(es, "MOD", [128, L, 72, 2], F32)
    GS = sb(es, "GS", [128, L, 3, KC, 2], F32)
    GT = sb(es, "GT", [128, L, 3, KC, 2], F32)
    gn_s = sb(es, "gn_s", [128, L, 3, KC], F32)
    gfin_s = sb(es, "gfin_s", [128, KC], F32)
    convw_s = sb(es, "convw_s", [128, L, 16, 3], F32)
    bg_s = sb(es, "bg_s", [128, L, 4], F32)
    gmh_s = sb(es, "gmh_s", [128, L, KC], F32)
    gqa_s = sb(es, "gqa_s", [128, L, 3], F32)
    gkva_s = sb(es, "gkva_s", [128, L, 2], F32)
    bada_s = sb(es, "bada_s", [128, L, 72], F32)
    ones_f = sb(es, "ones_f", [128, 1], F32)

    V = nc.vector
    A = nc.scalar
    P = nc.tensor

    def load_const(tl, src, q="sp"):
        sch.dma(q, tl.t[:], src, writes=[tl])

    load_const(ident_f, ident_in[:, :])
    load_const(ident_b, ident_in[:, :], "pool")
    load_const(sel_f, sel_in[:, :, :])
    load_const(tri_f, tri_in[:, :, :])
    load_const(gn_s, gn_in[:, :, :, :])
    load_const(gfin_s, gfin_in[:, :])
    load_const(convw_s, convw_in[:, :, :, :])
    load_const(bg_s, bg_in[:, :, :])
    load_const(gmh_s, gmh_in[:, :, :])
    load_const(gqa_s, gqa_in[:, :, :])
    load_const(gkva_s, gkva_in[:, :, :])
    load_const(bada_s, bada_in[:, :, :])
    sch.op(dve, lambda: V.memset(ones_b.t[:], 1.0), writes=[ones_b])
    sch.op(dve, lambda: V.memset(ones_f.t[:], 1.0), writes=[ones_f])

    with ExitStack() as ps:
        sc_f = sb(ps, "sc_f", [128, KC, 2], F32)
        sc_b = sb(ps, "sc_b", [128, KC, 2], BF16)
        wa = [sb(ps, f"wa{i}", [128, 8, KC, 128], BF16) for i in range(2)]
        load_const(sc_f, scT_in[:, :, :])
        sch.op(act, lambda: A.activation(out=sc_b.t[:], in_=sc_f.t[:], func=AF.Silu), reads=[sc_f], writes=[sc_b])
        it = 0
        for l in range(L):
            for mg in range(9):
                w = wa[it % 2]
                it += 1
                sch.dma("pool", w.t[:], wada_in[l, mg * 8:(mg + 1) * 8].rearrange("m p k c -> p m k c"),
                        writes=[w], max_dma_last_dim=4096)
                bk = sch.bank()
                for m in range(8):
                    for k in range(KC):
                        sch.op(pe, lambda m=m, k=k: P.matmul(bk.t[:, 2 * m:2 * m + 2], w.t[:, m, k, :], sc_b.t[:, k, :],
                                                             start=(k == 0), stop=(k == KC - 1)),
                               reads=[w, sc_b], writes=[bk], inc=(k == KC - 1))
                sch.op(dve, lambda: V.tensor_tensor(
                    out=MOD.t[:, l, mg * 8:(mg + 1) * 8, :],
                    in0=bk.t[:, 0:16].rearrange("p (m j) -> p m j", j=2),
                    in1=bada_s.t[:, l, mg * 8:(mg + 1) * 8].unsqueeze(2).to_broadcast([128, 8, 2]),
                    op=ALU.add), reads=[bk, bada_s], writes=[MOD])
        for l in range(L):
            for i in range(3):
                coef = 1.0 if i == 1 else 0.5
                sc = MOD.t[:, l, (3 * i + 1) * 8:(3 * i + 2) * 8, :]
                gt = MOD.t[:, l, (3 * i + 2) * 8:(3 * i + 3) * 8, :]
                sch.op(dve, lambda: V.scalar_tensor_tensor(
                    out=GS.t[:, l, i, :, :], in0=sc, scalar=1.0,
                    in1=gn_s.t[:, l, i, :].unsqueeze(2).to_broadcast([128, KC, 2]),
                    op0=ALU.add, op1=ALU.mult), reads=[MOD, gn_s], writes=[GS])
                sch.op(dve, lambda: V.tensor_scalar(out=GT.t[:, l, i, :, :], in0=gt, scalar1=coef, scalar2=None,
                                                    op0=ALU.mult), reads=[MOD], writes=[GT])
        sch.barrier()

    def SH(l, i, k, j):
        return MOD.t[:, l, 3 * i * 8 + k, j:j + 1]

    def rms_rstd(stack_tiles, src_ap_fn, nchunks, n, inv_dim, src_res):
        sq, rt = stack_tiles
        for c in range(nchunks):
            sch.op(act, lambda c=c: A.activation(out=sq.t[:, c, :n], in_=src_ap_fn(c), func=AF.Square),
                   reads=[src_res], writes=[sq])
        bk = sch.bank()
        for c in range(nchunks):
            sch.op(pe, lambda c=c: P.matmul(bk.t[:, :n], ones_b.t[:], sq.t[:, c, :n], start=(c == 0), stop=(c == nchunks - 1)),
                   reads=[ones_b, sq], writes=[bk], inc=(c == nchunks - 1))
        sch.op(act, lambda: A.activation(out=rt.t[:, :n], in_=bk.t[:, :n], func=AF.Sqrt, scale=inv_dim, bias=eps_s.t[:, 0:1]),
               reads=[bk, eps_s], writes=[rt])
        sch.op(dve, lambda: V.reciprocal(out=rt.t[:, :n], in_=rt.t[:, :n]), reads=[rt], writes=[rt])
        return rt

    eps_s = sb(es, "eps_s", [128, 1], F32)
    sch.op(dve, lambda: V.memset(eps_s.t[:], EPS), writes=[eps_s])

    def norm_mod(xt, n, l, i, j, sq, rt, tmp, hout, hcol0=0):
        rms_rstd((sq, rt), lambda c: xt.t[:, c, :n], KC, n, 1.0 / D, xt)
        sch.op(dve, lambda: V.tensor_tensor(out=tmp.t[:, :, :n], in0=xt.t[:, :, :n],
                                            in1=rt.t[:, :n].unsqueeze(1).to_broadcast([128, KC, n]), op=ALU.mult),
               reads=[xt, rt], writes=[tmp])
        for k in range(KC):
            sch.op(act, lambda k=k: A.activation(out=hout.t[:, k, hcol0:hcol0 + n], in_=tmp.t[:, k, :n], func=AF.Identity,
                                                 scale=GS.t[:, l, i, k, j:j + 1], bias=SH(l, i, k, j)),
                   reads=[tmp, GS, MOD], writes=[hout])

    def ffn_phase(l, which, first):
        i = 0 if which == 0 else 2
        with ExitStack() as ps:
            wdn = sb(ps, "wdn", [128, NF, D], BF16)
            wup = [sb(ps, f"wupb{q}", [128, KC, 2, 128], BF16) for q in range(3)]
            xts = [sb(ps, f"xt{q}", [128, KC, 512], F32) for q in range(2)]
            h = sb(ps, "h", [128, KC, 512], BF16)
            sq = sb(ps, "sq", [128, KC, 512], BF16)
            rt = sb(ps, "rt", [128, 512], F32)
            tmp = sb(ps, "tmp", [128, KC, 512], F32)
            h2 = sb(ps, "h2", [128, NF, 512], BF16)
            sil = [sb(ps, f"sil{q}", [128, 512], BF16) for q in range(2)]
            for q in range(2):
                sch.dma("pool", wdn.t[:, q * 11:(q + 1) * 11, :], wdn_in[which // 2][l, :, q * 11:(q + 1) * 11, :],
                        writes=[wdn], max_dma_last_dim=4096)

            def load(ti):
                t0, n, j = tiles[ti]
                xt = xts[ti % 2]
                if first:
                    sch.dma("sp", xt.t[:, :, :n], xT_in[:, :, t0:t0 + n].rearrange("k p t -> p k t"), reads=[r_xin], writes=[xt])
                else:
                    sch.dma("sp", xt.t[:, :, :n], XT[:, :, t0:t0 + n].rearrange("k p t -> p k t"), reads=[xres[ti]], writes=[xt])

            load(0)
            wi = 0
            for ti, (t0, n, j) in enumerate(tiles):
                if ti + 1 < len(tiles):
                    load(ti + 1)
                xt = xts[ti % 2]
                norm_mod(xt, n, l, i, j, sq, rt, tmp, h)
                for f in range(NF):
                    w = wup[wi % 3]
                    wi += 1
                    sch.dma("pool", w.t[:], wup_in[which // 2][l, f], writes=[w], max_dma_last_dim=4096)
                    pa = sch.bank()
                    pb = sch.bank()
                    for k in range(KC):
                        sch.op(pe, lambda k=k: P.matmul(pa.t[:, :n], w.t[:, k, 0, :], h.t[:, k, :n], start=(k == 0), stop=(k == KC - 1)),
                               reads=[w, h], writes=[pa], inc=(k == KC - 1))
                    for k in range(KC):
                        sch.op(pe, lambda k=k: P.matmul(pb.t[:, :n], w.t[:, k, 1, :], h.t[:, k, :n], start=(k == 0), stop=(k == KC - 1)),
                               reads=[w, h], writes=[pb], inc=(k == KC - 1))
                    s_ = sil[f % 2]
                    sch.op(act, lambda: A.activation(out=s_.t[:, :n], in_=pa.t[:, :n], func=AF.Silu), reads=[pa], writes=[s_])
                    sch.op(dve, lambda: V.tensor_tensor(out=h2.t[:, f, :n], in0=pb.t[:, :n], in1=s_.t[:, :n], op=ALU.mult),
                           reads=[pb, s_], writes=[h2])
                for m in range(KC):
                    py = sch.bank()
                    for f in range(NF):
                        sch.op(pe, lambda f=f: P.matmul(py.t[:, :n], wdn.t[:, f, m * 128:(m + 1) * 128], h2.t[:, f, :n],
                                                        start=(f == 0), stop=(f == NF - 1)),
                               reads=[wdn, h2], writes=[py], inc=(f == NF - 1))
                    sch.op(dve, lambda: V.scalar_tensor_tensor(out=xt.t[:, m, :n], in0=py.t[:, :n], scalar=GT.t[:, l, i, m, j:j + 1],
                                                               in1=xt.t[:, m, :n], op0=ALU.mult, op1=ALU.add),
                           reads=[py, GT, xt], writes=[xt])
                sch.dma("sp", XT[:, :, t0:t0 + n].rearrange("k p t -> p k t"), xt.t[:, :, :n], reads=[xt], writes=[xres[ti]])
            sch.barrier()

    def inproj_phase(l):
        with ExitStack() as ps:
            hall = sb(ps, "hall", [128, KC, T], BF16)
            with ExitStack() as p1:
                xts = [sb(p1, f"ixt{q}", [128, KC, 512], F32) for q in range(2)]
                sq = sb(p1, "isq", [128, KC, 512], BF16)
                rt = sb(p1, "irt", [128, 512], F32)
                tmp = sb(p1, "itmp", [128, KC, 512], F32)

                def load(ti):
                    t0, n, j = tiles[ti]
                    sch.dma("sp", xts[ti % 2].t[:, :, :n], XT[:, :, t0:t0 + n].rearrange("k p t -> p k t"),
                            reads=[xres[ti]], writes=[xts[ti % 2]])

                load(0)
                for ti, (t0, n, j) in enumerate(tiles):
                    if ti + 1 < len(tiles):
                        load(ti + 1)
                    norm_mod(xts[ti % 2], n, l, 1, j, sq, rt, tmp, hall, hcol0=t0)
                sch.barrier()

            with ExitStack() as p2:
                wfb = [sb(p2, f"wfb{q}", [128, KC, 128], BF16) for q in range(3)]
                zs = [sb(p2, f"zs{q}", [128, T], F32) for q in range(2)]
                ys = sb(p2, "ys", [128, T], F32)
                zb = [sb(p2, f"zb{q}", [128, T], BF16) for q in range(2)]
                ktk = sb(p2, "ktk", [128, NTB, 128], BF16)
                wi = 0

                def chunk_mm(ci, evac):
                    nonlocal wi
                    w = wfb[wi % 3]
                    wi += 1
                    sch.dma("pool", w.t[:], wf_in[l, ci], writes=[w], max_dma_last_dim=4096)
                    for (t0, n, j) in tiles:
                        bk = sch.bank()
                        for k in range(KC):
                            sch.op(pe, lambda k=k: P.matmul(bk.t[:, :n], w.t[:, k, :], hall.t[:, k, t0:t0 + n],
                                                            start=(k == 0), stop=(k == KC - 1)),
                                   reads=[w, hall], writes=[bk], inc=(k == KC - 1))
                        evac(bk, t0, n)

                segs = [(0, CT), (CT, T)]
                for ci in range(16):
                    z = zs[ci % 2]
                    o = zb[ci % 2]

                    def ev(bk, t0, n, z=z):
                        sch.op(act, lambda: A.copy(out=z.t[:, t0:t0 + n], in_=bk.t[:, :n]), reads=[bk], writes=[z])

                    chunk_mm(ci, ev)
                    w0 = convw_s.t[:, l, ci, 0:1]
                    w1 = convw_s.t[:, l, ci, 1:2]
                    w2 = convw_s.t[:, l, ci, 2:3]
                    for (a, b) in segs:
                        sch.op(dve, lambda: V.tensor_scalar(out=ys.t[:, a:b], in0=z.t[:, a:b], scalar1=w1, scalar2=None, op0=ALU.mult),
                               reads=[z, convw_s], writes=[ys])
                        sch.op(dve, lambda: V.scalar_tensor_tensor(out=ys.t[:, a + 1:b], in0=z.t[:, a:b - 1], scalar=w0,
                                                                   in1=ys.t[:, a + 1:b], op0=ALU.mult, op1=ALU.add),
                               reads=[z, ys, convw_s], writes=[ys])
                        sch.op(dve, lambda: V.scalar_tensor_tensor(out=ys.t[:, a:b - 1], in0=z.t[:, a + 1:b], scalar=w2,
                                                                   in1=ys.t[:, a:b - 1], op0=ALU.mult, op1=ALU.add),
                               reads=[z, ys, convw_s], writes=[ys])
                    if ci < 8:
                        sch.op(act, lambda: A.activation(out=z.t[:, :], in_=ys.t[:, :], func=AF.Silu), reads=[ys], writes=[z])
                        sch.op(dve, lambda: V.tensor_scalar(out=o.t[:, :], in0=z.t[:, :], scalar1=0.0625, scalar2=None, op0=ALU.mult),
                               reads=[z], writes=[o])
                        sch.dma("sp", QT[ci], o.t[:, :], reads=[o], writes=[RQT])
                    else:
                        m = ci - 8
                        sch.op(act, lambda: A.activation(out=o.t[:, :], in_=ys.t[:, :], func=AF.Silu), reads=[ys], writes=[o])
                        sch.dma("sp", KT[m], o.t[:, :], reads=[o], writes=[RKT])
                        for g in range(0, NTB, 8):
                            nb = min(8, NTB - g)
                            bk = sch.bank()
                            bv = bk.t[:, :].bitcast(BF16)
                            for q in range(nb):
                                tb = g + q
                                sch.op(pe, lambda q=q, tb=tb: P.transpose(bv[:, q * 128:(q + 1) * 128], o.t[:, tb * 128:(tb + 1) * 128], ident_b.t[:]),
                                       reads=[o, ident_b], writes=[bk], inc=(q == nb - 1))
                            sch.op(act, lambda: A.copy(out=ktk.t[:, g:g + nb, :], in_=bv[:, 0:nb * 128].rearrange("p (q f) -> p q f", f=128)),
                                   reads=[bk], writes=[ktk])
                        sch.dma("sp", KTOK[:, m * 128:(m + 1) * 128].rearrange("(tb p) f -> p tb f", p=128), ktk.t[:, :, :],
                                reads=[ktk], writes=[RKTOK])
                for ci in range(16, 40):
                    o = zb[ci % 2]

                    def ev(bk, t0, n, o=o):
                        sch.op(act, lambda: A.activation(out=o.t[:, t0:t0 + n], in_=bk.t[:, :n], func=AF.Sigmoid), reads=[bk], writes=[o])

                    chunk_mm(ci, ev)
                    if ci < 24:
                        sch.dma("sp", SOT[ci - 16], o.t[:, :], reads=[o], writes=[RSOT])
                    elif ci < 32:
                        sch.dma("sp", GMT[ci - 24], o.t[:, :], reads=[o], writes=[RGMT])
                    else:
                        sch.dma("sp", GAT[ci - 32], o.t[:, :], reads=[o], writes=[RGAT])
                for kind in range(4):
                    ci = 45 + kind
                    z = zs[ci % 2]

                    def ev(bk, t0, n, z=z, kind=kind):
                        sch.op(act, lambda: A.activation(out=z.t[:, t0:t0 + n], in_=bk.t[:, :n], func=AF.Identity,
                                                         bias=bg_s.t[:, l, kind:kind + 1], scale=1.0),
                               reads=[bk, bg_s], writes=[z])

                    chunk_mm(ci, ev)
                    sch.dma("sp", GROW[kind], z.t[:, :], reads=[z], writes=[RGROW])
                sch.barrier()

            with ExitStack() as p3:
                wv = sb(p3, "wv", [128, KC, D], BF16)
                vt = [sb(p3, f"vt{q}", [128, D], BF16) for q in range(2)]
                sch.dma("pool", wv.t[:], wv_in[l], writes=[wv], max_dma_last_dim=4096)
                for tb in range(NTB):
                    v_ = vt[tb % 2]
                    for hf in range(2):
                        bk = sch.bank()
                        for k in range(KC):
                            sch.op(pe, lambda k=k: P.matmul(bk.t[:, :], hall.t[:, k, tb * 128:(tb + 1) * 128], wv.t[:, k, hf * 512:(hf + 1) * 512],
                                                            start=(k == 0), stop=(k == KC - 1)),
                                   reads=[wv, hall], writes=[bk], inc=(k == KC - 1))
                        if hf == 0:
                            sch.op(act, lambda: A.copy(out=v_.t[:, 0:512], in_=bk.t[:, :]), reads=[bk], writes=[v_])
                        else:
                            sch.op(dve, lambda: V.tensor_copy(out=v_.t[:, 512:1024], in_=bk.t[:, :]), reads=[bk], writes=[v_])
                    sch.dma("sp", VTOK[tb * 128:(tb + 1) * 128, :], v_.t[:, :], reads=[v_], writes=[RVTOK])
                sch.barrier()

            with ExitStack() as p4:
                wl = sb(p4, "wl", [128, 6, KC, 128], BF16)
                wuq = sb(p4, "wuq", [128, 3, 16, 128], BF16)
                wkk = sb(p4, "wkk", [128, 2, 8, 128], BF16)
                wkv = sb(p4, "wkv", [128, 2, D], BF16)
                cqf = sb(p4, "cqf", [128, 3, 512], F32)
                sq = sb(p4, "lsq", [128, 3, 512], BF16)
                rt = sb(p4, "lrt", [128, 512], F32)
                cqn = sb(p4, "cqn", [128, 3, 512], BF16)
                ckn = sb(p4, "ckn", [128, 2, 512], BF16)
                rp = [sb(p4, f"rp{q}", [64, 2, 512], F32) for q in range(2)]
                r1 = sb(p4, "r1", [64, 512], F32)
                r2 = sb(p4, "r2", [64, 512], F32)
                ob = [sb(p4, f"ob{q}", [128, 512], BF16) for q in range(4)]
                vb = [sb(p4, f"vb{q}", [128, D], BF16) for q in range(2)]
                for c in range(5):
                    sch.dma("pool", wl.t[:, c], wf_in[l, 40 + c], writes=[wl], max_dma_last_dim=4096)
                sch.dma("pool", wl.t[:, 5], wf_in[l, 49], writes=[wl], max_dma_last_dim=4096)
                sch.dma("pool", wuq.t[:], wuq_in[l], writes=[wuq], max_dma_last_dim=4096)
                sch.dma("pool", wkk.t[:], wukvk_in[l], writes=[wkk], max_dma_last_dim=4096)
                sch.dma("pool", wkv.t[:], wukvv_in[l], writes=[wkv], max_dma_last_dim=4096)
                oi = 0

                def rope_out(pa, pb, n, rpt, dst_ap, dres):
                    nonlocal oi
                    o = ob[oi % 4]
                    oi += 1
                    sch.op(dve, lambda: V.tensor_tensor(out=r1.t[:, :n], in0=pa.t[0:64, :n], in1=rpt.t[:, 0, :n], op=ALU.mult),
                           reads=[pa, rpt], writes=[r1])
                    sch.op(dve, lambda: V.tensor_tensor(out=r2.t[:, :n], in0=pb.t[0:64, :n], in1=rpt.t[:, 1, :n], op=ALU.mult),
                           reads=[pb, rpt], writes=[r2])
                    sch.op(dve, lambda: V.tensor_tensor(out=o.t[0:64, :n], in0=r1.t[:, :n], in1=r2.t[:, :n], op=ALU.add),
                           reads=[r1, r2], writes=[o])
                    sch.dma("sp", dst_ap, o.t[0:64, :n], reads=[o], writes=[dres])

                for ti, (t0, n, j) in enumerate(tiles):
                    rpt = rp[ti % 2]
                    sch.dma("sp", rpt.t[:, :, :n], rope_in[:, :, t0:t0 + n], writes=[rpt])
                    for (c0, ncn, gsrc, dst, inv) in ((0, 3, gqa_s, cqn, 1.0 / 384), (3, 2, gkva_s, ckn, 1.0 / 256)):
                        for c in range(ncn):
                            bk = sch.bank()
                            for k in range(KC):
                                sch.op(pe, lambda k=k: P.matmul(bk.t[:, :n], wl.t[:, c0 + c, k, :], hall.t[:, k, t0:t0 + n],
                                                                start=(k == 0), stop=(k == KC - 1)),
                                       reads=[wl, hall], writes=[bk], inc=(k == KC - 1))
                            sch.op(act, lambda: A.copy(out=cqf.t[:, c, :n], in_=bk.t[:, :n]), reads=[bk], writes=[cqf])
                        rms_rstd((sq, rt), lambda c: cqf.t[:, c, :n], ncn, n, inv, cqf)
                        for c in range(ncn):
                            sch.op(dve, lambda c=c: V.scalar_tensor_tensor(out=dst.t[:, c, :n], in0=cqf.t[:, c, :n], scalar=gsrc.t[:, l, c:c + 1],
                                                                           in1=rt.t[:, :n], op0=ALU.mult, op1=ALU.mult),
                                   reads=[cqf, gsrc, rt], writes=[dst])
                    pa = sch.bank()
                    pb = sch.bank()
                    for k in range(KC):
                        sch.op(pe, lambda k=k: P.matmul(pa.t[0:64, :n], wl.t[:, 5, k, 0:64], hall.t[:, k, t0:t0 + n], start=(k == 0), stop=(k == KC - 1)),
                               reads=[wl, hall], writes=[pa], inc=(k == KC - 1))
                    for k in range(KC):
                        sch.op(pe, lambda k=k: P.matmul(pb.t[0:64, :n], wl.t[:, 5, k, 64:128], hall.t[:, k, t0:t0 + n], start=(k == 0), stop=(k == KC - 1)),
                               reads=[wl, hall], writes=[pb], inc=(k == KC - 1))
                    rope_out(pa, pb, n, rpt, KR[:, t0:t0 + n], RKR)
                    for hd in range(8):
                        bk = sch.bank()
                        for c in range(3):
                            sch.op(pe, lambda c=c: P.matmul(bk.t[:, :n], wuq.t[:, c, 2 * hd, :], cqn.t[:, c, :n], start=(c == 0), stop=(c == 2)),
                                   reads=[wuq, cqn], writes=[bk], inc=(c == 2))
                        o = ob[oi % 4]
                        oi += 1
                        sch.op(act, lambda: A.copy(out=o.t[:, :n], in_=bk.t[:, :n]), reads=[bk], writes=[o])
                        sch.dma("sp", QN[hd, :, t0:t0 + n], o.t[:, :n], reads=[o], writes=[RQN])
                        pa = sch.bank()
                        pb = sch.bank()
                        for c in range(3):
                            sch.op(pe, lambda c=c: P.matmul(pa.t[0:64, :n], wuq.t[:, c, 2 * hd + 1, 0:64], cqn.t[:, c, :n], start=(c == 0), stop=(c == 2)),
                                   reads=[wuq, cqn], writes=[pa], inc=(c == 2))
                        for c in range(3):
                            sch.op(pe, lambda c=c: P.matmul(pb.t[0:64, :n], wuq.t[:, c, 2 * hd + 1, 64:128], cqn.t[:, c, :n], start=(c == 0), stop=(c == 2)),
                                   reads=[wuq, cqn], writes=[pb], inc=(c == 2))
                        rope_out(pa, pb, n, rpt, QR[hd, :, t0:t0 + n], RQR)
                        bk = sch.bank()
                        for c in range(2):
                            sch.op(pe, lambda c=c: P.matmul(bk.t[:, :n], wkk.t[:, c, hd, :], ckn.t[:, c, :n], start=(c == 0), stop=(c == 1)),
                                   reads=[wkk, ckn], writes=[bk], inc=(c == 1))
                        o = ob[oi % 4]
                        oi += 1
                        sch.op(dve, lambda: V.tensor_copy(out=o.t[:, :n], in_=bk.t[:, :n]), reads=[bk], writes=[o])
                        sch.dma("sp", KN[hd, :, t0:t0 + n], o.t[:, :n], reads=[o], writes=[RKN])
                    for b in range(n // 128):
                        v_ = vb[b % 2]
                        for hf in range(2):
                            bk = sch.bank()
                            for c in range(2):
                                sch.op(pe, lambda c=c: P.matmul(bk.t[:, :], ckn.t[:, c, b * 128:(b + 1) * 128], wkv.t[:, c, hf * 512:(hf + 1) * 512],
                                                                start=(c == 0), stop=(c == 1)),
                                       reads=[wkv, ckn], writes=[bk], inc=(c == 1))
                            sch.op(act, lambda: A.copy(out=v_.t[:, hf * 512:(hf + 1) * 512], in_=bk.t[:, :]), reads=[bk], writes=[v_])
                        sch.dma("sp", VA[t0 + b * 128:t0 + (b + 1) * 128, :], v_.t[:, :], reads=[v_], writes=[RVA])
                sch.barrier()
            sch.barrier()

    def mlstm_phase(l):
        with ExitStack() as ps:
            CA = [sb(ps, f"CA{d}", [64, NCH, 64], F32) for d in range(2)]
            ABC = sb(ps, "ABC", [128, 2, 4, NCH], F32)
            with ExitStack() as pg:
                W1 = sb(pg, "W1", [64, T], F32)
                NB = sb(pg, "NB", [64, T], F32)
                U = sb(pg, "U", [64, T], F32)
                G = sb(pg, "G", [64, T], F32)
                GE = sb(pg, "GE", [128, NCH], F32)
                GP = sb(pg, "GP", [128, NCH], F32)
                AA = sb(pg, "AA", [128, NCH], F32)
                TM = sb(pg, "TM", [64, T], F32)
                RS = [sb(pg, f"RS{d}", [64, T], F32) for d in range(2)]
                sch.op(dve, lambda: V.memset(AA.t[:], 0.0), writes=[AA])
                for d in range(2):
                    sch.dma("sp", W1.t[:, :], GROW[2 * d + 1, 0:64, :], reads=[RGROW], writes=[W1])
                    sch.dma("sp", U.t[:, :], GROW[2 * d, 0:64, :], reads=[RGROW], writes=[U])
                    sch.op(act, lambda: A.activation(out=W1.t[:, :], in_=W1.t[:, :], func=AF.Exp, scale=-1.0), reads=[W1], writes=[W1])
                    sch.op(act, lambda: A.activation(out=W1.t[:, :], in_=W1.t[:, :], func=AF.Ln, bias=ones_f.t[0:64, 0:1], scale=1.0),
                           reads=[W1, ones_f], writes=[W1])
                    if d == 0:
                        scans = [(slice(0, T), None)]
                    else:
                        scans = [(slice(CT - 1, None, -1), None), (slice(T - 1, CT - 1, -1), 0)]
                    for (sl, init_col) in scans:
                        nn = len(range(T)[sl])
                        init = 0.0 if init_col is None else NB.t[:, init_col:init_col + 1]
                        sch.op(dve, lambda: V.tensor_tensor_scan(out=NB.t[:, sl], data0=ones_f.t[0:64, 0:1].to_broadcast([64, nn]),
                                                                 data1=W1.t[:, sl], initial=init, op0=ALU.mult, op1=ALU.add),
                               reads=[W1, ones_f, NB], writes=[NB])
                    sch.op(dve, lambda: V.tensor_tensor(out=U.t[:, :], in0=U.t[:, :], in1=NB.t[:, :], op=ALU.add), reads=[U, NB], writes=[U])
                    for (sl, init_col) in scans:
                        init = 0.0 if init_col is None else G.t[:, init_col:init_col + 1]
                        sch.op(dve, lambda: V.tensor_tensor_scan(out=G.t[:, sl], data0=U.t[:, sl], data1=U.t[:, sl], initial=init,
                                                                 op0=ALU.max, op1=ALU.max),
                               reads=[U, G], writes=[G])
                    Gv = G.t[:, :].rearrange("p (c s) -> p c s", s=64)
                    if d == 0:
                        sch.op(dve, lambda: V.tensor_copy(out=GE.t[0:64, :], in_=Gv[:, :, 63]), reads=[G], writes=[GE])
                        sch.op(dve, lambda: V.memset(GP.t[0:64, 0:1], 0.0), writes=[GP])
                        sch.op(dve, lambda: V.tensor_copy(out=GP.t[0:64, 1:NCH], in_=GE.t[0:64, 0:NCH - 1]), reads=[GE], writes=[GP])
                    else:
                        nck = CT // 64
                        sch.op(dve, lambda: V.tensor_copy(out=GE.t[0:64, :], in_=Gv[:, :, 0]), reads=[G], writes=[GE])
                        sch.op(dve, lambda: V.memset(GP.t[0:64, nck - 1:nck], 0.0), writes=[GP])
                        sch.op(dve, lambda: V.tensor_copy(out=GP.t[0:64, 0:nck - 1], in_=GE.t[0:64, 1:nck]), reads=[GE], writes=[GP])
                        sch.op(dve, lambda: V.tensor_copy(out=GP.t[0:64, NCH - 1:NCH], in_=GE.t[0:64, 0:1]), reads=[GE], writes=[GP])
                        sch.op(dve, lambda: V.tensor_copy(out=GP.t[0:64, nck:NCH - 1], in_=GE.t[0:64, nck + 1:NCH]), reads=[GE], writes=[GP])
                    GEb = GE.t[0:64, :].unsqueeze(2).to_broadcast([64, NCH, 64])
                    TMv = TM.t[:, :].rearrange("p (c s) -> p c s", s=64)
                    sch.op(dve, lambda: V.tensor_tensor(out=TMv[0:32], in0=U.t[0:32, :].rearrange("p (c s) -> p c s", s=64), in1=GEb[0:32], op=ALU.subtract),
                           reads=[U, GE], writes=[TM])
                    sch.op(dve, lambda: V.tensor_tensor(out=TMv[32:64], in0=NB.t[32:64, :].rearrange("p (c s) -> p c s", s=64),
                                                        in1=GE.t[32:64, :].unsqueeze(2).to_broadcast([32, NCH, 64]), op=ALU.subtract),
                           reads=[NB, GE], writes=[TM])
                    sch.op(act, lambda: A.activation(out=RS[d].t[:, :], in_=TM.t[:, :], func=AF.Exp), reads=[TM], writes=[RS[d]])
                    for g in range(0, NCH, 8):
                        ng = min(8, NCH - g)
                        bk = sch.bank()
                        for q in range(ng):
                            c = g + q
                            sch.op(pe, lambda q=q, c=c: P.transpose(bk.t[0:64, q * 64:(q + 1) * 64], RS[d].t[:, c * 64:(c + 1) * 64], ident_f.t[0:64, 0:64]),
                                   reads=[RS[d], ident_f], writes=[bk], inc=(q == ng - 1))
                        sch.op(dve, lambda: V.tensor_copy(out=CA[d].t[:, g:g + ng, :], in_=bk.t[0:64, 0:ng * 64].rearrange("p (q f) -> p q f", f=64)),
                               reads=[bk], writes=[CA[d]])
                    sch.op(dve, lambda: V.tensor_tensor(out=AA.t[0:64, :], in0=GP.t[0:64, :], in1=GE.t[0:64, :], op=ALU.subtract),
                           reads=[GP, GE], writes=[AA])
                    sch.op(act, lambda: A.activation(out=AA.t[0:64, :], in_=AA.t[0:64, :], func=AF.Exp), reads=[AA], writes=[AA])
                    for hd in range(4):
                        bk = sch.bank()
                        sch.op(pe, lambda: P.matmul(bk.t[:, 0:NCH], sel_f.t[:, hd, :], AA.t[:, :], start=True, stop=True),
                               reads=[sel_f, AA], writes=[bk])
                        sch.op(act, lambda: A.copy(out=ABC.t[:, d, hd, :], in_=bk.t[:, 0:NCH]), reads=[bk], writes=[ABC])
                sch.barrier()

            with ExitStack() as ph:
                WT = 256
                WC = WT // 64
                NW = T // WT
                qTw = [sb(ph, f"qTw{q}", [128, KC, WT], BF16) for q in range(2)]
                kTw = [sb(ph, f"kTw{q}", [128, KC, WT], BF16) for q in range(2)]
                ktw = [sb(ph, f"ktw{q}", [64, WC, D], BF16) for q in range(2)]
                vxw = [sb(ph, f"vxw{q}", [64, WC, 4, 258], BF16) for q in range(2)]
                Cf = sb(ph, "Cf", [128, 4, 2, 258], F32)
                Cb = sb(ph, "Cb", [128, 4, 2, 258], BF16)
                NS = 2
                sTm = [[sb(ph, f"sTm{q}_{h}", [64, 64], BF16) for h in range(4)] for q in range(NS)]
                t1 = [[sb(ph, f"t1{q}_{h}", [64, 258], F32) for h in range(4)] for q in range(NS)]
                nd = [[sb(ph, f"nd{q}_{h}", [64, 258], F32) for h in range(4)] for q in range(NS)]
                dn = [[sb(ph, f"dn{q}_{h}", [64, 2], F32) for h in range(4)] for q in range(NS)]
                kw = [[sb(ph, f"kw{q}_{h}", [64, 256], BF16) for h in range(4)] for q in range(NS)]
                ho = [sb(ph, f"ho{q}", [64, 4, 256], F32) for q in range(NS)]
                for q in range(2):
                    sch.op(dve, lambda q=q: V.memset(vxw[q].t[:, :, :, 256:258], 1.0), writes=[vxw[q]])
                nck = CT // 64
                step = 0
                wi = 0
                for d in range(2):
                    HD = HF if d == 0 else HB
                    RH = RHF if d == 0 else RHB
                    lat = list(range(CT // WT, NW))
                    worder = list(range(CT // WT)) + (lat if d == 0 else lat[::-1])
                    if d == 1:
                        worder = list(range(CT // WT))[::-1] + lat[::-1]
                    sch.op(dve, lambda: V.memset(Cf.t[:], 0.0), writes=[Cf])
                    sch.op(dve, lambda: V.memset(Cb.t[:], 0.0), writes=[Cb])

                    def loadw(w, slot):
                        ts = slice(w * WT, (w + 1) * WT)
                        sch.dma("sp", qTw[slot].t[:, :, :], QT[:, :, ts].rearrange("k p t -> p k t"), reads=[RQT], writes=[qTw[slot]])
                        sch.dma("sp", kTw[slot].t[:, :, :], KT[:, :, ts].rearrange("k p t -> p k t"), reads=[RKT], writes=[kTw[slot]])
                        sch.dma("sp", ktw[slot].t[:, :, :], KTOK[ts, :].rearrange("(c p) f -> p c f", p=64), reads=[RKTOK], writes=[ktw[slot]])
                        for hd in range(4):
                            sch.dma("sp", vxw[slot].t[:, :, hd, 0:256], VTOK[ts, hd * 256:(hd + 1) * 256].rearrange("(c p) f -> p c f", p=64),
                                    reads=[RVTOK], writes=[vxw[slot]])

                    loadw(worder[0], wi % 2)
                    for wn, w in enumerate(worder):
                        slot = wi % 2
                        wi += 1
                        if wn + 1 < len(worder):
                            loadw(worder[wn + 1], wi % 2)
                        qT_, kT_, kt_, vx_ = qTw[slot], kTw[slot], ktw[slot], vxw[slot]
                        corder = list(range(WC)) if d == 0 else list(range(WC - 1, -1, -1))
                        for cl_ in corder:
                            c = w * WC + cl_
                            q = step % NS
                            step += 1
                            cs = slice(cl_ * 64, (cl_ + 1) * 64)
                            WEc = [CA[d].t[:, c, hd:hd + 1] for hd in range(4)]
                            EMc = [CA[d].t[:, c, 32 + hd:32 + hd + 1] for hd in range(4)]
                            for hd in range(4):
                                sch.op(act, lambda hd=hd: A.activation(out=kw[q][hd].t[:, :], in_=kt_.t[:, cl_, hd * 256:(hd + 1) * 256], func=AF.Copy, scale=WEc[hd]),
                                       reads=[kt_, CA[d]], writes=[kw[q][hd]])
                            psc = sch.bank()
                            for hd in range(4):
                                for kc in range(2):
                                    sch.op(pe, lambda hd=hd, kc=kc: P.matmul(psc.t[0:64, hd * 64:(hd + 1) * 64], kT_.t[:, 2 * hd + kc, cs], qT_.t[:, 2 * hd + kc, cs],
                                                                             start=(kc == 0), stop=(kc == 1)),
                                           reads=[kT_, qT_], writes=[psc], inc=(kc == 1 and hd == 3))
                            pin = []
                            for hd in range(4):
                                pi = sch.bank()
                                pin.append(pi)
                                for kc in range(2):
                                    sch.op(pe, lambda hd=hd, kc=kc: P.matmul(pi.t[0:64, 0:258], qT_.t[:, 2 * hd + kc, cs], Cb.t[:, hd, kc, :], start=(kc == 0), stop=(kc == 1)),
                                           reads=[qT_, Cb], writes=[pi], inc=(kc == 1))
                            for hd in range(4):
                                sch.op(dve, lambda hd=hd: V.scalar_tensor_tensor(out=sTm[q][hd].t[:, :], in0=psc.t[0:64, hd * 64:(hd + 1) * 64], scalar=WEc[hd],
                                                                                 in1=tri_f.t[:, d, :], op0=ALU.mult, op1=ALU.mult),
                                       reads=[psc, CA[d], tri_f], writes=[sTm[q][hd]])
                            for hd in range(4):
                                sch.op(act, lambda hd=hd: A.activation(out=t1[q][hd].t[:, :], in_=pin[hd].t[0:64, 0:258], func=AF.Copy, scale=ABC.t[0:64, d, hd, c:c + 1]),
                                       reads=[pin[hd], ABC], writes=[t1[q][hd]])
                            pnn = []
                            for hd in range(4):
                                pn = sch.bank()
                                pnn.append(pn)
                                sch.op(pe, lambda hd=hd: P.matmul(pn.t[0:64, 0:258], sTm[q][hd].t[:, :], vx_.t[:, cl_, hd, :], start=True, stop=True),
                                       reads=[sTm[q][hd], vx_], writes=[pn])
                            for hd in range(4):
                                sch.op(dve, lambda hd=hd: V.tensor_tensor(out=nd[q][hd].t[:, :], in0=pnn[hd].t[0:64, 0:258], in1=t1[q][hd].t[:, :], op=ALU.add),
                                       reads=[pnn[hd], t1[q][hd]], writes=[nd[q][hd]])
                            for hd in range(4):
                                for kc in range(2):
                                    pu = sch.bank()
                                    sch.op(pe, lambda hd=hd, kc=kc: P.matmul(pu.t[:, 0:258], kw[q][hd].t[:, kc * 128:(kc + 1) * 128], vx_.t[:, cl_, hd, :], start=True, stop=True),
                                           reads=[kw[q][hd], vx_], writes=[pu])
                                    sch.op(dve, lambda hd=hd, kc=kc: V.scalar_tensor_tensor(out=Cf.t[:, hd, kc, :], in0=Cf.t[:, hd, kc, :], scalar=ABC.t[:, d, hd, c:c + 1],
                                                                                            in1=pu.t[:, 0:258], op0=ALU.mult, op1=ALU.add),
                                           reads=[Cf, ABC, pu], writes=[Cf])
                                sch.op(act, lambda hd=hd: A.copy(out=Cb.t[:, hd, :, :], in_=Cf.t[:, hd, :, :]), reads=[Cf], writes=[Cb])
                            for hd in range(4):
                                sch.op(act, lambda hd=hd: A.activation(out=dn[q][hd].t[:, 0:1], in_=nd[q][hd].t[:, 256:257], func=AF.Abs),
                                       reads=[nd[q][hd]], writes=[dn[q][hd]])
                            for hd in range(4):
                                sch.op(dve, lambda hd=hd: V.tensor_tensor(out=dn[q][hd].t[:, 0:1], in0=dn[q][hd].t[:, 0:1], in1=EMc[hd], op=ALU.max),
                                       reads=[dn[q][hd], CA[d]], writes=[dn[q][hd]])
                            for hd in range(4):
                                sch.op(dve, lambda hd=hd: V.reciprocal(out=dn[q][hd].t[:, 1:2], in_=dn[q][hd].t[:, 0:1]), reads=[dn[q][hd]], writes=[dn[q][hd]])
                            for hd in range(4):
                                sch.op(act, lambda hd=hd: A.activation(out=ho[q].t[:, hd, :], in_=nd[q][hd].t[:, 0:256], func=AF.Copy, scale=dn[q][hd].t[:, 1:2]),
                                       reads=[nd[q][hd], dn[q][hd]], writes=[ho[q]])
                            sch.dma("sp", HD[c * 64:(c + 1) * 64, :], ho[q].t[:, :, :].rearrange("p h f -> p (h f)"), reads=[ho[q]], writes=[RH])
                sch.barrier()
            sch.barrier()


    def attn_phase(l):
        sc = float(192 ** -0.5)
        with ExitStack() as ps:
            krT = sb(ps, "krT", [64, T], BF16)
            knT = sb(ps, "knT", [128, T], BF16)
            vh = sb(ps, "vh", [128, NTB, 128], BF16)
            qn = [sb(ps, f"qn{q}", [128, 512], BF16) for q in range(2)]
            qr = [sb(ps, f"qr{q}", [64, 512], BF16) for q in range(2)]
            pt = [sb(ps, f"pt{q}", [128, 512], BF16) for q in range(3)]
            rd = sb(ps, "rd", [128, 512], F32)
            oo = [sb(ps, f"oo{q}", [128, 512], BF16) for q in range(2)]
            sch.dma("sp", krT.t[:, :], KR[:, :], reads=[RKR], writes=[krT])
            it = 0
            pti = 0
            for hd in range(8):
                sch.dma("sp", knT.t[:, :], KN[hd], reads=[RKN], writes=[knT])
                sch.dma("sp", vh.t[:, :, :], VA[:, hd * 128:(hd + 1) * 128].rearrange("(tb p) f -> p tb f", p=128), reads=[RVA], writes=[vh])
                for ti, (t0, n, j) in enumerate(tiles):
                    q_n = qn[it % 2]
                    q_r = qr[it % 2]
                    o_ = oo[it % 2]
                    it += 1
                    sch.dma("sp", q_n.t[:, :n], QN[hd, :, t0:t0 + n], reads=[RQN], writes=[q_n])
                    sch.dma("sp", q_r.t[:, :n], QR[hd, :, t0:t0 + n], reads=[RQR], writes=[q_r])
                    kbs = list(range(CT // 128)) if j == 1 else list(range(NTB))
                    po = sch.bank(hold=True)
                    pd = sch.bank(hold=True)

                    def scores(kb):
                        b = sch.bank()
                        sch.op(pe, lambda: P.matmul(b.t[:, :n], knT.t[:, kb * 128:(kb + 1) * 128], q_n.t[:, :n], start=True, stop=False),
                               reads=[knT, q_n], writes=[b], inc=False)
                        sch.op(pe, lambda: P.matmul(b.t[:, :n], krT.t[:, kb * 128:(kb + 1) * 128], q_r.t[:, :n], start=False, stop=True),
                               reads=[krT, q_r], writes=[b])
                        return b

                    nxt = scores(kbs[0])
                    for ii, kb in enumerate(kbs):
                        b = nxt
                        p_ = pt[pti % 3]
                        pti += 1
                        sch.op(act, lambda: A.activation(out=p_.t[:, :n], in_=b.t[:, :n], func=AF.Exp, scale=sc), reads=[b], writes=[p_])
                        if ii + 1 < len(kbs):
                            nxt = scores(kbs[ii + 1])
                        last = ii == len(kbs) - 1
                        sch.op(pe, lambda: P.matmul(po.t[:, :n], vh.t[:, kb, :], p_.t[:, :n], start=(ii == 0), stop=last),
                               reads=[vh, p_], writes=[po], inc=last)
                        sch.op(pe, lambda: P.matmul(pd.t[:, :n], ones_b.t[:, :], p_.t[:, :n], start=(ii == 0), stop=last),
                               reads=[ones_b, p_], writes=[pd], inc=True)
                    sch.op(dve, lambda: V.reciprocal(out=rd.t[:, :n], in_=pd.t[:, :n]), reads=[pd], writes=[rd])
                    sch.op(dve, lambda: V.tensor_tensor(out=o_.t[:, :n], in0=po.t[:, :n], in1=rd.t[:, :n], op=ALU.mult), reads=[po, rd], writes=[o_])
                    sch.release(po)
                    sch.release(pd)
                    sch.dma("sp", HAT[hd, :, t0:t0 + n], o_.t[:, :n], reads=[o_], writes=[RHAT])
            sch.barrier()

    def merge_phase(l):
        with ExitStack() as ps:
            wbm = sb(ps, "wbm", [128, KC, D], BF16)
            wba = sb(ps, "wba", [128, KC, D], BF16)
            wout = sb(ps, "wout", [128, KC, D], BF16)
            sch.dma("pool", wbm.t[:], wbm_in[l], writes=[wbm], max_dma_last_dim=4096)
            sch.dma("pool", wba.t[:], wba_in[l], writes=[wba], max_dma_last_dim=4096)
            sch.dma("pool", wout.t[:], wout_in[l], writes=[wout], max_dma_last_dim=4096)
            hf = sb(ps, "hf", [128, 4, D], F32)
            hb = sb(ps, "hb", [128, 4, D], F32)
            hn = sb(ps, "hn", [128, 4, D], BF16)
            junk = sb(ps, "junk", [128, 256], BF16)
            ssq = sb(ps, "ssq", [128, 16], F32)
            so = sb(ps, "so", [128, KC, 512], BF16)
            gm = sb(ps, "gm", [128, KC, 512], BF16)
            ga = sb(ps, "ga", [128, KC, 512], BF16)
            hat = sb(ps, "hat", [128, KC, 512], BF16)
            xt = sb(ps, "mxt", [128, KC, 512], F32)
            hmT = sb(ps, "hmT", [128, KC, 512], BF16)
            tm = sb(ps, "tm", [128, KC, 512], F32)
            t2 = sb(ps, "t2", [128, 512], F32)
            tb_ = sb(ps, "tb", [128, KC, 512], BF16)
            for ti, (t0, n, j) in enumerate(tiles):
                nb = n // 128
                sch.dma("sp", hf.t[:, 0:nb, :], HF[t0:t0 + n, :].rearrange("(b p) f -> p b f", p=128), reads=[RHF], writes=[hf])
                sch.dma("sp", hb.t[:, 0:nb, :], HB[t0:t0 + n, :].rearrange("(b p) f -> p b f", p=128), reads=[RHB], writes=[hb])
                sch.dma("sp", so.t[:, :, :n], SOT[:, :, t0:t0 + n].rearrange("k p t -> p k t"), reads=[RSOT], writes=[so])
                sch.dma("sp", gm.t[:, :, :n], GMT[:, :, t0:t0 + n].rearrange("k p t -> p k t"), reads=[RGMT], writes=[gm])
                sch.dma("sp", ga.t[:, :, :n], GAT[:, :, t0:t0 + n].rearrange("k p t -> p k t"), reads=[RGAT], writes=[ga])
                sch.dma("sp", hat.t[:, :, :n], HAT[:, :, t0:t0 + n].rearrange("k p t -> p k t"), reads=[RHAT], writes=[hat])
                sch.dma("sp", xt.t[:, :, :n], XT[:, :, t0:t0 + n].rearrange("k p t -> p k t"), reads=[xres[ti]], writes=[xt])
                sch.op(dve, lambda: V.tensor_tensor(out=hf.t[:, 0:nb, :], in0=hf.t[:, 0:nb, :], in1=hb.t[:, 0:nb, :], op=ALU.add),
                       reads=[hf, hb], writes=[hf])
                for b in range(nb):
                    for hd in range(4):
                        sch.op(act, lambda b=b, hd=hd: A.activation(out=junk.t[:, :], in_=hf.t[:, b, hd * 256:(hd + 1) * 256], func=AF.Square,
                                                                    accum_out=ssq.t[:, b * 4 + hd:b * 4 + hd + 1]),
                               reads=[hf], writes=[junk, ssq])
                sch.op(act, lambda: A.activation(out=ssq.t[:, 0:4 * nb], in_=ssq.t[:, 0:4 * nb], func=AF.Sqrt, scale=1.0 / 256, bias=eps_s.t[:, 0:1]),
                       reads=[ssq, eps_s], writes=[ssq])
                sch.op(dve, lambda: V.reciprocal(out=ssq.t[:, 0:4 * nb], in_=ssq.t[:, 0:4 * nb]), reads=[ssq], writes=[ssq])
                for b in range(nb):
                    for hd in range(4):
                        sch.op(dve, lambda b=b, hd=hd: V.tensor_scalar(out=hn.t[:, b, hd * 256:(hd + 1) * 256], in0=hf.t[:, b, hd * 256:(hd + 1) * 256],
                                                                       scalar1=ssq.t[:, b * 4 + hd:b * 4 + hd + 1], scalar2=None, op0=ALU.mult),
                               reads=[hf, ssq], writes=[hn])
                for m in range(KC):
                    bk = sch.bank()
                    bv = bk.t[:, :].bitcast(BF16)
                    for b in range(nb):
                        sch.op(pe, lambda b=b: P.transpose(bv[:, b * 128:(b + 1) * 128], hn.t[:, b, m * 128:(m + 1) * 128], ident_b.t[:]),
                               reads=[hn, ident_b], writes=[bk], inc=(b == nb - 1))
                    sch.op(dve, lambda: V.scalar_tensor_tensor(out=hmT.t[:, m, :n], in0=bv[:, 0:n], scalar=gmh_s.t[:, l, m:m + 1], in1=so.t[:, m, :n],
                                                               op0=ALU.mult, op1=ALU.mult),
                           reads=[bk, gmh_s, so], writes=[hmT])
                for m in range(KC):
                    bk = sch.bank()
                    for k in range(KC):
                        sch.op(pe, lambda k=k: P.matmul(bk.t[:, :n], wbm.t[:, k, m * 128:(m + 1) * 128], hmT.t[:, k, :n], start=(k == 0), stop=(k == KC - 1)),
                               reads=[wbm, hmT], writes=[bk], inc=(k == KC - 1))
                    sch.op(dve, lambda: V.tensor_tensor(out=tm.t[:, m, :n], in0=bk.t[:, :n], in1=gm.t[:, m, :n], op=ALU.mult),
                           reads=[bk, gm], writes=[tm])
                    bk2 = sch.bank()
                    for k in range(KC):
                        sch.op(pe, lambda k=k: P.matmul(bk2.t[:, :n], wba.t[:, k, m * 128:(m + 1) * 128], hat.t[:, k, :n], start=(k == 0), stop=(k == KC - 1)),
                               reads=[wba, hat], writes=[bk2], inc=(k == KC - 1))
                    sch.op(dve, lambda: V.tensor_tensor(out=t2.t[:, :n], in0=bk2.t[:, :n], in1=ga.t[:, m, :n], op=ALU.mult),
                           reads=[bk2, ga], writes=[t2])
                    sch.op(dve, lambda: V.tensor_tensor(out=tb_.t[:, m, :n], in0=tm.t[:, m, :n], in1=t2.t[:, :n], op=ALU.add),
                           reads=[tm, t2], writes=[tb_])
                for m in range(KC):
                    bk = sch.bank()
                    for k in range(KC):
                        sch.op(pe, lambda k=k: P.matmul(bk.t[:, :n], wout.t[:, k, m * 128:(m + 1) * 128], tb_.t[:, k, :n], start=(k == 0), stop=(k == KC - 1)),
                               reads=[wout, tb_], writes=[bk], inc=(k == KC - 1))
                    sch.op(dve, lambda: V.scalar_tensor_tensor(out=xt.t[:, m, :n], in0=bk.t[:, :n], scalar=GT.t[:, l, 1, m, j:j + 1], in1=xt.t[:, m, :n],
                                                               op0=ALU.mult, op1=ALU.add),
                           reads=[bk, GT, xt], writes=[xt])
                sch.dma("sp", XT[:, :, t0:t0 + n].rearrange("k p t -> p k t"), xt.t[:, :, :n], reads=[xt], writes=[xres[ti]])
            sch.barrier()

    def final_phase():
        with ExitStack() as ps:
            xts = [sb(ps, f"fxt{q}", [128, KC, 512], F32) for q in range(2)]
            sq = sb(ps, "fsq", [128, KC, 512], BF16)
            rt = sb(ps, "frt", [128, 512], F32)
            ot = [sb(ps, f"fot{q}", [128, KC, 512], F32) for q in range(2)]
            for ti, (t0, n, j) in enumerate(tiles):
                if j == 1:
                    continue
                xt = xts[ti % 2]
                o = ot[ti % 2]
                sch.dma("sp", xt.t[:, :, :n], XT[:, :, t0:t0 + n].rearrange("k p t -> p k t"), reads=[xres[ti]], writes=[xt])
                rms_rstd((sq, rt), lambda c: xt.t[:, c, :n], KC, n, 1.0 / D, xt)
                for k in range(KC):
                    sch.op(dve, lambda k=k: V.scalar_tensor_tensor(out=o.t[:, k, :n], in0=xt.t[:, k, :n], scalar=gfin_s.t[:, k:k + 1], in1=rt.t[:, :n],
                                                                   op0=ALU.mult, op1=ALU.mult),
                           reads=[xt, gfin_s, rt], writes=[o])
                sch.dma("sp", outT[:, :, t0 - CT:t0 - CT + n].rearrange("k p t -> p k t"), o.t[:, :, :n], reads=[o], writes=[ROUT])
            sch.barrier()

    for l in range(L):
        ffn_phase(l, 0, first=(l == 0))
        inproj_phase(l)
        mlstm_phase(l)
        attn_phase(l)
        merge_phase(l)
        ffn_phase(l, 2, first=False)
    final_phase()
    es.close()
    build.ninst = sch.ninst
    return nc


def _prep_shared(inp, L, S):
    f = np.float32
    T = CT + S
    out = {}
    w_ada = np.asarray(inp["w_ada"], f)[:L]
    out["wada"] = np.ascontiguousarray(w_ada.reshape(L, KC, 128, 72, 128).transpose(0, 3, 2, 1, 4))
    out["bada"] = np.ascontiguousarray(np.asarray(inp["b_ada"], f)[:L].reshape(L, 72, 128).transpose(2, 0, 1))
    gn = np.stack([np.asarray(inp[k], f)[:L] for k in ("g_n1", "g_n2", "g_n3")], axis=1)
    out["gn"] = np.ascontiguousarray(gn.reshape(L, 3, KC, 128).transpose(3, 0, 1, 2))
    out["gfin"] = np.ascontiguousarray(np.asarray(inp["g_final"], f).reshape(KC, 128).T)
    for i, (ku, kd) in enumerate((("w_ff1_up", "w_ff1_dn"), ("w_ff2_up", "w_ff2_dn"))):
        wu = np.asarray(inp[ku], f)[:L]
        wu = wu.reshape(L, KC, 128, 2, NF, 128)
        out[f"wup{i}"] = np.ascontiguousarray(wu.transpose(0, 4, 2, 1, 3, 5))
        wd = np.asarray(inp[kd], f)[:L].reshape(L, NF, 128, D)
        out[f"wdn{i}"] = np.ascontiguousarray(wd.transpose(0, 2, 1, 3))
    w_in = np.asarray(inp["w_in"], f)[:L]
    o = 0
    offs = {}
    for name, n in (("m_q", 1024), ("m_k", 1024), ("m_v", 1024), ("m_o", 1024), ("m_gate", 16), ("a_cq", 384), ("a_ckv", 256), ("a_kr", 64), ("br_gate", 2048)):
        offs[name] = (o, n)
        o += n

    def grp(name):
        a, n = offs[name]
        return w_in[:, :, a:a + n]

    deint = np.concatenate([np.arange(0, 64, 2), np.arange(1, 64, 2)])
    swp = np.concatenate([deint[32:], deint[:32]])
    gates = grp("m_gate")
    grep = np.zeros((L, D, 4, 128), f)
    for d in range(2):
        for ki in range(2):
            for hd in range(4):
                for qd in range(2):
                    grep[:, :, d * 2 + ki, 32 * qd + hd] = gates[:, :, d * 8 + ki * 4 + hd]
    kr = grp("a_kr")
    kr2 = np.concatenate([kr[:, :, deint], kr[:, :, swp]], axis=2)
    cols = np.concatenate([grp("m_q"), grp("m_k"), grp("m_o"), grp("br_gate"), grp("a_cq"), grp("a_ckv"), grep.reshape(L, D, 512), kr2], axis=2)
    assert cols.shape[2] == NWF * 128
    out["wf"] = np.ascontiguousarray(cols.reshape(L, KC, 128, NWF, 128).transpose(0, 3, 2, 1, 4))
    out["wv"] = np.ascontiguousarray(grp("m_v").reshape(L, KC, 128, D).transpose(0, 2, 1, 3))
    wc = np.asarray(inp["w_conv"], f)[:L]
    out["convw"] = np.ascontiguousarray(wc.reshape(L, 3, 16, 128).transpose(3, 0, 2, 1))
    bgl = np.asarray(inp["b_gate"], f)[:L]
    bg = np.zeros((128, L, 4), f)
    for d in range(2):
        for ki in range(2):
            for hd in range(4):
                for qd in range(2):
                    bg[32 * qd + hd, :, d * 2 + ki] = bgl[:, d * 8 + ki * 4 + hd]
    out["bg"] = bg
    out["gmh"] = np.ascontiguousarray(np.asarray(inp["g_mh"], f)[:L].reshape(L, KC, 128).transpose(2, 0, 1))
    out["gqa"] = np.ascontiguousarray(np.asarray(inp["g_qa"], f)[:L].reshape(L, 3, 128).transpose(2, 0, 1))
    out["gkva"] = np.ascontiguousarray(np.asarray(inp["g_kva"], f)[:L].reshape(L, 2, 128).transpose(2, 0, 1))
    wuq = np.asarray(inp["w_uq"], f)[:L].reshape(L, 384, 8, 192)
    chunks = []
    for hd in range(8):
        chunks.append(wuq[:, :, hd, 0:128])
        r = wuq[:, :, hd, 128:192]
        chunks.append(np.concatenate([r[:, :, deint], r[:, :, swp]], axis=2))
    wuq2 = np.stack(chunks, axis=2)
    out["wuq"] = np.ascontiguousarray(wuq2.reshape(L, 3, 128, 16, 128).transpose(0, 2, 1, 3, 4))
    wukv = np.asarray(inp["w_ukv"], f)[:L].reshape(L, 256, 8, 256)
    out["wukvk"] = np.ascontiguousarray(wukv[:, :, :, 0:128].reshape(L, 2, 128, 8, 128).transpose(0, 2, 1, 3, 4))
    out["wukvv"] = np.ascontiguousarray(wukv[:, :, :, 128:256].reshape(L, 2, 128, 1024).transpose(0, 2, 1, 3))
    for k, kk in (("w_bm", "wbm"), ("w_ba", "wba"), ("w_out", "wout")):
        out[kk] = np.ascontiguousarray(np.asarray(inp[k], f)[:L].reshape(L, KC, 128, D).transpose(0, 2, 1, 3))
    inv = (10000.0 ** (-np.arange(0, 32, 2, dtype=np.float32) / 32)).astype(f)
    t = np.arange(S)
    row = (t // 64).astype(f)
    col = (t % 64).astype(f)
    ang = np.concatenate([row[:, None] * inv, col[:, None] * inv], axis=-1)
    cos = np.cos(ang).astype(f).T
    sin = np.sin(ang).astype(f).T
    rope = np.zeros((64, 2, T), f)
    rope[:, 0, :CT] = 1.0
    rope[0:32, 0, CT:] = cos
    rope[32:64, 0, CT:] = cos
    rope[0:32, 1, CT:] = -sin
    rope[32:64, 1, CT:] = sin
    out["rope"] = rope
    out["ident"] = np.eye(128, dtype=f)
    sel = np.zeros((128, 4, 128), f)
    for hd in range(4):
        sel[hd, hd, :] = 1.0
    out["sel"] = sel
    tri = np.zeros((64, 2, 64), f)
    s_ = np.arange(64)[:, None]
    j_ = np.arange(64)[None, :]
    tri[:, 0, :] = (s_ <= j_)
    tri[:, 1, :] = (s_ >= j_)
    out["tri"] = tri
    return out


def _run(inp, L, S, dbg=False):
    f = np.float32
    shared = _prep_shared(inp, L, S)
    x = np.asarray(inp["x"], f)
    ctx = np.asarray(inp["ctx"], f)
    c = np.asarray(inp["c"], f)
    cc = np.asarray(inp["c_ctx"], f)
    B = x.shape[0]
    T = CT + S
    in_maps = []
    for core in range(8):
        b = core % B
        cat = np.concatenate([ctx[b], x[b]], axis=0)
        m = dict(shared)
        m["xT"] = np.ascontiguousarray(cat.T.reshape(KC, 128, T))
        scT = np.stack([c[b].reshape(KC, 128).T, cc.reshape(KC, 128).T], axis=-1)
        m["scT"] = np.ascontiguousarray(scT)
        in_maps.append(m)
    nc = build(L, S, dbg)
    res = run_bass_kernel_spmd(nc, in_maps, core_ids=list(range(8)))
    outs = []
    for b in range(B):
        o = res.results[b]["outT"]
        outs.append(np.ascontiguousarray(o.reshape(D, S).T))
    out = np.stack(outs, axis=0).astype(f)
    if dbg:
        return out, res
    return out


def kernel(**inputs):
    return _run(inputs, 4, 4096)
```

```python
import numpy as np
from contextlib import ExitStack
import concourse.bass as bass
import concourse.mybir as mybir
from concourse.bass_utils import run_bass_kernel_spmd

F32 = mybir.dt.float32
BF16 = mybir.dt.bfloat16
AF = mybir.ActivationFunctionType
ALU = mybir.AluOpType

D = 1024
KC = 8
DFF = 2816
NF = 22
CT = 256
EPS = 1e-6
NWF = 50


class SemObj:
    _n = 0

    def __init__(self, h):
        self.h = h
        self.count = 0
        SemObj._n += 1
        self.id = SemObj._n


class Res:
    def __init__(self, name=""):
        self.lw = {}
        self.rd = {}
        self.name = name


class Tl:
    def __init__(self, t, name=""):
        self.t = t
        self.r = Res(name)


class EngW:
    def __init__(self, name, eng, so):
        self.name = name
        self.eng = eng
        self.so = so
        self.waited = {}


class Sched:
    def __init__(self, nc, es):
        self.nc = nc

        def mk(name):
            return SemObj(es.enter_context(nc.semaphore(name)))

        self.pe = EngW("pe", nc.tensor, mk("s_pe"))
        self.act = EngW("act", nc.scalar, mk("s_act"))
        self.dve = EngW("dve", nc.vector, mk("s_dve"))
        self.pool = EngW("pool", nc.gpsimd, mk("s_pool"))
        self.sp = EngW("sp", nc.sync, mk("s_sp"))
        self.engs = [self.pe, self.act, self.dve, self.pool, self.sp]
        self.dq = {"sp": [mk(f"dsp{i}") for i in range(12)], "pool": [mk(f"dpl{i}") for i in range(12)]}
        self.dqn = {"sp": 0, "pool": 0}
        self.banks = []
        self.bank_i = 0
        self.held = set()
        self.ninst = 0

    def _wait(self, E, tick):
        so, v = tick
        if E.waited.get(so.id, 0) >= v:
            return
        E.eng.wait_ge(so.h, v)
        E.waited[so.id] = v
        self.ninst += 1

    def _deps(self, reads, writes, own):
        deps = {}

        def add(t, raw):
            so, v = t
            if so is own and not raw:
                return
            if deps.get(so.id, (None, 0))[1] < v:
                deps[so.id] = (so, v)

        for r in reads:
            for t in r.lw.values():
                add(t, True)
        for w in writes:
            for t in w.lw.values():
                add(t, False)
            for t in w.rd.values():
                add(t, False)
        return deps

    def _commit(self, tick, reads, writes):
        so, v = tick
        for w in writes:
            o = w.lw.get(so.id)
            if o is None or o[1] < v:
                w.lw[so.id] = tick
        for r in reads:
            o = r.rd.get(so.id)
            if o is None or o[1] < v:
                r.rd[so.id] = tick

    def op(self, E, fn, reads=(), writes=(), inc=True):
        reads = [x.r if isinstance(x, Tl) else x for x in reads]
        writes = [x.r if isinstance(x, Tl) else x for x in writes]
        for t in self._deps(reads, writes, E.so).values():
            self._wait(E, t)
        ins = fn()
        self.ninst += 1
        if inc:
            ins.then_inc(E.so.h, 1)
            E.so.count += 1
            tick = (E.so, E.so.count)
        else:
            tick = (E.so, E.so.count + 1)
        self._commit(tick, reads, writes)
        return ins

    def dma(self, q, out, in_, reads=(), writes=(), **kw):
        reads = [x.r if isinstance(x, Tl) else x for x in reads]
        writes = [x.r if isinstance(x, Tl) else x for x in writes]
        E = self.sp if q == "sp" else self.pool
        pool = self.dq[q]
        so = pool[self.dqn[q] % len(pool)]
        self.dqn[q] += 1
        if so.count > 0:
            self._wait(E, (so, so.count))
        for t in self._deps(reads, writes, None).values():
            self._wait(E, t)
        E.eng.dma_start(out=out, in_=in_, **kw).then_inc(so.h, 16)
        self.ninst += 1
        so.count += 16
        self._commit((so, so.count), reads, writes)

    def barrier(self):
        ticks = []
        for E in self.engs:
            if E.so.count > 0:
                ticks.append((E.so, E.so.count))
        for q in self.dq.values():
            for so in q:
                if so.count > 0:
                    ticks.append((so, so.count))
        for E in self.engs:
            for t in ticks:
                if t[0] is not E.so:
                    self._wait(E, t)

    def bank(self, hold=False):
        for _ in range(16):
            i = self.bank_i % 8
            self.bank_i += 1
            if i not in self.held:
                if hold:
                    self.held.add(i)
                return self.banks[i]
        raise RuntimeError("no psum bank")

    def release(self, b):
        self.held.discard(self.banks.index(b))


def build(L, S, dbg=False):
    T = CT + S
    NCH = T // 64
    NTB = T // 128
    nc = bass.Bass("TRN2", target_bir_lowering=False)
    es = ExitStack()

    def din(name, shape, dt=F32):
        return nc.dram_tensor(name, list(shape), dt, kind="ExternalInput").ap()

    scratch_kind = "ExternalOutput" if dbg else "Internal"

    def dscr(name, shape, dt):
        return nc.dram_tensor(name, list(shape), dt, kind=scratch_kind).ap()

    xT_in = din("xT", [KC, 128, T])
    scT_in = din("scT", [128, KC, 2])
    wada_in = din("wada", [L, 72, 128, KC, 128])
    bada_in = din("bada", [128, L, 72])
    gn_in = din("gn", [128, L, 3, KC])
    gfin_in = din("gfin", [128, KC])
    wup_in = [din(f"wup{i}", [L, NF, 128, KC, 2, 128]) for i in range(2)]
    wdn_in = [din(f"wdn{i}", [L, 128, NF, D]) for i in range(2)]
    wf_in = din("wf", [L, NWF, 128, KC, 128])
    wv_in = din("wv", [L, 128, KC, D])
    convw_in = din("convw", [128, L, 16, 3])
    bg_in = din("bg", [128, L, 4])
    gmh_in = din("gmh", [128, L, KC])
    gqa_in = din("gqa", [128, L, 3])
    gkva_in = din("gkva", [128, L, 2])
    wuq_in = din("wuq", [L, 128, 3, 16, 128])
    wukvk_in = din("wukvk", [L, 128, 2, 8, 128])
    wukvv_in = din("wukvv", [L, 128, 2, D])
    wbm_in = din("wbm", [L, 128, KC, D])
    wba_in = din("wba", [L, 128, KC, D])
    wout_in = din("wout", [L, 128, KC, D])
    rope_in = din("rope", [64, 2, T])
    ident_in = din("ident", [128, 128])
    sel_in = din("sel", [128, 4, 128])
    tri_in = din("tri", [64, 2, 64])
    outT = nc.dram_tensor("outT", [KC, 128, S], F32, kind="ExternalOutput").ap()

    XT = dscr("XT", [KC, 128, T], F32)
    QT = dscr("QT", [KC, 128, T], BF16)
    KT = dscr("KT", [KC, 128, T], BF16)
    SOT = dscr("SOT", [KC, 128, T], BF16)
    GMT = dscr("GMT", [KC, 128, T], BF16)
    GAT = dscr("GAT", [KC, 128, T], BF16)
    KTOK = dscr("KTOK", [T, D], BF16)
    VTOK = dscr("VTOK", [T, D], BF16)
    GROW = dscr("GROW", [4, 128, T], F32)
    HF = dscr("HF", [T, D], F32)
    HB = dscr("HB", [T, D], F32)
    QN = dscr("QN", [8, 128, T], BF16)
    QR = dscr("QR", [8, 64, T], BF16)
    KN = dscr("KN", [8, 128, T], BF16)
    KR = dscr("KR", [64, T], BF16)
    VA = dscr("VA", [T, D], BF16)
    HAT = dscr("HAT", [8, 128, T], BF16)

    sch = Sched(nc, es)
    pe, act, dve, pool, sp = sch.pe, sch.act, sch.dve, sch.pool, sch.sp

    for i in range(8):
        sch.banks.append(Tl(es.enter_context(nc.psum_tensor(f"bank{i}", [128, 512], F32)), f"bank{i}"))

    sbn = [0]

    def sb(stack, name, shape, dt):
        sbn[0] += 1
        nm = f"s{sbn[0]}_{name}"
        return Tl(stack.enter_context(nc.sbuf_tensor(nm, list(shape), dt)), nm)

    tiles = [(0, CT, 1)] + [(CT + 512 * i, 512, 0) for i in range(S // 512)]
    xres = [Res(f"XT{i}") for i in range(len(tiles))]
    r_xin = Res("xin")
    RQT, RKT, RSOT, RGMT, RGAT, RKTOK, RVTOK, RGROW, RHF, RHB, RQN, RQR, RKN, RKR, RVA, RHAT, ROUT = [Res() for _ in range(17)]

    ident_f = sb(es, "ident_f", [128, 128], F32)
    ident_b = sb(es, "ident_b", [128, 128], BF16)
    ones_b = sb(es, "ones_b", [128, 128], BF16)
    sel_f = sb(es, "sel_f", [128, 4, 128], F32)
    tri_f = sb(es, "tri_f", [64, 2, 64], F32)
    MOD = sb(es, "MOD", [128, L, 72, 2], F32)
    GS = sb(es, "GS", [128, L, 3, KC, 2], F32)
    GT = sb(es, "GT", [128, L, 3, KC, 2], F32)
    gn_s = sb(es, "gn_s", [128, L, 3, KC], F32)
    gfin_s = sb(es, "gfin_s", [128, KC], F32)
    convw_s = sb(es, "convw_s", [128, L, 16, 3], F32)
    bg_s = sb(es, "bg_s", [128, L, 4], F32)
    gmh_s = sb(es, "gmh_s", [128, L, KC], F32)
    gqa_s = sb(es, "gqa_s", [128, L, 3], F32)
    gkva_s = sb(es, "gkva_s", [128, L, 2], F32)
    bada_s = sb(es, "bada_s", [128, L, 72], F32)
    ones_f = sb(es, "ones_f", [128, 1], F32)

    V = nc.vector
    A = nc.scalar
    P = nc.tensor

    def load_const(tl, src, q="sp"):
        sch.dma(q, tl.t[:], src, writes=[tl])

    load_const(ident_f, ident_in[:, :])
    load_const(ident_b, ident_in[:, :], "pool")
    load_const(sel_f, sel_in[:, :, :])
    load_const(tri_f, tri_in[:, :, :])
    load_const(gn_s, gn_in[:, :, :, :])
    load_const(gfin_s, gfin_in[:, :])
    load_const(convw_s, convw_in[:, :, :, :])
    load_const(bg_s, bg_in[:, :, :])
    load_const(gmh_s, gmh_in[:, :, :])
    load_const(gqa_s, gqa_in[:, :, :])
    load_const(gkva_s, gkva_in[:, :, :])
    load_const(bada_s, bada_in[:, :, :])
    sch.op(dve, lambda: V.memset(ones_b.t[:], 1.0), writes=[ones_b])
    sch.op(dve, lambda: V.memset(ones_f.t[:], 1.0), writes=[ones_f])

    with ExitStack() as ps:
        sc_f = sb(ps, "sc_f", [128, KC, 2], F32)
        sc_b = sb(ps, "sc_b", [128, KC, 2], BF16)
        wa = [sb(ps, f"wa{i}", [128, 8, KC, 128], BF16) for i in range(2)]
        load_const(sc_f, scT_in[:, :, :])
        sch.op(act, lambda: A.activation(out=sc_b.t[:], in_=sc_f.t[:], func=AF.Silu), reads=[sc_f], writes=[sc_b])
        it = 0
        for l in range(L):
            for mg in range(9):
                w = wa[it % 2]
                it += 1
                sch.dma("pool", w.t[:], wada_in[l, mg * 8:(mg + 1) * 8].rearrange("m p k c -> p m k c"),
                        writes=[w], max_dma_last_dim=4096)
                bk = sch.bank()
                for m in range(8):
                    for k in range(KC):
                        sch.op(pe, lambda m=m, k=k: P.matmul(bk.t[:, 2 * m:2 * m + 2], w.t[:, m, k, :], sc_b.t[:, k, :],
                                                             start=(k == 0), stop=(k == KC - 1)),
                               reads=[w, sc_b], writes=[bk], inc=(k == KC - 1))
                sch.op(dve, lambda: V.tensor_tensor(
                    out=MOD.t[:, l, mg * 8:(mg + 1) * 8, :],
                    in0=bk.t[:, 0:16].rearrange("p (m j) -> p m j", j=2),
                    in1=bada_s.t[:, l, mg * 8:(mg + 1) * 8].unsqueeze(2).to_broadcast([128, 8, 2]),
                    op=ALU.add), reads=[bk, bada_s], writes=[MOD])
        for l in range(L):
            for i in range(3):
                coef = 1.0 if i == 1 else 0.5
                sc = MOD.t[:, l, (3 * i + 1) * 8:(3 * i + 2) * 8, :]
                gt = MOD.t[:, l, (3 * i + 2) * 8:(3 * i + 3) * 8, :]
                sch.op(dve, lambda: V.scalar_tensor_tensor(
                    out=GS.t[:, l, i, :, :], in0=sc, scalar=1.0,
                    in1=gn_s.t[:, l, i, :].unsqueeze(2).to_broadcast([128, KC, 2]),
                    op0=ALU.add, op1=ALU.mult), reads=[MOD, gn_s], writes=[GS])
                sch.op(dve, lambda: V.tensor_scalar(out=GT.t[:, l, i, :, :], in0=gt, scalar1=coef, scalar2=None,
                                                    op0=ALU.mult), reads=[MOD], writes=[GT])
        sch.barrier()

    def SH(l, i, k, j):
        return MOD.t[:, l, 3 * i * 8 + k, j:j + 1]

    def rms_rstd(stack_tiles, src_ap_fn, nchunks, n, inv_dim, src_res):
        sq, rt = stack_tiles
        for c in range(nchunks):
            sch.op(act, lambda c=c: A.activation(out=sq.t[:, c, :n], in_=src_ap_fn(c), func=AF.Square),
                   reads=[src_res], writes=[sq])
        bk = sch.bank()
        for c in range(nchunks):
            sch.op(pe, lambda c=c: P.matmul(bk.t[:, :n], ones_b.t[:], sq.t[:, c, :n], start=(c == 0), stop=(c == nchunks - 1)),
                   reads=[ones_b, sq], writes=[bk], inc=(c == nchunks - 1))
        sch.op(act, lambda: A.activation(out=rt.t[:, :n], in_=bk.t[:, :n], func=AF.Sqrt, scale=inv_dim, bias=eps_s.t[:, 0:1]),
               reads=[bk, eps_s], writes=[rt])
        sch.op(dve, lambda: V.reciprocal(out=rt.t[:, :n], in_=rt.t[:, :n]), reads=[rt], writes=[rt])
        return rt

    eps_s = sb(es, "eps_s", [128, 1], F32)
    sch.op(dve, lambda: V.memset(eps_s.t[:], EPS), writes=[eps_s])

    def norm_mod(xt, n, l, i, j, sq, rt, tmp, hout, hcol0=0):
        rms_rstd((sq, rt), lambda c: xt.t[:, c, :n], KC, n, 1.0 / D, xt)
        sch.op(dve, lambda: V.tensor_tensor(out=tmp.t[:, :, :n], in0=xt.t[:, :, :n],
                                            in1=rt.t[:, :n].unsqueeze(1).to_broadcast([128, KC, n]), op=ALU.mult),
               reads=[xt, rt], writes=[tmp])
        for k in range(KC):
            sch.op(act, lambda k=k: A.activation(out=hout.t[:, k, hcol0:hcol0 + n], in_=tmp.t[:, k, :n], func=AF.Identity,
                                                 scale=GS.t[:, l, i, k, j:j + 1], bias=SH(l, i, k, j)),
                   reads=[tmp, GS, MOD], writes=[hout])

    def ffn_phase(l, which, first):
        i = 0 if which == 0 else 2
        with ExitStack() as ps:
            wdn = sb(ps, "wdn", [128, NF, D], BF16)
            wup = [sb(ps, f"wupb{q}", [128, KC, 2, 128], BF16) for q in range(3)]
            xts = [sb(ps, f"xt{q}", [128, KC, 512], F32) for q in range(2)]
            hs = [sb(ps, f"h{q}", [128, KC, 512], BF16) for q in range(2)]
            sq = sb(ps, "sq", [128, KC, 512], BF16)
            rt = sb(ps, "rt", [128, 512], F32)
            tmp = sb(ps, "tmp", [128, KC, 512], F32)
            h2 = sb(ps, "h2", [128, NF, 512], BF16)
            sil = [sb(ps, f"sil{q}", [128, 512], BF16) for q in range(2)]
            for q in range(2):
                sch.dma("pool", wdn.t[:, q * 11:(q + 1) * 11, :], wdn_in[which // 2][l, :, q * 11:(q + 1) * 11, :],
                        writes=[wdn], max_dma_last_dim=4096)

            def load(ti):
                t0, n, j = tiles[ti]
                xt = xts[ti % 2]
                if first:
                    sch.dma("sp", xt.t[:, :, :n], xT_in[:, :, t0:t0 + n].rearrange("k p t -> p k t"), reads=[r_xin], writes=[xt])
                else:
                    sch.dma("sp", xt.t[:, :, :n], XT[:, :, t0:t0 + n].rearrange("k p t -> p k t"), reads=[xres[ti]], writes=[xt])

            load(0)
            load(1)
            wi = 0
            norm_mod(xts[0], tiles[0][1], l, i, tiles[0][2], sq, rt, tmp, hs[0])
            for ti, (t0, n, j) in enumerate(tiles):
                xt = xts[ti % 2]
                h = hs[ti % 2]
                for f in range(NF):
                    w = wup[wi % 3]
                    wi += 1
                    sch.dma("pool", w.t[:], wup_in[which // 2][l, f], writes=[w], max_dma_last_dim=4096)
                    pa = sch.bank()
                    pb = sch.bank()
                    for k in range(KC):
                        sch.op(pe, lambda k=k: P.matmul(pa.t[:, :n], w.t[:, k, 0, :], h.t[:, k, :n], start=(k == 0), stop=(k == KC - 1)),
                               reads=[w, h], writes=[pa], inc=(k == KC - 1))
                    for k in range(KC):
                        sch.op(pe, lambda k=k: P.matmul(pb.t[:, :n], w.t[:, k, 1, :], h.t[:, k, :n], start=(k == 0), stop=(k == KC - 1)),
                               reads=[w, h], writes=[pb], inc=(k == KC - 1))
                    s_ = sil[f % 2]
                    sch.op(act, lambda: A.activation(out=s_.t[:, :n], in_=pa.t[:, :n], func=AF.Silu), reads=[pa], writes=[s_])
                    sch.op(dve, lambda: V.tensor_tensor(out=h2.t[:, f, :n], in0=pb.t[:, :n], in1=s_.t[:, :n], op=ALU.mult),
                           reads=[pb, s_], writes=[h2])
                if ti + 1 < len(tiles):
                    t0n, nn_, jn = tiles[ti + 1]
                    norm_mod(xts[(ti + 1) % 2], nn_, l, i, jn, sq, rt, tmp, hs[(ti + 1) % 2])
                for m in range(KC):
                    py = sch.bank()
                    for f in range(NF):
                        sch.op(pe, lambda f=f: P.matmul(py.t[:, :n], wdn.t[:, f, m * 128:(m + 1) * 128], h2.t[:, f, :n],
                                                        start=(f == 0), stop=(f == NF - 1)),
                               reads=[wdn, h2], writes=[py], inc=(f == NF - 1))
                    sch.op(dve, lambda: V.scalar_tensor_tensor(out=xt.t[:, m, :n], in0=py.t[:, :n], scalar=GT.t[:, l, i, m, j:j + 1],
                                                               in1=xt.t[:, m, :n], op0=ALU.mult, op1=ALU.add),
                           reads=[py, GT, xt], writes=[xt])
                sch.dma("sp", XT[:, :, t0:t0 + n].rearrange("k p t -> p k t"), xt.t[:, :, :n], reads=[xt], writes=[xres[ti]])
                if ti + 2 < len(tiles):
                    load(ti + 2)
            sch.barrier()

    def inproj_phase(l):
        with ExitStack() as ps:
            hall = sb(ps, "hall", [128, KC, T], BF16)
            with ExitStack() as p1:
                xts = [sb(p1, f"ixt{q}", [128, KC, 512], F32) for q in range(2)]
                sq = sb(p1, "isq", [128, KC, 512], BF16)
                rt = sb(p1, "irt", [128, 512], F32)
                tmp = sb(p1, "itmp", [128, KC, 512], F32)

                def load(ti):
                    t0, n, j = tiles[ti]
                    sch.dma("sp", xts[ti % 2].t[:, :, :n], XT[:, :, t0:t0 + n].rearrange("k p t -> p k t"),
                            reads=[xres[ti]], writes=[xts[ti % 2]])

                load(0)
                for ti, (t0, n, j) in enumerate(tiles):
                    if ti + 1 < len(tiles):
                        load(ti + 1)
                    norm_mod(xts[ti % 2], n, l, 1, j, sq, rt, tmp, hall, hcol0=t0)
                sch.barrier()

            with ExitStack() as p2:
                wfb = [sb(p2, f"wfb{q}", [128, KC, 128], BF16) for q in range(3)]
                zs = [sb(p2, f"zs{q}", [128, T], F32) for q in range(2)]
                ys = sb(p2, "ys", [128, T], F32)
                zb = [sb(p2, f"zb{q}", [128, T], BF16) for q in range(2)]
                ktk = sb(p2, "ktk", [128, NTB, 128], BF16)
                wi = 0

                def chunk_mm(ci, evac):
                    nonlocal wi
                    w = wfb[wi % 3]
                    wi += 1
                    sch.dma("pool", w.t[:], wf_in[l, ci], writes=[w], max_dma_last_dim=4096)
                    for (t0, n, j) in tiles:
                        bk = sch.bank()
                        for k in range(KC):
                            sch.op(pe, lambda k=k: P.matmul(bk.t[:, :n], w.t[:, k, :], hall.t[:, k, t0:t0 + n],
                                                            start=(k == 0), stop=(k == KC - 1)),
                                   reads=[w, hall], writes=[bk], inc=(k == KC - 1))
                        evac(bk, t0, n)

                segs = [(0, CT), (CT, T)]
                for ci in range(16):
                    z = zs[ci % 2]
                    o = zb[ci % 2]

                    def ev(bk, t0, n, z=z):
                        sch.op(act, lambda: A.copy(out=z.t[:, t0:t0 + n], in_=bk.t[:, :n]), reads=[bk], writes=[z])

                    chunk_mm(ci, ev)
                    w0 = convw_s.t[:, l, ci, 0:1]
                    w1 = convw_s.t[:, l, ci, 1:2]
                    w2 = convw_s.t[:, l, ci, 2:3]
                    for (a, b) in segs:
                        sch.op(dve, lambda: V.tensor_scalar(out=ys.t[:, a:b], in0=z.t[:, a:b], scalar1=w1, scalar2=None, op0=ALU.mult),
                               reads=[z, convw_s], writes=[ys])
                        sch.op(dve, lambda: V.scalar_tensor_tensor(out=ys.t[:, a + 1:b], in0=z.t[:, a:b - 1], scalar=w0,
                                                                   in1=ys.t[:, a + 1:b], op0=ALU.mult, op1=ALU.add),
                               reads=[z, ys, convw_s], writes=[ys])
                        sch.op(dve, lambda: V.scalar_tensor_tensor(out=ys.t[:, a:b - 1], in0=z.t[:, a + 1:b], scalar=w2,
                                                                   in1=ys.t[:, a:b - 1], op0=ALU.mult, op1=ALU.add),
                               reads=[z, ys, convw_s], writes=[ys])
                    if ci < 8:
                        sch.op(act, lambda: A.activation(out=z.t[:, :], in_=ys.t[:, :], func=AF.Silu), reads=[ys], writes=[z])
                        sch.op(dve, lambda: V.tensor_scalar(out=o.t[:, :], in0=z.t[:, :], scalar1=0.0625, scalar2=None, op0=ALU.mult),
                               reads=[z], writes=[o])
                        sch.dma("sp", QT[ci], o.t[:, :], reads=[o], writes=[RQT])
                    else:
                        m = ci - 8
                        sch.op(act, lambda: A.activation(out=o.t[:, :], in_=ys.t[:, :], func=AF.Silu), reads=[ys], writes=[o])
                        sch.dma("sp", KT[m], o.t[:, :], reads=[o], writes=[RKT])
                        for g in range(0, NTB, 8):
                            nb = min(8, NTB - g)
                            bk = sch.bank()
                            bv = bk.t[:, :].bitcast(BF16)
                            for q in range(nb):
                                tb = g + q
                                sch.op(pe, lambda q=q, tb=tb: P.transpose(bv[:, q * 128:(q + 1) * 128], o.t[:, tb * 128:(tb + 1) * 128], ident_b.t[:]),
                                       reads=[o, ident_b], writes=[bk], inc=(q == nb - 1))
                            sch.op(act, lambda: A.copy(out=ktk.t[:, g:g + nb, :], in_=bv[:, 0:nb * 128].rearrange("p (q f) -> p q f", f=128)),
                                   reads=[bk], writes=[ktk])
                        sch.dma("sp", KTOK[:, m * 128:(m + 1) * 128].rearrange("(tb p) f -> p tb f", p=128), ktk.t[:, :, :],
                                reads=[ktk], writes=[RKTOK])
                for ci in range(16, 40):
                    o = zb[ci % 2]

                    def ev(bk, t0, n, o=o):
                        sch.op(act, lambda: A.activation(out=o.t[:, t0:t0 + n], in_=bk.t[:, :n], func=AF.Sigmoid), reads=[bk], writes=[o])

                    chunk_mm(ci, ev)
                    if ci < 24:
                        sch.dma("sp", SOT[ci - 16], o.t[:, :], reads=[o], writes=[RSOT])
                    elif ci < 32:
                        sch.dma("sp", GMT[ci - 24], o.t[:, :], reads=[o], writes=[RGMT])
                    else:
                        sch.dma("sp", GAT[ci - 32], o.t[:, :], reads=[o], writes=[RGAT])
                for kind in range(4):
                    ci = 45 + kind
                    z = zs[ci % 2]

                    def ev(bk, t0, n, z=z, kind=kind):
                        sch.op(act, lambda: A.activation(out=z.t[:, t0:t0 + n], in_=bk.t[:, :n], func=AF.Identity,
                                                         bias=bg_s.t[:, l, kind:kind + 1], scale=1.0),
                               reads=[bk, bg_s], writes=[z])

                    chunk_mm(ci, ev)
                    sch.dma("sp", GROW[kind], z.t[:, :], reads=[z], writes=[RGROW])
                sch.barrier()

            with ExitStack() as p3:
                wv = sb(p3, "wv", [128, KC, D], BF16)
                vt = [sb(p3, f"vt{q}", [128, D], BF16) for q in range(2)]
                sch.dma("pool", wv.t[:], wv_in[l], writes=[wv], max_dma_last_dim=4096)
                for tb in range(NTB):
                    v_ = vt[tb % 2]
                    for hf in range(2):
                        bk = sch.bank()
                        for k in range(KC):
                            sch.op(pe, lambda k=k: P.matmul(bk.t[:, :], hall.t[:, k, tb * 128:(tb + 1) * 128], wv.t[:, k, hf * 512:(hf + 1) * 512],
                                                            start=(k == 0), stop=(k == KC - 1)),
                                   reads=[wv, hall], writes=[bk], inc=(k == KC - 1))
                        if hf == 0:
                            sch.op(act, lambda: A.copy(out=v_.t[:, 0:512], in_=bk.t[:, :]), reads=[bk], writes=[v_])
                        else:
                            sch.op(dve, lambda: V.tensor_copy(out=v_.t[:, 512:1024], in_=bk.t[:, :]), reads=[bk], writes=[v_])
                    sch.dma("sp", VTOK[tb * 128:(tb + 1) * 128, :], v_.t[:, :], reads=[v_], writes=[RVTOK])
                sch.barrier()

            with ExitStack() as p4:
                wl = sb(p4, "wl", [128, 6, KC, 128], BF16)
                wuq = sb(p4, "wuq", [128, 3, 16, 128], BF16)
                wkk = sb(p4, "wkk", [128, 2, 8, 128], BF16)
                wkv = sb(p4, "wkv", [128, 2, D], BF16)
                cqf = sb(p4, "cqf", [128, 3, 512], F32)
                sq = sb(p4, "lsq", [128, 3, 512], BF16)
                rt = sb(p4, "lrt", [128, 512], F32)
                cqn = sb(p4, "cqn", [128, 3, 512], BF16)
                ckn = sb(p4, "ckn", [128, 2, 512], BF16)
                rp = [sb(p4, f"rp{q}", [64, 2, 512], F32) for q in range(2)]
                r1 = sb(p4, "r1", [64, 512], F32)
                r2 = sb(p4, "r2", [64, 512], F32)
                ob = [sb(p4, f"ob{q}", [128, 512], BF16) for q in range(4)]
                vb = [sb(p4, f"vb{q}", [128, D], BF16) for q in range(2)]
                for c in range(5):
                    sch.dma("pool", wl.t[:, c], wf_in[l, 40 + c], writes=[wl], max_dma_last_dim=4096)
                sch.dma("pool", wl.t[:, 5], wf_in[l, 49], writes=[wl], max_dma_last_dim=4096)
                sch.dma("pool", wuq.t[:], wuq_in[l], writes=[wuq], max_dma_last_dim=4096)
                sch.dma("pool", wkk.t[:], wukvk_in[l], writes=[wkk], max_dma_last_dim=4096)
                sch.dma("pool", wkv.t[:], wukvv_in[l], writes=[wkv], max_dma_last_dim=4096)
                oi = 0

                def rope_out(pa, pb, n, rpt, dst_ap, dres):
                    nonlocal oi
                    o = ob[oi % 4]
                    oi += 1
                    sch.op(dve, lambda: V.tensor_tensor(out=r1.t[:, :n], in0=pa.t[0:64, :n], in1=rpt.t[:, 0, :n], op=ALU.mult),
                           reads=[pa, rpt], writes=[r1])
                    sch.op(dve, lambda: V.tensor_tensor(out=r2.t[:, :n], in0=pb.t[0:64, :n], in1=rpt.t[:, 1, :n], op=ALU.mult),
                           reads=[pb, rpt], writes=[r2])
                    sch.op(dve, lambda: V.tensor_tensor(out=o.t[0:64, :n], in0=r1.t[:, :n], in1=r2.t[:, :n], op=ALU.add),
                           reads=[r1, r2], writes=[o])
                    sch.dma("sp", dst_ap, o.t[0:64, :n], reads=[o], writes=[dres])

                for ti, (t0, n, j) in enumerate(tiles):
                    rpt = rp[ti % 2]
                    sch.dma("sp", rpt.t[:, :, :n], rope_in[:, :, t0:t0 + n], writes=[rpt])
                    for (c0, ncn, gsrc, dst, inv) in ((0, 3, gqa_s, cqn, 1.0 / 384), (3, 2, gkva_s, ckn, 1.0 / 256)):
                        for c in range(ncn):
                            bk = sch.bank()
                            for k in range(KC):
                                sch.op(pe, lambda k=k: P.matmul(bk.t[:, :n], wl.t[:, c0 + c, k, :], hall.t[:, k, t0:t0 + n],
                                                                start=(k == 0), stop=(k == KC - 1)),
                                       reads=[wl, hall], writes=[bk], inc=(k == KC - 1))
                            sch.op(act, lambda: A.copy(out=cqf.t[:, c, :n], in_=bk.t[:, :n]), reads=[bk], writes=[cqf])
                        rms_rstd((sq, rt), lambda c: cqf.t[:, c, :n], ncn, n, inv, cqf)
                        for c in range(ncn):
                            sch.op(dve, lambda c=c: V.scalar_tensor_tensor(out=dst.t[:, c, :n], in0=cqf.t[:, c, :n], scalar=gsrc.t[:, l, c:c + 1],
                                                                           in1=rt.t[:, :n], op0=ALU.mult, op1=ALU.mult),
                                   reads=[cqf, gsrc, rt], writes=[dst])
                    pa = sch.bank()
                    pb = sch.bank()
                    for k in range(KC):
                        sch.op(pe, lambda k=k: P.matmul(pa.t[0:64, :n], wl.t[:, 5, k, 0:64], hall.t[:, k, t0:t0 + n], start=(k == 0), stop=(k == KC - 1)),
                               reads=[wl, hall], writes=[pa], inc=(k == KC - 1))
                    for k in range(KC):
                        sch.op(pe, lambda k=k: P.matmul(pb.t[0:64, :n], wl.t[:, 5, k, 64:128], hall.t[:, k, t0:t0 + n], start=(k == 0), stop=(k == KC - 1)),
                               reads=[wl, hall], writes=[pb], inc=(k == KC - 1))
                    rope_out(pa, pb, n, rpt, KR[:, t0:t0 + n], RKR)
                    for hd in range(8):
                        bk = sch.bank()
                        for c in range(3):
                            sch.op(pe, lambda c=c: P.matmul(bk.t[:, :n], wuq.t[:, c, 2 * hd, :], cqn.t[:, c, :n], start=(c == 0), stop=(c == 2)),
                                   reads=[wuq, cqn], writes=[bk], inc=(c == 2))
                        o = ob[oi % 4]
                        oi += 1
                        sch.op(act, lambda: A.copy(out=o.t[:, :n], in_=bk.t[:, :n]), reads=[bk], writes=[o])
                        sch.dma("sp", QN[hd, :, t0:t0 + n], o.t[:, :n], reads=[o], writes=[RQN])
                        pa = sch.bank()
                        pb = sch.bank()
                        for c in range(3):
                            sch.op(pe, lambda c=c: P.matmul(pa.t[0:64, :n], wuq.t[:, c, 2 * hd + 1, 0:64], cqn.t[:, c, :n], start=(c == 0), stop=(c == 2)),
                                   reads=[wuq, cqn], writes=[pa], inc=(c == 2))
                        for c in range(3):
                            sch.op(pe, lambda c=c: P.matmul(pb.t[0:64, :n], wuq.t[:, c, 2 * hd + 1, 64:128], cqn.t[:, c, :n], start=(c == 0), stop=(c == 2)),
                                   reads=[wuq, cqn], writes=[pb], inc=(c == 2))
                        rope_out(pa, pb, n, rpt, QR[hd, :, t0:t0 + n], RQR)
                        bk = sch.bank()
                        for c in range(2):
                            sch.op(pe, lambda c=c: P.matmul(bk.t[:, :n], wkk.t[:, c, hd, :], ckn.t[:, c, :n], start=(c == 0), stop=(c == 1)),
                                   reads=[wkk, ckn], writes=[bk], inc=(c == 1))
                        o = ob[oi % 4]
                        oi += 1
                        sch.op(dve, lambda: V.tensor_copy(out=o.t[:, :n], in_=bk.t[:, :n]), reads=[bk], writes=[o])
                        sch.dma("sp", KN[hd, :, t0:t0 + n], o.t[:, :n], reads=[o], writes=[RKN])
                    for b in range(n // 128):
                        v_ = vb[b % 2]
                        for hf in range(2):
                            bk = sch.bank()
                            for c in range(2):
                                sch.op(pe, lambda c=c: P.matmul(bk.t[:, :], ckn.t[:, c, b * 128:(b + 1) * 128], wkv.t[:, c, hf * 512:(hf + 1) * 512],
                                                                start=(c == 0), stop=(c == 1)),
                                       reads=[wkv, ckn], writes=[bk], inc=(c == 1))
                            sch.op(act, lambda: A.copy(out=v_.t[:, hf * 512:(hf + 1) * 512], in_=bk.t[:, :]), reads=[bk], writes=[v_])
                        sch.dma("sp", VA[t0 + b * 128:t0 + (b + 1) * 128, :], v_.t[:, :], reads=[v_], writes=[RVA])
                sch.barrier()
            sch.barrier()

    def mlstm_phase(l):
        with ExitStack() as ps:
            CA = [sb(ps, f"CA{d}", [64, NCH, 64], F32) for d in range(2)]
            ABC = sb(ps, "ABC", [128, 2, 4, NCH], F32)
            with ExitStack() as pg:
                W1 = sb(pg, "W1", [64, T], F32)
                NB = sb(pg, "NB", [64, T], F32)
                U = sb(pg, "U", [64, T], F32)
                G = sb(pg, "G", [64, T], F32)
                GE = sb(pg, "GE", [128, NCH], F32)
                GP = sb(pg, "GP", [128, NCH], F32)
                AA = sb(pg, "AA", [128, NCH], F32)
                TM = sb(pg, "TM", [64, T], F32)
                RS = [sb(pg, f"RS{d}", [64, T], F32) for d in range(2)]
                sch.op(dve, lambda: V.memset(AA.t[:], 0.0), writes=[AA])
                for d in range(2):
                    sch.dma("sp", W1.t[:, :], GROW[2 * d + 1, 0:64, :], reads=[RGROW], writes=[W1])
                    sch.dma("sp", U.t[:, :], GROW[2 * d, 0:64, :], reads=[RGROW], writes=[U])
                    sch.op(act, lambda: A.activation(out=W1.t[:, :], in_=W1.t[:, :], func=AF.Exp, scale=-1.0), reads=[W1], writes=[W1])
                    sch.op(act, lambda: A.activation(out=W1.t[:, :], in_=W1.t[:, :], func=AF.Ln, bias=ones_f.t[0:64, 0:1], scale=1.0),
                           reads=[W1, ones_f], writes=[W1])
                    if d == 0:
                        scans = [(slice(0, T), None)]
                    else:
                        scans = [(slice(CT - 1, None, -1), None), (slice(T - 1, CT - 1, -1), 0)]
                    for (sl, init_col) in scans:
                        nn = len(range(T)[sl])
                        init = 0.0 if init_col is None else NB.t[:, init_col:init_col + 1]
                        sch.op(dve, lambda: V.tensor_tensor_scan(out=NB.t[:, sl], data0=ones_f.t[0:64, 0:1].to_broadcast([64, nn]),
                                                                 data1=W1.t[:, sl], initial=init, op0=ALU.mult, op1=ALU.add),
                               reads=[W1, ones_f, NB], writes=[NB])
                    sch.op(dve, lambda: V.tensor_tensor(out=U.t[:, :], in0=U.t[:, :], in1=NB.t[:, :], op=ALU.add), reads=[U, NB], writes=[U])
                    for (sl, init_col) in scans:
                        init = 0.0 if init_col is None else G.t[:, init_col:init_col + 1]
                        sch.op(dve, lambda: V.tensor_tensor_scan(out=G.t[:, sl], data0=U.t[:, sl], data1=U.t[:, sl], initial=init,
                                                                 op0=ALU.max, op1=ALU.max),
                               reads=[U, G], writes=[G])
                    Gv = G.t[:, :].rearrange("p (c s) -> p c s", s=64)
                    if d == 0:
                        sch.op(dve, lambda: V.tensor_copy(out=GE.t[0:64, :], in_=Gv[:, :, 63]), reads=[G], writes=[GE])
                        sch.op(dve, lambda: V.memset(GP.t[0:64, 0:1], 0.0), writes=[GP])
                        sch.op(dve, lambda: V.tensor_copy(out=GP.t[0:64, 1:NCH], in_=GE.t[0:64, 0:NCH - 1]), reads=[GE], writes=[GP])
                    else:
                        nck = CT // 64
                        sch.op(dve, lambda: V.tensor_copy(out=GE.t[0:64, :], in_=Gv[:, :, 0]), reads=[G], writes=[GE])
                        sch.op(dve, lambda: V.memset(GP.t[0:64, nck - 1:nck], 0.0), writes=[GP])
                        sch.op(dve, lambda: V.tensor_copy(out=GP.t[0:64, 0:nck - 1], in_=GE.t[0:64, 1:nck]), reads=[GE], writes=[GP])
                        sch.op(dve, lambda: V.tensor_copy(out=GP.t[0:64, NCH - 1:NCH], in_=GE.t[0:64, 0:1]), reads=[GE], writes=[GP])
                        sch.op(dve, lambda: V.tensor_copy(out=GP.t[0:64, nck:NCH - 1], in_=GE.t[0:64, nck + 1:NCH]), reads=[GE], writes=[GP])
                    GEb = GE.t[0:64, :].unsqueeze(2).to_broadcast([64, NCH, 64])
                    TMv = TM.t[:, :].rearrange("p (c s) -> p c s", s=64)
                    sch.op(dve, lambda: V.tensor_tensor(out=TMv[0:32], in0=U.t[0:32, :].rearrange("p (c s) -> p c s", s=64), in1=GEb[0:32], op=ALU.subtract),
                           reads=[U, GE], writes=[TM])
                    sch.op(dve, lambda: V.tensor_tensor(out=TMv[32:64], in0=NB.t[32:64, :].rearrange("p (c s) -> p c s", s=64),
                                                        in1=GE.t[32:64, :].unsqueeze(2).to_broadcast([32, NCH, 64]), op=ALU.subtract),
                           reads=[NB, GE], writes=[TM])
                    sch.op(act, lambda: A.activation(out=RS[d].t[:, :], in_=TM.t[:, :], func=AF.Exp), reads=[TM], writes=[RS[d]])
                    for g in range(0, NCH, 8):
                        ng = min(8, NCH - g)
                        bk = sch.bank()
                        for q in range(ng):
                            c = g + q
                            sch.op(pe, lambda q=q, c=c: P.transpose(bk.t[0:64, q * 64:(q + 1) * 64], RS[d].t[:, c * 64:(c + 1) * 64], ident_f.t[0:64, 0:64]),
                                   reads=[RS[d], ident_f], writes=[bk], inc=(q == ng - 1))
                        sch.op(dve, lambda: V.tensor_copy(out=CA[d].t[:, g:g + ng, :], in_=bk.t[0:64, 0:ng * 64].rearrange("p (q f) -> p q f", f=64)),
                               reads=[bk], writes=[CA[d]])
                    sch.op(dve, lambda: V.tensor_tensor(out=AA.t[0:64, :], in0=GP.t[0:64, :], in1=GE.t[0:64, :], op=ALU.subtract),
                           reads=[GP, GE], writes=[AA])
                    sch.op(act, lambda: A.activation(out=AA.t[0:64, :], in_=AA.t[0:64, :], func=AF.Exp), reads=[AA], writes=[AA])
                    for hd in range(4):
                        bk = sch.bank()
                        sch.op(pe, lambda: P.matmul(bk.t[:, 0:NCH], sel_f.t[:, hd, :], AA.t[:, :], start=True, stop=True),
                               reads=[sel_f, AA], writes=[bk])
                        sch.op(act, lambda: A.copy(out=ABC.t[:, d, hd, :], in_=bk.t[:, 0:NCH]), reads=[bk], writes=[ABC])
                sch.barrier()

            with ExitStack() as ph:
                WT = 256
                WC = WT // 64
                NW = T // WT
                qTw = [sb(ph, f"qTw{q}", [128, KC, WT], BF16) for q in range(2)]
                kTw = [sb(ph, f"kTw{q}", [128, KC, WT], BF16) for q in range(2)]
                ktw = [sb(ph, f"ktw{q}", [64, WC, D], BF16) for q in range(2)]
                vxw = [sb(ph, f"vxw{q}", [64, WC, 4, 258], BF16) for q in range(2)]
                Cf = sb(ph, "Cf", [128, 4, 2, 258], F32)
                Cb = sb(ph, "Cb", [128, 4, 2, 258], BF16)
                NS = 2
                sTm = [[sb(ph, f"sTm{q}_{h}", [64, 64], BF16) for h in range(4)] for q in range(NS)]
                t1 = [[sb(ph, f"t1{q}_{h}", [64, 258], F32) for h in range(4)] for q in range(NS)]
                nd = [[sb(ph, f"nd{q}_{h}", [64, 258], F32) for h in range(4)] for q in range(NS)]
                dn = [[sb(ph, f"dn{q}_{h}", [64, 2], F32) for h in range(4)] for q in range(NS)]
                kw = [[sb(ph, f"kw{q}_{h}", [64, 256], BF16) for h in range(4)] for q in range(NS)]
                ho = [sb(ph, f"ho{q}", [64, 4, 256], F32) for q in range(NS)]
                for q in range(2):
                    sch.op(dve, lambda q=q: V.memset(vxw[q].t[:, :, :, 256:258], 1.0), writes=[vxw[q]])
                nck = CT // 64
                step = 0
                wi = 0
                for d in range(2):
                    HD = HF if d == 0 else HB
                    RH = RHF if d == 0 else RHB
                    lat = list(range(CT // WT, NW))
                    worder = list(range(CT // WT)) + (lat if d == 0 else lat[::-1])
                    if d == 1:
                        worder = list(range(CT // WT))[::-1] + lat[::-1]
                    sch.op(dve, lambda: V.memset(Cf.t[:], 0.0), writes=[Cf])
                    sch.op(dve, lambda: V.memset(Cb.t[:], 0.0), writes=[Cb])

                    def loadw(w, slot):
                        ts = slice(w * WT, (w + 1) * WT)
                        sch.dma("sp", qTw[slot].t[:, :, :], QT[:, :, ts].rearrange("k p t -> p k t"), reads=[RQT], writes=[qTw[slot]])
                        sch.dma("sp", kTw[slot].t[:, :, :], KT[:, :, ts].rearrange("k p t -> p k t"), reads=[RKT], writes=[kTw[slot]])
                        sch.dma("sp", ktw[slot].t[:, :, :], KTOK[ts, :].rearrange("(c p) f -> p c f", p=64), reads=[RKTOK], writes=[ktw[slot]])
                        for hd in range(4):
                            sch.dma("sp", vxw[slot].t[:, :, hd, 0:256], VTOK[ts, hd * 256:(hd + 1) * 256].rearrange("(c p) f -> p c f", p=64),
                                    reads=[RVTOK], writes=[vxw[slot]])

                    loadw(worder[0], wi % 2)
                    for wn, w in enumerate(worder):
                        slot = wi % 2
                        wi += 1
                        if wn + 1 < len(worder):
                            loadw(worder[wn + 1], wi % 2)
                        qT_, kT_, kt_, vx_ = qTw[slot], kTw[slot], ktw[slot], vxw[slot]
                        corder = list(range(WC)) if d == 0 else list(range(WC - 1, -1, -1))
                        for cl_ in corder:
                            c = w * WC + cl_
                            q = step % NS
                            step += 1
                            cs = slice(cl_ * 64, (cl_ + 1) * 64)
                            WEc = [CA[d].t[:, c, hd:hd + 1] for hd in range(4)]
                            EMc = [CA[d].t[:, c, 32 + hd:32 + hd + 1] for hd in range(4)]
                            for hd in range(4):
                                sch.op(act, lambda hd=hd: A.activation(out=kw[q][hd].t[:, :], in_=kt_.t[:, cl_, hd * 256:(hd + 1) * 256], func=AF.Copy, scale=WEc[hd]),
                                       reads=[kt_, CA[d]], writes=[kw[q][hd]])
                            for hd in range(4):
                                for kc in range(2):
                                    pu = sch.bank()
                                    sch.op(pe, lambda hd=hd, kc=kc: P.matmul(pu.t[:, 0:258], kw[q][hd].t[:, kc * 128:(kc + 1) * 128], vx_.t[:, cl_, hd, :], start=True, stop=True),
                                           reads=[kw[q][hd], vx_], writes=[pu])
                                    sch.op(dve, lambda hd=hd, kc=kc: V.scalar_tensor_tensor(out=Cf.t[:, hd, kc, :], in0=Cf.t[:, hd, kc, :], scalar=ABC.t[:, d, hd, c:c + 1],
                                                                                            in1=pu.t[:, 0:258], op0=ALU.mult, op1=ALU.add),
                                           reads=[Cf, ABC, pu], writes=[Cf])
                            psc = sch.bank()
                            for hd in range(4):
                                for kc in range(2):
                                    sch.op(pe, lambda hd=hd, kc=kc: P.matmul(psc.t[0:64, hd * 64:(hd + 1) * 64], kT_.t[:, 2 * hd + kc, cs], qT_.t[:, 2 * hd + kc, cs],
                                                                             start=(kc == 0), stop=(kc == 1)),
                                           reads=[kT_, qT_], writes=[psc], inc=(kc == 1 and hd == 3))
                            pin = []
                            for hd in range(4):
                                pi = sch.bank()
                                pin.append(pi)
                                for kc in range(2):
                                    sch.op(pe, lambda hd=hd, kc=kc: P.matmul(pi.t[0:64, 0:258], qT_.t[:, 2 * hd + kc, cs], Cb.t[:, hd, kc, :], start=(kc == 0), stop=(kc == 1)),
                                           reads=[qT_, Cb], writes=[pi], inc=(kc == 1))
                            for hd in range(4):
                                sch.op(act, lambda hd=hd: A.copy(out=Cb.t[:, hd, :, :], in_=Cf.t[:, hd, :, :]), reads=[Cf], writes=[Cb])
                            for hd in range(4):
                                sch.op(dve, lambda hd=hd: V.scalar_tensor_tensor(out=sTm[q][hd].t[:, :], in0=psc.t[0:64, hd * 64:(hd + 1) * 64], scalar=WEc[hd],
                                                                                 in1=tri_f.t[:, d, :], op0=ALU.mult, op1=ALU.mult),
                                       reads=[psc, CA[d], tri_f], writes=[sTm[q][hd]])
                            for hd in range(4):
                                sch.op(act, lambda hd=hd: A.activation(out=t1[q][hd].t[:, :], in_=pin[hd].t[0:64, 0:258], func=AF.Copy, scale=ABC.t[0:64, d, hd, c:c + 1]),
                                       reads=[pin[hd], ABC], writes=[t1[q][hd]])
                            pnn = []
                            for hd in range(4):
                                pn = sch.bank()
                                pnn.append(pn)
                                sch.op(pe, lambda hd=hd: P.matmul(pn.t[0:64, 0:258], sTm[q][hd].t[:, :], vx_.t[:, cl_, hd, :], start=True, stop=True),
                                       reads=[sTm[q][hd], vx_], writes=[pn])
                            for hd in range(4):
                                sch.op(dve, lambda hd=hd: V.tensor_tensor(out=nd[q][hd].t[:, :], in0=pnn[hd].t[0:64, 0:258], in1=t1[q][hd].t[:, :], op=ALU.add),
                                       reads=[pnn[hd], t1[q][hd]], writes=[nd[q][hd]])
                            for hd in range(4):
                                sch.op(act, lambda hd=hd: A.activation(out=dn[q][hd].t[:, 0:1], in_=nd[q][hd].t[:, 256:257], func=AF.Abs),
                                       reads=[nd[q][hd]], writes=[dn[q][hd]])
                            for hd in range(4):
                                sch.op(dve, lambda hd=hd: V.tensor_tensor(out=dn[q][hd].t[:, 0:1], in0=dn[q][hd].t[:, 0:1], in1=EMc[hd], op=ALU.max),
                                       reads=[dn[q][hd], CA[d]], writes=[dn[q][hd]])
                            for hd in range(4):
                                sch.op(dve, lambda hd=hd: V.reciprocal(out=dn[q][hd].t[:, 1:2], in_=dn[q][hd].t[:, 0:1]), reads=[dn[q][hd]], writes=[dn[q][hd]])
                            for hd in range(4):
                                sch.op(act, lambda hd=hd: A.activation(out=ho[q].t[:, hd, :], in_=nd[q][hd].t[:, 0:256], func=AF.Copy, scale=dn[q][hd].t[:, 1:2]),
                                       reads=[nd[q][hd], dn[q][hd]], writes=[ho[q]])
                            sch.dma("sp", HD[c * 64:(c + 1) * 64, :], ho[q].t[:, :, :].rearrange("p h f -> p (h f)"), reads=[ho[q]], writes=[RH])
                sch.barrier()
            sch.barrier()


    def attn_phase(l):
        sc = float(192 ** -0.5)
        with ExitStack() as ps:
            krT = sb(ps, "krT", [128, T], BF16)
            knT = sb(ps, "knT", [128, T], BF16)
            vh = sb(ps, "vh", [128, NTB, 128], BF16)
            qn = [sb(ps, f"qn{q}", [128, 512], BF16) for q in range(2)]
            qr = [sb(ps, f"qr{q}", [128, 512], BF16) for q in range(2)]
            pt = [sb(ps, f"pt{q}", [128, 512], BF16) for q in range(3)]
            rd = sb(ps, "rd", [128, 512], F32)
            oo = [sb(ps, f"oo{q}", [128, 512], BF16) for q in range(2)]
            sch.op(dve, lambda: V.memset(krT.t[64:128, :], 0.0), writes=[krT])
            for q in range(2):
                sch.op(dve, lambda q=q: V.memset(qr[q].t[64:128, :], 0.0), writes=[qr[q]])
            sch.dma("sp", krT.t[0:64, :], KR[:, :], reads=[RKR], writes=[krT])
            it = 0
            pti = 0
            for hd in range(8):
                sch.dma("sp", knT.t[:, :], KN[hd], reads=[RKN], writes=[knT])
                sch.dma("sp", vh.t[:, :, :], VA[:, hd * 128:(hd + 1) * 128].rearrange("(tb p) f -> p tb f", p=128), reads=[RVA], writes=[vh])
                for ti, (t0, n, j) in enumerate(tiles):
                    q_n = qn[it % 2]
                    q_r = qr[it % 2]
                    o_ = oo[it % 2]
                    it += 1
                    sch.dma("sp", q_n.t[:, :n], QN[hd, :, t0:t0 + n], reads=[RQN], writes=[q_n])
                    sch.dma("sp", q_r.t[0:64, :n], QR[hd, :, t0:t0 + n], reads=[RQR], writes=[q_r])
                    kbs = list(range(CT // 128)) if j == 1 else list(range(NTB))
                    po = sch.bank(hold=True)
                    pd = sch.bank(hold=True)

                    def scores(kb):
                        b = sch.bank()
                        sch.op(pe, lambda: P.matmul(b.t[:, :n], knT.t[:, kb * 128:(kb + 1) * 128], q_n.t[:, :n], start=True, stop=False),
                               reads=[knT, q_n], writes=[b], inc=False)
                        sch.op(pe, lambda: P.matmul(b.t[:, :n], krT.t[:, kb * 128:(kb + 1) * 128], q_r.t[:, :n], start=False, stop=True),
                               reads=[krT, q_r], writes=[b])
                        return b

                    nxt = scores(kbs[0])
                    for ii, kb in enumerate(kbs):
                        b = nxt
                        p_ = pt[pti % 3]
                        pti += 1
                        sch.op(act, lambda: A.activation(out=p_.t[:, :n], in_=b.t[:, :n], func=AF.Exp, scale=sc), reads=[b], writes=[p_])
                        if ii + 1 < len(kbs):
                            nxt = scores(kbs[ii + 1])
                        last = ii == len(kbs) - 1
                        sch.op(pe, lambda: P.matmul(po.t[:, :n], vh.t[:, kb, :], p_.t[:, :n], start=(ii == 0), stop=last),
                               reads=[vh, p_], writes=[po], inc=last)
                        sch.op(pe, lambda: P.matmul(pd.t[:, :n], ones_b.t[:, :], p_.t[:, :n], start=(ii == 0), stop=last),
                               reads=[ones_b, p_], writes=[pd], inc=True)
                    sch.op(dve, lambda: V.reciprocal(out=rd.t[:, :n], in_=pd.t[:, :n]), reads=[pd], writes=[rd])
                    sch.op(dve, lambda: V.tensor_tensor(out=o_.t[:, :n], in0=po.t[:, :n], in1=rd.t[:, :n], op=ALU.mult), reads=[po, rd], writes=[o_])
                    sch.release(po)
                    sch.release(pd)
                    sch.dma("sp", HAT[hd, :, t0:t0 + n], o_.t[:, :n], reads=[o_], writes=[RHAT])
            sch.barrier()

    def merge_phase(l):
        with ExitStack() as ps:
            wbm = sb(ps, "wbm", [128, KC, D], BF16)
            wba = sb(ps, "wba", [128, KC, D], BF16)
            wout = sb(ps, "wout", [128, KC, D], BF16)
            sch.dma("pool", wbm.t[:], wbm_in[l], writes=[wbm], max_dma_last_dim=4096)
            sch.dma("pool", wba.t[:], wba_in[l], writes=[wba], max_dma_last_dim=4096)
            sch.dma("pool", wout.t[:], wout_in[l], writes=[wout], max_dma_last_dim=4096)
            hf = sb(ps, "hf", [128, 4, D], F32)
            hb = sb(ps, "hb", [128, 4, D], F32)
            hn = sb(ps, "hn", [128, 4, D], BF16)
            junk = sb(ps, "junk", [128, 256], BF16)
            ssq = sb(ps, "ssq", [128, 16], F32)
            so = sb(ps, "so", [128, KC, 512], BF16)
            gm = sb(ps, "gm", [128, KC, 512], BF16)
            ga = sb(ps, "ga", [128, KC, 512], BF16)
            hat = sb(ps, "hat", [128, KC, 512], BF16)
            xt = sb(ps, "mxt", [128, KC, 512], F32)
            hmT = sb(ps, "hmT", [128, KC, 512], BF16)
            tm = sb(ps, "tm", [128, KC, 512], F32)
            t2 = sb(ps, "t2", [128, 512], F32)
            tb_ = sb(ps, "tb", [128, KC, 512], BF16)
            for ti, (t0, n, j) in enumerate(tiles):
                nb = n // 128
                sch.dma("sp", hf.t[:, 0:nb, :], HF[t0:t0 + n, :].rearrange("(b p) f -> p b f", p=128), reads=[RHF], writes=[hf])
                sch.dma("sp", hb.t[:, 0:nb, :], HB[t0:t0 + n, :].rearrange("(b p) f -> p b f", p=128), reads=[RHB], writes=[hb])
                sch.dma("sp", so.t[:, :, :n], SOT[:, :, t0:t0 + n].rearrange("k p t -> p k t"), reads=[RSOT], writes=[so])
                sch.dma("sp", gm.t[:, :, :n], GMT[:, :, t0:t0 + n].rearrange("k p t -> p k t"), reads=[RGMT], writes=[gm])
                sch.dma("sp", ga.t[:, :, :n], GAT[:, :, t0:t0 + n].rearrange("k p t -> p k t"), reads=[RGAT], writes=[ga])
                sch.dma("sp", hat.t[:, :, :n], HAT[:, :, t0:t0 + n].rearrange("k p t -> p k t"), reads=[RHAT], writes=[hat])
                sch.dma("sp", xt.t[:, :, :n], XT[:, :, t0:t0 + n].rearrange("k p t -> p k t"), reads=[xres[ti]], writes=[xt])
                sch.op(dve, lambda: V.tensor_tensor(out=hf.t[:, 0:nb, :], in0=hf.t[:, 0:nb, :], in1=hb.t[:, 0:nb, :], op=ALU.add),
                       reads=[hf, hb], writes=[hf])
                for b in range(nb):
                    for hd in range(4):
                        sch.op(act, lambda b=b, hd=hd: A.activation(out=junk.t[:, :], in_=hf.t[:, b, hd * 256:(hd + 1) * 256], func=AF.Square,
                                                                    accum_out=ssq.t[:, b * 4 + hd:b * 4 + hd + 1]),
                               reads=[hf], writes=[junk, ssq])
                sch.op(act, lambda: A.activation(out=ssq.t[:, 0:4 * nb], in_=ssq.t[:, 0:4 * nb], func=AF.Sqrt, scale=1.0 / 256, bias=eps_s.t[:, 0:1]),
                       reads=[ssq, eps_s], writes=[ssq])
                sch.op(dve, lambda: V.reciprocal(out=ssq.t[:, 0:4 * nb], in_=ssq.t[:, 0:4 * nb]), reads=[ssq], writes=[ssq])
                for b in range(nb):
                    for hd in range(4):
                        sch.op(dve, lambda b=b, hd=hd: V.tensor_scalar(out=hn.t[:, b, hd * 256:(hd + 1) * 256], in0=hf.t[:, b, hd * 256:(hd + 1) * 256],
                                                                       scalar1=ssq.t[:, b * 4 + hd:b * 4 + hd + 1], scalar2=None, op0=ALU.mult),
                               reads=[hf, ssq], writes=[hn])
                for m in range(KC):
                    bk = sch.bank()
                    bv = bk.t[:, :].bitcast(BF16)
                    for b in range(nb):
                        sch.op(pe, lambda b=b: P.transpose(bv[:, b * 128:(b + 1) * 128], hn.t[:, b, m * 128:(m + 1) * 128], ident_b.t[:]),
                               reads=[hn, ident_b], writes=[bk], inc=(b == nb - 1))
                    sch.op(dve, lambda: V.scalar_tensor_tensor(out=hmT.t[:, m, :n], in0=bv[:, 0:n], scalar=gmh_s.t[:, l, m:m + 1], in1=so.t[:, m, :n],
                                                               op0=ALU.mult, op1=ALU.mult),
                           reads=[bk, gmh_s, so], writes=[hmT])
                for m in range(KC):
                    bk = sch.bank()
                    for k in range(KC):
                        sch.op(pe, lambda k=k: P.matmul(bk.t[:, :n], wbm.t[:, k, m * 128:(m + 1) * 128], hmT.t[:, k, :n], start=(k == 0), stop=(k == KC - 1)),
                               reads=[wbm, hmT], writes=[bk], inc=(k == KC - 1))
                    sch.op(dve, lambda: V.tensor_tensor(out=tm.t[:, m, :n], in0=bk.t[:, :n], in1=gm.t[:, m, :n], op=ALU.mult),
                           reads=[bk, gm], writes=[tm])
                    bk2 = sch.bank()
                    for k in range(KC):
                        sch.op(pe, lambda k=k: P.matmul(bk2.t[:, :n], wba.t[:, k, m * 128:(m + 1) * 128], hat.t[:, k, :n], start=(k == 0), stop=(k == KC - 1)),
                               reads=[wba, hat], writes=[bk2], inc=(k == KC - 1))
                    sch.op(dve, lambda: V.tensor_tensor(out=t2.t[:, :n], in0=bk2.t[:, :n], in1=ga.t[:, m, :n], op=ALU.mult),
                           reads=[bk2, ga], writes=[t2])
                    sch.op(dve, lambda: V.tensor_tensor(out=tb_.t[:, m, :n], in0=tm.t[:, m, :n], in1=t2.t[:, :n], op=ALU.add),
                           reads=[tm, t2], writes=[tb_])
                for m in range(KC):
                    bk = sch.bank()
                    for k in range(KC):
                        sch.op(pe, lambda k=k: P.matmul(bk.t[:, :n], wout.t[:, k, m * 128:(m + 1) * 128], tb_.t[:, k, :n], start=(k == 0), stop=(k == KC - 1)),
                               reads=[wout, tb_], writes=[bk], inc=(k == KC - 1))
                    sch.op(dve, lambda: V.scalar_tensor_tensor(out=xt.t[:, m, :n], in0=bk.t[:, :n], scalar=GT.t[:, l, 1, m, j:j + 1], in1=xt.t[:, m, :n],
                                                               op0=ALU.mult, op1=ALU.add),
                           reads=[bk, GT, xt], writes=[xt])
                sch.dma("sp", XT[:, :, t0:t0 + n].rearrange("k p t -> p k t"), xt.t[:, :, :n], reads=[xt], writes=[xres[ti]])
            sch.barrier()

    def final_phase():
        with ExitStack() as ps:
            xts = [sb(ps, f"fxt{q}", [128, KC, 512], F32) for q in range(2)]
            sq = sb(ps, "fsq", [128, KC, 512], BF16)
            rt = sb(ps, "frt", [128, 512], F32)
            ot = [sb(ps, f"fot{q}", [128, KC, 512], F32) for q in range(2)]
            for ti, (t0, n, j) in enumerate(tiles):
                if j == 1:
                    continue
                xt = xts[ti % 2]
                o = ot[ti % 2]
                sch.dma("sp", xt.t[:, :, :n], XT[:, :, t0:t0 + n].rearrange("k p t -> p k t"), reads=[xres[ti]], writes=[xt])
                rms_rstd((sq, rt), lambda c: xt.t[:, c, :n], KC, n, 1.0 / D, xt)
                for k in range(KC):
                    sch.op(dve, lambda k=k: V.scalar_tensor_tensor(out=o.t[:, k, :n], in0=xt.t[:, k, :n], scalar=gfin_s.t[:, k:k + 1], in1=rt.t[:, :n],
                                                                   op0=ALU.mult, op1=ALU.mult),
                           reads=[xt, gfin_s, rt], writes=[o])
                sch.dma("sp", outT[:, :, t0 - CT:t0 - CT + n].rearrange("k p t -> p k t"), o.t[:, :, :n], reads=[o], writes=[ROUT])
            sch.barrier()

    for l in range(L):
        ffn_phase(l, 0, first=(l == 0))
        inproj_phase(l)
        mlstm_phase(l)
        attn_phase(l)
        merge_phase(l)
        ffn_phase(l, 2, first=False)
    final_phase()
    es.close()
    build.ninst = sch.ninst
    return nc


def _prep_shared(inp, L, S):
    f = np.float32
    T = CT + S
    out = {}
    w_ada = np.asarray(inp["w_ada"], f)[:L]
    out["wada"] = np.ascontiguousarray(w_ada.reshape(L, KC, 128, 72, 128).transpose(0, 3, 2, 1, 4))
    out["bada"] = np.ascontiguousarray(np.asarray(inp["b_ada"], f)[:L].reshape(L, 72, 128).transpose(2, 0, 1))
    gn = np.stack([np.asarray(inp[k], f)[:L] for k in ("g_n1", "g_n2", "g_n3")], axis=1)
    out["gn"] = np.ascontiguousarray(gn.reshape(L, 3, KC, 128).transpose(3, 0, 1, 2))
    out["gfin"] = np.ascontiguousarray(np.asarray(inp["g_final"], f).reshape(KC, 128).T)
    for i, (ku, kd) in enumerate((("w_ff1_up", "w_ff1_dn"), ("w_ff2_up", "w_ff2_dn"))):
        wu = np.asarray(inp[ku], f)[:L]
        wu = wu.reshape(L, KC, 128, 2, NF, 128)
        out[f"wup{i}"] = np.ascontiguousarray(wu.transpose(0, 4, 2, 1, 3, 5))
        wd = np.asarray(inp[kd], f)[:L].reshape(L, NF, 128, D)
        out[f"wdn{i}"] = np.ascontiguousarray(wd.transpose(0, 2, 1, 3))
    w_in = np.asarray(inp["w_in"], f)[:L]
    o = 0
    offs = {}
    for name, n in (("m_q", 1024), ("m_k", 1024), ("m_v", 1024), ("m_o", 1024), ("m_gate", 16), ("a_cq", 384), ("a_ckv", 256), ("a_kr", 64), ("br_gate", 2048)):
        offs[name] = (o, n)
        o += n

    def grp(name):
        a, n = offs[name]
        return w_in[:, :, a:a + n]

    deint = np.concatenate([np.arange(0, 64, 2), np.arange(1, 64, 2)])
    swp = np.concatenate([deint[32:], deint[:32]])
    gates = grp("m_gate")
    grep = np.zeros((L, D, 4, 128), f)
    for d in range(2):
        for ki in range(2):
            for hd in range(4):
                for qd in range(2):
                    grep[:, :, d * 2 + ki, 32 * qd + hd] = gates[:, :, d * 8 + ki * 4 + hd]
    kr = grp("a_kr")
    kr2 = np.concatenate([kr[:, :, deint], kr[:, :, swp]], axis=2)
    cols = np.concatenate([grp("m_q"), grp("m_k"), grp("m_o"), grp("br_gate"), grp("a_cq"), grp("a_ckv"), grep.reshape(L, D, 512), kr2], axis=2)
    assert cols.shape[2] == NWF * 128
    out["wf"] = np.ascontiguousarray(cols.reshape(L, KC, 128, NWF, 128).transpose(0, 3, 2, 1, 4))
    out["wv"] = np.ascontiguousarray(grp("m_v").reshape(L, KC, 128, D).transpose(0, 2, 1, 3))
    wc = np.asarray(inp["w_conv"], f)[:L]
    out["convw"] = np.ascontiguousarray(wc.reshape(L, 3, 16, 128).transpose(3, 0, 2, 1))
    bgl = np.asarray(inp["b_gate"], f)[:L]
    bg = np.zeros((128, L, 4), f)
    for d in range(2):
        for ki in range(2):
            for hd in range(4):
                for qd in range(2):
                    bg[32 * qd + hd, :, d * 2 + ki] = bgl[:, d * 8 + ki * 4 + hd]
    out["bg"] = bg
    out["gmh"] = np.ascontiguousarray(np.asarray(inp["g_mh"], f)[:L].reshape(L, KC, 128).transpose(2, 0, 1))
    out["gqa"] = np.ascontiguousarray(np.asarray(inp["g_qa"], f)[:L].reshape(L, 3, 128).transpose(2, 0, 1))
    out["gkva"] = np.ascontiguousarray(np.asarray(inp["g_kva"], f)[:L].reshape(L, 2, 128).transpose(2, 0, 1))
    wuq = np.asarray(inp["w_uq"], f)[:L].reshape(L, 384, 8, 192)
    chunks = []
    for hd in range(8):
        chunks.append(wuq[:, :, hd, 0:128])
        r = wuq[:, :, hd, 128:192]
        chunks.append(np.concatenate([r[:, :, deint], r[:, :, swp]], axis=2))
    wuq2 = np.stack(chunks, axis=2)
    out["wuq"] = np.ascontiguousarray(wuq2.reshape(L, 3, 128, 16, 128).transpose(0, 2, 1, 3, 4))
    wukv = np.asarray(inp["w_ukv"], f)[:L].reshape(L, 256, 8, 256)
    out["wukvk"] = np.ascontiguousarray(wukv[:, :, :, 0:128].reshape(L, 2, 128, 8, 128).transpose(0, 2, 1, 3, 4))
    out["wukvv"] = np.ascontiguousarray(wukv[:, :, :, 128:256].reshape(L, 2, 128, 1024).transpose(0, 2, 1, 3))
    for k, kk in (("w_bm", "wbm"), ("w_ba", "wba"), ("w_out", "wout")):
        out[kk] = np.ascontiguousarray(np.asarray(inp[k], f)[:L].reshape(L, KC, 128, D).transpose(0, 2, 1, 3))
    inv = (10000.0 ** (-np.arange(0, 32, 2, dtype=np.float32) / 32)).astype(f)
    t = np.arange(S)
    row = (t // 64).astype(f)
    col = (t % 64).astype(f)
    ang = np.concatenate([row[:, None] * inv, col[:, None] * inv], axis=-1)
    cos = np.cos(ang).astype(f).T
    sin = np.sin(ang).astype(f).T
    rope = np.zeros((64, 2, T), f)
    rope[:, 0, :CT] = 1.0
    rope[0:32, 0, CT:] = cos
    rope[32:64, 0, CT:] = cos
    rope[0:32, 1, CT:] = -sin
    rope[32:64, 1, CT:] = sin
    out["rope"] = rope
    out["ident"] = np.eye(128, dtype=f)
    sel = np.zeros((128, 4, 128), f)
    for hd in range(4):
        sel[hd, hd, :] = 1.0
    out["sel"] = sel
    tri = np.zeros((64, 2, 64), f)
    s_ = np.arange(64)[:, None]
    j_ = np.arange(64)[None, :]
    tri[:, 0, :] = (s_ <= j_)
    tri[:, 1, :] = (s_ >= j_)
    out["tri"] = tri
    return out


def _run(inp, L, S, dbg=False):
    f = np.float32
    shared = _prep_shared(inp, L, S)
    x = np.asarray(inp["x"], f)
    ctx = np.asarray(inp["ctx"], f)
    c = np.asarray(inp["c"], f)
    cc = np.asarray(inp["c_ctx"], f)
    B = x.shape[0]
    T = CT + S
    in_maps = []
    for core in range(8):
        b = core % B
        cat = np.concatenate([ctx[b], x[b]], axis=0)
        m = dict(shared)
        m["xT"] = np.ascontiguousarray(cat.T.reshape(KC, 128, T))
        scT = np.stack([c[b].reshape(KC, 128).T, cc.reshape(KC, 128).T], axis=-1)
        m["scT"] = np.ascontiguousarray(scT)
        in_maps.append(m)
    nc = build(L, S, dbg)
    res = run_bass_kernel_spmd(nc, in_maps, core_ids=list(range(8)))
    outs = []
    for b in range(B):
        o = res.results[b]["outT"]
        outs.append(np.ascontiguousarray(o.reshape(D, S).T))
    out = np.stack(outs, axis=0).astype(f)
    if dbg:
        return out, res
    return out


def kernel(**inputs):
    return _run(inputs, 4, 4096)
```

```python
import numpy as np
from contextlib import ExitStack
import concourse.bass as bass
import concourse.mybir as mybir
from concourse.bass_utils import run_bass_kernel_spmd

F32 = mybir.dt.float32
BF16 = mybir.dt.bfloat16
AF = mybir.ActivationFunctionType
ALU = mybir.AluOpType

D = 1024
KC = 8
DFF = 2816
NF = 22
CT = 256
EPS = 1e-6
NWF = 50


class SemObj:
    _n = 0

    def __init__(self, h):
        self.h = h
        self.count = 0
        SemObj._n += 1
        self.id = SemObj._n


class Res:
    def __init__(self, name=""):
        self.lw = {}
        self.rd = {}
        self.name = name


class Tl:
    def __init__(self, t, name=""):
        self.t = t
        self.r = Res(name)


class EngW:
    def __init__(self, name, eng, so):
        self.name = name
        self.eng = eng
        self.so = so
        self.waited = {}


class Sched:
    def __init__(self, nc, es):
        self.nc = nc

        def mk(name):
            return SemObj(es.enter_context(nc.semaphore(name)))

        self.pe = EngW("pe", nc.tensor, mk("s_pe"))
        self.act = EngW("act", nc.scalar, mk("s_act"))
        self.dve = EngW("dve", nc.vector, mk("s_dve"))
        self.pool = EngW("pool", nc.gpsimd, mk("s_pool"))
        self.sp = EngW("sp", nc.sync, mk("s_sp"))
        self.engs = [self.pe, self.act, self.dve, self.pool, self.sp]
        self.dq = {"sp": [mk(f"dsp{i}") for i in range(12)], "pool": [mk(f"dpl{i}") for i in range(12)]}
        self.dqn = {"sp": 0, "pool": 0}
        self.banks = []
        self.bank_i = 0
        self.held = set()
        self.ninst = 0

    def _wait(self, E, tick):
        so, v = tick
        if E.waited.get(so.id, 0) >= v:
            return
        E.eng.wait_ge(so.h, v)
        E.waited[so.id] = v
        self.ninst += 1

    def _deps(self, reads, writes, own):
        deps = {}

        def add(t, raw):
            so, v = t
            if so is own and not raw:
                return
            if deps.get(so.id, (None, 0))[1] < v:
                deps[so.id] = (so, v)

        for r in reads:
            for t in r.lw.values():
                add(t, True)
        for w in writes:
            for t in w.lw.values():
                add(t, False)
            for t in w.rd.values():
                add(t, False)
        return deps

    def _commit(self, tick, reads, writes):
        so, v = tick
        for w in writes:
            o = w.lw.get(so.id)
            if o is None or o[1] < v:
                w.lw[so.id] = tick
        for r in reads:
            o = r.rd.get(so.id)
            if o is None or o[1] < v:
                r.rd[so.id] = tick

    def op(self, E, fn, reads=(), writes=(), inc=True):
        reads = [x.r if isinstance(x, Tl) else x for x in reads]
        writes = [x.r if isinstance(x, Tl) else x for x in writes]
        for t in self._deps(reads, writes, E.so).values():
            self._wait(E, t)
        ins = fn()
        self.ninst += 1
        if inc:
            ins.then_inc(E.so.h, 1)
            E.so.count += 1
            tick = (E.so, E.so.count)
        else:
            tick = (E.so, E.so.count + 1)
        self._commit(tick, reads, writes)
        return ins

    def dma(self, q, out, in_, reads=(), writes=(), **kw):
        reads = [x.r if isinstance(x, Tl) else x for x in reads]
        writes = [x.r if isinstance(x, Tl) else x for x in writes]
        E = self.sp if q == "sp" else self.pool
        pool = self.dq[q]
        so = pool[self.dqn[q] % len(pool)]
        self.dqn[q] += 1
        if so.count > 0:
            self._wait(E, (so, so.count))
        for t in self._deps(reads, writes, None).values():
            self._wait(E, t)
        E.eng.dma_start(out=out, in_=in_, **kw).then_inc(so.h, 16)
        self.ninst += 1
        so.count += 16
        self._commit((so, so.count), reads, writes)

    def barrier(self):
        ticks = []
        for E in self.engs:
            if E.so.count > 0:
                ticks.append((E.so, E.so.count))
        for q in self.dq.values():
            for so in q:
                if so.count > 0:
                    ticks.append((so, so.count))
        for E in self.engs:
            for t in ticks:
                if t[0] is not E.so:
                    self._wait(E, t)

    def bank(self, hold=False):
        for _ in range(16):
            i = self.bank_i % 8
            self.bank_i += 1
            if i not in self.held:
                if hold:
                    self.held.add(i)
                return self.banks[i]
        raise RuntimeError("no psum bank")

    def release(self, b):
        self.held.discard(self.banks.index(b))


def build(L, S, dbg=False):
    T = CT + S
    NCH = T // 64
    NTB = T // 128
    nc = bass.Bass("TRN2", target_bir_lowering=False)
    es = ExitStack()

    def din(name, shape, dt=F32):
        return nc.dram_tensor(name, list(shape), dt, kind="ExternalInput").ap()

    scratch_kind = "ExternalOutput" if dbg else "Internal"

    def dscr(name, shape, dt):
        return nc.dram_tensor(name, list(shape), dt, kind=scratch_kind).ap()

    xT_in = din("xT", [KC, 128, T])
    scT_in = din("scT", [128, KC, 2])
    wada_in = din("wada", [L, 72, 128, KC, 128])
    bada_in = din("bada", [128, L, 72])
    gn_in = din("gn", [128, L, 3, KC])
    gfin_in = din("gfin", [128, KC])
    wup_in = [din(f"wup{i}", [L, NF, 128, KC, 2, 128]) for i in range(2)]
    wdn_in = [din(f"wdn{i}", [L, 128, NF, D]) for i in range(2)]
    wf_in = din("wf", [L, NWF, 128, KC, 128])
    wv_in = din("wv", [L, 128, KC, D])
    convw_in = din("convw", [128, L, 16, 3])
    bg_in = din("bg", [128, L, 4])
    gmh_in = din("gmh", [128, L, KC])
    gqa_in = din("gqa", [128, L, 3])
    gkva_in = din("gkva", [128, L, 2])
    wuq_in = din("wuq", [L, 128, 3, 16, 128])
    wukvk_in = din("wukvk", [L, 128, 2, 8, 128])
    wukvv_in = din("wukvv", [L, 128, 2, D])
    wbm_in = din("wbm", [L, 128, KC, D])
    wba_in = din("wba", [L, 128, KC, D])
    wout_in = din("wout", [L, 128, KC, D])
    rope_in = din("rope", [64, 2, T])
    ident_in = din("ident", [128, 128])
    sel_in = din("sel", [128, 4, 128])
    tri_in = din("tri", [64, 2, 64])
    outT = nc.dram_tensor("outT", [KC, 128, S], F32, kind="ExternalOutput").ap()

    XT = dscr("XT", [KC, 128, T], F32)
    QT = dscr("QT", [KC, 128, T], BF16)
    KT = dscr("KT", [KC, 128, T], BF16)
    SOT = dscr("SOT", [KC, 128, T], BF16)
    GMT = dscr("GMT", [KC, 128, T], BF16)
    GAT = dscr("GAT", [KC, 128, T], BF16)
    KTOK = dscr("KTOK", [T, D], BF16)
    VTOK = dscr("VTOK", [T, D], BF16)
    GROW = dscr("GROW", [4, 128, T], F32)
    HF = dscr("HF", [T, D], F32)
    HB = dscr("HB", [T, D], F32)
    QN = dscr("QN", [8, 128, T], BF16)
    QR = dscr("QR", [8, 64, T], BF16)
    KN = dscr("KN", [8, 128, T], BF16)
    KR = dscr("KR", [64, T], BF16)
    VA = dscr("VA", [T, D], BF16)
    HAT = dscr("HAT", [8, 128, T], BF16)

    sch = Sched(nc, es)
    pe, act, dve, pool, sp = sch.pe, sch.act, sch.dve, sch.pool, sch.sp

    for i in range(8):
        sch.banks.append(Tl(es.enter_context(nc.psum_tensor(f"bank{i}", [128, 512], F32)), f"bank{i}"))

    sbn = [0]

    def sb(stack, name, shape, dt):
        sbn[0] += 1
        nm = f"s{sbn[0]}_{name}"
        return Tl(stack.enter_context(nc.sbuf_tensor(nm, list(shape), dt)), nm)

    tiles = [(0, CT, 1)] + [(CT + 512 * i, 512, 0) for i in range(S // 512)]
    xres = [Res(f"XT{i}") for i in range(len(tiles))]
    r_xin = Res("xin")
    RQT, RKT, RSOT, RGMT, RGAT, RKTOK, RVTOK, RGROW, RHF, RHB, RQN, RQR, RKN, RKR, RVA, RHAT, ROUT = [Res() for _ in range(17)]

    ident_f = sb(es, "ident_f", [128, 128], F32)
    ident_b = sb(es, "ident_b", [128, 128], BF16)
    ones_b = sb(es, "ones_b", [128, 128], BF16)
    sel_f = sb(es, "sel_f", [128, 4, 128], F32)
    tri_f = sb(es, "tri_f", [64, 2, 64], F32)
    MOD = sb(es, "MOD", [128, L, 72, 2], F32)
    GS = sb(es, "GS", [128, L, 3, KC, 2], F32)
    GT = sb(es, "GT", [128, L, 3, KC, 2], F32)
    gn_s = sb(es, "gn_s", [128, L, 3, KC], F32)
    gfin_s = sb(es, "gfin_s", [128, KC], F32)
    convw_s = sb(es, "convw_s", [128, L, 16, 3], F32)
    bg_s = sb(es, "bg_s", [128, L, 4], F32)
    gmh_s = sb(es, "gmh_s", [128, L, KC], F32)
    gqa_s = sb(es, "gqa_s", [128, L, 3], F32)
    gkva_s = sb(es, "gkva_s", [128, L, 2], F32)
    bada_s = sb(es, "bada_s", [128, L, 72], F32)
    ones_f = sb(es, "ones_f", [128, 1], F32)

    V = nc.vector
    A = nc.scalar
    P = nc.tensor

    def load_const(tl, src, q="sp"):
        sch.dma(q, tl.t[:], src, writes=[tl])

    load_const(ident_f, ident_in[:, :])
    load_const(ident_b, ident_in[:, :], "pool")
    load_const(sel_f, sel_in[:, :, :])
    load_const(tri_f, tri_in[:, :, :])
    load_const(gn_s, gn_in[:, :, :, :])
    load_const(gfin_s, gfin_in[:, :])
    load_const(convw_s, convw_in[:, :, :, :])
    load_const(bg_s, bg_in[:, :, :])
    load_const(gmh_s, gmh_in[:, :, :])
    load_const(gqa_s, gqa_in[:, :, :])
    load_const(gkva_s, gkva_in[:, :, :])
    load_const(bada_s, bada_in[:, :, :])
    sch.op(dve, lambda: V.memset(ones_b.t[:], 1.0), writes=[ones_b])
    sch.op(dve, lambda: V.memset(ones_f.t[:], 1.0), writes=[ones_f])

    with ExitStack() as ps:
        sc_f = sb(ps, "sc_f", [128, KC, 2], F32)
        sc_b = sb(ps, "sc_b", [128, KC, 2], BF16)
        wa = [sb(ps, f"wa{i}", [128, 8, KC, 128], BF16) for i in range(2)]
        load_const(sc_f, scT_in[:, :, :])
        sch.op(act, lambda: A.activation(out=sc_b.t[:], in_=sc_f.t[:], func=AF.Silu), reads=[sc_f], writes=[sc_b])
        it = 0
        for l in range(L):
            for mg in range(9):
                w = wa[it % 2]
                it += 1
                sch.dma("pool", w.t[:], wada_in[l, mg * 8:(mg + 1) * 8].rearrange("m p k c -> p m k c"),
                        writes=[w], max_dma_last_dim=4096)
                bk = sch.bank()
                for m in range(8):
                    for k in range(KC):
                        sch.op(pe, lambda m=m, k=k: P.matmul(bk.t[:, 2 * m:2 * m + 2], w.t[:, m, k, :], sc_b.t[:, k, :],
                                                             start=(k == 0), stop=(k == KC - 1)),
                               reads=[w, sc_b], writes=[bk], inc=(k == KC - 1))
                sch.op(dve, lambda: V.tensor_tensor(
                    out=MOD.t[:, l, mg * 8:(mg + 1) * 8, :],
                    in0=bk.t[:, 0:16].rearrange("p (m j) -> p m j", j=2),
                    in1=bada_s.t[:, l, mg * 8:(mg + 1) * 8].unsqueeze(2).to_broadcast([128, 8, 2]),
                    op=ALU.add), reads=[bk, bada_s], writes=[MOD])
        for l in range(L):
            for i in range(3):
                coef = 1.0 if i == 1 else 0.5
                sc = MOD.t[:, l, (3 * i + 1) * 8:(3 * i + 2) * 8, :]
                gt = MOD.t[:, l, (3 * i + 2) * 8:(3 * i + 3) * 8, :]
                sch.op(dve, lambda: V.scalar_tensor_tensor(
                    out=GS.t[:, l, i, :, :], in0=sc, scalar=1.0,
                    in1=gn_s.t[:, l, i, :].unsqueeze(2).to_broadcast([128, KC, 2]),
                    op0=ALU.add, op1=ALU.mult), reads=[MOD, gn_s], writes=[GS])
                sch.op(dve, lambda: V.tensor_scalar(out=GT.t[:, l, i, :, :], in0=gt, scalar1=coef, scalar2=None,
                                                    op0=ALU.mult), reads=[MOD], writes=[GT])
        sch.barrier()

    def SH(l, i, k, j):
        return MOD.t[:, l, 3 * i * 8 + k, j:j + 1]

    def rms_rstd(stack_tiles, src_ap_fn, nchunks, n, inv_dim, src_res):
        sq, rt = stack_tiles
        for c in range(nchunks):
            sch.op(act, lambda c=c: A.activation(out=sq.t[:, c, :n], in_=src_ap_fn(c), func=AF.Square),
                   reads=[src_res], writes=[sq])
        bk = sch.bank()
        for c in range(nchunks):
            sch.op(pe, lambda c=c: P.matmul(bk.t[:, :n], ones_b.t[:], sq.t[:, c, :n], start=(c == 0), stop=(c == nchunks - 1)),
                   reads=[ones_b, sq], writes=[bk], inc=(c == nchunks - 1))
        sch.op(act, lambda: A.activation(out=rt.t[:, :n], in_=bk.t[:, :n], func=AF.Sqrt, scale=inv_dim, bias=eps_s.t[:, 0:1]),
               reads=[bk, eps_s], writes=[rt])
        sch.op(dve, lambda: V.reciprocal(out=rt.t[:, :n], in_=rt.t[:, :n]), reads=[rt], writes=[rt])
        return rt

    eps_s = sb(es, "eps_s", [128, 1], F32)
    sch.op(dve, lambda: V.memset(eps_s.t[:], EPS), writes=[eps_s])

    def norm_mod(xt, n, l, i, j, sq, rt, tmp, hout, hcol0=0):
        rms_rstd((sq, rt), lambda c: xt.t[:, c, :n], KC, n, 1.0 / D, xt)
        sch.op(dve, lambda: V.tensor_tensor(out=tmp.t[:, :, :n], in0=xt.t[:, :, :n],
                                            in1=rt.t[:, :n].unsqueeze(1).to_broadcast([128, KC, n]), op=ALU.mult),
               reads=[xt, rt], writes=[tmp])
        for k in range(KC):
            sch.op(act, lambda k=k: A.activation(out=hout.t[:, k, hcol0:hcol0 + n], in_=tmp.t[:, k, :n], func=AF.Identity,
                                                 scale=GS.t[:, l, i, k, j:j + 1], bias=SH(l, i, k, j)),
                   reads=[tmp, GS, MOD], writes=[hout])

    def ffn_phase(l, which, first):
        i = 0 if which == 0 else 2
        with ExitStack() as ps:
            wdn = sb(ps, "wdn", [128, NF, D], BF16)
            wup = [sb(ps, f"wupb{q}", [128, KC, 2, 128], BF16) for q in range(3)]
            xts = [sb(ps, f"xt{q}", [128, KC, 512], F32) for q in range(2)]
            hs = [sb(ps, f"h{q}", [128, KC, 512], BF16) for q in range(2)]
            sq = sb(ps, "sq", [128, KC, 512], BF16)
            rt = sb(ps, "rt", [128, 512], F32)
            tmp = sb(ps, "tmp", [128, KC, 512], F32)
            h2 = sb(ps, "h2", [128, NF, 512], BF16)
            sil = [sb(ps, f"sil{q}", [128, 512], BF16) for q in range(2)]
            for q in range(2):
                sch.dma("pool", wdn.t[:, q * 11:(q + 1) * 11, :], wdn_in[which // 2][l, :, q * 11:(q + 1) * 11, :],
                        writes=[wdn], max_dma_last_dim=4096)

            def load(ti):
                t0, n, j = tiles[ti]
                xt = xts[ti % 2]
                if first:
                    sch.dma("sp", xt.t[:, :, :n], xT_in[:, :, t0:t0 + n].rearrange("k p t -> p k t"), reads=[r_xin], writes=[xt])
                else:
                    sch.dma("sp", xt.t[:, :, :n], XT[:, :, t0:t0 + n].rearrange("k p t -> p k t"), reads=[xres[ti]], writes=[xt])

            load(0)
            load(1)
            wi = 0
            norm_mod(xts[0], tiles[0][1], l, i, tiles[0][2], sq, rt, tmp, hs[0])
            for ti, (t0, n, j) in enumerate(tiles):
                xt = xts[ti % 2]
                h = hs[ti % 2]
                for f in range(NF):
                    w = wup[wi % 3]
                    wi += 1
                    sch.dma("pool", w.t[:], wup_in[which // 2][l, f], writes=[w], max_dma_last_dim=4096)
                    pa = sch.bank()
                    pb = sch.bank()
                    for k in range(KC):
                        sch.op(pe, lambda k=k: P.matmul(pa.t[:, :n], w.t[:, k, 0, :], h.t[:, k, :n], start=(k == 0), stop=(k == KC - 1)),
                               reads=[w, h], writes=[pa], inc=(k == KC - 1))
                    for k in range(KC):
                        sch.op(pe, lambda k=k: P.matmul(pb.t[:, :n], w.t[:, k, 1, :], h.t[:, k, :n], start=(k == 0), stop=(k == KC - 1)),
                               reads=[w, h], writes=[pb], inc=(k == KC - 1))
                    s_ = sil[f % 2]
                    sch.op(act, lambda: A.activation(out=s_.t[:, :n], in_=pa.t[:, :n], func=AF.Silu), reads=[pa], writes=[s_])
                    sch.op(dve, lambda: V.tensor_tensor(out=h2.t[:, f, :n], in0=pb.t[:, :n], in1=s_.t[:, :n], op=ALU.mult),
                           reads=[pb, s_], writes=[h2])
                if ti + 1 < len(tiles):
                    t0n, nn_, jn = tiles[ti + 1]
                    norm_mod(xts[(ti + 1) % 2], nn_, l, i, jn, sq, rt, tmp, hs[(ti + 1) % 2])
                for m in range(KC):
                    py = sch.bank()
                    for f in range(NF):
                        sch.op(pe, lambda f=f: P.matmul(py.t[:, :n], wdn.t[:, f, m * 128:(m + 1) * 128], h2.t[:, f, :n],
                                                        start=(f == 0), stop=(f == NF - 1)),
                               reads=[wdn, h2], writes=[py], inc=(f == NF - 1))
                    sch.op(dve, lambda: V.scalar_tensor_tensor(out=xt.t[:, m, :n], in0=py.t[:, :n], scalar=GT.t[:, l, i, m, j:j + 1],
                                                               in1=xt.t[:, m, :n], op0=ALU.mult, op1=ALU.add),
                           reads=[py, GT, xt], writes=[xt])
                sch.dma("sp", XT[:, :, t0:t0 + n].rearrange("k p t -> p k t"), xt.t[:, :, :n], reads=[xt], writes=[xres[ti]])
                if ti + 2 < len(tiles):
                    load(ti + 2)
            sch.barrier()

    def inproj_phase(l):
        with ExitStack() as ps:
            hall = sb(ps, "hall", [128, KC, T], BF16)
            with ExitStack() as p1:
                xts = [sb(p1, f"ixt{q}", [128, KC, 512], F32) for q in range(2)]
                sq = sb(p1, "isq", [128, KC, 512], BF16)
                rt = sb(p1, "irt", [128, 512], F32)
                tmp = sb(p1, "itmp", [128, KC, 512], F32)

                def load(ti):
                    t0, n, j = tiles[ti]
                    sch.dma("sp", xts[ti % 2].t[:, :, :n], XT[:, :, t0:t0 + n].rearrange("k p t -> p k t"),
                            reads=[xres[ti]], writes=[xts[ti % 2]])

                load(0)
                for ti, (t0, n, j) in enumerate(tiles):
                    if ti + 1 < len(tiles):
                        load(ti + 1)
                    norm_mod(xts[ti % 2], n, l, 1, j, sq, rt, tmp, hall, hcol0=t0)
                sch.barrier()

            with ExitStack() as p2:
                wfb = [sb(p2, f"wfb{q}", [128, KC, 128], BF16) for q in range(3)]
                zs = [sb(p2, f"zs{q}", [128, T], F32) for q in range(2)]
                ys = sb(p2, "ys", [128, T], F32)
                zb = [sb(p2, f"zb{q}", [128, T], BF16) for q in range(2)]
                ktk = sb(p2, "ktk", [128, NTB, 128], BF16)
                wi = 0

                def chunk_mm(ci, evac):
                    nonlocal wi
                    w = wfb[wi % 3]
                    wi += 1
                    sch.dma("pool", w.t[:], wf_in[l, ci], writes=[w], max_dma_last_dim=4096)
                    for (t0, n, j) in tiles:
                        bk = sch.bank()
                        for k in range(KC):
                            sch.op(pe, lambda k=k: P.matmul(bk.t[:, :n], w.t[:, k, :], hall.t[:, k, t0:t0 + n],
                                                            start=(k == 0), stop=(k == KC - 1)),
                                   reads=[w, hall], writes=[bk], inc=(k == KC - 1))
                        evac(bk, t0, n)

                segs = [(0, CT), (CT, T)]
                for ci in range(16):
                    z = zs[ci % 2]
                    o = zb[ci % 2]

                    def ev(bk, t0, n, z=z):
                        sch.op(act, lambda: A.copy(out=z.t[:, t0:t0 + n], in_=bk.t[:, :n]), reads=[bk], writes=[z])

                    chunk_mm(ci, ev)
                    w0 = convw_s.t[:, l, ci, 0:1]
                    w1 = convw_s.t[:, l, ci, 1:2]
                    w2 = convw_s.t[:, l, ci, 2:3]
                    for (a, b) in segs:
                        sch.op(dve, lambda: V.tensor_scalar(out=ys.t[:, a:b], in0=z.t[:, a:b], scalar1=w1, scalar2=None, op0=ALU.mult),
                               reads=[z, convw_s], writes=[ys])
                        sch.op(dve, lambda: V.scalar_tensor_tensor(out=ys.t[:, a + 1:b], in0=z.t[:, a:b - 1], scalar=w0,
                                                                   in1=ys.t[:, a + 1:b], op0=ALU.mult, op1=ALU.add),
                               reads=[z, ys, convw_s], writes=[ys])
                        sch.op(dve, lambda: V.scalar_tensor_tensor(out=ys.t[:, a:b - 1], in0=z.t[:, a + 1:b], scalar=w2,
                                                                   in1=ys.t[:, a:b - 1], op0=ALU.mult, op1=ALU.add),
                               reads=[z, ys, convw_s], writes=[ys])
                    if ci < 8:
                        sch.op(act, lambda: A.activation(out=z.t[:, :], in_=ys.t[:, :], func=AF.Silu), reads=[ys], writes=[z])
                        sch.op(dve, lambda: V.tensor_scalar(out=o.t[:, :], in0=z.t[:, :], scalar1=0.0625, scalar2=None, op0=ALU.mult),
                               reads=[z], writes=[o])
                        sch.dma("sp", QT[ci], o.t[:, :], reads=[o], writes=[RQT])
                    else:
                        m = ci - 8
                        sch.op(act, lambda: A.activation(out=o.t[:, :], in_=ys.t[:, :], func=AF.Silu), reads=[ys], writes=[o])
                        sch.dma("sp", KT[m], o.t[:, :], reads=[o], writes=[RKT])
                        for g in range(0, NTB, 8):
                            nb = min(8, NTB - g)
                            bk = sch.bank()
                            bv = bk.t[:, :].bitcast(BF16)
                            for q in range(nb):
                                tb = g + q
                                sch.op(pe, lambda q=q, tb=tb: P.transpose(bv[:, q * 128:(q + 1) * 128], o.t[:, tb * 128:(tb + 1) * 128], ident_b.t[:]),
                                       reads=[o, ident_b], writes=[bk], inc=(q == nb - 1))
                            sch.op(act, lambda: A.copy(out=ktk.t[:, g:g + nb, :], in_=bv[:, 0:nb * 128].rearrange("p (q f) -> p q f", f=128)),
                                   reads=[bk], writes=[ktk])
                        sch.dma("sp", KTOK[:, m * 128:(m + 1) * 128].rearrange("(tb p) f -> p tb f", p=128), ktk.t[:, :, :],
                                reads=[ktk], writes=[RKTOK])
                for ci in range(16, 40):
                    o = zb[ci % 2]

                    def ev(bk, t0, n, o=o):
                        sch.op(act, lambda: A.activation(out=o.t[:, t0:t0 + n], in_=bk.t[:, :n], func=AF.Sigmoid), reads=[bk], writes=[o])

                    chunk_mm(ci, ev)
                    if ci < 24:
                        sch.dma("sp", SOT[ci - 16], o.t[:, :], reads=[o], writes=[RSOT])
                    elif ci < 32:
                        sch.dma("sp", GMT[ci - 24], o.t[:, :], reads=[o], writes=[RGMT])
                    else:
                        sch.dma("sp", GAT[ci - 32], o.t[:, :], reads=[o], writes=[RGAT])
                for kind in range(4):
                    ci = 45 + kind
                    z = zs[ci % 2]

                    def ev(bk, t0, n, z=z, kind=kind):
                        sch.op(act, lambda: A.activation(out=z.t[:, t0:t0 + n], in_=bk.t[:, :n], func=AF.Identity,
                                                         bias=bg_s.t[:, l, kind:kind + 1], scale=1.0),
                               reads=[bk, bg_s], writes=[z])

                    chunk_mm(ci, ev)
                    sch.dma("sp", GROW[kind], z.t[:, :], reads=[z], writes=[RGROW])
                sch.barrier()

            with ExitStack() as p3:
                wv = sb(p3, "wv", [128, KC, D], BF16)
                vt = [sb(p3, f"vt{q}", [128, D], BF16) for q in range(2)]
                sch.dma("pool", wv.t[:], wv_in[l], writes=[wv], max_dma_last_dim=4096)
                for tb in range(NTB):
                    v_ = vt[tb % 2]
                    for hf in range(2):
                        bk = sch.bank()
                        for k in range(KC):
                            sch.op(pe, lambda k=k: P.matmul(bk.t[:, :], hall.t[:, k, tb * 128:(tb + 1) * 128], wv.t[:, k, hf * 512:(hf + 1) * 512],
                                                            start=(k == 0), stop=(k == KC - 1)),
                                   reads=[wv, hall], writes=[bk], inc=(k == KC - 1))
                        if hf == 0:
                            sch.op(act, lambda: A.copy(out=v_.t[:, 0:512], in_=bk.t[:, :]), reads=[bk], writes=[v_])
                        else:
                            sch.op(dve, lambda: V.tensor_copy(out=v_.t[:, 512:1024], in_=bk.t[:, :]), reads=[bk], writes=[v_])
                    sch.dma("sp", VTOK[tb * 128:(tb + 1) * 128, :], v_.t[:, :], reads=[v_], writes=[RVTOK])
                sch.barrier()

            with ExitStack() as p4:
                wl = sb(p4, "wl", [128, 6, KC, 128], BF16)
                wuq = sb(p4, "wuq", [128, 3, 16, 128], BF16)
                wkk = sb(p4, "wkk", [128, 2, 8, 128], BF16)
                wkv = sb(p4, "wkv", [128, 2, D], BF16)
                cqf = sb(p4, "cqf", [128, 3, 512], F32)
                sq = sb(p4, "lsq", [128, 3, 512], BF16)
                rt = sb(p4, "lrt", [128, 512], F32)
                cqn = sb(p4, "cqn", [128, 3, 512], BF16)
                ckn = sb(p4, "ckn", [128, 2, 512], BF16)
                rp = [sb(p4, f"rp{q}", [64, 2, 512], F32) for q in range(2)]
                r1 = sb(p4, "r1", [64, 512], F32)
                r2 = sb(p4, "r2", [64, 512], F32)
                ob = [sb(p4, f"ob{q}", [128, 512], BF16) for q in range(4)]
                vb = [sb(p4, f"vb{q}", [128, D], BF16) for q in range(2)]
                for c in range(5):
                    sch.dma("pool", wl.t[:, c], wf_in[l, 40 + c], writes=[wl], max_dma_last_dim=4096)
                sch.dma("pool", wl.t[:, 5], wf_in[l, 49], writes=[wl], max_dma_last_dim=4096)
                sch.dma("pool", wuq.t[:], wuq_in[l], writes=[wuq], max_dma_last_dim=4096)
                sch.dma("pool", wkk.t[:], wukvk_in[l], writes=[wkk], max_dma_last_dim=4096)
                sch.dma("pool", wkv.t[:], wukvv_in[l], writes=[wkv], max_dma_last_dim=4096)
                oi = 0

                def rope_out(pa, pb, n, rpt, dst_ap, dres):
                    nonlocal oi
                    o = ob[oi % 4]
                    oi += 1
                    sch.op(dve, lambda: V.tensor_tensor(out=r1.t[:, :n], in0=pa.t[0:64, :n], in1=rpt.t[:, 0, :n], op=ALU.mult),
                           reads=[pa, rpt], writes=[r1])
                    sch.op(dve, lambda: V.tensor_tensor(out=r2.t[:, :n], in0=pb.t[0:64, :n], in1=rpt.t[:, 1, :n], op=ALU.mult),
                           reads=[pb, rpt], writes=[r2])
                    sch.op(dve, lambda: V.tensor_tensor(out=o.t[0:64, :n], in0=r1.t[:, :n], in1=r2.t[:, :n], op=ALU.add),
                           reads=[r1, r2], writes=[o])
                    sch.dma("sp", dst_ap, o.t[0:64, :n], reads=[o], writes=[dres])

                for ti, (t0, n, j) in enumerate(tiles):
                    rpt = rp[ti % 2]
                    sch.dma("sp", rpt.t[:, :, :n], rope_in[:, :, t0:t0 + n], writes=[rpt])
                    for (c0, ncn, gsrc, dst, inv) in ((0, 3, gqa_s, cqn, 1.0 / 384), (3, 2, gkva_s, ckn, 1.0 / 256)):
                        for c in range(ncn):
                            bk = sch.bank()
                            for k in range(KC):
                                sch.op(pe, lambda k=k: P.matmul(bk.t[:, :n], wl.t[:, c0 + c, k, :], hall.t[:, k, t0:t0 + n],
                                                                start=(k == 0), stop=(k == KC - 1)),
                                       reads=[wl, hall], writes=[bk], inc=(k == KC - 1))
                            sch.op(act, lambda: A.copy(out=cqf.t[:, c, :n], in_=bk.t[:, :n]), reads=[bk], writes=[cqf])
                        rms_rstd((sq, rt), lambda c: cqf.t[:, c, :n], ncn, n, inv, cqf)
                        for c in range(ncn):
                            sch.op(dve, lambda c=c: V.scalar_tensor_tensor(out=dst.t[:, c, :n], in0=cqf.t[:, c, :n], scalar=gsrc.t[:, l, c:c + 1],
                                                                           in1=rt.t[:, :n], op0=ALU.mult, op1=ALU.mult),
                                   reads=[cqf, gsrc, rt], writes=[dst])
                    pa = sch.bank()
                    pb = sch.bank()
                    for k in range(KC):
                        sch.op(pe, lambda k=k: P.matmul(pa.t[0:64, :n], wl.t[:, 5, k, 0:64], hall.t[:, k, t0:t0 + n], start=(k == 0), stop=(k == KC - 1)),
                               reads=[wl, hall], writes=[pa], inc=(k == KC - 1))
                    for k in range(KC):
                        sch.op(pe, lambda k=k: P.matmul(pb.t[0:64, :n], wl.t[:, 5, k, 64:128], hall.t[:, k, t0:t0 + n], start=(k == 0), stop=(k == KC - 1)),
                               reads=[wl, hall], writes=[pb], inc=(k == KC - 1))
                    rope_out(pa, pb, n, rpt, KR[:, t0:t0 + n], RKR)
                    for hd in range(8):
                        bk = sch.bank()
                        for c in range(3):
                            sch.op(pe, lambda c=c: P.matmul(bk.t[:, :n], wuq.t[:, c, 2 * hd, :], cqn.t[:, c, :n], start=(c == 0), stop=(c == 2)),
                                   reads=[wuq, cqn], writes=[bk], inc=(c == 2))
                        o = ob[oi % 4]
                        oi += 1
                        sch.op(act, lambda: A.copy(out=o.t[:, :n], in_=bk.t[:, :n]), reads=[bk], writes=[o])
                        sch.dma("sp", QN[hd, :, t0:t0 + n], o.t[:, :n], reads=[o], writes=[RQN])
                        pa = sch.bank()
                        pb = sch.bank()
                        for c in range(3):
                            sch.op(pe, lambda c=c: P.matmul(pa.t[0:64, :n], wuq.t[:, c, 2 * hd + 1, 0:64], cqn.t[:, c, :n], start=(c == 0), stop=(c == 2)),
                                   reads=[wuq, cqn], writes=[pa], inc=(c == 2))
                        for c in range(3):
                            sch.op(pe, lambda c=c: P.matmul(pb.t[0:64, :n], wuq.t[:, c, 2 * hd + 1, 64:128], cqn.t[:, c, :n], start=(c == 0), stop=(c == 2)),
                                   reads=[wuq, cqn], writes=[pb], inc=(c == 2))
                        rope_out(pa, pb, n, rpt, QR[hd, :, t0:t0 + n], RQR)
                        bk = sch.bank()
                        for c in range(2):
                            sch.op(pe, lambda c=c: P.matmul(bk.t[:, :n], wkk.t[:, c, hd, :], ckn.t[:, c, :n], start=(c == 0), stop=(c == 1)),
                                   reads=[wkk, ckn], writes=[bk], inc=(c == 1))
                        o = ob[oi % 4]
                        oi += 1
                        sch.op(dve, lambda: V.tensor_copy(out=o.t[:, :n], in_=bk.t[:, :n]), reads=[bk], writes=[o])
                        sch.dma("sp", KN[hd, :, t0:t0 + n], o.t[:, :n], reads=[o], writes=[RKN])
                    for b in range(n // 128):
                        v_ = vb[b % 2]
                        for hf in range(2):
                            bk = sch.bank()
                            for c in range(2):
                                sch.op(pe, lambda c=c: P.matmul(bk.t[:, :], ckn.t[:, c, b * 128:(b + 1) * 128], wkv.t[:, c, hf * 512:(hf + 1) * 512],
                                                                start=(c == 0), stop=(c == 1)),
                                       reads=[wkv, ckn], writes=[bk], inc=(c == 1))
                            sch.op(act, lambda: A.copy(out=v_.t[:, hf * 512:(hf + 1) * 512], in_=bk.t[:, :]), reads=[bk], writes=[v_])
                        sch.dma("sp", VA[t0 + b * 128:t0 + (b + 1) * 128, :], v_.t[:, :], reads=[v_], writes=[RVA])
                sch.barrier()
            sch.barrier()

    def mlstm_phase(l):
        with ExitStack() as ps:
            CA = [sb(ps, f"CA{d}", [64, NCH, 64], F32) for d in range(2)]
            ABC = sb(ps, "ABC", [128, 2, 4, NCH], F32)
            with ExitStack() as pg:
                W1 = sb(pg, "W1", [64, T], F32)
                NB = sb(pg, "NB", [64, T], F32)
                U = sb(pg, "U", [64, T], F32)
                G = sb(pg, "G", [64, T], F32)
                GE = sb(pg, "GE", [128, NCH], F32)
                GP = sb(pg, "GP", [128, NCH], F32)
                AA = sb(pg, "AA", [128, NCH], F32)
                TM = sb(pg, "TM", [64, T], F32)
                RS = [sb(pg, f"RS{d}", [64, T], F32) for d in range(2)]
                sch.op(dve, lambda: V.memset(AA.t[:], 0.0), writes=[AA])
                for d in range(2):
                    sch.dma("sp", W1.t[:, :], GROW[2 * d + 1, 0:64, :], reads=[RGROW], writes=[W1])
                    sch.dma("sp", U.t[:, :], GROW[2 * d, 0:64, :], reads=[RGROW], writes=[U])
                    sch.op(act, lambda: A.activation(out=W1.t[:, :], in_=W1.t[:, :], func=AF.Exp, scale=-1.0), reads=[W1], writes=[W1])
                    sch.op(act, lambda: A.activation(out=W1.t[:, :], in_=W1.t[:, :], func=AF.Ln, bias=ones_f.t[0:64, 0:1], scale=1.0),
                           reads=[W1, ones_f], writes=[W1])
                    if d == 0:
                        scans = [(slice(0, T), None)]
                    else:
                        scans = [(slice(CT - 1, None, -1), None), (slice(T - 1, CT - 1, -1), 0)]
                    for (sl, init_col) in scans:
                        nn = len(range(T)[sl])
                        init = 0.0 if init_col is None else NB.t[:, init_col:init_col + 1]
                        sch.op(dve, lambda: V.tensor_tensor_scan(out=NB.t[:, sl], data0=ones_f.t[0:64, 0:1].to_broadcast([64, nn]),
                                                                 data1=W1.t[:, sl], initial=init, op0=ALU.mult, op1=ALU.add),
                               reads=[W1, ones_f, NB], writes=[NB])
                    sch.op(dve, lambda: V.tensor_tensor(out=U.t[:, :], in0=U.t[:, :], in1=NB.t[:, :], op=ALU.add), reads=[U, NB], writes=[U])
                    for (sl, init_col) in scans:
                        init = 0.0 if init_col is None else G.t[:, init_col:init_col + 1]
                        sch.op(dve, lambda: V.tensor_tensor_scan(out=G.t[:, sl], data0=U.t[:, sl], data1=U.t[:, sl], initial=init,
                                                                 op0=ALU.max, op1=ALU.max),
                               reads=[U, G], writes=[G])
                    Gv = G.t[:, :].rearrange("p (c s) -> p c s", s=64)
                    if d == 0:
                        sch.op(dve, lambda: V.tensor_copy(out=GE.t[0:64, :], in_=Gv[:, :, 63]), reads=[G], writes=[GE])
                        sch.op(dve, lambda: V.memset(GP.t[0:64, 0:1], 0.0), writes=[GP])
                        sch.op(dve, lambda: V.tensor_copy(out=GP.t[0:64, 1:NCH], in_=GE.t[0:64, 0:NCH - 1]), reads=[GE], writes=[GP])
                    else:
                        nck = CT // 64
                        sch.op(dve, lambda: V.tensor_copy(out=GE.t[0:64, :], in_=Gv[:, :, 0]), reads=[G], writes=[GE])
                        sch.op(dve, lambda: V.memset(GP.t[0:64, nck - 1:nck], 0.0), writes=[GP])
                        sch.op(dve, lambda: V.tensor_copy(out=GP.t[0:64, 0:nck - 1], in_=GE.t[0:64, 1:nck]), reads=[GE], writes=[GP])
                        sch.op(dve, lambda: V.tensor_copy(out=GP.t[0:64, NCH - 1:NCH], in_=GE.t[0:64, 0:1]), reads=[GE], writes=[GP])
                        sch.op(dve, lambda: V.tensor_copy(out=GP.t[0:64, nck:NCH - 1], in_=GE.t[0:64, nck + 1:NCH]), reads=[GE], writes=[GP])
                    GEb = GE.t[0:64, :].unsqueeze(2).to_broadcast([64, NCH, 64])
                    TMv = TM.t[:, :].rearrange("p (c s) -> p c s", s=64)
                    sch.op(dve, lambda: V.tensor_tensor(out=TMv[0:32], in0=U.t[0:32, :].rearrange("p (c s) -> p c s", s=64), in1=GEb[0:32], op=ALU.subtract),
                           reads=[U, GE], writes=[TM])
                    sch.op(dve, lambda: V.tensor_tensor(out=TMv[32:64], in0=NB.t[32:64, :].rearrange("p (c s) -> p c s", s=64),
                                                        in1=GE.t[32:64, :].unsqueeze(2).to_broadcast([32, NCH, 64]), op=ALU.subtract),
                           reads=[NB, GE], writes=[TM])
                    sch.op(act, lambda: A.activation(out=RS[d].t[:, :], in_=TM.t[:, :], func=AF.Exp), reads=[TM], writes=[RS[d]])
                    for g in range(0, NCH, 8):
                        ng = min(8, NCH - g)
                        bk = sch.bank()
                        for q in range(ng):
                            c = g + q
                            sch.op(pe, lambda q=q, c=c: P.transpose(bk.t[0:64, q * 64:(q + 1) * 64], RS[d].t[:, c * 64:(c + 1) * 64], ident_f.t[0:64, 0:64]),
                                   reads=[RS[d], ident_f], writes=[bk], inc=(q == ng - 1))
                        sch.op(dve, lambda: V.tensor_copy(out=CA[d].t[:, g:g + ng, :], in_=bk.t[0:64, 0:ng * 64].rearrange("p (q f) -> p q f", f=64)),
                               reads=[bk], writes=[CA[d]])
                    sch.op(dve, lambda: V.tensor_tensor(out=AA.t[0:64, :], in0=GP.t[0:64, :], in1=GE.t[0:64, :], op=ALU.subtract),
                           reads=[GP, GE], writes=[AA])
                    sch.op(act, lambda: A.activation(out=AA.t[0:64, :], in_=AA.t[0:64, :], func=AF.Exp), reads=[AA], writes=[AA])
                    for hd in range(4):
                        bk = sch.bank()
                        sch.op(pe, lambda: P.matmul(bk.t[:, 0:NCH], sel_f.t[:, hd, :], AA.t[:, :], start=True, stop=True),
                               reads=[sel_f, AA], writes=[bk])
                        sch.op(act, lambda: A.copy(out=ABC.t[:, d, hd, :], in_=bk.t[:, 0:NCH]), reads=[bk], writes=[ABC])
                sch.barrier()

            with ExitStack() as ph:
                WT = 256
                WC = WT // 64
                NW = T // WT
                qTw = [sb(ph, f"qTw{q}", [128, KC, WT], BF16) for q in range(2)]
                kTw = [sb(ph, f"kTw{q}", [128, KC, WT], BF16) for q in range(2)]
                ktw = [sb(ph, f"ktw{q}", [64, WC, D], BF16) for q in range(2)]
                vxw = [sb(ph, f"vxw{q}", [64, WC, 4, 258], BF16) for q in range(2)]
                Cf = sb(ph, "Cf", [128, 4, 2, 258], F32)
                Cb = sb(ph, "Cb", [128, 4, 2, 258], BF16)
                NS = 2
                sTm = [[sb(ph, f"sTm{q}_{h}", [64, 64], BF16) for h in range(4)] for q in range(NS)]
                t1 = [[sb(ph, f"t1{q}_{h}", [64, 258], F32) for h in range(4)] for q in range(NS)]
                nd = [[sb(ph, f"nd{q}_{h}", [64, 258], F32) for h in range(4)] for q in range(NS)]
                dn = [[sb(ph, f"dn{q}_{h}", [64, 2], F32) for h in range(4)] for q in range(NS)]
                kw = [[sb(ph, f"kw{q}_{h}", [64, 256], BF16) for h in range(4)] for q in range(NS)]
                ho = [sb(ph, f"ho{q}", [64, 4, 256], F32) for q in range(NS)]
                for q in range(2):
                    sch.op(dve, lambda q=q: V.memset(vxw[q].t[:, :, :, 256:258], 1.0), writes=[vxw[q]])
                nck = CT // 64
                step = 0
                wi = 0
                for d in range(2):
                    HD = HF if d == 0 else HB
                    RH = RHF if d == 0 else RHB
                    lat = list(range(CT // WT, NW))
                    worder = list(range(CT // WT)) + (lat if d == 0 else lat[::-1])
                    if d == 1:
                        worder = list(range(CT // WT))[::-1] + lat[::-1]
                    sch.op(dve, lambda: V.memset(Cf.t[:], 0.0), writes=[Cf])
                    sch.op(dve, lambda: V.memset(Cb.t[:], 0.0), writes=[Cb])

                    def loadw(w, slot):
                        ts = slice(w * WT, (w + 1) * WT)
                        sch.dma("sp", qTw[slot].t[:, :, :], QT[:, :, ts].rearrange("k p t -> p k t"), reads=[RQT], writes=[qTw[slot]])
                        sch.dma("sp", kTw[slot].t[:, :, :], KT[:, :, ts].rearrange("k p t -> p k t"), reads=[RKT], writes=[kTw[slot]])
                        sch.dma("sp", ktw[slot].t[:, :, :], KTOK[ts, :].rearrange("(c p) f -> p c f", p=64), reads=[RKTOK], writes=[ktw[slot]])
                        for hd in range(4):
                            sch.dma("sp", vxw[slot].t[:, :, hd, 0:256], VTOK[ts, hd * 256:(hd + 1) * 256].rearrange("(c p) f -> p c f", p=64),
                                    reads=[RVTOK], writes=[vxw[slot]])

                    loadw(worder[0], wi % 2)
                    for wn, w in enumerate(worder):
                        slot = wi % 2
                        wi += 1
                        if wn + 1 < len(worder):
                            loadw(worder[wn + 1], wi % 2)
                        qT_, kT_, kt_, vx_ = qTw[slot], kTw[slot], ktw[slot], vxw[slot]
                        corder = list(range(WC)) if d == 0 else list(range(WC - 1, -1, -1))
                        for cl_ in corder:
                            c = w * WC + cl_
                            q = step % NS
                            step += 1
                            cs = slice(cl_ * 64, (cl_ + 1) * 64)
                            WEc = [CA[d].t[:, c, hd:hd + 1] for hd in range(4)]
                            EMc = [CA[d].t[:, c, 32 + hd:32 + hd + 1] for hd in range(4)]
                            for hd in range(4):
                                sch.op(act, lambda hd=hd: A.activation(out=kw[q][hd].t[:, :], in_=kt_.t[:, cl_, hd * 256:(hd + 1) * 256], func=AF.Copy, scale=WEc[hd]),
                                       reads=[kt_, CA[d]], writes=[kw[q][hd]])
                            for hd in range(4):
                                for kc in range(2):
                                    pu = sch.bank()
                                    sch.op(pe, lambda hd=hd, kc=kc: P.matmul(pu.t[:, 0:258], kw[q][hd].t[:, kc * 128:(kc + 1) * 128], vx_.t[:, cl_, hd, :], start=True, stop=True),
                                           reads=[kw[q][hd], vx_], writes=[pu])
                                    sch.op(dve, lambda hd=hd, kc=kc: V.scalar_tensor_tensor(out=Cf.t[:, hd, kc, :], in0=Cf.t[:, hd, kc, :], scalar=ABC.t[:, d, hd, c:c + 1],
                                                                                            in1=pu.t[:, 0:258], op0=ALU.mult, op1=ALU.add),
                                           reads=[Cf, ABC, pu], writes=[Cf])
                            psc = sch.bank()
                            for hd in range(4):
                                for kc in range(2):
                                    sch.op(pe, lambda hd=hd, kc=kc: P.matmul(psc.t[0:64, hd * 64:(hd + 1) * 64], kT_.t[:, 2 * hd + kc, cs], qT_.t[:, 2 * hd + kc, cs],
                                                                             start=(kc == 0), stop=(kc == 1)),
                                           reads=[kT_, qT_], writes=[psc], inc=(kc == 1 and hd == 3))
                            pin = []
                            for hd in range(4):
                                pi = sch.bank()
                                pin.append(pi)
                                for kc in range(2):
                                    sch.op(pe, lambda hd=hd, kc=kc: P.matmul(pi.t[0:64, 0:258], qT_.t[:, 2 * hd + kc, cs], Cb.t[:, hd, kc, :], start=(kc == 0), stop=(kc == 1)),
                                           reads=[qT_, Cb], writes=[pi], inc=(kc == 1))
                            for hd in range(4):
                                sch.op(pool, lambda hd=hd: nc.gpsimd.tensor_copy(out=Cb.t[:, hd, :, :], in_=Cf.t[:, hd, :, :]), reads=[Cf], writes=[Cb])
                            for hd in range(4):
                                sch.op(dve, lambda hd=hd: V.scalar_tensor_tensor(out=sTm[q][hd].t[:, :], in0=psc.t[0:64, hd * 64:(hd + 1) * 64], scalar=WEc[hd],
                                                                                 in1=tri_f.t[:, d, :], op0=ALU.mult, op1=ALU.mult),
                                       reads=[psc, CA[d], tri_f], writes=[sTm[q][hd]])
                            for hd in range(4):
                                sch.op(act, lambda hd=hd: A.activation(out=t1[q][hd].t[:, :], in_=pin[hd].t[0:64, 0:258], func=AF.Copy, scale=ABC.t[0:64, d, hd, c:c + 1]),
                                       reads=[pin[hd], ABC], writes=[t1[q][hd]])
                            pnn = []
                            for hd in range(4):
                                pn = sch.bank()
                                pnn.append(pn)
                                sch.op(pe, lambda hd=hd: P.matmul(pn.t[0:64, 0:258], sTm[q][hd].t[:, :], vx_.t[:, cl_, hd, :], start=True, stop=True),
                                       reads=[sTm[q][hd], vx_], writes=[pn])
                            for hd in range(4):
                                sch.op(dve, lambda hd=hd: V.tensor_tensor(out=nd[q][hd].t[:, :], in0=pnn[hd].t[0:64, 0:258], in1=t1[q][hd].t[:, :], op=ALU.add),
                                       reads=[pnn[hd], t1[q][hd]], writes=[nd[q][hd]])
                            for hd in range(4):
                                sch.op(act, lambda hd=hd: A.activation(out=dn[q][hd].t[:, 0:1], in_=nd[q][hd].t[:, 256:257], func=AF.Abs),
                                       reads=[nd[q][hd]], writes=[dn[q][hd]])
                            for hd in range(4):
                                sch.op(dve, lambda hd=hd: V.tensor_tensor(out=dn[q][hd].t[:, 0:1], in0=dn[q][hd].t[:, 0:1], in1=EMc[hd], op=ALU.max),
                                       reads=[dn[q][hd], CA[d]], writes=[dn[q][hd]])
                            for hd in range(4):
                                sch.op(dve, lambda hd=hd: V.reciprocal(out=dn[q][hd].t[:, 1:2], in_=dn[q][hd].t[:, 0:1]), reads=[dn[q][hd]], writes=[dn[q][hd]])
                            for hd in range(4):
                                sch.op(act, lambda hd=hd: A.activation(out=ho[q].t[:, hd, :], in_=nd[q][hd].t[:, 0:256], func=AF.Copy, scale=dn[q][hd].t[:, 1:2]),
                                       reads=[nd[q][hd], dn[q][hd]], writes=[ho[q]])
                            sch.dma("sp", HD[c * 64:(c + 1) * 64, :], ho[q].t[:, :, :].rearrange("p h f -> p (h f)"), reads=[ho[q]], writes=[RH])
                sch.barrier()
            sch.barrier()


    def attn_phase(l):
        sc = float(192 ** -0.5)
        with ExitStack() as ps:
            krT = sb(ps, "krT", [128, T], BF16)
            knT = sb(ps, "knT", [128, T], BF16)
            vh = sb(ps, "vh", [128, NTB, 128], BF16)
            qn = [sb(ps, f"qn{q}", [128, 512], BF16) for q in range(2)]
            qr = [sb(ps, f"qr{q}", [128, 512], BF16) for q in range(2)]
            pt = [sb(ps, f"pt{q}", [128, 512], BF16) for q in range(5)]
            ps2 = [sb(ps, f"ps2{q}", [128, 512], BF16) for q in range(2)]
            rd = sb(ps, "rd", [128, 512], F32)
            oo = [sb(ps, f"oo{q}", [128, 512], BF16) for q in range(2)]
            sch.op(dve, lambda: V.memset(krT.t[64:128, :], 0.0), writes=[krT])
            for q in range(2):
                sch.op(dve, lambda q=q: V.memset(qr[q].t[64:128, :], 0.0), writes=[qr[q]])
            sch.dma("sp", krT.t[0:64, :], KR[:, :], reads=[RKR], writes=[krT])
            it = 0
            pti = 0
            for hd in range(8):
                sch.dma("sp", knT.t[:, :], KN[hd], reads=[RKN], writes=[knT])
                sch.dma("sp", vh.t[:, :, :], VA[:, hd * 128:(hd + 1) * 128].rearrange("(tb p) f -> p tb f", p=128), reads=[RVA], writes=[vh])
                for ti, (t0, n, j) in enumerate(tiles):
                    q_n = qn[it % 2]
                    q_r = qr[it % 2]
                    o_ = oo[it % 2]
                    it += 1
                    sch.dma("sp", q_n.t[:, :n], QN[hd, :, t0:t0 + n], reads=[RQN], writes=[q_n])
                    sch.dma("sp", q_r.t[0:64, :n], QR[hd, :, t0:t0 + n], reads=[RQR], writes=[q_r])
                    kbs = list(range(CT // 128)) if j == 1 else list(range(NTB))
                    po = sch.bank(hold=True)
                    pd = sch.bank(hold=True)

                    def scores(kb):
                        b = sch.bank()
                        sch.op(pe, lambda: P.matmul(b.t[:, :n], knT.t[:, kb * 128:(kb + 1) * 128], q_n.t[:, :n], start=True, stop=False),
                               reads=[knT, q_n], writes=[b], inc=False)
                        sch.op(pe, lambda: P.matmul(b.t[:, :n], krT.t[:, kb * 128:(kb + 1) * 128], q_r.t[:, :n], start=False, stop=True),
                               reads=[krT, q_r], writes=[b])
                        return b

                    LA = 2
                    pend = [scores(kb) for kb in kbs[:LA]]
                    nk = len(kbs)
                    prev_p = None
                    first_den = True
                    for ii, kb in enumerate(kbs):
                        b = pend.pop(0)
                        p_ = pt[pti % len(pt)]
                        pti += 1
                        sch.op(act, lambda: A.activation(out=p_.t[:, :n], in_=b.t[:, :n], func=AF.Exp, scale=sc), reads=[b], writes=[p_])
                        if ii + LA < nk:
                            pend.append(scores(kbs[ii + LA]))
                        last = ii == nk - 1
                        sch.op(pe, lambda: P.matmul(po.t[:, :n], vh.t[:, kb, :], p_.t[:, :n], start=(ii == 0), stop=last),
                               reads=[vh, p_], writes=[po], inc=last)
                        if ii % 2 == 1:
                            s2 = ps2[(ii // 2) % 2]
                            sch.op(dve, lambda: V.tensor_tensor(out=s2.t[:, :n], in0=prev_p.t[:, :n], in1=p_.t[:, :n], op=ALU.add),
                                   reads=[prev_p, p_], writes=[s2])
                            sch.op(pe, lambda: P.matmul(pd.t[:, :n], ones_b.t[:, :], s2.t[:, :n], start=first_den, stop=last),
                                   reads=[ones_b, s2], writes=[pd], inc=True)
                            first_den = False
                        elif last:
                            sch.op(pe, lambda: P.matmul(pd.t[:, :n], ones_b.t[:, :], p_.t[:, :n], start=first_den, stop=True),
                                   reads=[ones_b, p_], writes=[pd], inc=True)
                        prev_p = p_
                    sch.op(dve, lambda: V.reciprocal(out=rd.t[:, :n], in_=pd.t[:, :n]), reads=[pd], writes=[rd])
                    sch.op(dve, lambda: V.tensor_tensor(out=o_.t[:, :n], in0=po.t[:, :n], in1=rd.t[:, :n], op=ALU.mult), reads=[po, rd], writes=[o_])
                    sch.release(po)
                    sch.release(pd)
                    sch.dma("sp", HAT[hd, :, t0:t0 + n], o_.t[:, :n], reads=[o_], writes=[RHAT])
            sch.barrier()

    def merge_phase(l):
        with ExitStack() as ps:
            wbm = sb(ps, "wbm", [128, KC, D], BF16)
            wba = sb(ps, "wba", [128, KC, D], BF16)
            wout = sb(ps, "wout", [128, KC, D], BF16)
            sch.dma("pool", wbm.t[:], wbm_in[l], writes=[wbm], max_dma_last_dim=4096)
            sch.dma("pool", wba.t[:], wba_in[l], writes=[wba], max_dma_last_dim=4096)
            sch.dma("pool", wout.t[:], wout_in[l], writes=[wout], max_dma_last_dim=4096)
            hf = sb(ps, "hf", [128, 4, D], F32)
            hb = sb(ps, "hb", [128, 4, D], F32)
            hn = sb(ps, "hn", [128, 4, D], BF16)
            junk = sb(ps, "junk", [128, 256], BF16)
            ssq = sb(ps, "ssq", [128, 16], F32)
            so = sb(ps, "so", [128, KC, 512], BF16)
            gm = sb(ps, "gm", [128, KC, 512], BF16)
            ga = sb(ps, "ga", [128, KC, 512], BF16)
            hat = sb(ps, "hat", [128, KC, 512], BF16)
            xt = sb(ps, "mxt", [128, KC, 512], F32)
            hmT = sb(ps, "hmT", [128, KC, 512], BF16)
            tm = sb(ps, "tm", [128, KC, 512], F32)
            t2 = sb(ps, "t2", [128, 512], F32)
            tb_ = sb(ps, "tb", [128, KC, 512], BF16)
            for ti, (t0, n, j) in enumerate(tiles):
                nb = n // 128
                sch.dma("sp", hf.t[:, 0:nb, :], HF[t0:t0 + n, :].rearrange("(b p) f -> p b f", p=128), reads=[RHF], writes=[hf])
                sch.dma("sp", hb.t[:, 0:nb, :], HB[t0:t0 + n, :].rearrange("(b p) f -> p b f", p=128), reads=[RHB], writes=[hb])
                sch.dma("sp", so.t[:, :, :n], SOT[:, :, t0:t0 + n].rearrange("k p t -> p k t"), reads=[RSOT], writes=[so])
                sch.dma("sp", gm.t[:, :, :n], GMT[:, :, t0:t0 + n].rearrange("k p t -> p k t"), reads=[RGMT], writes=[gm])
                sch.dma("sp", ga.t[:, :, :n], GAT[:, :, t0:t0 + n].rearrange("k p t -> p k t"), reads=[RGAT], writes=[ga])
                sch.dma("sp", hat.t[:, :, :n], HAT[:, :, t0:t0 + n].rearrange("k p t -> p k t"), reads=[RHAT], writes=[hat])
                sch.dma("sp", xt.t[:, :, :n], XT[:, :, t0:t0 + n].rearrange("k p t -> p k t"), reads=[xres[ti]], writes=[xt])
                sch.op(dve, lambda: V.tensor_tensor(out=hf.t[:, 0:nb, :], in0=hf.t[:, 0:nb, :], in1=hb.t[:, 0:nb, :], op=ALU.add),
                       reads=[hf, hb], writes=[hf])
                for b in range(nb):
                    for hd in range(4):
                        sch.op(act, lambda b=b, hd=hd: A.activation(out=junk.t[:, :], in_=hf.t[:, b, hd * 256:(hd + 1) * 256], func=AF.Square,
                                                                    accum_out=ssq.t[:, b * 4 + hd:b * 4 + hd + 1]),
                               reads=[hf], writes=[junk, ssq])
                sch.op(act, lambda: A.activation(out=ssq.t[:, 0:4 * nb], in_=ssq.t[:, 0:4 * nb], func=AF.Sqrt, scale=1.0 / 256, bias=eps_s.t[:, 0:1]),
                       reads=[ssq, eps_s], writes=[ssq])
                sch.op(dve, lambda: V.reciprocal(out=ssq.t[:, 0:4 * nb], in_=ssq.t[:, 0:4 * nb]), reads=[ssq], writes=[ssq])
                for b in range(nb):
                    for hd in range(4):
                        sch.op(dve, lambda b=b, hd=hd: V.tensor_scalar(out=hn.t[:, b, hd * 256:(hd + 1) * 256], in0=hf.t[:, b, hd * 256:(hd + 1) * 256],
                                                                       scalar1=ssq.t[:, b * 4 + hd:b * 4 + hd + 1], scalar2=None, op0=ALU.mult),
                               reads=[hf, ssq], writes=[hn])
                for m in range(KC):
                    bk = sch.bank()
                    bv = bk.t[:, :].bitcast(BF16)
                    for b in range(nb):
                        sch.op(pe, lambda b=b: P.transpose(bv[:, b * 128:(b + 1) * 128], hn.t[:, b, m * 128:(m + 1) * 128], ident_b.t[:]),
                               reads=[hn, ident_b], writes=[bk], inc=(b == nb - 1))
                    sch.op(dve, lambda: V.scalar_tensor_tensor(out=hmT.t[:, m, :n], in0=bv[:, 0:n], scalar=gmh_s.t[:, l, m:m + 1], in1=so.t[:, m, :n],
                                                               op0=ALU.mult, op1=ALU.mult),
                           reads=[bk, gmh_s, so], writes=[hmT])
                for m in range(KC):
                    bk = sch.bank()
                    for k in range(KC):
                        sch.op(pe, lambda k=k: P.matmul(bk.t[:, :n], wbm.t[:, k, m * 128:(m + 1) * 128], hmT.t[:, k, :n], start=(k == 0), stop=(k == KC - 1)),
                               reads=[wbm, hmT], writes=[bk], inc=(k == KC - 1))
                    sch.op(dve, lambda: V.tensor_tensor(out=tm.t[:, m, :n], in0=bk.t[:, :n], in1=gm.t[:, m, :n], op=ALU.mult),
                           reads=[bk, gm], writes=[tm])
                    bk2 = sch.bank()
                    for k in range(KC):
                        sch.op(pe, lambda k=k: P.matmul(bk2.t[:, :n], wba.t[:, k, m * 128:(m + 1) * 128], hat.t[:, k, :n], start=(k == 0), stop=(k == KC - 1)),
                               reads=[wba, hat], writes=[bk2], inc=(k == KC - 1))
                    sch.op(dve, lambda: V.tensor_tensor(out=t2.t[:, :n], in0=bk2.t[:, :n], in1=ga.t[:, m, :n], op=ALU.mult),
                           reads=[bk2, ga], writes=[t2])
                    sch.op(dve, lambda: V.tensor_tensor(out=tb_.t[:, m, :n], in0=tm.t[:, m, :n], in1=t2.t[:, :n], op=ALU.add),
                           reads=[tm, t2], writes=[tb_])
                for m in range(KC):
                    bk = sch.bank()
                    for k in range(KC):
                        sch.op(pe, lambda k=k: P.matmul(bk.t[:, :n], wout.t[:, k, m * 128:(m + 1) * 128], tb_.t[:, k, :n], start=(k == 0), stop=(k == KC - 1)),
                               reads=[wout, tb_], writes=[bk], inc=(k == KC - 1))
                    sch.op(dve, lambda: V.scalar_tensor_tensor(out=xt.t[:, m, :n], in0=bk.t[:, :n], scalar=GT.t[:, l, 1, m, j:j + 1], in1=xt.t[:, m, :n],
                                                               op0=ALU.mult, op1=ALU.add),
                           reads=[bk, GT, xt], writes=[xt])
                sch.dma("sp", XT[:, :, t0:t0 + n].rearrange("k p t -> p k t"), xt.t[:, :, :n], reads=[xt], writes=[xres[ti]])
            sch.barrier()

    def final_phase():
        with ExitStack() as ps:
            xts = [sb(ps, f"fxt{q}", [128, KC, 512], F32) for q in range(2)]
            sq = sb(ps, "fsq", [128, KC, 512], BF16)
            rt = sb(ps, "frt", [128, 512], F32)
            ot = [sb(ps, f"fot{q}", [128, KC, 512], F32) for q in range(2)]
            for ti, (t0, n, j) in enumerate(tiles):
                if j == 1:
                    continue
                xt = xts[ti % 2]
                o = ot[ti % 2]
                sch.dma("sp", xt.t[:, :, :n], XT[:, :, t0:t0 + n].rearrange("k p t -> p k t"), reads=[xres[ti]], writes=[xt])
                rms_rstd((sq, rt), lambda c: xt.t[:, c, :n], KC, n, 1.0 / D, xt)
                for k in range(KC):
                    sch.op(dve, lambda k=k: V.scalar_tensor_tensor(out=o.t[:, k, :n], in0=xt.t[:, k, :n], scalar=gfin_s.t[:, k:k + 1], in1=rt.t[:, :n],
                                                                   op0=ALU.mult, op1=ALU.mult),
                           reads=[xt, gfin_s, rt], writes=[o])
                sch.dma("sp", outT[:, :, t0 - CT:t0 - CT + n].rearrange("k p t -> p k t"), o.t[:, :, :n], reads=[o], writes=[ROUT])
            sch.barrier()

    for l in range(L):
        ffn_phase(l, 0, first=(l == 0))
        inproj_phase(l)
        mlstm_phase(l)
        attn_phase(l)
        merge_phase(l)
        ffn_phase(l, 2, first=False)
    final_phase()
    es.close()
    build.ninst = sch.ninst
    return nc


def _prep_shared(inp, L, S):
    f = np.float32
    T = CT + S
    out = {}
    w_ada = np.asarray(inp["w_ada"], f)[:L]
    out["wada"] = np.ascontiguousarray(w_ada.reshape(L, KC, 128, 72, 128).transpose(0, 3, 2, 1, 4))
    out["bada"] = np.ascontiguousarray(np.asarray(inp["b_ada"], f)[:L].reshape(L, 72, 128).transpose(2, 0, 1))
    gn = np.stack([np.asarray(inp[k], f)[:L] for k in ("g_n1", "g_n2", "g_n3")], axis=1)
    out["gn"] = np.ascontiguousarray(gn.reshape(L, 3, KC, 128).transpose(3, 0, 1, 2))
    out["gfin"] = np.ascontiguousarray(np.asarray(inp["g_final"], f).reshape(KC, 128).T)
    for i, (ku, kd) in enumerate((("w_ff1_up", "w_ff1_dn"), ("w_ff2_up", "w_ff2_dn"))):
        wu = np.asarray(inp[ku], f)[:L]
        wu = wu.reshape(L, KC, 128, 2, NF, 128)
        out[f"wup{i}"] = np.ascontiguousarray(wu.transpose(0, 4, 2, 1, 3, 5))
        wd = np.asarray(inp[kd], f)[:L].reshape(L, NF, 128, D)
        out[f"wdn{i}"] = np.ascontiguousarray(wd.transpose(0, 2, 1, 3))
    w_in = np.asarray(inp["w_in"], f)[:L]
    o = 0
    offs = {}
    for name, n in (("m_q", 1024), ("m_k", 1024), ("m_v", 1024), ("m_o", 1024), ("m_gate", 16), ("a_cq", 384), ("a_ckv", 256), ("a_kr", 64), ("br_gate", 2048)):
        offs[name] = (o, n)
        o += n

    def grp(name):
        a, n = offs[name]
        return w_in[:, :, a:a + n]

    deint = np.concatenate([np.arange(0, 64, 2), np.arange(1, 64, 2)])
    swp = np.concatenate([deint[32:], deint[:32]])
    gates = grp("m_gate")
    grep = np.zeros((L, D, 4, 128), f)
    for d in range(2):
        for ki in range(2):
            for hd in range(4):
                for qd in range(2):
                    grep[:, :, d * 2 + ki, 32 * qd + hd] = gates[:, :, d * 8 + ki * 4 + hd]
    kr = grp("a_kr")
    kr2 = np.concatenate([kr[:, :, deint], kr[:, :, swp]], axis=2)
    cols = np.concatenate([grp("m_q"), grp("m_k"), grp("m_o"), grp("br_gate"), grp("a_cq"), grp("a_ckv"), grep.reshape(L, D, 512), kr2], axis=2)
    assert cols.shape[2] == NWF * 128
    out["wf"] = np.ascontiguousarray(cols.reshape(L, KC, 128, NWF, 128).transpose(0, 3, 2, 1, 4))
    out["wv"] = np.ascontiguousarray(grp("m_v").reshape(L, KC, 128, D).transpose(0, 2, 1, 3))
    wc = np.asarray(inp["w_conv"], f)[:L]
    out["convw"] = np.ascontiguousarray(wc.reshape(L, 3, 16, 128).transpose(3, 0, 2, 1))
    bgl = np.asarray(inp["b_gate"], f)[:L]
    bg = np.zeros((128, L, 4), f)
    for d in range(2):
        for ki in range(2):
            for hd in range(4):
                for qd in range(2):
                    bg[32 * qd + hd, :, d * 2 + ki] = bgl[:, d * 8 + ki * 4 + hd]
    out["bg"] = bg
    out["gmh"] = np.ascontiguousarray(np.asarray(inp["g_mh"], f)[:L].reshape(L, KC, 128).transpose(2, 0, 1))
    out["gqa"] = np.ascontiguousarray(np.asarray(inp["g_qa"], f)[:L].reshape(L, 3, 128).transpose(2, 0, 1))
    out["gkva"] = np.ascontiguousarray(np.asarray(inp["g_kva"], f)[:L].reshape(L, 2, 128).transpose(2, 0, 1))
    wuq = np.asarray(inp["w_uq"], f)[:L].reshape(L, 384, 8, 192)
    chunks = []
    for hd in range(8):
        chunks.append(wuq[:, :, hd, 0:128])
        r = wuq[:, :, hd, 128:192]
        chunks.append(np.concatenate([r[:, :, deint], r[:, :, swp]], axis=2))
    wuq2 = np.stack(chunks, axis=2)
    out["wuq"] = np.ascontiguousarray(wuq2.reshape(L, 3, 128, 16, 128).transpose(0, 2, 1, 3, 4))
    wukv = np.asarray(inp["w_ukv"], f)[:L].reshape(L, 256, 8, 256)
    out["wukvk"] = np.ascontiguousarray(wukv[:, :, :, 0:128].reshape(L, 2, 128, 8, 128).transpose(0, 2, 1, 3, 4))
    out["wukvv"] = np.ascontiguousarray(wukv[:, :, :, 128:256].reshape(L, 2, 128, 1024).transpose(0, 2, 1, 3))
    for k, kk in (("w_bm", "wbm"), ("w_ba", "wba"), ("w_out", "wout")):
        out[kk] = np.ascontiguousarray(np.asarray(inp[k], f)[:L].reshape(L, KC, 128, D).transpose(0, 2, 1, 3))
    inv = (10000.0 ** (-np.arange(0, 32, 2, dtype=np.float32) / 32)).astype(f)
    t = np.arange(S)
    row = (t // 64).astype(f)
    col = (t % 64).astype(f)
    ang = np.concatenate([row[:, None] * inv, col[:, None] * inv], axis=-1)
    cos = np.cos(ang).astype(f).T
    sin = np.sin(ang).astype(f).T
    rope = np.zeros((64, 2, T), f)
    rope[:, 0, :CT] = 1.0
    rope[0:32, 0, CT:] = cos
    rope[32:64, 0, CT:] = cos
    rope[0:32, 1, CT:] = -sin
    rope[32:64, 1, CT:] = sin
    out["rope"] = rope
    out["ident"] = np.eye(128, dtype=f)
    sel = np.zeros((128, 4, 128), f)
    for hd in range(4):
        sel[hd, hd, :] = 1.0
    out["sel"] = sel
    tri = np.zeros((64, 2, 64), f)
    s_ = np.arange(64)[:, None]
    j_ = np.arange(64)[None, :]
    tri[:, 0, :] = (s_ <= j_)
    tri[:, 1, :] = (s_ >= j_)
    out["tri"] = tri
    return out


def _run(inp, L, S, dbg=False):
    f = np.float32
    shared = _prep_shared(inp, L, S)
    x = np.asarray(inp["x"], f)
    ctx = np.asarray(inp["ctx"], f)
    c = np.asarray(inp["c"], f)
    cc = np.asarray(inp["c_ctx"], f)
    B = x.shape[0]
    T = CT + S
    in_maps = []
    for core in range(8):
        b = core % B
        cat = np.concatenate([ctx[b], x[b]], axis=0)
        m = dict(shared)
        m["xT"] = np.ascontiguousarray(cat.T.reshape(KC, 128, T))
        scT = np.stack([c[b].reshape(KC, 128).T, cc.reshape(KC, 128).T], axis=-1)
        m["scT"] = np.ascontiguousarray(scT)
        in_maps.append(m)
    nc = build(L, S, dbg)
    res = run_bass_kernel_spmd(nc, in_maps, core_ids=list(range(8)))
    outs = []
    for b in range(B):
        o = res.results[b]["outT"]
        outs.append(np.ascontiguousarray(o.reshape(D, S).T))
    out = np.stack(outs, axis=0).astype(f)
    if dbg:
        return out, res
    return out


def kernel(**inputs):
    return _run(inputs, 4, 4096)
```

```python
import numpy as np
from contextlib import ExitStack
import concourse.bass as bass
import concourse.mybir as mybir
from concourse.bass_utils import run_bass_kernel_spmd

F32 = mybir.dt.float32
BF16 = mybir.dt.bfloat16
AF = mybir.ActivationFunctionType
ALU = mybir.AluOpType

D = 1024
KC = 8
DFF = 2816
NF = 22
CT = 256
EPS = 1e-6
NWF = 50


class SemObj:
    _n = 0

    def __init__(self, h):
        self.h = h
        self.count = 0
        SemObj._n += 1
        self.id = SemObj._n


class Res:
    def __init__(self, name=""):
        self.lw = {}
        self.rd = {}
        self.name = name


class Tl:
    def __init__(self, t, name=""):
        self.t = t
        self.r = Res(name)


class EngW:
    def __init__(self, name, eng, so):
        self.name = name
        self.eng = eng
        self.so = so
        self.waited = {}


class Sched:
    def __init__(self, nc, es):
        self.nc = nc

        def mk(name):
            return SemObj(es.enter_context(nc.semaphore(name)))

        self.pe = EngW("pe", nc.tensor, mk("s_pe"))
        self.act = EngW("act", nc.scalar, mk("s_act"))
        self.dve = EngW("dve", nc.vector, mk("s_dve"))
        self.pool = EngW("pool", nc.gpsimd, mk("s_pool"))
        self.sp = EngW("sp", nc.sync, mk("s_sp"))
        self.engs = [self.pe, self.act, self.dve, self.pool, self.sp]
        self.dq = {"sp": [mk(f"dsp{i}") for i in range(12)], "pool": [mk(f"dpl{i}") for i in range(12)]}
        self.dqn = {"sp": 0, "pool": 0}
        self.banks = []
        self.bank_i = 0
        self.held = set()
        self.ninst = 0

    def _wait(self, E, tick):
        so, v = tick
        if E.waited.get(so.id, 0) >= v:
            return
        E.eng.wait_ge(so.h, v)
        E.waited[so.id] = v
        self.ninst += 1

    def _deps(self, reads, writes, own):
        deps = {}

        def add(t, raw):
            so, v = t
            if so is own and not raw:
                return
            if deps.get(so.id, (None, 0))[1] < v:
                deps[so.id] = (so, v)

        for r in reads:
            for t in r.lw.values():
                add(t, True)
        for w in writes:
            for t in w.lw.values():
                add(t, False)
            for t in w.rd.values():
                add(t, False)
        return deps

    def _commit(self, tick, reads, writes):
        so, v = tick
        for w in writes:
            o = w.lw.get(so.id)
            if o is None or o[1] < v:
                w.lw[so.id] = tick
        for r in reads:
            o = r.rd.get(so.id)
            if o is None or o[1] < v:
                r.rd[so.id] = tick

    def op(self, E, fn, reads=(), writes=(), inc=True):
        reads = [x.r if isinstance(x, Tl) else x for x in reads]
        writes = [x.r if isinstance(x, Tl) else x for x in writes]
        for t in self._deps(reads, writes, E.so).values():
            self._wait(E, t)
        ins = fn()
        self.ninst += 1
        if inc:
            ins.then_inc(E.so.h, 1)
            E.so.count += 1
            tick = (E.so, E.so.count)
        else:
            tick = (E.so, E.so.count + 1)
        self._commit(tick, reads, writes)
        return ins

    def dma(self, q, out, in_, reads=(), writes=(), **kw):
        reads = [x.r if isinstance(x, Tl) else x for x in reads]
        writes = [x.r if isinstance(x, Tl) else x for x in writes]
        E = self.sp if q == "sp" else self.pool
        pool = self.dq[q]
        so = pool[self.dqn[q] % len(pool)]
        self.dqn[q] += 1
        if so.count > 0:
            self._wait(E, (so, so.count))
        for t in self._deps(reads, writes, None).values():
            self._wait(E, t)
        E.eng.dma_start(out=out, in_=in_, **kw).then_inc(so.h, 16)
        self.ninst += 1
        so.count += 16
        self._commit((so, so.count), reads, writes)

    def barrier(self):
        ticks = []
        for E in self.engs:
            if E.so.count > 0:
                ticks.append((E.so, E.so.count))
        for q in self.dq.values():
            for so in q:
                if so.count > 0:
                    ticks.append((so, so.count))
        for E in self.engs:
            for t in ticks:
                if t[0] is not E.so:
                    self._wait(E, t)

    def bank(self, hold=False):
        for _ in range(16):
            i = self.bank_i % 8
            self.bank_i += 1
            if i not in self.held:
                if hold:
                    self.held.add(i)
                return self.banks[i]
        raise RuntimeError("no psum bank")

    def release(self, b):
        self.held.discard(self.banks.index(b))


def build(L, S, dbg=False):
    T = CT + S
    NCH = T // 64
    NTB = T // 128
    nc = bass.Bass("TRN2", target_bir_lowering=False)
    es = ExitStack()

    def din(name, shape, dt=F32):
        return nc.dram_tensor(name, list(shape), dt, kind="ExternalInput").ap()

    scratch_kind = "ExternalOutput" if dbg else "Internal"

    def dscr(name, shape, dt):
        return nc.dram_tensor(name, list(shape), dt, kind=scratch_kind).ap()

    xT_in = din("xT", [KC, 128, T])
    scT_in = din("scT", [128, KC, 2])
    wada_in = din("wada", [L, 72, 128, KC, 128])
    bada_in = din("bada", [128, L, 72])
    gn_in = din("gn", [128, L, 3, KC])
    gfin_in = din("gfin", [128, KC])
    wup_in = [din(f"wup{i}", [L, NF, 128, KC, 2, 128]) for i in range(2)]
    wdn_in = [din(f"wdn{i}", [L, 128, NF, D]) for i in range(2)]
    wf_in = din("wf", [L, NWF, 128, KC, 128])
    wv_in = din("wv", [L, 128, KC, D])
    convw_in = din("convw", [128, L, 16, 3])
    bg_in = din("bg", [128, L, 4])
    gmh_in = din("gmh", [128, L, KC])
    gqa_in = din("gqa", [128, L, 3])
    gkva_in = din("gkva", [128, L, 2])
    wuq_in = din("wuq", [L, 128, 3, 16, 128])
    wukvk_in = din("wukvk", [L, 128, 2, 8, 128])
    wukvv_in = din("wukvv", [L, 128, 2, D])
    wbm_in = din("wbm", [L, 128, KC, D])
    wba_in = din("wba", [L, 128, KC, D])
    wout_in = din("wout", [L, 128, KC, D])
    rope_in = din("rope", [64, 2, T])
    ident_in = din("ident", [128, 128])
    sel_in = din("sel", [128, 4, 128])
    tri_in = din("tri", [64, 2, 64])
    outT = nc.dram_tensor("outT", [KC, 128, S], F32, kind="ExternalOutput").ap()

    XT = dscr("XT", [KC, 128, T], F32)
    QT = dscr("QT", [KC, 128, T], BF16)
    KT = dscr("KT", [KC, 128, T], BF16)
    SOT = dscr("SOT", [KC, 128, T], BF16)
    GMT = dscr("GMT", [KC, 128, T], BF16)
    GAT = dscr("GAT", [KC, 128, T], BF16)
    KTOK = dscr("KTOK", [T, D], BF16)
    VTOK = dscr("VTOK", [T, D], BF16)
    GROW = dscr("GROW", [4, 128, T], F32)
    HF = dscr("HF", [T, D], F32)
    HB = dscr("HB", [T, D], F32)
    QN = dscr("QN", [8, 128, T], BF16)
    QR = dscr("QR", [8, 64, T], BF16)
    KN = dscr("KN", [8, 128, T], BF16)
    KR = dscr("KR", [64, T], BF16)
    VA = dscr("VA", [T, D], BF16)
    HAT = dscr("HAT", [8, 128, T], BF16)

    sch = Sched(nc, es)
    pe, act, dve, pool, sp = sch.pe, sch.act, sch.dve, sch.pool, sch.sp

    for i in range(8):
        sch.banks.append(Tl(es.enter_context(nc.psum_tensor(f"bank{i}", [128, 512], F32)), f"bank{i}"))

    sbn = [0]

    def sb(stack, name, shape, dt):
        sbn[0] += 1
        nm = f"s{sbn[0]}_{name}"
        return Tl(stack.enter_context(nc.sbuf_tensor(nm, list(shape), dt)), nm)

    tiles = [(0, CT, 1)] + [(CT + 512 * i, 512, 0) for i in range(S // 512)]
    xres = [Res(f"XT{i}") for i in range(len(tiles))]
    r_xin = Res("xin")
    RQT, RKT, RSOT, RGMT, RGAT, RKTOK, RVTOK, RGROW, RHF, RHB, RQN, RQR, RKN, RKR, RVA, RHAT, ROUT = [Res() for _ in range(17)]

    ident_f = sb(es, "ident_f", [128, 128], F32)
    ident_b = sb(es, "ident_b", [128, 128], BF16)
    ones_b = sb(es, "ones_b", [128, 128], BF16)
    sel_f = sb(es, "sel_f", [128, 4, 128], F32)
    tri_f = sb(es, "tri_f", [64, 2, 64], F32)
    MOD = sb(es, "MOD", [128, L, 72, 2], F32)
    GS = sb(es, "GS", [128, L, 3, KC, 2], F32)
    GT = sb(es, "GT", [128, L, 3, KC, 2], F32)
    gn_s = sb(es, "gn_s", [128, L, 3, KC], F32)
    gfin_s = sb(es, "gfin_s", [128, KC], F32)
    convw_s = sb(es, "convw_s", [128, L, 16, 3], F32)
    bg_s = sb(es, "bg_s", [128, L, 4], F32)
    gmh_s = sb(es, "gmh_s", [128, L, KC], F32)
    gqa_s = sb(es, "gqa_s", [128, L, 3], F32)
    gkva_s = sb(es, "gkva_s", [128, L, 2], F32)
    bada_s = sb(es, "bada_s", [128, L, 72], F32)
    ones_f = sb(es, "ones_f", [128, 1], F32)

    V = nc.vector
    A = nc.scalar
    P = nc.tensor

    def load_const(tl, src, q="sp"):
        sch.dma(q, tl.t[:], src, writes=[tl])

    load_const(ident_f, ident_in[:, :])
    load_const(ident_b, ident_in[:, :], "pool")
    load_const(sel_f, sel_in[:, :, :])
    load_const(tri_f, tri_in[:, :, :])
    load_const(gn_s, gn_in[:, :, :, :])
    load_const(gfin_s, gfin_in[:, :])
    load_const(convw_s, convw_in[:, :, :, :])
    load_const(bg_s, bg_in[:, :, :])
    load_const(gmh_s, gmh_in[:, :, :])
    load_const(gqa_s, gqa_in[:, :, :])
    load_const(gkva_s, gkva_in[:, :, :])
    load_const(bada_s, bada_in[:, :, :])
    sch.op(dve, lambda: V.memset(ones_b.t[:], 1.0), writes=[ones_b])
    sch.op(dve, lambda: V.memset(ones_f.t[:], 1.0), writes=[ones_f])

    with ExitStack() as ps:
        sc_f = sb(ps, "sc_f", [128, KC, 2], F32)
        sc_b = sb(ps, "sc_b", [128, KC, 2], BF16)
        wa = [sb(ps, f"wa{i}", [128, 8, KC, 128], BF16) for i in range(2)]
        load_const(sc_f, scT_in[:, :, :])
        sch.op(act, lambda: A.activation(out=sc_b.t[:], in_=sc_f.t[:], func=AF.Silu), reads=[sc_f], writes=[sc_b])
        it = 0
        for l in range(L):
            for mg in range(9):
                w = wa[it % 2]
                it += 1
                sch.dma("pool", w.t[:], wada_in[l, mg * 8:(mg + 1) * 8].rearrange("m p k c -> p m k c"),
                        writes=[w], max_dma_last_dim=4096)
                bk = sch.bank()
                for m in range(8):
                    for k in range(KC):
                        sch.op(pe, lambda m=m, k=k: P.matmul(bk.t[:, 2 * m:2 * m + 2], w.t[:, m, k, :], sc_b.t[:, k, :],
                                                             start=(k == 0), stop=(k == KC - 1)),
                               reads=[w, sc_b], writes=[bk], inc=(k == KC - 1))
                sch.op(dve, lambda: V.tensor_tensor(
                    out=MOD.t[:, l, mg * 8:(mg + 1) * 8, :],
                    in0=bk.t[:, 0:16].rearrange("p (m j) -> p m j", j=2),
                    in1=bada_s.t[:, l, mg * 8:(mg + 1) * 8].unsqueeze(2).to_broadcast([128, 8, 2]),
                    op=ALU.add), reads=[bk, bada_s], writes=[MOD])
        for l in range(L):
            for i in range(3):
                coef = 1.0 if i == 1 else 0.5
                sc = MOD.t[:, l, (3 * i + 1) * 8:(3 * i + 2) * 8, :]
                gt = MOD.t[:, l, (3 * i + 2) * 8:(3 * i + 3) * 8, :]
                sch.op(dve, lambda: V.scalar_tensor_tensor(
                    out=GS.t[:, l, i, :, :], in0=sc, scalar=1.0,
                    in1=gn_s.t[:, l, i, :].unsqueeze(2).to_broadcast([128, KC, 2]),
                    op0=ALU.add, op1=ALU.mult), reads=[MOD, gn_s], writes=[GS])
                sch.op(dve, lambda: V.tensor_scalar(out=GT.t[:, l, i, :, :], in0=gt, scalar1=coef, scalar2=None,
                                                    op0=ALU.mult), reads=[MOD], writes=[GT])
        sch.barrier()

    def SH(l, i, k, j):
        return MOD.t[:, l, 3 * i * 8 + k, j:j + 1]

    def rms_rstd(stack_tiles, src_ap_fn, nchunks, n, inv_dim, src_res):
        sq, rt = stack_tiles
        for c in range(nchunks):
            sch.op(act, lambda c=c: A.activation(out=sq.t[:, c, :n], in_=src_ap_fn(c), func=AF.Square),
                   reads=[src_res], writes=[sq])
        bk = sch.bank()
        for c in range(nchunks):
            sch.op(pe, lambda c=c: P.matmul(bk.t[:, :n], ones_b.t[:], sq.t[:, c, :n], start=(c == 0), stop=(c == nchunks - 1)),
                   reads=[ones_b, sq], writes=[bk], inc=(c == nchunks - 1))
        sch.op(act, lambda: A.activation(out=rt.t[:, :n], in_=bk.t[:, :n], func=AF.Sqrt, scale=inv_dim, bias=eps_s.t[:, 0:1]),
               reads=[bk, eps_s], writes=[rt])
        sch.op(dve, lambda: V.reciprocal(out=rt.t[:, :n], in_=rt.t[:, :n]), reads=[rt], writes=[rt])
        return rt

    eps_s = sb(es, "eps_s", [128, 1], F32)
    sch.op(dve, lambda: V.memset(eps_s.t[:], EPS), writes=[eps_s])

    def norm_mod(xt, n, l, i, j, sq, rt, tmp, hout, hcol0=0):
        rms_rstd((sq, rt), lambda c: xt.t[:, c, :n], KC, n, 1.0 / D, xt)
        sch.op(dve, lambda: V.tensor_tensor(out=tmp.t[:, :, :n], in0=xt.t[:, :, :n],
                                            in1=rt.t[:, :n].unsqueeze(1).to_broadcast([128, KC, n]), op=ALU.mult),
               reads=[xt, rt], writes=[tmp])
        for k in range(KC):
            sch.op(act, lambda k=k: A.activation(out=hout.t[:, k, hcol0:hcol0 + n], in_=tmp.t[:, k, :n], func=AF.Identity,
                                                 scale=GS.t[:, l, i, k, j:j + 1], bias=SH(l, i, k, j)),
                   reads=[tmp, GS, MOD], writes=[hout])

    def ffn_phase(l, which, first):
        i = 0 if which == 0 else 2
        with ExitStack() as ps:
            wdn = sb(ps, "wdn", [128, NF, D], BF16)
            wup = [sb(ps, f"wupb{q}", [128, KC, 2, 128], BF16) for q in range(3)]
            xts = [sb(ps, f"xt{q}", [128, KC, 512], F32) for q in range(2)]
            hs = [sb(ps, f"h{q}", [128, KC, 512], BF16) for q in range(2)]
            sq = sb(ps, "sq", [128, KC, 512], BF16)
            rt = sb(ps, "rt", [128, 512], F32)
            tmp = sb(ps, "tmp", [128, KC, 512], F32)
            h2 = sb(ps, "h2", [128, NF, 512], BF16)
            sil = [sb(ps, f"sil{q}", [128, 512], BF16) for q in range(2)]
            for q in range(2):
                sch.dma("pool", wdn.t[:, q * 11:(q + 1) * 11, :], wdn_in[which // 2][l, :, q * 11:(q + 1) * 11, :],
                        writes=[wdn], max_dma_last_dim=4096)

            def load(ti):
                t0, n, j = tiles[ti]
                xt = xts[ti % 2]
                if first:
                    sch.dma("sp", xt.t[:, :, :n], xT_in[:, :, t0:t0 + n].rearrange("k p t -> p k t"), reads=[r_xin], writes=[xt])
                else:
                    sch.dma("sp", xt.t[:, :, :n], XT[:, :, t0:t0 + n].rearrange("k p t -> p k t"), reads=[xres[ti]], writes=[xt])

            load(0)
            load(1)
            wi = 0
            norm_mod(xts[0], tiles[0][1], l, i, tiles[0][2], sq, rt, tmp, hs[0])
            for ti, (t0, n, j) in enumerate(tiles):
                xt = xts[ti % 2]
                h = hs[ti % 2]
                for f in range(NF):
                    w = wup[wi % 3]
                    wi += 1
                    sch.dma("pool", w.t[:], wup_in[which // 2][l, f], writes=[w], max_dma_last_dim=4096)
                    pa = sch.bank()
                    pb = sch.bank()
                    for k in range(KC):
                        sch.op(pe, lambda k=k: P.matmul(pa.t[:, :n], w.t[:, k, 0, :], h.t[:, k, :n], start=(k == 0), stop=(k == KC - 1)),
                               reads=[w, h], writes=[pa], inc=(k == KC - 1))
                    for k in range(KC):
                        sch.op(pe, lambda k=k: P.matmul(pb.t[:, :n], w.t[:, k, 1, :], h.t[:, k, :n], start=(k == 0), stop=(k == KC - 1)),
                               reads=[w, h], writes=[pb], inc=(k == KC - 1))
                    s_ = sil[f % 2]
                    sch.op(act, lambda: A.activation(out=s_.t[:, :n], in_=pa.t[:, :n], func=AF.Silu), reads=[pa], writes=[s_])
                    sch.op(dve, lambda: V.tensor_tensor(out=h2.t[:, f, :n], in0=pb.t[:, :n], in1=s_.t[:, :n], op=ALU.mult),
                           reads=[pb, s_], writes=[h2])
                if ti + 1 < len(tiles):
                    t0n, nn_, jn = tiles[ti + 1]
                    norm_mod(xts[(ti + 1) % 2], nn_, l, i, jn, sq, rt, tmp, hs[(ti + 1) % 2])
                for m in range(KC):
                    py = sch.bank()
                    for f in range(NF):
                        sch.op(pe, lambda f=f: P.matmul(py.t[:, :n], wdn.t[:, f, m * 128:(m + 1) * 128], h2.t[:, f, :n],
                                                        start=(f == 0), stop=(f == NF - 1)),
                               reads=[wdn, h2], writes=[py], inc=(f == NF - 1))
                    sch.op(dve, lambda: V.scalar_tensor_tensor(out=xt.t[:, m, :n], in0=py.t[:, :n], scalar=GT.t[:, l, i, m, j:j + 1],
                                                               in1=xt.t[:, m, :n], op0=ALU.mult, op1=ALU.add),
                           reads=[py, GT, xt], writes=[xt])
                sch.dma("sp", XT[:, :, t0:t0 + n].rearrange("k p t -> p k t"), xt.t[:, :, :n], reads=[xt], writes=[xres[ti]])
                if ti + 2 < len(tiles):
                    load(ti + 2)
            sch.barrier()

    def inproj_phase(l):
        with ExitStack() as ps:
            hall = sb(ps, "hall", [128, KC, T], BF16)
            with ExitStack() as p1:
                xts = [sb(p1, f"ixt{q}", [128, KC, 512], F32) for q in range(2)]
                sq = sb(p1, "isq", [128, KC, 512], BF16)
                rt = sb(p1, "irt", [128, 512], F32)
                tmp = sb(p1, "itmp", [128, KC, 512], F32)

                def load(ti):
                    t0, n, j = tiles[ti]
                    sch.dma("sp", xts[ti % 2].t[:, :, :n], XT[:, :, t0:t0 + n].rearrange("k p t -> p k t"),
                            reads=[xres[ti]], writes=[xts[ti % 2]])

                load(0)
                for ti, (t0, n, j) in enumerate(tiles):
                    if ti + 1 < len(tiles):
                        load(ti + 1)
                    norm_mod(xts[ti % 2], n, l, 1, j, sq, rt, tmp, hall, hcol0=t0)
                sch.barrier()

            with ExitStack() as p2:
                wfb = [sb(p2, f"wfb{q}", [128, KC, 128], BF16) for q in range(3)]
                zs = [sb(p2, f"zs{q}", [128, T], F32) for q in range(2)]
                ys = sb(p2, "ys", [128, T], F32)
                zb = [sb(p2, f"zb{q}", [128, T], BF16) for q in range(2)]
                ktk = sb(p2, "ktk", [128, NTB, 128], BF16)
                wi = 0

                def chunk_mm(ci, evac):
                    nonlocal wi
                    w = wfb[wi % 3]
                    wi += 1
                    sch.dma("pool", w.t[:], wf_in[l, ci], writes=[w], max_dma_last_dim=4096)
                    for (t0, n, j) in tiles:
                        bk = sch.bank()
                        for k in range(KC):
                            sch.op(pe, lambda k=k: P.matmul(bk.t[:, :n], w.t[:, k, :], hall.t[:, k, t0:t0 + n],
                                                            start=(k == 0), stop=(k == KC - 1)),
                                   reads=[w, hall], writes=[bk], inc=(k == KC - 1))
                        evac(bk, t0, n)

                segs = [(0, CT), (CT, T)]
                for ci in range(16):
                    z = zs[ci % 2]
                    o = zb[ci % 2]

                    def ev(bk, t0, n, z=z):
                        sch.op(act, lambda: A.copy(out=z.t[:, t0:t0 + n], in_=bk.t[:, :n]), reads=[bk], writes=[z])

                    chunk_mm(ci, ev)
                    w0 = convw_s.t[:, l, ci, 0:1]
                    w1 = convw_s.t[:, l, ci, 1:2]
                    w2 = convw_s.t[:, l, ci, 2:3]
                    for (a, b) in segs:
                        sch.op(dve, lambda: V.tensor_scalar(out=ys.t[:, a:b], in0=z.t[:, a:b], scalar1=w1, scalar2=None, op0=ALU.mult),
                               reads=[z, convw_s], writes=[ys])
                        sch.op(dve, lambda: V.scalar_tensor_tensor(out=ys.t[:, a + 1:b], in0=z.t[:, a:b - 1], scalar=w0,
                                                                   in1=ys.t[:, a + 1:b], op0=ALU.mult, op1=ALU.add),
                               reads=[z, ys, convw_s], writes=[ys])
                        sch.op(dve, lambda: V.scalar_tensor_tensor(out=ys.t[:, a:b - 1], in0=z.t[:, a + 1:b], scalar=w2,
                                                                   in1=ys.t[:, a:b - 1], op0=ALU.mult, op1=ALU.add),
                               reads=[z, ys, convw_s], writes=[ys])
                    if ci < 8:
                        sch.op(act, lambda: A.activation(out=z.t[:, :], in_=ys.t[:, :], func=AF.Silu), reads=[ys], writes=[z])
                        sch.op(dve, lambda: V.tensor_scalar(out=o.t[:, :], in0=z.t[:, :], scalar1=0.0625, scalar2=None, op0=ALU.mult),
                               reads=[z], writes=[o])
                        sch.dma("sp", QT[ci], o.t[:, :], reads=[o], writes=[RQT])
                    else:
                        m = ci - 8
                        sch.op(act, lambda: A.activation(out=o.t[:, :], in_=ys.t[:, :], func=AF.Silu), reads=[ys], writes=[o])
                        sch.dma("sp", KT[m], o.t[:, :], reads=[o], writes=[RKT])
                        for g in range(0, NTB, 8):
                            nb = min(8, NTB - g)
                            bk = sch.bank()
                            bv = bk.t[:, :].bitcast(BF16)
                            for q in range(nb):
                                tb = g + q
                                sch.op(pe, lambda q=q, tb=tb: P.transpose(bv[:, q * 128:(q + 1) * 128], o.t[:, tb * 128:(tb + 1) * 128], ident_b.t[:]),
                                       reads=[o, ident_b], writes=[bk], inc=(q == nb - 1))
                            sch.op(act, lambda: A.copy(out=ktk.t[:, g:g + nb, :], in_=bv[:, 0:nb * 128].rearrange("p (q f) -> p q f", f=128)),
                                   reads=[bk], writes=[ktk])
                        sch.dma("sp", KTOK[:, m * 128:(m + 1) * 128].rearrange("(tb p) f -> p tb f", p=128), ktk.t[:, :, :],
                                reads=[ktk], writes=[RKTOK])
                for ci in range(16, 40):
                    o = zb[ci % 2]

                    def ev(bk, t0, n, o=o):
                        sch.op(act, lambda: A.activation(out=o.t[:, t0:t0 + n], in_=bk.t[:, :n], func=AF.Sigmoid), reads=[bk], writes=[o])

                    chunk_mm(ci, ev)
                    if ci < 24:
                        sch.dma("sp", SOT[ci - 16], o.t[:, :], reads=[o], writes=[RSOT])
                    elif ci < 32:
                        sch.dma("sp", GMT[ci - 24], o.t[:, :], reads=[o], writes=[RGMT])
                    else:
                        sch.dma("sp", GAT[ci - 32], o.t[:, :], reads=[o], writes=[RGAT])
                for kind in range(4):
                    ci = 45 + kind
                    z = zs[ci % 2]

                    def ev(bk, t0, n, z=z, kind=kind):
                        sch.op(act, lambda: A.activation(out=z.t[:, t0:t0 + n], in_=bk.t[:, :n], func=AF.Identity,
                                                         bias=bg_s.t[:, l, kind:kind + 1], scale=1.0),
                               reads=[bk, bg_s], writes=[z])

                    chunk_mm(ci, ev)
                    sch.dma("sp", GROW[kind], z.t[:, :], reads=[z], writes=[RGROW])
                sch.barrier()

            with ExitStack() as p3:
                wv = sb(p3, "wv", [128, KC, D], BF16)
                vt = [sb(p3, f"vt{q}", [128, D], BF16) for q in range(2)]
                sch.dma("pool", wv.t[:], wv_in[l], writes=[wv], max_dma_last_dim=4096)
                for tb in range(NTB):
                    v_ = vt[tb % 2]
                    for hf in range(2):
                        bk = sch.bank()
                        for k in range(KC):
                            sch.op(pe, lambda k=k: P.matmul(bk.t[:, :], hall.t[:, k, tb * 128:(tb + 1) * 128], wv.t[:, k, hf * 512:(hf + 1) * 512],
                                                            start=(k == 0), stop=(k == KC - 1)),
                                   reads=[wv, hall], writes=[bk], inc=(k == KC - 1))
                        if hf == 0:
                            sch.op(act, lambda: A.copy(out=v_.t[:, 0:512], in_=bk.t[:, :]), reads=[bk], writes=[v_])
                        else:
                            sch.op(dve, lambda: V.tensor_copy(out=v_.t[:, 512:1024], in_=bk.t[:, :]), reads=[bk], writes=[v_])
                    sch.dma("sp", VTOK[tb * 128:(tb + 1) * 128, :], v_.t[:, :], reads=[v_], writes=[RVTOK])
                sch.barrier()

            with ExitStack() as p4:
                wl = sb(p4, "wl", [128, 6, KC, 128], BF16)
                wuq = sb(p4, "wuq", [128, 3, 16, 128], BF16)
                wkk = sb(p4, "wkk", [128, 2, 8, 128], BF16)
                wkv = sb(p4, "wkv", [128, 2, D], BF16)
                cqf = sb(p4, "cqf", [128, 3, 512], F32)
                sq = sb(p4, "lsq", [128, 3, 512], BF16)
                rt = sb(p4, "lrt", [128, 512], F32)
                cqn = sb(p4, "cqn", [128, 3, 512], BF16)
                ckn = sb(p4, "ckn", [128, 2, 512], BF16)
                rp = [sb(p4, f"rp{q}", [64, 2, 512], F32) for q in range(2)]
                r1 = sb(p4, "r1", [64, 512], F32)
                r2 = sb(p4, "r2", [64, 512], F32)
                ob = [sb(p4, f"ob{q}", [128, 512], BF16) for q in range(4)]
                vb = [sb(p4, f"vb{q}", [128, D], BF16) for q in range(2)]
                for c in range(5):
                    sch.dma("pool", wl.t[:, c], wf_in[l, 40 + c], writes=[wl], max_dma_last_dim=4096)
                sch.dma("pool", wl.t[:, 5], wf_in[l, 49], writes=[wl], max_dma_last_dim=4096)
                sch.dma("pool", wuq.t[:], wuq_in[l], writes=[wuq], max_dma_last_dim=4096)
                sch.dma("pool", wkk.t[:], wukvk_in[l], writes=[wkk], max_dma_last_dim=4096)
                sch.dma("pool", wkv.t[:], wukvv_in[l], writes=[wkv], max_dma_last_dim=4096)
                oi = 0

                def rope_out(pa, pb, n, rpt, dst_ap, dres):
                    nonlocal oi
                    o = ob[oi % 4]
                    oi += 1
                    sch.op(dve, lambda: V.tensor_tensor(out=r1.t[:, :n], in0=pa.t[0:64, :n], in1=rpt.t[:, 0, :n], op=ALU.mult),
                           reads=[pa, rpt], writes=[r1])
                    sch.op(dve, lambda: V.tensor_tensor(out=r2.t[:, :n], in0=pb.t[0:64, :n], in1=rpt.t[:, 1, :n], op=ALU.mult),
                           reads=[pb, rpt], writes=[r2])
                    sch.op(dve, lambda: V.tensor_tensor(out=o.t[0:64, :n], in0=r1.t[:, :n], in1=r2.t[:, :n], op=ALU.add),
                           reads=[r1, r2], writes=[o])
                    sch.dma("sp", dst_ap, o.t[0:64, :n], reads=[o], writes=[dres])

                for ti, (t0, n, j) in enumerate(tiles):
                    rpt = rp[ti % 2]
                    sch.dma("sp", rpt.t[:, :, :n], rope_in[:, :, t0:t0 + n], writes=[rpt])
                    for (c0, ncn, gsrc, dst, inv) in ((0, 3, gqa_s, cqn, 1.0 / 384), (3, 2, gkva_s, ckn, 1.0 / 256)):
                        for c in range(ncn):
                            bk = sch.bank()
                            for k in range(KC):
                                sch.op(pe, lambda k=k: P.matmul(bk.t[:, :n], wl.t[:, c0 + c, k, :], hall.t[:, k, t0:t0 + n],
                                                                start=(k == 0), stop=(k == KC - 1)),
                                       reads=[wl, hall], writes=[bk], inc=(k == KC - 1))
                            sch.op(act, lambda: A.copy(out=cqf.t[:, c, :n], in_=bk.t[:, :n]), reads=[bk], writes=[cqf])
                        rms_rstd((sq, rt), lambda c: cqf.t[:, c, :n], ncn, n, inv, cqf)
                        for c in range(ncn):
                            sch.op(dve, lambda c=c: V.scalar_tensor_tensor(out=dst.t[:, c, :n], in0=cqf.t[:, c, :n], scalar=gsrc.t[:, l, c:c + 1],
                                                                           in1=rt.t[:, :n], op0=ALU.mult, op1=ALU.mult),
                                   reads=[cqf, gsrc, rt], writes=[dst])
                    pa = sch.bank()
                    pb = sch.bank()
                    for k in range(KC):
                        sch.op(pe, lambda k=k: P.matmul(pa.t[0:64, :n], wl.t[:, 5, k, 0:64], hall.t[:, k, t0:t0 + n], start=(k == 0), stop=(k == KC - 1)),
                               reads=[wl, hall], writes=[pa], inc=(k == KC - 1))
                    for k in range(KC):
                        sch.op(pe, lambda k=k: P.matmul(pb.t[0:64, :n], wl.t[:, 5, k, 64:128], hall.t[:, k, t0:t0 + n], start=(k == 0), stop=(k == KC - 1)),
                               reads=[wl, hall], writes=[pb], inc=(k == KC - 1))
                    rope_out(pa, pb, n, rpt, KR[:, t0:t0 + n], RKR)
                    for hd in range(8):
                        bk = sch.bank()
                        for c in range(3):
                            sch.op(pe, lambda c=c: P.matmul(bk.t[:, :n], wuq.t[:, c, 2 * hd, :], cqn.t[:, c, :n], start=(c == 0), stop=(c == 2)),
                                   reads=[wuq, cqn], writes=[bk], inc=(c == 2))
                        o = ob[oi % 4]
                        oi += 1
                        sch.op(act, lambda: A.copy(out=o.t[:, :n], in_=bk.t[:, :n]), reads=[bk], writes=[o])
                        sch.dma("sp", QN[hd, :, t0:t0 + n], o.t[:, :n], reads=[o], writes=[RQN])
                        pa = sch.bank()
                        pb = sch.bank()
                        for c in range(3):
                            sch.op(pe, lambda c=c: P.matmul(pa.t[0:64, :n], wuq.t[:, c, 2 * hd + 1, 0:64], cqn.t[:, c, :n], start=(c == 0), stop=(c == 2)),
                                   reads=[wuq, cqn], writes=[pa], inc=(c == 2))
                        for c in range(3):
                            sch.op(pe, lambda c=c: P.matmul(pb.t[0:64, :n], wuq.t[:, c, 2 * hd + 1, 64:128], cqn.t[:, c, :n], start=(c == 0), stop=(c == 2)),
                                   reads=[wuq, cqn], writes=[pb], inc=(c == 2))
                        rope_out(pa, pb, n, rpt, QR[hd, :, t0:t0 + n], RQR)
                        bk = sch.bank()
                        for c in range(2):
                            sch.op(pe, lambda c=c: P.matmul(bk.t[:, :n], wkk.t[:, c, hd, :], ckn.t[:, c, :n], start=(c == 0), stop=(c == 1)),
                                   reads=[wkk, ckn], writes=[bk], inc=(c == 1))
                        o = ob[oi % 4]
                        oi += 1
                        sch.op(dve, lambda: V.tensor_copy(out=o.t[:, :n], in_=bk.t[:, :n]), reads=[bk], writes=[o])
                        sch.dma("sp", KN[hd, :, t0:t0 + n], o.t[:, :n], reads=[o], writes=[RKN])
                    for b in range(n // 128):
                        v_ = vb[b % 2]
                        for hf in range(2):
                            bk = sch.bank()
                            for c in range(2):
                                sch.op(pe, lambda c=c: P.matmul(bk.t[:, :], ckn.t[:, c, b * 128:(b + 1) * 128], wkv.t[:, c, hf * 512:(hf + 1) * 512],
                                                                start=(c == 0), stop=(c == 1)),
                                       reads=[wkv, ckn], writes=[bk], inc=(c == 1))
                            sch.op(act, lambda: A.copy(out=v_.t[:, hf * 512:(hf + 1) * 512], in_=bk.t[:, :]), reads=[bk], writes=[v_])
                        sch.dma("sp", VA[t0 + b * 128:t0 + (b + 1) * 128, :], v_.t[:, :], reads=[v_], writes=[RVA])
                sch.barrier()
            sch.barrier()

    def mlstm_phase(l):
        with ExitStack() as ps:
            CA = [sb(ps, f"CA{d}", [64, NCH, 64], F32) for d in range(2)]
            ABC = sb(ps, "ABC", [128, 2, 4, NCH], F32)
            with ExitStack() as pg:
                W1 = sb(pg, "W1", [64, T], F32)
                NB = sb(pg, "NB", [64, T], F32)
                U = sb(pg, "U", [64, T], F32)
                G = sb(pg, "G", [64, T], F32)
                GE = sb(pg, "GE", [128, NCH], F32)
                GP = sb(pg, "GP", [128, NCH], F32)
                AA = sb(pg, "AA", [128, NCH], F32)
                TM = sb(pg, "TM", [64, T], F32)
                RS = [sb(pg, f"RS{d}", [64, T], F32) for d in range(2)]
                sch.op(dve, lambda: V.memset(AA.t[:], 0.0), writes=[AA])
                for d in range(2):
                    sch.dma("sp", W1.t[:, :], GROW[2 * d + 1, 0:64, :], reads=[RGROW], writes=[W1])
                    sch.dma("sp", U.t[:, :], GROW[2 * d, 0:64, :], reads=[RGROW], writes=[U])
                    sch.op(act, lambda: A.activation(out=W1.t[:, :], in_=W1.t[:, :], func=AF.Exp, scale=-1.0), reads=[W1], writes=[W1])
                    sch.op(act, lambda: A.activation(out=W1.t[:, :], in_=W1.t[:, :], func=AF.Ln, bias=ones_f.t[0:64, 0:1], scale=1.0),
                           reads=[W1, ones_f], writes=[W1])
                    if d == 0:
                        scans = [(slice(0, T), None)]
                    else:
                        scans = [(slice(CT - 1, None, -1), None), (slice(T - 1, CT - 1, -1), 0)]
                    for (sl, init_col) in scans:
                        nn = len(range(T)[sl])
                        init = 0.0 if init_col is None else NB.t[:, init_col:init_col + 1]
                        sch.op(dve, lambda: V.tensor_tensor_scan(out=NB.t[:, sl], data0=ones_f.t[0:64, 0:1].to_broadcast([64, nn]),
                                                                 data1=W1.t[:, sl], initial=init, op0=ALU.mult, op1=ALU.add),
                               reads=[W1, ones_f, NB], writes=[NB])
                    sch.op(dve, lambda: V.tensor_tensor(out=U.t[:, :], in0=U.t[:, :], in1=NB.t[:, :], op=ALU.add), reads=[U, NB], writes=[U])
                    for (sl, init_col) in scans:
                        init = 0.0 if init_col is None else G.t[:, init_col:init_col + 1]
                        sch.op(dve, lambda: V.tensor_tensor_scan(out=G.t[:, sl], data0=U.t[:, sl], data1=U.t[:, sl], initial=init,
                                                                 op0=ALU.max, op1=ALU.max),
                               reads=[U, G], writes=[G])
                    Gv = G.t[:, :].rearrange("p (c s) -> p c s", s=64)
                    if d == 0:
                        sch.op(dve, lambda: V.tensor_copy(out=GE.t[0:64, :], in_=Gv[:, :, 63]), reads=[G], writes=[GE])
                        sch.op(dve, lambda: V.memset(GP.t[0:64, 0:1], 0.0), writes=[GP])
                        sch.op(dve, lambda: V.tensor_copy(out=GP.t[0:64, 1:NCH], in_=GE.t[0:64, 0:NCH - 1]), reads=[GE], writes=[GP])
                    else:
                        nck = CT // 64
                        sch.op(dve, lambda: V.tensor_copy(out=GE.t[0:64, :], in_=Gv[:, :, 0]), reads=[G], writes=[GE])
                        sch.op(dve, lambda: V.memset(GP.t[0:64, nck - 1:nck], 0.0), writes=[GP])
                        sch.op(dve, lambda: V.tensor_copy(out=GP.t[0:64, 0:nck - 1], in_=GE.t[0:64, 1:nck]), reads=[GE], writes=[GP])
                        sch.op(dve, lambda: V.tensor_copy(out=GP.t[0:64, NCH - 1:NCH], in_=GE.t[0:64, 0:1]), reads=[GE], writes=[GP])
                        sch.op(dve, lambda: V.tensor_copy(out=GP.t[0:64, nck:NCH - 1], in_=GE.t[0:64, nck + 1:NCH]), reads=[GE], writes=[GP])
                    GEb = GE.t[0:64, :].unsqueeze(2).to_broadcast([64, NCH, 64])
                    TMv = TM.t[:, :].rearrange("p (c s) -> p c s", s=64)
                    sch.op(dve, lambda: V.tensor_tensor(out=TMv[0:32], in0=U.t[0:32, :].rearrange("p (c s) -> p c s", s=64), in1=GEb[0:32], op=ALU.subtract),
                           reads=[U, GE], writes=[TM])
                    sch.op(dve, lambda: V.tensor_tensor(out=TMv[32:64], in0=NB.t[32:64, :].rearrange("p (c s) -> p c s", s=64),
                                                        in1=GE.t[32:64, :].unsqueeze(2).to_broadcast([32, NCH, 64]), op=ALU.subtract),
                           reads=[NB, GE], writes=[TM])
                    sch.op(act, lambda: A.activation(out=RS[d].t[:, :], in_=TM.t[:, :], func=AF.Exp), reads=[TM], writes=[RS[d]])
                    for g in range(0, NCH, 8):
                        ng = min(8, NCH - g)
                        bk = sch.bank()
                        for q in range(ng):
                            c = g + q
                            sch.op(pe, lambda q=q, c=c: P.transpose(bk.t[0:64, q * 64:(q + 1) * 64], RS[d].t[:, c * 64:(c + 1) * 64], ident_f.t[0:64, 0:64]),
                                   reads=[RS[d], ident_f], writes=[bk], inc=(q == ng - 1))
                        sch.op(dve, lambda: V.tensor_copy(out=CA[d].t[:, g:g + ng, :], in_=bk.t[0:64, 0:ng * 64].rearrange("p (q f) -> p q f", f=64)),
                               reads=[bk], writes=[CA[d]])
                    sch.op(dve, lambda: V.tensor_tensor(out=AA.t[0:64, :], in0=GP.t[0:64, :], in1=GE.t[0:64, :], op=ALU.subtract),
                           reads=[GP, GE], writes=[AA])
                    sch.op(act, lambda: A.activation(out=AA.t[0:64, :], in_=AA.t[0:64, :], func=AF.Exp), reads=[AA], writes=[AA])
                    for hd in range(4):
                        bk = sch.bank()
                        sch.op(pe, lambda: P.matmul(bk.t[:, 0:NCH], sel_f.t[:, hd, :], AA.t[:, :], start=True, stop=True),
                               reads=[sel_f, AA], writes=[bk])
                        sch.op(act, lambda: A.copy(out=ABC.t[:, d, hd, :], in_=bk.t[:, 0:NCH]), reads=[bk], writes=[ABC])
                sch.barrier()

            with ExitStack() as ph:
                WT = 256
                WC = WT // 64
                NW = T // WT
                qTw = [sb(ph, f"qTw{q}", [128, KC, WT], BF16) for q in range(2)]
                kTw = [sb(ph, f"kTw{q}", [128, KC, WT], BF16) for q in range(2)]
                ktw = [sb(ph, f"ktw{q}", [64, WC, D], BF16) for q in range(2)]
                vxw = [sb(ph, f"vxw{q}", [64, WC, 4, 258], BF16) for q in range(2)]
                Cf = [[sb(ph, f"Cf{h}_{k}", [128, 258], F32) for k in range(2)] for h in range(4)]
                Cb = [[sb(ph, f"Cb{h}_{k}", [128, 258], BF16) for k in range(2)] for h in range(4)]
                NS = 2
                sTm = [[sb(ph, f"sTm{q}_{h}", [64, 64], BF16) for h in range(4)] for q in range(NS)]
                t1 = [[sb(ph, f"t1{q}_{h}", [64, 258], F32) for h in range(4)] for q in range(NS)]
                nd = [[sb(ph, f"nd{q}_{h}", [64, 258], F32) for h in range(4)] for q in range(NS)]
                dn = [[sb(ph, f"dn{q}_{h}", [64, 2], F32) for h in range(4)] for q in range(NS)]
                kw = [[sb(ph, f"kw{q}_{h}", [64, 256], BF16) for h in range(4)] for q in range(NS)]
                ho = [sb(ph, f"ho{q}", [64, 4, 256], F32) for q in range(NS)]
                for q in range(2):
                    sch.op(dve, lambda q=q: V.memset(vxw[q].t[:, :, :, 256:258], 1.0), writes=[vxw[q]])
                nck = CT // 64
                step = 0
                wi = 0
                for d in range(2):
                    HD = HF if d == 0 else HB
                    RH = RHF if d == 0 else RHB
                    lat = list(range(CT // WT, NW))
                    worder = list(range(CT // WT)) + (lat if d == 0 else lat[::-1])
                    if d == 1:
                        worder = list(range(CT // WT))[::-1] + lat[::-1]
                    for h_ in range(4):
                        for k_ in range(2):
                            sch.op(dve, lambda: V.memset(Cf[h_][k_].t[:], 0.0), writes=[Cf[h_][k_]])
                            sch.op(dve, lambda: V.memset(Cb[h_][k_].t[:], 0.0), writes=[Cb[h_][k_]])

                    def loadw(w, slot):
                        ts = slice(w * WT, (w + 1) * WT)
                        sch.dma("sp", qTw[slot].t[:, :, :], QT[:, :, ts].rearrange("k p t -> p k t"), reads=[RQT], writes=[qTw[slot]])
                        sch.dma("sp", kTw[slot].t[:, :, :], KT[:, :, ts].rearrange("k p t -> p k t"), reads=[RKT], writes=[kTw[slot]])
                        sch.dma("sp", ktw[slot].t[:, :, :], KTOK[ts, :].rearrange("(c p) f -> p c f", p=64), reads=[RKTOK], writes=[ktw[slot]])
                        for hd in range(4):
                            sch.dma("sp", vxw[slot].t[:, :, hd, 0:256], VTOK[ts, hd * 256:(hd + 1) * 256].rearrange("(c p) f -> p c f", p=64),
                                    reads=[RVTOK], writes=[vxw[slot]])

                    loadw(worder[0], wi % 2)
                    for wn, w in enumerate(worder):
                        slot = wi % 2
                        wi += 1
                        if wn + 1 < len(worder):
                            loadw(worder[wn + 1], wi % 2)
                        qT_, kT_, kt_, vx_ = qTw[slot], kTw[slot], ktw[slot], vxw[slot]
                        corder = list(range(WC)) if d == 0 else list(range(WC - 1, -1, -1))
                        for cl_ in corder:
                            c = w * WC + cl_
                            q = step % NS
                            step += 1
                            cs = slice(cl_ * 64, (cl_ + 1) * 64)
                            WEc = [CA[d].t[:, c, hd:hd + 1] for hd in range(4)]
                            EMc = [CA[d].t[:, c, 32 + hd:32 + hd + 1] for hd in range(4)]
                            for hd in range(4):
                                sch.op(act, lambda hd=hd: A.activation(out=kw[q][hd].t[:, :], in_=kt_.t[:, cl_, hd * 256:(hd + 1) * 256], func=AF.Copy, scale=WEc[hd]),
                                       reads=[kt_, CA[d]], writes=[kw[q][hd]])
                            for hd in range(4):
                                for kc in range(2):
                                    pu = sch.bank()
                                    sch.op(pe, lambda hd=hd, kc=kc: P.matmul(pu.t[:, 0:258], kw[q][hd].t[:, kc * 128:(kc + 1) * 128], vx_.t[:, cl_, hd, :], start=True, stop=True),
                                           reads=[kw[q][hd], vx_], writes=[pu])
                                    sch.op(dve, lambda hd=hd, kc=kc: V.scalar_tensor_tensor(out=Cf[hd][kc].t[:, :], in0=Cf[hd][kc].t[:, :], scalar=ABC.t[:, d, hd, c:c + 1],
                                                                                            in1=pu.t[:, 0:258], op0=ALU.mult, op1=ALU.add),
                                           reads=[Cf[hd][kc], ABC, pu], writes=[Cf[hd][kc]])
                            psc = sch.bank()
                            for hd in range(4):
                                for kc in range(2):
                                    sch.op(pe, lambda hd=hd, kc=kc: P.matmul(psc.t[0:64, hd * 64:(hd + 1) * 64], kT_.t[:, 2 * hd + kc, cs], qT_.t[:, 2 * hd + kc, cs],
                                                                             start=(kc == 0), stop=(kc == 1)),
                                           reads=[kT_, qT_], writes=[psc], inc=(kc == 1 and hd == 3))
                            pin = []
                            for hd in range(4):
                                pi = sch.bank()
                                pin.append(pi)
                                for kc in range(2):
                                    sch.op(pe, lambda hd=hd, kc=kc: P.matmul(pi.t[0:64, 0:258], qT_.t[:, 2 * hd + kc, cs], Cb[hd][kc].t[:, :], start=(kc == 0), stop=(kc == 1)),
                                           reads=[qT_, Cb[hd][kc]], writes=[pi], inc=(kc == 1))
                            for hd in range(4):
                                for kc in range(2):
                                    sch.op(pool, lambda hd=hd, kc=kc: nc.gpsimd.tensor_copy(out=Cb[hd][kc].t[:, :], in_=Cf[hd][kc].t[:, :]),
                                           reads=[Cf[hd][kc]], writes=[Cb[hd][kc]])
                            for hd in range(4):
                                sch.op(dve, lambda hd=hd: V.scalar_tensor_tensor(out=sTm[q][hd].t[:, :], in0=psc.t[0:64, hd * 64:(hd + 1) * 64], scalar=WEc[hd],
                                                                                 in1=tri_f.t[:, d, :], op0=ALU.mult, op1=ALU.mult),
                                       reads=[psc, CA[d], tri_f], writes=[sTm[q][hd]])
                            for hd in range(4):
                                sch.op(act, lambda hd=hd: A.activation(out=t1[q][hd].t[:, :], in_=pin[hd].t[0:64, 0:258], func=AF.Copy, scale=ABC.t[0:64, d, hd, c:c + 1]),
                                       reads=[pin[hd], ABC], writes=[t1[q][hd]])
                            pnn = []
                            for hd in range(4):
                                pn = sch.bank()
                                pnn.append(pn)
                                sch.op(pe, lambda hd=hd: P.matmul(pn.t[0:64, 0:258], sTm[q][hd].t[:, :], vx_.t[:, cl_, hd, :], start=True, stop=True),
                                       reads=[sTm[q][hd], vx_], writes=[pn])
                            for hd in range(4):
                                sch.op(dve, lambda hd=hd: V.tensor_tensor(out=nd[q][hd].t[:, :], in0=pnn[hd].t[0:64, 0:258], in1=t1[q][hd].t[:, :], op=ALU.add),
                                       reads=[pnn[hd], t1[q][hd]], writes=[nd[q][hd]])
                            for hd in range(4):
                                sch.op(act, lambda hd=hd: A.activation(out=dn[q][hd].t[:, 0:1], in_=nd[q][hd].t[:, 256:257], func=AF.Abs),
                                       reads=[nd[q][hd]], writes=[dn[q][hd]])
                            for hd in range(4):
                                sch.op(dve, lambda hd=hd: V.tensor_tensor(out=dn[q][hd].t[:, 0:1], in0=dn[q][hd].t[:, 0:1], in1=EMc[hd], op=ALU.max),
                                       reads=[dn[q][hd], CA[d]], writes=[dn[q][hd]])
                            for hd in range(4):
                                sch.op(dve, lambda hd=hd: V.reciprocal(out=dn[q][hd].t[:, 1:2], in_=dn[q][hd].t[:, 0:1]), reads=[dn[q][hd]], writes=[dn[q][hd]])
                            for hd in range(4):
                                sch.op(act, lambda hd=hd: A.activation(out=ho[q].t[:, hd, :], in_=nd[q][hd].t[:, 0:256], func=AF.Copy, scale=dn[q][hd].t[:, 1:2]),
                                       reads=[nd[q][hd], dn[q][hd]], writes=[ho[q]])
                            sch.dma("sp", HD[c * 64:(c + 1) * 64, :], ho[q].t[:, :, :].rearrange("p h f -> p (h f)"), reads=[ho[q]], writes=[RH])
                sch.barrier()
            sch.barrier()


    def attn_phase(l):
        sc = float(192 ** -0.5)
        with ExitStack() as ps:
            krT = sb(ps, "krT", [128, T], BF16)
            knT = sb(ps, "knT", [128, T], BF16)
            vh = sb(ps, "vh", [128, NTB, 128], BF16)
            qn = [sb(ps, f"qn{q}", [128, 512], BF16) for q in range(2)]
            qr = [sb(ps, f"qr{q}", [128, 512], BF16) for q in range(2)]
            pt = [sb(ps, f"pt{q}", [128, 512], BF16) for q in range(5)]
            ps2 = [sb(ps, f"ps2{q}", [128, 512], BF16) for q in range(2)]
            rd = sb(ps, "rd", [128, 512], F32)
            oo = [sb(ps, f"oo{q}", [128, 512], BF16) for q in range(2)]
            sch.op(dve, lambda: V.memset(krT.t[64:128, :], 0.0), writes=[krT])
            for q in range(2):
                sch.op(dve, lambda q=q: V.memset(qr[q].t[64:128, :], 0.0), writes=[qr[q]])
            sch.dma("sp", krT.t[0:64, :], KR[:, :], reads=[RKR], writes=[krT])
            it = 0
            pti = 0
            for hd in range(8):
                sch.dma("sp", knT.t[:, :], KN[hd], reads=[RKN], writes=[knT])
                sch.dma("sp", vh.t[:, :, :], VA[:, hd * 128:(hd + 1) * 128].rearrange("(tb p) f -> p tb f", p=128), reads=[RVA], writes=[vh])
                for ti, (t0, n, j) in enumerate(tiles):
                    q_n = qn[it % 2]
                    q_r = qr[it % 2]
                    o_ = oo[it % 2]
                    it += 1
                    sch.dma("sp", q_n.t[:, :n], QN[hd, :, t0:t0 + n], reads=[RQN], writes=[q_n])
                    sch.dma("sp", q_r.t[0:64, :n], QR[hd, :, t0:t0 + n], reads=[RQR], writes=[q_r])
                    kbs = list(range(CT // 128)) if j == 1 else list(range(NTB))
                    po = sch.bank(hold=True)
                    pd = sch.bank(hold=True)

                    def scores(kb):
                        b = sch.bank()
                        sch.op(pe, lambda: P.matmul(b.t[:, :n], knT.t[:, kb * 128:(kb + 1) * 128], q_n.t[:, :n], start=True, stop=False),
                               reads=[knT, q_n], writes=[b], inc=False)
                        sch.op(pe, lambda: P.matmul(b.t[:, :n], krT.t[:, kb * 128:(kb + 1) * 128], q_r.t[:, :n], start=False, stop=True),
                               reads=[krT, q_r], writes=[b])
                        return b

                    LA = 2
                    pend = [scores(kb) for kb in kbs[:LA]]
                    nk = len(kbs)
                    prev_p = None
                    first_den = True
                    for ii, kb in enumerate(kbs):
                        b = pend.pop(0)
                        p_ = pt[pti % len(pt)]
                        pti += 1
                        sch.op(act, lambda: A.activation(out=p_.t[:, :n], in_=b.t[:, :n], func=AF.Exp, scale=sc), reads=[b], writes=[p_])
                        if ii + LA < nk:
                            pend.append(scores(kbs[ii + LA]))
                        last = ii == nk - 1
                        sch.op(pe, lambda: P.matmul(po.t[:, :n], vh.t[:, kb, :], p_.t[:, :n], start=(ii == 0), stop=last),
                               reads=[vh, p_], writes=[po], inc=last)
                        if ii % 2 == 1:
                            s2 = ps2[(ii // 2) % 2]
                            sch.op(dve, lambda: V.tensor_tensor(out=s2.t[:, :n], in0=prev_p.t[:, :n], in1=p_.t[:, :n], op=ALU.add),
                                   reads=[prev_p, p_], writes=[s2])
                            sch.op(pe, lambda: P.matmul(pd.t[:, :n], ones_b.t[:, :], s2.t[:, :n], start=first_den, stop=last),
                                   reads=[ones_b, s2], writes=[pd], inc=True)
                            first_den = False
                        elif last:
                            sch.op(pe, lambda: P.matmul(pd.t[:, :n], ones_b.t[:, :], p_.t[:, :n], start=first_den, stop=True),
                                   reads=[ones_b, p_], writes=[pd], inc=True)
                        prev_p = p_
                    sch.op(dve, lambda: V.reciprocal(out=rd.t[:, :n], in_=pd.t[:, :n]), reads=[pd], writes=[rd])
                    sch.op(dve, lambda: V.tensor_tensor(out=o_.t[:, :n], in0=po.t[:, :n], in1=rd.t[:, :n], op=ALU.mult), reads=[po, rd], writes=[o_])
                    sch.release(po)
                    sch.release(pd)
                    sch.dma("sp", HAT[hd, :, t0:t0 + n], o_.t[:, :n], reads=[o_], writes=[RHAT])
            sch.barrier()

    def merge_phase(l):
        with ExitStack() as ps:
            wbm = sb(ps, "wbm", [128, KC, D], BF16)
            wba = sb(ps, "wba", [128, KC, D], BF16)
            wout = sb(ps, "wout", [128, KC, D], BF16)
            sch.dma("pool", wbm.t[:], wbm_in[l], writes=[wbm], max_dma_last_dim=4096)
            sch.dma("pool", wba.t[:], wba_in[l], writes=[wba], max_dma_last_dim=4096)
            sch.dma("pool", wout.t[:], wout_in[l], writes=[wout], max_dma_last_dim=4096)
            hf = sb(ps, "hf", [128, 4, D], F32)
            hb = sb(ps, "hb", [128, 4, D], F32)
            hn = sb(ps, "hn", [128, 4, D], BF16)
            junk = sb(ps, "junk", [128, 256], BF16)
            ssq = sb(ps, "ssq", [128, 16], F32)
            so = sb(ps, "so", [128, KC, 512], BF16)
            gm = sb(ps, "gm", [128, KC, 512], BF16)
            ga = sb(ps, "ga", [128, KC, 512], BF16)
            hat = sb(ps, "hat", [128, KC, 512], BF16)
            xt = sb(ps, "mxt", [128, KC, 512], F32)
            hmT = sb(ps, "hmT", [128, KC, 512], BF16)
            tm = sb(ps, "tm", [128, KC, 512], F32)
            t2 = sb(ps, "t2", [128, 512], F32)
            tb_ = sb(ps, "tb", [128, KC, 512], BF16)
            def ld_h(ti):
                t0, n, j = tiles[ti]
                nb = n // 128
                sch.dma("sp", hf.t[:, 0:nb, :], HF[t0:t0 + n, :].rearrange("(b p) f -> p b f", p=128), reads=[RHF], writes=[hf])
                sch.dma("sp", hb.t[:, 0:nb, :], HB[t0:t0 + n, :].rearrange("(b p) f -> p b f", p=128), reads=[RHB], writes=[hb])

            def ld_so(ti):
                t0, n, j = tiles[ti]
                sch.dma("sp", so.t[:, :, :n], SOT[:, :, t0:t0 + n].rearrange("k p t -> p k t"), reads=[RSOT], writes=[so])

            def ld_g(ti):
                t0, n, j = tiles[ti]
                sch.dma("sp", gm.t[:, :, :n], GMT[:, :, t0:t0 + n].rearrange("k p t -> p k t"), reads=[RGMT], writes=[gm])
                sch.dma("sp", ga.t[:, :, :n], GAT[:, :, t0:t0 + n].rearrange("k p t -> p k t"), reads=[RGAT], writes=[ga])
                sch.dma("sp", hat.t[:, :, :n], HAT[:, :, t0:t0 + n].rearrange("k p t -> p k t"), reads=[RHAT], writes=[hat])

            def ld_x(ti):
                t0, n, j = tiles[ti]
                sch.dma("sp", xt.t[:, :, :n], XT[:, :, t0:t0 + n].rearrange("k p t -> p k t"), reads=[xres[ti]], writes=[xt])

            ld_h(0)
            ld_so(0)
            ld_g(0)
            ld_x(0)
            for ti, (t0, n, j) in enumerate(tiles):
                nb = n // 128
                more = ti + 1 < len(tiles)
                sch.op(dve, lambda: V.tensor_tensor(out=hf.t[:, 0:nb, :], in0=hf.t[:, 0:nb, :], in1=hb.t[:, 0:nb, :], op=ALU.add),
                       reads=[hf, hb], writes=[hf])
                for b in range(nb):
                    for hd in range(4):
                        sch.op(act, lambda b=b, hd=hd: A.activation(out=junk.t[:, :], in_=hf.t[:, b, hd * 256:(hd + 1) * 256], func=AF.Square,
                                                                    accum_out=ssq.t[:, b * 4 + hd:b * 4 + hd + 1]),
                               reads=[hf], writes=[junk, ssq])
                sch.op(act, lambda: A.activation(out=ssq.t[:, 0:4 * nb], in_=ssq.t[:, 0:4 * nb], func=AF.Sqrt, scale=1.0 / 256, bias=eps_s.t[:, 0:1]),
                       reads=[ssq, eps_s], writes=[ssq])
                sch.op(dve, lambda: V.reciprocal(out=ssq.t[:, 0:4 * nb], in_=ssq.t[:, 0:4 * nb]), reads=[ssq], writes=[ssq])
                for b in range(nb):
                    for hd in range(4):
                        sch.op(dve, lambda b=b, hd=hd: V.tensor_scalar(out=hn.t[:, b, hd * 256:(hd + 1) * 256], in0=hf.t[:, b, hd * 256:(hd + 1) * 256],
                                                                       scalar1=ssq.t[:, b * 4 + hd:b * 4 + hd + 1], scalar2=None, op0=ALU.mult),
                               reads=[hf, ssq], writes=[hn])
                if more:
                    ld_h(ti + 1)
                for m in range(KC):
                    bk = sch.bank()
                    bv = bk.t[:, :].bitcast(BF16)
                    for b in range(nb):
                        sch.op(pe, lambda b=b: P.transpose(bv[:, b * 128:(b + 1) * 128], hn.t[:, b, m * 128:(m + 1) * 128], ident_b.t[:]),
                               reads=[hn, ident_b], writes=[bk], inc=(b == nb - 1))
                    sch.op(dve, lambda: V.scalar_tensor_tensor(out=hmT.t[:, m, :n], in0=bv[:, 0:n], scalar=gmh_s.t[:, l, m:m + 1], in1=so.t[:, m, :n],
                                                               op0=ALU.mult, op1=ALU.mult),
                           reads=[bk, gmh_s, so], writes=[hmT])
                if more:
                    ld_so(ti + 1)
                for m in range(KC):
                    bk = sch.bank()
                    for k in range(KC):
                        sch.op(pe, lambda k=k: P.matmul(bk.t[:, :n], wbm.t[:, k, m * 128:(m + 1) * 128], hmT.t[:, k, :n], start=(k == 0), stop=(k == KC - 1)),
                               reads=[wbm, hmT], writes=[bk], inc=(k == KC - 1))
                    sch.op(dve, lambda: V.tensor_tensor(out=tm.t[:, m, :n], in0=bk.t[:, :n], in1=gm.t[:, m, :n], op=ALU.mult),
                           reads=[bk, gm], writes=[tm])
                    bk2 = sch.bank()
                    for k in range(KC):
                        sch.op(pe, lambda k=k: P.matmul(bk2.t[:, :n], wba.t[:, k, m * 128:(m + 1) * 128], hat.t[:, k, :n], start=(k == 0), stop=(k == KC - 1)),
                               reads=[wba, hat], writes=[bk2], inc=(k == KC - 1))
                    sch.op(dve, lambda: V.tensor_tensor(out=t2.t[:, :n], in0=bk2.t[:, :n], in1=ga.t[:, m, :n], op=ALU.mult),
                           reads=[bk2, ga], writes=[t2])
                    sch.op(dve, lambda: V.tensor_tensor(out=tb_.t[:, m, :n], in0=tm.t[:, m, :n], in1=t2.t[:, :n], op=ALU.add),
                           reads=[tm, t2], writes=[tb_])
                if more:
                    ld_g(ti + 1)
                for m in range(KC):
                    bk = sch.bank()
                    for k in range(KC):
                        sch.op(pe, lambda k=k: P.matmul(bk.t[:, :n], wout.t[:, k, m * 128:(m + 1) * 128], tb_.t[:, k, :n], start=(k == 0), stop=(k == KC - 1)),
                               reads=[wout, tb_], writes=[bk], inc=(k == KC - 1))
                    sch.op(dve, lambda: V.scalar_tensor_tensor(out=xt.t[:, m, :n], in0=bk.t[:, :n], scalar=GT.t[:, l, 1, m, j:j + 1], in1=xt.t[:, m, :n],
                                                               op0=ALU.mult, op1=ALU.add),
                           reads=[bk, GT, xt], writes=[xt])
                sch.dma("sp", XT[:, :, t0:t0 + n].rearrange("k p t -> p k t"), xt.t[:, :, :n], reads=[xt], writes=[xres[ti]])
                if more:
                    ld_x(ti + 1)
            sch.barrier()

    def final_phase():
        with ExitStack() as ps:
            xts = [sb(ps, f"fxt{q}", [128, KC, 512], F32) for q in range(2)]
            sq = sb(ps, "fsq", [128, KC, 512], BF16)
            rt = sb(ps, "frt", [128, 512], F32)
            ot = [sb(ps, f"fot{q}", [128, KC, 512], F32) for q in range(2)]
            for ti, (t0, n, j) in enumerate(tiles):
                if j == 1:
                    continue
                xt = xts[ti % 2]
                o = ot[ti % 2]
                sch.dma("sp", xt.t[:, :, :n], XT[:, :, t0:t0 + n].rearrange("k p t -> p k t"), reads=[xres[ti]], writes=[xt])
                rms_rstd((sq, rt), lambda c: xt.t[:, c, :n], KC, n, 1.0 / D, xt)
                for k in range(KC):
                    sch.op(dve, lambda k=k: V.scalar_tensor_tensor(out=o.t[:, k, :n], in0=xt.t[:, k, :n], scalar=gfin_s.t[:, k:k + 1], in1=rt.t[:, :n],
                                                                   op0=ALU.mult, op1=ALU.mult),
                           reads=[xt, gfin_s, rt], writes=[o])
                sch.dma("sp", outT[:, :, t0 - CT:t0 - CT + n].rearrange("k p t -> p k t"), o.t[:, :, :n], reads=[o], writes=[ROUT])
            sch.barrier()

    for l in range(L):
        ffn_phase(l, 0, first=(l == 0))
        inproj_phase(l)
        mlstm_phase(l)
        attn_phase(l)
        merge_phase(l)
        ffn_phase(l, 2, first=False)
    final_phase()
    es.close()
    build.ninst = sch.ninst
    return nc


def _prep_shared(inp, L, S):
    f = np.float32
    T = CT + S
    out = {}
    w_ada = np.asarray(inp["w_ada"], f)[:L]
    out["wada"] = np.ascontiguousarray(w_ada.reshape(L, KC, 128, 72, 128).transpose(0, 3, 2, 1, 4))
    out["bada"] = np.ascontiguousarray(np.asarray(inp["b_ada"], f)[:L].reshape(L, 72, 128).transpose(2, 0, 1))
    gn = np.stack([np.asarray(inp[k], f)[:L] for k in ("g_n1", "g_n2", "g_n3")], axis=1)
    out["gn"] = np.ascontiguousarray(gn.reshape(L, 3, KC, 128).transpose(3, 0, 1, 2))
    out["gfin"] = np.ascontiguousarray(np.asarray(inp["g_final"], f).reshape(KC, 128).T)
    for i, (ku, kd) in enumerate((("w_ff1_up", "w_ff1_dn"), ("w_ff2_up", "w_ff2_dn"))):
        wu = np.asarray(inp[ku], f)[:L]
        wu = wu.reshape(L, KC, 128, 2, NF, 128)
        out[f"wup{i}"] = np.ascontiguousarray(wu.transpose(0, 4, 2, 1, 3, 5))
        wd = np.asarray(inp[kd], f)[:L].reshape(L, NF, 128, D)
        out[f"wdn{i}"] = np.ascontiguousarray(wd.transpose(0, 2, 1, 3))
    w_in = np.asarray(inp["w_in"], f)[:L]
    o = 0
    offs = {}
    for name, n in (("m_q", 1024), ("m_k", 1024), ("m_v", 1024), ("m_o", 1024), ("m_gate", 16), ("a_cq", 384), ("a_ckv", 256), ("a_kr", 64), ("br_gate", 2048)):
        offs[name] = (o, n)
        o += n

    def grp(name):
        a, n = offs[name]
        return w_in[:, :, a:a + n]

    deint = np.concatenate([np.arange(0, 64, 2), np.arange(1, 64, 2)])
    swp = np.concatenate([deint[32:], deint[:32]])
    gates = grp("m_gate")
    grep = np.zeros((L, D, 4, 128), f)
    for d in range(2):
        for ki in range(2):
            for hd in range(4):
                for qd in range(2):
                    grep[:, :, d * 2 + ki, 32 * qd + hd] = gates[:, :, d * 8 + ki * 4 + hd]
    kr = grp("a_kr")
    kr2 = np.concatenate([kr[:, :, deint], kr[:, :, swp]], axis=2)
    cols = np.concatenate([grp("m_q"), grp("m_k"), grp("m_o"), grp("br_gate"), grp("a_cq"), grp("a_ckv"), grep.reshape(L, D, 512), kr2], axis=2)
    assert cols.shape[2] == NWF * 128
    out["wf"] = np.ascontiguousarray(cols.reshape(L, KC, 128, NWF, 128).transpose(0, 3, 2, 1, 4))
    out["wv"] = np.ascontiguousarray(grp("m_v").reshape(L, KC, 128, D).transpose(0, 2, 1, 3))
    wc = np.asarray(inp["w_conv"], f)[:L]
    out["convw"] = np.ascontiguousarray(wc.reshape(L, 3, 16, 128).transpose(3, 0, 2, 1))
    bgl = np.asarray(inp["b_gate"], f)[:L]
    bg = np.zeros((128, L, 4), f)
    for d in range(2):
        for ki in range(2):
            for hd in range(4):
                for qd in range(2):
                    bg[32 * qd + hd, :, d * 2 + ki] = bgl[:, d * 8 + ki * 4 + hd]
    out["bg"] = bg
    out["gmh"] = np.ascontiguousarray(np.asarray(inp["g_mh"], f)[:L].reshape(L, KC, 128).transpose(2, 0, 1))
    out["gqa"] = np.ascontiguousarray(np.asarray(inp["g_qa"], f)[:L].reshape(L, 3, 128).transpose(2, 0, 1))
    out["gkva"] = np.ascontiguousarray(np.asarray(inp["g_kva"], f)[:L].reshape(L, 2, 128).transpose(2, 0, 1))
    wuq = np.asarray(inp["w_uq"], f)[:L].reshape(L, 384, 8, 192)
    chunks = []
    for hd in range(8):
        chunks.append(wuq[:, :, hd, 0:128])
        r = wuq[:, :, hd, 128:192]
        chunks.append(np.concatenate([r[:, :, deint], r[:, :, swp]], axis=2))
    wuq2 = np.stack(chunks, axis=2)
    out["wuq"] = np.ascontiguousarray(wuq2.reshape(L, 3, 128, 16, 128).transpose(0, 2, 1, 3, 4))
    wukv = np.asarray(inp["w_ukv"], f)[:L].reshape(L, 256, 8, 256)
    out["wukvk"] = np.ascontiguousarray(wukv[:, :, :, 0:128].reshape(L, 2, 128, 8, 128).transpose(0, 2, 1, 3, 4))
    out["wukvv"] = np.ascontiguousarray(wukv[:, :, :, 128:256].reshape(L, 2, 128, 1024).transpose(0, 2, 1, 3))
    for k, kk in (("w_bm", "wbm"), ("w_ba", "wba"), ("w_out", "wout")):
        out[kk] = np.ascontiguousarray(np.asarray(inp[k], f)[:L].reshape(L, KC, 128, D).transpose(0, 2, 1, 3))
    inv = (10000.0 ** (-np.arange(0, 32, 2, dtype=np.float32) / 32)).astype(f)
    t = np.arange(S)
    row = (t // 64).astype(f)
    col = (t % 64).astype(f)
    ang = np.concatenate([row[:, None] * inv, col[:, None] * inv], axis=-1)
    cos = np.cos(ang).astype(f).T
    sin = np.sin(ang).astype(f).T
    rope = np.zeros((64, 2, T), f)
    rope[:, 0, :CT] = 1.0
    rope[0:32, 0, CT:] = cos
    rope[32:64, 0, CT:] = cos
    rope[0:32, 1, CT:] = -sin
    rope[32:64, 1, CT:] = sin
    out["rope"] = rope
    out["ident"] = np.eye(128, dtype=f)
    sel = np.zeros((128, 4, 128), f)
    for hd in range(4):
        sel[hd, hd, :] = 1.0
    out["sel"] = sel
    tri = np.zeros((64, 2, 64), f)
    s_ = np.arange(64)[:, None]
    j_ = np.arange(64)[None, :]
    tri[:, 0, :] = (s_ <= j_)
    tri[:, 1, :] = (s_ >= j_)
    out["tri"] = tri
    return out


def _run(inp, L, S, dbg=False):
    f = np.float32
    shared = _prep_shared(inp, L, S)
    x = np.asarray(inp["x"], f)
    ctx = np.asarray(inp["ctx"], f)
    c = np.asarray(inp["c"], f)
    cc = np.asarray(inp["c_ctx"], f)
    B = x.shape[0]
    T = CT + S
    in_maps = []
    for core in range(8):
        b = core % B
        cat = np.concatenate([ctx[b], x[b]], axis=0)
        m = dict(shared)
        m["xT"] = np.ascontiguousarray(cat.T.reshape(KC, 128, T))
        scT = np.stack([c[b].reshape(KC, 128).T, cc.reshape(KC, 128).T], axis=-1)
        m["scT"] = np.ascontiguousarray(scT)
        in_maps.append(m)
    nc = build(L, S, dbg)
    res = run_bass_kernel_spmd(nc, in_maps, core_ids=list(range(8)))
    outs = []
    for b in range(B):
        o = res.results[b]["outT"]
        outs.append(np.ascontiguousarray(o.reshape(D, S).T))
    out = np.stack(outs, axis=0).astype(f)
    if dbg:
        return out, res
    return out


def kernel(**inputs):
    return _run(inputs, 4, 4096)
```

```python
import numpy as np
from contextlib import ExitStack
import concourse.bass as bass
import concourse.mybir as mybir
from concourse.bass_utils import run_bass_kernel_spmd

F32 = mybir.dt.float32
BF16 = mybir.dt.bfloat16
AF = mybir.ActivationFunctionType
ALU = mybir.AluOpType

D = 1024
KC = 8
DFF = 2816
NF = 22
CT = 256
EPS = 1e-6
NWF = 50


class SemObj:
    _n = 0

    def __init__(self, h):
        self.h = h
        self.count = 0
        SemObj._n += 1
        self.id = SemObj._n


class Res:
    def __init__(self, name=""):
        self.lw = {}
        self.rd = {}
        self.name = name


class Tl:
    def __init__(self, t, name=""):
        self.t = t
        self.r = Res(name)


class EngW:
    def __init__(self, name, eng, so):
        self.name = name
        self.eng = eng
        self.so = so
        self.waited = {}


class Sched:
    def __init__(self, nc, es):
        self.nc = nc

        def mk(name):
            return SemObj(es.enter_context(nc.semaphore(name)))

        self.pe = EngW("pe", nc.tensor, mk("s_pe"))
        self.act = EngW("act", nc.scalar, mk("s_act"))
        self.dve = EngW("dve", nc.vector, mk("s_dve"))
        self.pool = EngW("pool", nc.gpsimd, mk("s_pool"))
        self.sp = EngW("sp", nc.sync, mk("s_sp"))
        self.engs = [self.pe, self.act, self.dve, self.pool, self.sp]
        self.dq = {"sp": [mk(f"dsp{i}") for i in range(12)], "pool": [mk(f"dpl{i}") for i in range(12)]}
        self.dqn = {"sp": 0, "pool": 0}
        self.banks = []
        self.bank_i = 0
        self.held = set()
        self.ninst = 0

    def _wait(self, E, tick):
        so, v = tick
        if E.waited.get(so.id, 0) >= v:
            return
        E.eng.wait_ge(so.h, v)
        E.waited[so.id] = v
        self.ninst += 1

    def _deps(self, reads, writes, own):
        deps = {}

        def add(t, raw):
            so, v = t
            if so is own and not raw:
                return
            if deps.get(so.id, (None, 0))[1] < v:
                deps[so.id] = (so, v)

        for r in reads:
            for t in r.lw.values():
                add(t, True)
        for w in writes:
            for t in w.lw.values():
                add(t, False)
            for t in w.rd.values():
                add(t, False)
        return deps

    def _commit(self, tick, reads, writes):
        so, v = tick
        for w in writes:
            o = w.lw.get(so.id)
            if o is None or o[1] < v:
                w.lw[so.id] = tick
        for r in reads:
            o = r.rd.get(so.id)
            if o is None or o[1] < v:
                r.rd[so.id] = tick

    def op(self, E, fn, reads=(), writes=(), inc=True):
        reads = [x.r if isinstance(x, Tl) else x for x in reads]
        writes = [x.r if isinstance(x, Tl) else x for x in writes]
        for t in self._deps(reads, writes, E.so).values():
            self._wait(E, t)
        ins = fn()
        self.ninst += 1
        if inc:
            ins.then_inc(E.so.h, 1)
            E.so.count += 1
            tick = (E.so, E.so.count)
        else:
            tick = (E.so, E.so.count + 1)
        self._commit(tick, reads, writes)
        return ins

    def dma(self, q, out, in_, reads=(), writes=(), **kw):
        reads = [x.r if isinstance(x, Tl) else x for x in reads]
        writes = [x.r if isinstance(x, Tl) else x for x in writes]
        E = self.sp if q == "sp" else self.pool
        pool = self.dq[q]
        so = pool[self.dqn[q] % len(pool)]
        self.dqn[q] += 1
        if so.count > 0:
            self._wait(E, (so, so.count))
        for t in self._deps(reads, writes, None).values():
            self._wait(E, t)
        E.eng.dma_start(out=out, in_=in_, **kw).then_inc(so.h, 16)
        self.ninst += 1
        so.count += 16
        self._commit((so, so.count), reads, writes)

    def barrier(self):
        ticks = []
        for E in self.engs:
            if E.so.count > 0:
                ticks.append((E.so, E.so.count))
        for q in self.dq.values():
            for so in q:
                if so.count > 0:
                    ticks.append((so, so.count))
        for E in self.engs:
            for t in ticks:
                if t[0] is not E.so:
                    self._wait(E, t)

    def bank(self, hold=False):
        for _ in range(16):
            i = self.bank_i % 8
            self.bank_i += 1
            if i not in self.held:
                if hold:
                    self.held.add(i)
                return self.banks[i]
        raise RuntimeError("no psum bank")

    def release(self, b):
        self.held.discard(self.banks.index(b))


def build(L, S, dbg=False):
    T = CT + S
    NCH = T // 64
    NTB = T // 128
    nc = bass.Bass("TRN2", target_bir_lowering=False)
    es = ExitStack()

    def din(name, shape, dt=F32):
        return nc.dram_tensor(name, list(shape), dt, kind="ExternalInput").ap()

    scratch_kind = "ExternalOutput" if dbg else "Internal"

    def dscr(name, shape, dt):
        return nc.dram_tensor(name, list(shape), dt, kind=scratch_kind).ap()

    xT_in = din("xT", [KC, 128, T])
    scT_in = din("scT", [128, KC, 2])
    wada_in = din("wada", [L, 72, 128, KC, 128])
    bada_in = din("bada", [128, L, 72])
    gn_in = din("gn", [128, L, 3, KC])
    gfin_in = din("gfin", [128, KC])
    wup_in = [din(f"wup{i}", [L, NF, 128, KC, 2, 128]) for i in range(2)]
    wdn_in = [din(f"wdn{i}", [L, 128, NF, D]) for i in range(2)]
    wf_in = din("wf", [L, NWF, 128, KC, 128])
    wv_in = din("wv", [L, 128, KC, D])
    convw_in = din("convw", [128, L, 16, 3])
    bg_in = din("bg", [128, L, 4])
    gmh_in = din("gmh", [128, L, KC])
    gqa_in = din("gqa", [128, L, 3])
    gkva_in = din("gkva", [128, L, 2])
    wuq_in = din("wuq", [L, 128, 3, 16, 128])
    wukvk_in = din("wukvk", [L, 128, 2, 8, 128])
    wukvv_in = din("wukvv", [L, 128, 2, D])
    wbm_in = din("wbm", [L, 128, KC, D])
    wba_in = din("wba", [L, 128, KC, D])
    wout_in = din("wout", [L, 128, KC, D])
    rope_in = din("rope", [64, 2, T])
    ident_in = din("ident", [128, 128])
    sel_in = din("sel", [128, 4, 128])
    tri_in = din("tri", [64, 2, 64])
    outT = nc.dram_tensor("outT", [KC, 128, S], F32, kind="ExternalOutput").ap()

    XT = dscr("XT", [KC, 128, T], F32)
    QT = dscr("QT", [KC, 128, T], BF16)
    KT = dscr("KT", [KC, 128, T], BF16)
    SOT = dscr("SOT", [KC, 128, T], BF16)
    GMT = dscr("GMT", [KC, 128, T], BF16)
    GAT = dscr("GAT", [KC, 128, T], BF16)
    KTOK = dscr("KTOK", [T, D], BF16)
    VTOK = dscr("VTOK", [T, D], BF16)
    GROW = dscr("GROW", [4, 128, T], F32)
    HF = dscr("HF", [T, D], F32)
    HB = dscr("HB", [T, D], F32)
    QN = dscr("QN", [8, 128, T], BF16)
    QR = dscr("QR", [8, 64, T], BF16)
    KN = dscr("KN", [8, 128, T], BF16)
    KR = dscr("KR", [64, T], BF16)
    VA = dscr("VA", [T, D], BF16)
    HAT = dscr("HAT", [8, 128, T], BF16)

    sch = Sched(nc, es)
    pe, act, dve, pool, sp = sch.pe, sch.act, sch.dve, sch.pool, sch.sp

    for i in range(8):
        sch.banks.append(Tl(es.enter_context(nc.psum_tensor(f"bank{i}", [128, 512], F32)), f"bank{i}"))

    sbn = [0]

    def sb(stack, name, shape, dt):
        sbn[0] += 1
        nm = f"s{sbn[0]}_{name}"
        return Tl(stack.enter_context(nc.sbuf_tensor(nm, list(shape), dt)), nm)

    tiles = [(0, CT, 1)] + [(CT + 512 * i, 512, 0) for i in range(S // 512)]
    xres = [Res(f"XT{i}") for i in range(len(tiles))]
    r_xin = Res("xin")
    RQT, RKT, RSOT, RGMT, RGAT, RKTOK, RVTOK, RGROW, RHF, RHB, RQN, RQR, RKN, RKR, RVA, RHAT, ROUT = [Res() for _ in range(17)]

    ident_f = sb(es, "ident_f", [128, 128], F32)
    ident_b = sb(es, "ident_b", [128, 128], BF16)
    ones_b = sb(es, "ones_b", [128, 128], BF16)
    sel_f = sb(es, "sel_f", [128, 4, 128], F32)
    tri_f = sb(es, "tri_f", [64, 2, 64], F32)
    MOD = sb(es, "MOD", [128, L, 72, 2], F32)
    GS = sb(es, "GS", [128, L, 3, KC, 2], F32)
    GT = sb(es, "GT", [128, L, 3, KC, 2], F32)
    gn_s = sb(es, "gn_s", [128, L, 3, KC], F32)
    gfin_s = sb(es, "gfin_s", [128, KC], F32)
    convw_s = sb(es, "convw_s", [128, L, 16, 3], F32)
    bg_s = sb(es, "bg_s", [128, L, 4], F32)
    gmh_s = sb(es, "gmh_s", [128, L, KC], F32)
    gqa_s = sb(es, "gqa_s", [128, L, 3], F32)
    gkva_s = sb(es, "gkva_s", [128, L, 2], F32)
    bada_s = sb(es, "bada_s", [128, L, 72], F32)
    ones_f = sb(es, "ones_f", [128, 1], F32)

    V = nc.vector
    A = nc.scalar
    P = nc.tensor

    def load_const(tl, src, q="sp"):
        sch.dma(q, tl.t[:], src, writes=[tl])

    load_const(ident_f, ident_in[:, :])
    load_const(ident_b, ident_in[:, :], "pool")
    load_const(sel_f, sel_in[:, :, :])
    load_const(tri_f, tri_in[:, :, :])
    load_const(gn_s, gn_in[:, :, :, :])
    load_const(gfin_s, gfin_in[:, :])
    load_const(convw_s, convw_in[:, :, :, :])
    load_const(bg_s, bg_in[:, :, :])
    load_const(gmh_s, gmh_in[:, :, :])
    load_const(gqa_s, gqa_in[:, :, :])
    load_const(gkva_s, gkva_in[:, :, :])
    load_const(bada_s, bada_in[:, :, :])
    sch.op(dve, lambda: V.memset(ones_b.t[:], 1.0), writes=[ones_b])
    sch.op(dve, lambda: V.memset(ones_f.t[:], 1.0), writes=[ones_f])

    with ExitStack() as ps:
        sc_f = sb(ps, "sc_f", [128, KC, 2], F32)
        sc_b = sb(ps, "sc_b", [128, KC, 2], BF16)
        wa = [sb(ps, f"wa{i}", [128, 8, KC, 128], BF16) for i in range(2)]
        load_const(sc_f, scT_in[:, :, :])
        sch.op(act, lambda: A.activation(out=sc_b.t[:], in_=sc_f.t[:], func=AF.Silu), reads=[sc_f], writes=[sc_b])
        it = 0
        for l in range(L):
            for mg in range(9):
                w = wa[it % 2]
                it += 1
                sch.dma("pool", w.t[:], wada_in[l, mg * 8:(mg + 1) * 8].rearrange("m p k c -> p m k c"),
                        writes=[w], max_dma_last_dim=4096)
                bk = sch.bank()
                for m in range(8):
                    for k in range(KC):
                        sch.op(pe, lambda m=m, k=k: P.matmul(bk.t[:, 2 * m:2 * m + 2], w.t[:, m, k, :], sc_b.t[:, k, :],
                                                             start=(k == 0), stop=(k == KC - 1)),
                               reads=[w, sc_b], writes=[bk], inc=(k == KC - 1))
                sch.op(dve, lambda: V.tensor_tensor(
                    out=MOD.t[:, l, mg * 8:(mg + 1) * 8, :],
                    in0=bk.t[:, 0:16].rearrange("p (m j) -> p m j", j=2),
                    in1=bada_s.t[:, l, mg * 8:(mg + 1) * 8].unsqueeze(2).to_broadcast([128, 8, 2]),
                    op=ALU.add), reads=[bk, bada_s], writes=[MOD])
        for l in range(L):
            for i in range(3):
                coef = 1.0 if i == 1 else 0.5
                sc = MOD.t[:, l, (3 * i + 1) * 8:(3 * i + 2) * 8, :]
                gt = MOD.t[:, l, (3 * i + 2) * 8:(3 * i + 3) * 8, :]
                sch.op(dve, lambda: V.scalar_tensor_tensor(
                    out=GS.t[:, l, i, :, :], in0=sc, scalar=1.0,
                    in1=gn_s.t[:, l, i, :].unsqueeze(2).to_broadcast([128, KC, 2]),
                    op0=ALU.add, op1=ALU.mult), reads=[MOD, gn_s], writes=[GS])
                sch.op(dve, lambda: V.tensor_scalar(out=GT.t[:, l, i, :, :], in0=gt, scalar1=coef, scalar2=None,
                                                    op0=ALU.mult), reads=[MOD], writes=[GT])
        sch.barrier()

    def SH(l, i, k, j):
        return MOD.t[:, l, 3 * i * 8 + k, j:j + 1]

    def rms_rstd(stack_tiles, src_ap_fn, nchunks, n, inv_dim, src_res):
        sq, rt = stack_tiles
        for c in range(nchunks):
            sch.op(act, lambda c=c: A.activation(out=sq.t[:, c, :n], in_=src_ap_fn(c), func=AF.Square),
                   reads=[src_res], writes=[sq])
        bk = sch.bank()
        for c in range(nchunks):
            sch.op(pe, lambda c=c: P.matmul(bk.t[:, :n], ones_b.t[:], sq.t[:, c, :n], start=(c == 0), stop=(c == nchunks - 1)),
                   reads=[ones_b, sq], writes=[bk], inc=(c == nchunks - 1))
        sch.op(act, lambda: A.activation(out=rt.t[:, :n], in_=bk.t[:, :n], func=AF.Sqrt, scale=inv_dim, bias=eps_s.t[:, 0:1]),
               reads=[bk, eps_s], writes=[rt])
        sch.op(dve, lambda: V.reciprocal(out=rt.t[:, :n], in_=rt.t[:, :n]), reads=[rt], writes=[rt])
        return rt

    eps_s = sb(es, "eps_s", [128, 1], F32)
    sch.op(dve, lambda: V.memset(eps_s.t[:], EPS), writes=[eps_s])

    def norm_mod(xt, n, l, i, j, sq, rt, tmp, hout, hcol0=0):
        rms_rstd((sq, rt), lambda c: xt.t[:, c, :n], KC, n, 1.0 / D, xt)
        sch.op(dve, lambda: V.tensor_tensor(out=tmp.t[:, :, :n], in0=xt.t[:, :, :n],
                                            in1=rt.t[:, :n].unsqueeze(1).to_broadcast([128, KC, n]), op=ALU.mult),
               reads=[xt, rt], writes=[tmp])
        for k in range(KC):
            sch.op(act, lambda k=k: A.activation(out=hout.t[:, k, hcol0:hcol0 + n], in_=tmp.t[:, k, :n], func=AF.Identity,
                                                 scale=GS.t[:, l, i, k, j:j + 1], bias=SH(l, i, k, j)),
                   reads=[tmp, GS, MOD], writes=[hout])

    def ffn_phase(l, which, first):
        i = 0 if which == 0 else 2
        with ExitStack() as ps:
            wdn = sb(ps, "wdn", [128, NF, D], BF16)
            wup = [sb(ps, f"wupb{q}", [128, KC, 2, 128], BF16) for q in range(3)]
            xts = [sb(ps, f"xt{q}", [128, KC, 512], F32) for q in range(2)]
            hs = [sb(ps, f"h{q}", [128, KC, 512], BF16) for q in range(2)]
            sq = sb(ps, "sq", [128, KC, 512], BF16)
            rt = sb(ps, "rt", [128, 512], F32)
            tmp = sb(ps, "tmp", [128, KC, 512], F32)
            h2 = sb(ps, "h2", [128, NF, 512], BF16)
            sil = [sb(ps, f"sil{q}", [128, 512], BF16) for q in range(2)]
            for q in range(2):
                sch.dma("pool", wdn.t[:, q * 11:(q + 1) * 11, :], wdn_in[which // 2][l, :, q * 11:(q + 1) * 11, :],
                        writes=[wdn], max_dma_last_dim=4096)

            def load(ti):
                t0, n, j = tiles[ti]
                xt = xts[ti % 2]
                if first:
                    sch.dma("sp", xt.t[:, :, :n], xT_in[:, :, t0:t0 + n].rearrange("k p t -> p k t"), reads=[r_xin], writes=[xt])
                else:
                    sch.dma("sp", xt.t[:, :, :n], XT[:, :, t0:t0 + n].rearrange("k p t -> p k t"), reads=[xres[ti]], writes=[xt])

            load(0)
            load(1)
            wi = 0
            norm_mod(xts[0], tiles[0][1], l, i, tiles[0][2], sq, rt, tmp, hs[0])
            for ti, (t0, n, j) in enumerate(tiles):
                xt = xts[ti % 2]
                h = hs[ti % 2]
                for f in range(NF):
                    w = wup[wi % 3]
                    wi += 1
                    sch.dma("pool", w.t[:], wup_in[which // 2][l, f], writes=[w], max_dma_last_dim=4096)
                    pa = sch.bank()
                    pb = sch.bank()
                    for k in range(KC):
                        sch.op(pe, lambda k=k: P.matmul(pa.t[:, :n], w.t[:, k, 0, :], h.t[:, k, :n], start=(k == 0), stop=(k == KC - 1)),
                               reads=[w, h], writes=[pa], inc=(k == KC - 1))
                    for k in range(KC):
                        sch.op(pe, lambda k=k: P.matmul(pb.t[:, :n], w.t[:, k, 1, :], h.t[:, k, :n], start=(k == 0), stop=(k == KC - 1)),
                               reads=[w, h], writes=[pb], inc=(k == KC - 1))
                    s_ = sil[f % 2]
                    sch.op(act, lambda: A.activation(out=s_.t[:, :n], in_=pa.t[:, :n], func=AF.Silu), reads=[pa], writes=[s_])
                    sch.op(dve, lambda: V.tensor_tensor(out=h2.t[:, f, :n], in0=pb.t[:, :n], in1=s_.t[:, :n], op=ALU.mult),
                           reads=[pb, s_], writes=[h2])
                if ti + 1 < len(tiles):
                    t0n, nn_, jn = tiles[ti + 1]
                    norm_mod(xts[(ti + 1) % 2], nn_, l, i, jn, sq, rt, tmp, hs[(ti + 1) % 2])
                for m in range(KC):
                    py = sch.bank()
                    for f in range(NF):
                        sch.op(pe, lambda f=f: P.matmul(py.t[:, :n], wdn.t[:, f, m * 128:(m + 1) * 128], h2.t[:, f, :n],
                                                        start=(f == 0), stop=(f == NF - 1)),
                               reads=[wdn, h2], writes=[py], inc=(f == NF - 1))
                    sch.op(dve, lambda: V.scalar_tensor_tensor(out=xt.t[:, m, :n], in0=py.t[:, :n], scalar=GT.t[:, l, i, m, j:j + 1],
                                                               in1=xt.t[:, m, :n], op0=ALU.mult, op1=ALU.add),
                           reads=[py, GT, xt], writes=[xt])
                sch.dma("sp", XT[:, :, t0:t0 + n].rearrange("k p t -> p k t"), xt.t[:, :, :n], reads=[xt], writes=[xres[ti]])
                if ti + 2 < len(tiles):
                    load(ti + 2)
            sch.barrier()

    def inproj_phase(l):
        with ExitStack() as ps:
            hall = sb(ps, "hall", [128, KC, T], BF16)
            with ExitStack() as p1:
                xts = [sb(p1, f"ixt{q}", [128, KC, 512], F32) for q in range(2)]
                sq = sb(p1, "isq", [128, KC, 512], BF16)
                rt = sb(p1, "irt", [128, 512], F32)
                tmp = sb(p1, "itmp", [128, KC, 512], F32)

                def load(ti):
                    t0, n, j = tiles[ti]
                    sch.dma("sp", xts[ti % 2].t[:, :, :n], XT[:, :, t0:t0 + n].rearrange("k p t -> p k t"),
                            reads=[xres[ti]], writes=[xts[ti % 2]])

                load(0)
                for ti, (t0, n, j) in enumerate(tiles):
                    if ti + 1 < len(tiles):
                        load(ti + 1)
                    norm_mod(xts[ti % 2], n, l, 1, j, sq, rt, tmp, hall, hcol0=t0)
                sch.barrier()

            with ExitStack() as p2:
                wfb = [sb(p2, f"wfb{q}", [128, KC, 128], BF16) for q in range(3)]
                zs = [sb(p2, f"zs{q}", [128, T], F32) for q in range(2)]
                ys = sb(p2, "ys", [128, T], F32)
                zb = [sb(p2, f"zb{q}", [128, T], BF16) for q in range(2)]
                ktk = sb(p2, "ktk", [128, NTB, 128], BF16)
                wi = 0

                def chunk_mm(ci, evac):
                    nonlocal wi
                    w = wfb[wi % 3]
                    wi += 1
                    sch.dma("pool", w.t[:], wf_in[l, ci], writes=[w], max_dma_last_dim=4096)
                    for (t0, n, j) in tiles:
                        bk = sch.bank()
                        for k in range(KC):
                            sch.op(pe, lambda k=k: P.matmul(bk.t[:, :n], w.t[:, k, :], hall.t[:, k, t0:t0 + n],
                                                            start=(k == 0), stop=(k == KC - 1)),
                                   reads=[w, hall], writes=[bk], inc=(k == KC - 1))
                        evac(bk, t0, n)

                segs = [(0, CT), (CT, T)]
                for ci in range(16):
                    z = zs[ci % 2]
                    o = zb[ci % 2]

                    def ev(bk, t0, n, z=z):
                        sch.op(act, lambda: A.copy(out=z.t[:, t0:t0 + n], in_=bk.t[:, :n]), reads=[bk], writes=[z])

                    chunk_mm(ci, ev)
                    w0 = convw_s.t[:, l, ci, 0:1]
                    w1 = convw_s.t[:, l, ci, 1:2]
                    w2 = convw_s.t[:, l, ci, 2:3]
                    for (a, b) in segs:
                        sch.op(act, lambda: A.activation(out=ys.t[:, a:b], in_=z.t[:, a:b], func=AF.Copy, scale=w1),
                               reads=[z, convw_s], writes=[ys])
                        sch.op(dve, lambda: V.scalar_tensor_tensor(out=ys.t[:, a + 1:b], in0=z.t[:, a:b - 1], scalar=w0,
                                                                   in1=ys.t[:, a + 1:b], op0=ALU.mult, op1=ALU.add),
                               reads=[z, ys, convw_s], writes=[ys])
                        sch.op(dve, lambda: V.scalar_tensor_tensor(out=ys.t[:, a:b - 1], in0=z.t[:, a + 1:b], scalar=w2,
                                                                   in1=ys.t[:, a:b - 1], op0=ALU.mult, op1=ALU.add),
                               reads=[z, ys, convw_s], writes=[ys])
                    if ci < 8:
                        sch.op(act, lambda: A.activation(out=z.t[:, :], in_=ys.t[:, :], func=AF.Silu), reads=[ys], writes=[z])
                        sch.op(dve, lambda: V.tensor_scalar(out=o.t[:, :], in0=z.t[:, :], scalar1=0.0625, scalar2=None, op0=ALU.mult),
                               reads=[z], writes=[o])
                        sch.dma("sp", QT[ci], o.t[:, :], reads=[o], writes=[RQT])
                    else:
                        m = ci - 8
                        sch.op(act, lambda: A.activation(out=o.t[:, :], in_=ys.t[:, :], func=AF.Silu), reads=[ys], writes=[o])
                        sch.dma("sp", KT[m], o.t[:, :], reads=[o], writes=[RKT])
                        for g in range(0, NTB, 8):
                            nb = min(8, NTB - g)
                            bk = sch.bank()
                            bv = bk.t[:, :].bitcast(BF16)
                            for q in range(nb):
                                tb = g + q
                                sch.op(pe, lambda q=q, tb=tb: P.transpose(bv[:, q * 128:(q + 1) * 128], o.t[:, tb * 128:(tb + 1) * 128], ident_b.t[:]),
                                       reads=[o, ident_b], writes=[bk], inc=(q == nb - 1))
                            sch.op(act, lambda: A.copy(out=ktk.t[:, g:g + nb, :], in_=bv[:, 0:nb * 128].rearrange("p (q f) -> p q f", f=128)),
                                   reads=[bk], writes=[ktk])
                        sch.dma("sp", KTOK[:, m * 128:(m + 1) * 128].rearrange("(tb p) f -> p tb f", p=128), ktk.t[:, :, :],
                                reads=[ktk], writes=[RKTOK])
                for ci in range(16, 40):
                    o = zb[ci % 2]

                    def ev(bk, t0, n, o=o):
                        sch.op(act, lambda: A.activation(out=o.t[:, t0:t0 + n], in_=bk.t[:, :n], func=AF.Sigmoid), reads=[bk], writes=[o])

                    chunk_mm(ci, ev)
                    if ci < 24:
                        sch.dma("sp", SOT[ci - 16], o.t[:, :], reads=[o], writes=[RSOT])
                    elif ci < 32:
                        sch.dma("sp", GMT[ci - 24], o.t[:, :], reads=[o], writes=[RGMT])
                    else:
                        sch.dma("sp", GAT[ci - 32], o.t[:, :], reads=[o], writes=[RGAT])
                for kind in range(4):
                    ci = 45 + kind
                    z = zs[ci % 2]

                    def ev(bk, t0, n, z=z, kind=kind):
                        sch.op(act, lambda: A.activation(out=z.t[:, t0:t0 + n], in_=bk.t[:, :n], func=AF.Identity,
                                                         bias=bg_s.t[:, l, kind:kind + 1], scale=1.0),
                               reads=[bk, bg_s], writes=[z])

                    chunk_mm(ci, ev)
                    sch.dma("sp", GROW[kind], z.t[:, :], reads=[z], writes=[RGROW])
                sch.barrier()

            with ExitStack() as p3:
                wv = sb(p3, "wv", [128, KC, D], BF16)
                vt = [sb(p3, f"vt{q}", [128, D], BF16) for q in range(2)]
                sch.dma("pool", wv.t[:], wv_in[l], writes=[wv], max_dma_last_dim=4096)
                for tb in range(NTB):
                    v_ = vt[tb % 2]
                    for hf in range(2):
                        bk = sch.bank()
                        for k in range(KC):
                            sch.op(pe, lambda k=k: P.matmul(bk.t[:, :], hall.t[:, k, tb * 128:(tb + 1) * 128], wv.t[:, k, hf * 512:(hf + 1) * 512],
                                                            start=(k == 0), stop=(k == KC - 1)),
                                   reads=[wv, hall], writes=[bk], inc=(k == KC - 1))
                        if hf == 0:
                            sch.op(act, lambda: A.copy(out=v_.t[:, 0:512], in_=bk.t[:, :]), reads=[bk], writes=[v_])
                        else:
                            sch.op(dve, lambda: V.tensor_copy(out=v_.t[:, 512:1024], in_=bk.t[:, :]), reads=[bk], writes=[v_])
                    sch.dma("sp", VTOK[tb * 128:(tb + 1) * 128, :], v_.t[:, :], reads=[v_], writes=[RVTOK])
                sch.barrier()

            with ExitStack() as p4:
                wl = sb(p4, "wl", [128, 6, KC, 128], BF16)
                wuq = sb(p4, "wuq", [128, 3, 16, 128], BF16)
                wkk = sb(p4, "wkk", [128, 2, 8, 128], BF16)
                wkv = sb(p4, "wkv", [128, 2, D], BF16)
                cqf = sb(p4, "cqf", [128, 3, 512], F32)
                sq = sb(p4, "lsq", [128, 3, 512], BF16)
                rt = sb(p4, "lrt", [128, 512], F32)
                cqn = sb(p4, "cqn", [128, 3, 512], BF16)
                ckn = sb(p4, "ckn", [128, 2, 512], BF16)
                rp = [sb(p4, f"rp{q}", [64, 2, 512], F32) for q in range(2)]
                r1 = sb(p4, "r1", [64, 512], F32)
                r2 = sb(p4, "r2", [64, 512], F32)
                ob = [sb(p4, f"ob{q}", [128, 512], BF16) for q in range(4)]
                vb = [sb(p4, f"vb{q}", [128, D], BF16) for q in range(2)]
                for c in range(5):
                    sch.dma("pool", wl.t[:, c], wf_in[l, 40 + c], writes=[wl], max_dma_last_dim=4096)
                sch.dma("pool", wl.t[:, 5], wf_in[l, 49], writes=[wl], max_dma_last_dim=4096)
                sch.dma("pool", wuq.t[:], wuq_in[l], writes=[wuq], max_dma_last_dim=4096)
                sch.dma("pool", wkk.t[:], wukvk_in[l], writes=[wkk], max_dma_last_dim=4096)
                sch.dma("pool", wkv.t[:], wukvv_in[l], writes=[wkv], max_dma_last_dim=4096)
                oi = 0

                def rope_out(pa, pb, n, rpt, dst_ap, dres):
                    nonlocal oi
                    o = ob[oi % 4]
                    oi += 1
                    sch.op(dve, lambda: V.tensor_tensor(out=r1.t[:, :n], in0=pa.t[0:64, :n], in1=rpt.t[:, 0, :n], op=ALU.mult),
                           reads=[pa, rpt], writes=[r1])
                    sch.op(dve, lambda: V.tensor_tensor(out=r2.t[:, :n], in0=pb.t[0:64, :n], in1=rpt.t[:, 1, :n], op=ALU.mult),
                           reads=[pb, rpt], writes=[r2])
                    sch.op(dve, lambda: V.tensor_tensor(out=o.t[0:64, :n], in0=r1.t[:, :n], in1=r2.t[:, :n], op=ALU.add),
                           reads=[r1, r2], writes=[o])
                    sch.dma("sp", dst_ap, o.t[0:64, :n], reads=[o], writes=[dres])

                for ti, (t0, n, j) in enumerate(tiles):
                    rpt = rp[ti % 2]
                    sch.dma("sp", rpt.t[:, :, :n], rope_in[:, :, t0:t0 + n], writes=[rpt])
                    for (c0, ncn, gsrc, dst, inv) in ((0, 3, gqa_s, cqn, 1.0 / 384), (3, 2, gkva_s, ckn, 1.0 / 256)):
                        for c in range(ncn):
                            bk = sch.bank()
                            for k in range(KC):
                                sch.op(pe, lambda k=k: P.matmul(bk.t[:, :n], wl.t[:, c0 + c, k, :], hall.t[:, k, t0:t0 + n],
                                                                start=(k == 0), stop=(k == KC - 1)),
                                       reads=[wl, hall], writes=[bk], inc=(k == KC - 1))
                            sch.op(act, lambda: A.copy(out=cqf.t[:, c, :n], in_=bk.t[:, :n]), reads=[bk], writes=[cqf])
                        rms_rstd((sq, rt), lambda c: cqf.t[:, c, :n], ncn, n, inv, cqf)
                        for c in range(ncn):
                            sch.op(dve, lambda c=c: V.scalar_tensor_tensor(out=dst.t[:, c, :n], in0=cqf.t[:, c, :n], scalar=gsrc.t[:, l, c:c + 1],
                                                                           in1=rt.t[:, :n], op0=ALU.mult, op1=ALU.mult),
                                   reads=[cqf, gsrc, rt], writes=[dst])
                    pa = sch.bank()
                    pb = sch.bank()
                    for k in range(KC):
                        sch.op(pe, lambda k=k: P.matmul(pa.t[0:64, :n], wl.t[:, 5, k, 0:64], hall.t[:, k, t0:t0 + n], start=(k == 0), stop=(k == KC - 1)),
                               reads=[wl, hall], writes=[pa], inc=(k == KC - 1))
                    for k in range(KC):
                        sch.op(pe, lambda k=k: P.matmul(pb.t[0:64, :n], wl.t[:, 5, k, 64:128], hall.t[:, k, t0:t0 + n], start=(k == 0), stop=(k == KC - 1)),
                               reads=[wl, hall], writes=[pb], inc=(k == KC - 1))
                    rope_out(pa, pb, n, rpt, KR[:, t0:t0 + n], RKR)
                    for hd in range(8):
                        bk = sch.bank()
                        for c in range(3):
                            sch.op(pe, lambda c=c: P.matmul(bk.t[:, :n], wuq.t[:, c, 2 * hd, :], cqn.t[:, c, :n], start=(c == 0), stop=(c == 2)),
                                   reads=[wuq, cqn], writes=[bk], inc=(c == 2))
                        o = ob[oi % 4]
                        oi += 1
                        sch.op(act, lambda: A.copy(out=o.t[:, :n], in_=bk.t[:, :n]), reads=[bk], writes=[o])
                        sch.dma("sp", QN[hd, :, t0:t0 + n], o.t[:, :n], reads=[o], writes=[RQN])
                        pa = sch.bank()
                        pb = sch.bank()
                        for c in range(3):
                            sch.op(pe, lambda c=c: P.matmul(pa.t[0:64, :n], wuq.t[:, c, 2 * hd + 1, 0:64], cqn.t[:, c, :n], start=(c == 0), stop=(c == 2)),
                                   reads=[wuq, cqn], writes=[pa], inc=(c == 2))
                        for c in range(3):
                            sch.op(pe, lambda c=c: P.matmul(pb.t[0:64, :n], wuq.t[:, c, 2 * hd + 1, 64:128], cqn.t[:, c, :n], start=(c == 0), stop=(c == 2)),
                                   reads=[wuq, cqn], writes=[pb], inc=(c == 2))
                        rope_out(pa, pb, n, rpt, QR[hd, :, t0:t0 + n], RQR)
                        bk = sch.bank()
                        for c in range(2):
                            sch.op(pe, lambda c=c: P.matmul(bk.t[:, :n], wkk.t[:, c, hd, :], ckn.t[:, c, :n], start=(c == 0), stop=(c == 1)),
                                   reads=[wkk, ckn], writes=[bk], inc=(c == 1))
                        o = ob[oi % 4]
                        oi += 1
                        sch.op(dve, lambda: V.tensor_copy(out=o.t[:, :n], in_=bk.t[:, :n]), reads=[bk], writes=[o])
                        sch.dma("sp", KN[hd, :, t0:t0 + n], o.t[:, :n], reads=[o], writes=[RKN])
                    for b in range(n // 128):
                        v_ = vb[b % 2]
                        for hf in range(2):
                            bk = sch.bank()
                            for c in range(2):
                                sch.op(pe, lambda c=c: P.matmul(bk.t[:, :], ckn.t[:, c, b * 128:(b + 1) * 128], wkv.t[:, c, hf * 512:(hf + 1) * 512],
                                                                start=(c == 0), stop=(c == 1)),
                                       reads=[wkv, ckn], writes=[bk], inc=(c == 1))
                            sch.op(act, lambda: A.copy(out=v_.t[:, hf * 512:(hf + 1) * 512], in_=bk.t[:, :]), reads=[bk], writes=[v_])
                        sch.dma("sp", VA[t0 + b * 128:t0 + (b + 1) * 128, :], v_.t[:, :], reads=[v_], writes=[RVA])
                sch.barrier()
            sch.barrier()

    def mlstm_phase(l):
        with ExitStack() as ps:
            CA = [sb(ps, f"CA{d}", [64, NCH, 64], F32) for d in range(2)]
            ABC = sb(ps, "ABC", [128, 2, 4, NCH], F32)
            with ExitStack() as pg:
                W1 = sb(pg, "W1", [64, T], F32)
                NB = sb(pg, "NB", [64, T], F32)
                U = sb(pg, "U", [64, T], F32)
                G = sb(pg, "G", [64, T], F32)
                GE = sb(pg, "GE", [128, NCH], F32)
                GP = sb(pg, "GP", [128, NCH], F32)
                AA = sb(pg, "AA", [128, NCH], F32)
                TM = sb(pg, "TM", [64, T], F32)
                RS = [sb(pg, f"RS{d}", [64, T], F32) for d in range(2)]
                sch.op(dve, lambda: V.memset(AA.t[:], 0.0), writes=[AA])
                for d in range(2):
                    sch.dma("sp", W1.t[:, :], GROW[2 * d + 1, 0:64, :], reads=[RGROW], writes=[W1])
                    sch.dma("sp", U.t[:, :], GROW[2 * d, 0:64, :], reads=[RGROW], writes=[U])
                    sch.op(act, lambda: A.activation(out=W1.t[:, :], in_=W1.t[:, :], func=AF.Exp, scale=-1.0), reads=[W1], writes=[W1])
                    sch.op(act, lambda: A.activation(out=W1.t[:, :], in_=W1.t[:, :], func=AF.Ln, bias=ones_f.t[0:64, 0:1], scale=1.0),
                           reads=[W1, ones_f], writes=[W1])
                    if d == 0:
                        scans = [(slice(0, T), None)]
                    else:
                        scans = [(slice(CT - 1, None, -1), None), (slice(T - 1, CT - 1, -1), 0)]
                    for (sl, init_col) in scans:
                        nn = len(range(T)[sl])
                        init = 0.0 if init_col is None else NB.t[:, init_col:init_col + 1]
                        sch.op(dve, lambda: V.tensor_tensor_scan(out=NB.t[:, sl], data0=ones_f.t[0:64, 0:1].to_broadcast([64, nn]),
                                                                 data1=W1.t[:, sl], initial=init, op0=ALU.mult, op1=ALU.add),
                               reads=[W1, ones_f, NB], writes=[NB])
                    sch.op(dve, lambda: V.tensor_tensor(out=U.t[:, :], in0=U.t[:, :], in1=NB.t[:, :], op=ALU.add), reads=[U, NB], writes=[U])
                    for (sl, init_col) in scans:
                        init = 0.0 if init_col is None else G.t[:, init_col:init_col + 1]
                        sch.op(dve, lambda: V.tensor_tensor_scan(out=G.t[:, sl], data0=U.t[:, sl], data1=U.t[:, sl], initial=init,
                                                                 op0=ALU.max, op1=ALU.max),
                               reads=[U, G], writes=[G])
                    Gv = G.t[:, :].rearrange("p (c s) -> p c s", s=64)
                    if d == 0:
                        sch.op(dve, lambda: V.tensor_copy(out=GE.t[0:64, :], in_=Gv[:, :, 63]), reads=[G], writes=[GE])
                        sch.op(dve, lambda: V.memset(GP.t[0:64, 0:1], 0.0), writes=[GP])
                        sch.op(dve, lambda: V.tensor_copy(out=GP.t[0:64, 1:NCH], in_=GE.t[0:64, 0:NCH - 1]), reads=[GE], writes=[GP])
                    else:
                        nck = CT // 64
                        sch.op(dve, lambda: V.tensor_copy(out=GE.t[0:64, :], in_=Gv[:, :, 0]), reads=[G], writes=[GE])
                        sch.op(dve, lambda: V.memset(GP.t[0:64, nck - 1:nck], 0.0), writes=[GP])
                        sch.op(dve, lambda: V.tensor_copy(out=GP.t[0:64, 0:nck - 1], in_=GE.t[0:64, 1:nck]), reads=[GE], writes=[GP])
                        sch.op(dve, lambda: V.tensor_copy(out=GP.t[0:64, NCH - 1:NCH], in_=GE.t[0:64, 0:1]), reads=[GE], writes=[GP])
                        sch.op(dve, lambda: V.tensor_copy(out=GP.t[0:64, nck:NCH - 1], in_=GE.t[0:64, nck + 1:NCH]), reads=[GE], writes=[GP])
                    GEb = GE.t[0:64, :].unsqueeze(2).to_broadcast([64, NCH, 64])
                    TMv = TM.t[:, :].rearrange("p (c s) -> p c s", s=64)
                    sch.op(dve, lambda: V.tensor_tensor(out=TMv[0:32], in0=U.t[0:32, :].rearrange("p (c s) -> p c s", s=64), in1=GEb[0:32], op=ALU.subtract),
                           reads=[U, GE], writes=[TM])
                    sch.op(dve, lambda: V.tensor_tensor(out=TMv[32:64], in0=NB.t[32:64, :].rearrange("p (c s) -> p c s", s=64),
                                                        in1=GE.t[32:64, :].unsqueeze(2).to_broadcast([32, NCH, 64]), op=ALU.subtract),
                           reads=[NB, GE], writes=[TM])
                    sch.op(act, lambda: A.activation(out=RS[d].t[:, :], in_=TM.t[:, :], func=AF.Exp), reads=[TM], writes=[RS[d]])
                    for g in range(0, NCH, 8):
                        ng = min(8, NCH - g)
                        bk = sch.bank()
                        for q in range(ng):
                            c = g + q
                            sch.op(pe, lambda q=q, c=c: P.transpose(bk.t[0:64, q * 64:(q + 1) * 64], RS[d].t[:, c * 64:(c + 1) * 64], ident_f.t[0:64, 0:64]),
                                   reads=[RS[d], ident_f], writes=[bk], inc=(q == ng - 1))
                        sch.op(dve, lambda: V.tensor_copy(out=CA[d].t[:, g:g + ng, :], in_=bk.t[0:64, 0:ng * 64].rearrange("p (q f) -> p q f", f=64)),
                               reads=[bk], writes=[CA[d]])
                    sch.op(dve, lambda: V.tensor_tensor(out=AA.t[0:64, :], in0=GP.t[0:64, :], in1=GE.t[0:64, :], op=ALU.subtract),
                           reads=[GP, GE], writes=[AA])
                    sch.op(act, lambda: A.activation(out=AA.t[0:64, :], in_=AA.t[0:64, :], func=AF.Exp), reads=[AA], writes=[AA])
                    for hd in range(4):
                        bk = sch.bank()
                        sch.op(pe, lambda: P.matmul(bk.t[:, 0:NCH], sel_f.t[:, hd, :], AA.t[:, :], start=True, stop=True),
                               reads=[sel_f, AA], writes=[bk])
                        sch.op(act, lambda: A.copy(out=ABC.t[:, d, hd, :], in_=bk.t[:, 0:NCH]), reads=[bk], writes=[ABC])
                sch.barrier()

            with ExitStack() as ph:
                WT = 256
                WC = WT // 64
                NW = T // WT
                qTw = [sb(ph, f"qTw{q}", [128, KC, WT], BF16) for q in range(2)]
                kTw = [sb(ph, f"kTw{q}", [128, KC, WT], BF16) for q in range(2)]
                ktw = [sb(ph, f"ktw{q}", [64, WC, D], BF16) for q in range(2)]
                vxw = [sb(ph, f"vxw{q}", [64, WC, 4, 258], BF16) for q in range(2)]
                Cf = [[sb(ph, f"Cf{h}_{k}", [128, 258], F32) for k in range(2)] for h in range(4)]
                Cb = [[sb(ph, f"Cb{h}_{k}", [128, 258], BF16) for k in range(2)] for h in range(4)]
                NS = 2
                sTm = [[sb(ph, f"sTm{q}_{h}", [64, 64], BF16) for h in range(4)] for q in range(NS)]
                t1 = [[sb(ph, f"t1{q}_{h}", [64, 258], F32) for h in range(4)] for q in range(NS)]
                nd = [[sb(ph, f"nd{q}_{h}", [64, 258], F32) for h in range(4)] for q in range(NS)]
                dn = [[sb(ph, f"dn{q}_{h}", [64, 2], F32) for h in range(4)] for q in range(NS)]
                kw = [[sb(ph, f"kw{q}_{h}", [64, 256], BF16) for h in range(4)] for q in range(NS)]
                ho = [sb(ph, f"ho{q}", [64, 4, 256], F32) for q in range(NS)]
                for q in range(2):
                    sch.op(dve, lambda q=q: V.memset(vxw[q].t[:, :, :, 256:258], 1.0), writes=[vxw[q]])
                nck = CT // 64
                step = 0
                wi = 0
                for d in range(2):
                    HD = HF if d == 0 else HB
                    RH = RHF if d == 0 else RHB
                    lat = list(range(CT // WT, NW))
                    worder = list(range(CT // WT)) + (lat if d == 0 else lat[::-1])
                    if d == 1:
                        worder = list(range(CT // WT))[::-1] + lat[::-1]
                    for h_ in range(4):
                        for k_ in range(2):
                            sch.op(dve, lambda: V.memset(Cf[h_][k_].t[:], 0.0), writes=[Cf[h_][k_]])
                            sch.op(dve, lambda: V.memset(Cb[h_][k_].t[:], 0.0), writes=[Cb[h_][k_]])

                    def loadw(w, slot):
                        ts = slice(w * WT, (w + 1) * WT)
                        sch.dma("sp", qTw[slot].t[:, :, :], QT[:, :, ts].rearrange("k p t -> p k t"), reads=[RQT], writes=[qTw[slot]])
                        sch.dma("sp", kTw[slot].t[:, :, :], KT[:, :, ts].rearrange("k p t -> p k t"), reads=[RKT], writes=[kTw[slot]])
                        sch.dma("sp", ktw[slot].t[:, :, :], KTOK[ts, :].rearrange("(c p) f -> p c f", p=64), reads=[RKTOK], writes=[ktw[slot]])
                        for hd in range(4):
                            sch.dma("sp", vxw[slot].t[:, :, hd, 0:256], VTOK[ts, hd * 256:(hd + 1) * 256].rearrange("(c p) f -> p c f", p=64),
                                    reads=[RVTOK], writes=[vxw[slot]])

                    loadw(worder[0], wi % 2)
                    for wn, w in enumerate(worder):
                        slot = wi % 2
                        wi += 1
                        if wn + 1 < len(worder):
                            loadw(worder[wn + 1], wi % 2)
                        qT_, kT_, kt_, vx_ = qTw[slot], kTw[slot], ktw[slot], vxw[slot]
                        corder = list(range(WC)) if d == 0 else list(range(WC - 1, -1, -1))
                        for cl_ in corder:
                            c = w * WC + cl_
                            q = step % NS
                            step += 1
                            cs = slice(cl_ * 64, (cl_ + 1) * 64)
                            WEc = [CA[d].t[:, c, hd:hd + 1] for hd in range(4)]
                            EMc = [CA[d].t[:, c, 32 + hd:32 + hd + 1] for hd in range(4)]
                            for hd in range(4):
                                sch.op(act, lambda hd=hd: A.activation(out=kw[q][hd].t[:, :], in_=kt_.t[:, cl_, hd * 256:(hd + 1) * 256], func=AF.Copy, scale=WEc[hd]),
                                       reads=[kt_, CA[d]], writes=[kw[q][hd]])
                            for hd in range(4):
                                for kc in range(2):
                                    pu = sch.bank()
                                    sch.op(pe, lambda hd=hd, kc=kc: P.matmul(pu.t[:, 0:258], kw[q][hd].t[:, kc * 128:(kc + 1) * 128], vx_.t[:, cl_, hd, :], start=True, stop=True),
                                           reads=[kw[q][hd], vx_], writes=[pu])
                                    sch.op(dve, lambda hd=hd, kc=kc: V.scalar_tensor_tensor(out=Cf[hd][kc].t[:, :], in0=Cf[hd][kc].t[:, :], scalar=ABC.t[:, d, hd, c:c + 1],
                                                                                            in1=pu.t[:, 0:258], op0=ALU.mult, op1=ALU.add),
                                           reads=[Cf[hd][kc], ABC, pu], writes=[Cf[hd][kc]])
                            psc = sch.bank()
                            for hd in range(4):
                                for kc in range(2):
                                    sch.op(pe, lambda hd=hd, kc=kc: P.matmul(psc.t[0:64, hd * 64:(hd + 1) * 64], kT_.t[:, 2 * hd + kc, cs], qT_.t[:, 2 * hd + kc, cs],
                                                                             start=(kc == 0), stop=(kc == 1)),
                                           reads=[kT_, qT_], writes=[psc], inc=(kc == 1 and hd == 3))
                            pin = []
                            for hd in range(4):
                                pi = sch.bank()
                                pin.append(pi)
                                for kc in range(2):
                                    sch.op(pe, lambda hd=hd, kc=kc: P.matmul(pi.t[0:64, 0:258], qT_.t[:, 2 * hd + kc, cs], Cb[hd][kc].t[:, :], start=(kc == 0), stop=(kc == 1)),
                                           reads=[qT_, Cb[hd][kc]], writes=[pi], inc=(kc == 1))
                            for hd in range(4):
                                for kc in range(2):
                                    sch.op(pool, lambda hd=hd, kc=kc: nc.gpsimd.tensor_copy(out=Cb[hd][kc].t[:, :], in_=Cf[hd][kc].t[:, :]),
                                           reads=[Cf[hd][kc]], writes=[Cb[hd][kc]])
                            for hd in range(4):
                                sch.op(dve, lambda hd=hd: V.scalar_tensor_tensor(out=sTm[q][hd].t[:, :], in0=psc.t[0:64, hd * 64:(hd + 1) * 64], scalar=WEc[hd],
                                                                                 in1=tri_f.t[:, d, :], op0=ALU.mult, op1=ALU.mult),
                                       reads=[psc, CA[d], tri_f], writes=[sTm[q][hd]])
                            for hd in range(4):
                                sch.op(act, lambda hd=hd: A.activation(out=t1[q][hd].t[:, :], in_=pin[hd].t[0:64, 0:258], func=AF.Copy, scale=ABC.t[0:64, d, hd, c:c + 1]),
                                       reads=[pin[hd], ABC], writes=[t1[q][hd]])
                            pnn = []
                            for hd in range(4):
                                pn = sch.bank()
                                pnn.append(pn)
                                sch.op(pe, lambda hd=hd: P.matmul(pn.t[0:64, 0:258], sTm[q][hd].t[:, :], vx_.t[:, cl_, hd, :], start=True, stop=True),
                                       reads=[sTm[q][hd], vx_], writes=[pn])
                            for hd in range(4):
                                sch.op(dve, lambda hd=hd: V.tensor_tensor(out=nd[q][hd].t[:, :], in0=pnn[hd].t[0:64, 0:258], in1=t1[q][hd].t[:, :], op=ALU.add),
                                       reads=[pnn[hd], t1[q][hd]], writes=[nd[q][hd]])
                            for hd in range(4):
                                sch.op(act, lambda hd=hd: A.activation(out=dn[q][hd].t[:, 0:1], in_=nd[q][hd].t[:, 256:257], func=AF.Abs),
                                       reads=[nd[q][hd]], writes=[dn[q][hd]])
                            for hd in range(4):
                                sch.op(dve, lambda hd=hd: V.tensor_tensor(out=dn[q][hd].t[:, 0:1], in0=dn[q][hd].t[:, 0:1], in1=EMc[hd], op=ALU.max),
                                       reads=[dn[q][hd], CA[d]], writes=[dn[q][hd]])
                            for hd in range(4):
                                sch.op(dve, lambda hd=hd: V.reciprocal(out=dn[q][hd].t[:, 1:2], in_=dn[q][hd].t[:, 0:1]), reads=[dn[q][hd]], writes=[dn[q][hd]])
                            for hd in range(4):
                                sch.op(act, lambda hd=hd: A.activation(out=ho[q].t[:, hd, :], in_=nd[q][hd].t[:, 0:256], func=AF.Copy, scale=dn[q][hd].t[:, 1:2]),
                                       reads=[nd[q][hd], dn[q][hd]], writes=[ho[q]])
                            sch.dma("sp", HD[c * 64:(c + 1) * 64, :], ho[q].t[:, :, :].rearrange("p h f -> p (h f)"), reads=[ho[q]], writes=[RH])
                sch.barrier()
            sch.barrier()


    def attn_phase(l):
        sc = float(192 ** -0.5)
        with ExitStack() as ps:
            krT = sb(ps, "krT", [128, T], BF16)
            knTs = [sb(ps, f"knT{q}", [128, T], BF16) for q in range(2)]
            vhs = [sb(ps, f"vh{q}", [128, NTB, 128], BF16) for q in range(2)]
            qn = [sb(ps, f"qn{q}", [128, 512], BF16) for q in range(2)]
            qr = [sb(ps, f"qr{q}", [128, 512], BF16) for q in range(2)]
            pt = [sb(ps, f"pt{q}", [128, 512], BF16) for q in range(5)]
            ps2 = [sb(ps, f"ps2{q}", [128, 512], BF16) for q in range(2)]
            rd = sb(ps, "rd", [128, 512], F32)
            oo = [sb(ps, f"oo{q}", [128, 512], BF16) for q in range(2)]
            sch.op(dve, lambda: V.memset(krT.t[64:128, :], 0.0), writes=[krT])
            for q in range(2):
                sch.op(dve, lambda q=q: V.memset(qr[q].t[64:128, :], 0.0), writes=[qr[q]])
            sch.dma("sp", krT.t[0:64, :], KR[:, :], reads=[RKR], writes=[krT])
            it = 0
            pti = 0
            jobs = [(hd_, ti_) for hd_ in range(8) for ti_ in range(len(tiles))]

            def ldq(k):
                hd_, ti_ = jobs[k]
                t0_, n_, j_ = tiles[ti_]
                sch.dma("sp", qn[k % 2].t[:, :n_], QN[hd_, :, t0_:t0_ + n_], reads=[RQN], writes=[qn[k % 2]])
                sch.dma("sp", qr[k % 2].t[0:64, :n_], QR[hd_, :, t0_:t0_ + n_], reads=[RQR], writes=[qr[k % 2]])

            def ldh(hd_):
                sch.dma("sp", knTs[hd_ % 2].t[:, :], KN[hd_], reads=[RKN], writes=[knTs[hd_ % 2]])
                sch.dma("sp", vhs[hd_ % 2].t[:, :, :], VA[:, hd_ * 128:(hd_ + 1) * 128].rearrange("(tb p) f -> p tb f", p=128),
                        reads=[RVA], writes=[vhs[hd_ % 2]])

            ldh(0)
            ldq(0)
            for hd in range(8):
                knT = knTs[hd % 2]
                vh = vhs[hd % 2]
                for ti, (t0, n, j) in enumerate(tiles):
                    q_n = qn[it % 2]
                    q_r = qr[it % 2]
                    o_ = oo[it % 2]
                    if it + 1 < len(jobs):
                        ldq(it + 1)
                    if ti == 1 and hd + 1 < 8:
                        ldh(hd + 1)
                    it += 1
                    kbs = list(range(CT // 128)) if j == 1 else list(range(NTB))
                    po = sch.bank(hold=True)
                    pd = sch.bank(hold=True)

                    def scores(kb):
                        b = sch.bank()
                        sch.op(pe, lambda: P.matmul(b.t[:, :n], knT.t[:, kb * 128:(kb + 1) * 128], q_n.t[:, :n], start=True, stop=False),
                               reads=[knT, q_n], writes=[b], inc=False)
                        sch.op(pe, lambda: P.matmul(b.t[:, :n], krT.t[:, kb * 128:(kb + 1) * 128], q_r.t[:, :n], start=False, stop=True),
                               reads=[krT, q_r], writes=[b])
                        return b

                    LA = 2
                    pend = [scores(kb) for kb in kbs[:LA]]
                    nk = len(kbs)
                    prev_p = None
                    first_den = True
                    for ii, kb in enumerate(kbs):
                        b = pend.pop(0)
                        p_ = pt[pti % len(pt)]
                        pti += 1
                        sch.op(act, lambda: A.activation(out=p_.t[:, :n], in_=b.t[:, :n], func=AF.Exp, scale=sc), reads=[b], writes=[p_])
                        if ii + LA < nk:
                            pend.append(scores(kbs[ii + LA]))
                        last = ii == nk - 1
                        sch.op(pe, lambda: P.matmul(po.t[:, :n], vh.t[:, kb, :], p_.t[:, :n], start=(ii == 0), stop=last),
                               reads=[vh, p_], writes=[po], inc=last)
                        if ii % 2 == 1:
                            s2 = ps2[(ii // 2) % 2]
                            sch.op(dve, lambda: V.tensor_tensor(out=s2.t[:, :n], in0=prev_p.t[:, :n], in1=p_.t[:, :n], op=ALU.add),
                                   reads=[prev_p, p_], writes=[s2])
                            sch.op(pe, lambda: P.matmul(pd.t[:, :n], ones_b.t[:, :], s2.t[:, :n], start=first_den, stop=last),
                                   reads=[ones_b, s2], writes=[pd], inc=True)
                            first_den = False
                        elif last:
                            sch.op(pe, lambda: P.matmul(pd.t[:, :n], ones_b.t[:, :], p_.t[:, :n], start=first_den, stop=True),
                                   reads=[ones_b, p_], writes=[pd], inc=True)
                        prev_p = p_
                    sch.op(dve, lambda: V.reciprocal(out=rd.t[:, :n], in_=pd.t[:, :n]), reads=[pd], writes=[rd])
                    sch.op(dve, lambda: V.tensor_tensor(out=o_.t[:, :n], in0=po.t[:, :n], in1=rd.t[:, :n], op=ALU.mult), reads=[po, rd], writes=[o_])
                    sch.release(po)
                    sch.release(pd)
                    sch.dma("sp", HAT[hd, :, t0:t0 + n], o_.t[:, :n], reads=[o_], writes=[RHAT])
            sch.barrier()

    def merge_phase(l):
        with ExitStack() as ps:
            wbm = sb(ps, "wbm", [128, KC, D], BF16)
            wba = sb(ps, "wba", [128, KC, D], BF16)
            wout = sb(ps, "wout", [128, KC, D], BF16)
            sch.dma("pool", wbm.t[:], wbm_in[l], writes=[wbm], max_dma_last_dim=4096)
            sch.dma("pool", wba.t[:], wba_in[l], writes=[wba], max_dma_last_dim=4096)
            sch.dma("pool", wout.t[:], wout_in[l], writes=[wout], max_dma_last_dim=4096)
            hf = sb(ps, "hf", [128, 4, D], F32)
            hb = sb(ps, "hb", [128, 4, D], F32)
            hn = sb(ps, "hn", [128, 4, D], BF16)
            junk = sb(ps, "junk", [128, 256], BF16)
            ssq = sb(ps, "ssq", [128, 16], F32)
            so = sb(ps, "so", [128, KC, 512], BF16)
            gm = sb(ps, "gm", [128, KC, 512], BF16)
            ga = sb(ps, "ga", [128, KC, 512], BF16)
            hat = sb(ps, "hat", [128, KC, 512], BF16)
            xt = sb(ps, "mxt", [128, KC, 512], F32)
            hmT = sb(ps, "hmT", [128, KC, 512], BF16)
            tm = sb(ps, "tm", [128, KC, 512], F32)
            t2 = sb(ps, "t2", [128, 512], F32)
            tb_ = sb(ps, "tb", [128, KC, 512], BF16)
            def ld_h(ti):
                t0, n, j = tiles[ti]
                nb = n // 128
                sch.dma("sp", hf.t[:, 0:nb, :], HF[t0:t0 + n, :].rearrange("(b p) f -> p b f", p=128), reads=[RHF], writes=[hf])
                sch.dma("sp", hb.t[:, 0:nb, :], HB[t0:t0 + n, :].rearrange("(b p) f -> p b f", p=128), reads=[RHB], writes=[hb])

            def ld_so(ti):
                t0, n, j = tiles[ti]
                sch.dma("sp", so.t[:, :, :n], SOT[:, :, t0:t0 + n].rearrange("k p t -> p k t"), reads=[RSOT], writes=[so])

            def ld_g(ti):
                t0, n, j = tiles[ti]
                sch.dma("sp", gm.t[:, :, :n], GMT[:, :, t0:t0 + n].rearrange("k p t -> p k t"), reads=[RGMT], writes=[gm])
                sch.dma("sp", ga.t[:, :, :n], GAT[:, :, t0:t0 + n].rearrange("k p t -> p k t"), reads=[RGAT], writes=[ga])
                sch.dma("sp", hat.t[:, :, :n], HAT[:, :, t0:t0 + n].rearrange("k p t -> p k t"), reads=[RHAT], writes=[hat])

            def ld_x(ti):
                t0, n, j = tiles[ti]
                sch.dma("sp", xt.t[:, :, :n], XT[:, :, t0:t0 + n].rearrange("k p t -> p k t"), reads=[xres[ti]], writes=[xt])

            ld_h(0)
            ld_so(0)
            ld_g(0)
            ld_x(0)
            for ti, (t0, n, j) in enumerate(tiles):
                nb = n // 128
                more = ti + 1 < len(tiles)
                sch.op(dve, lambda: V.tensor_tensor(out=hf.t[:, 0:nb, :], in0=hf.t[:, 0:nb, :], in1=hb.t[:, 0:nb, :], op=ALU.add),
                       reads=[hf, hb], writes=[hf])
                for b in range(nb):
                    for hd in range(4):
                        sch.op(act, lambda b=b, hd=hd: A.activation(out=junk.t[:, :], in_=hf.t[:, b, hd * 256:(hd + 1) * 256], func=AF.Square,
                                                                    accum_out=ssq.t[:, b * 4 + hd:b * 4 + hd + 1]),
                               reads=[hf], writes=[junk, ssq])
                sch.op(act, lambda: A.activation(out=ssq.t[:, 0:4 * nb], in_=ssq.t[:, 0:4 * nb], func=AF.Sqrt, scale=1.0 / 256, bias=eps_s.t[:, 0:1]),
                       reads=[ssq, eps_s], writes=[ssq])
                sch.op(dve, lambda: V.reciprocal(out=ssq.t[:, 0:4 * nb], in_=ssq.t[:, 0:4 * nb]), reads=[ssq], writes=[ssq])
                for b in range(nb):
                    for hd in range(4):
                        sch.op(dve, lambda b=b, hd=hd: V.tensor_scalar(out=hn.t[:, b, hd * 256:(hd + 1) * 256], in0=hf.t[:, b, hd * 256:(hd + 1) * 256],
                                                                       scalar1=ssq.t[:, b * 4 + hd:b * 4 + hd + 1], scalar2=None, op0=ALU.mult),
                               reads=[hf, ssq], writes=[hn])
                if more:
                    ld_h(ti + 1)
                for m in range(KC):
                    bk = sch.bank()
                    bv = bk.t[:, :].bitcast(BF16)
                    for b in range(nb):
                        sch.op(pe, lambda b=b: P.transpose(bv[:, b * 128:(b + 1) * 128], hn.t[:, b, m * 128:(m + 1) * 128], ident_b.t[:]),
                               reads=[hn, ident_b], writes=[bk], inc=(b == nb - 1))
                    sch.op(dve, lambda: V.scalar_tensor_tensor(out=hmT.t[:, m, :n], in0=bv[:, 0:n], scalar=gmh_s.t[:, l, m:m + 1], in1=so.t[:, m, :n],
                                                               op0=ALU.mult, op1=ALU.mult),
                           reads=[bk, gmh_s, so], writes=[hmT])
                if more:
                    ld_so(ti + 1)
                for m in range(KC):
                    bk = sch.bank()
                    for k in range(KC):
                        sch.op(pe, lambda k=k: P.matmul(bk.t[:, :n], wbm.t[:, k, m * 128:(m + 1) * 128], hmT.t[:, k, :n], start=(k == 0), stop=(k == KC - 1)),
                               reads=[wbm, hmT], writes=[bk], inc=(k == KC - 1))
                    sch.op(dve, lambda: V.tensor_tensor(out=tm.t[:, m, :n], in0=bk.t[:, :n], in1=gm.t[:, m, :n], op=ALU.mult),
                           reads=[bk, gm], writes=[tm])
                    bk2 = sch.bank()
                    for k in range(KC):
                        sch.op(pe, lambda k=k: P.matmul(bk2.t[:, :n], wba.t[:, k, m * 128:(m + 1) * 128], hat.t[:, k, :n], start=(k == 0), stop=(k == KC - 1)),
                               reads=[wba, hat], writes=[bk2], inc=(k == KC - 1))
                    sch.op(dve, lambda: V.tensor_tensor(out=t2.t[:, :n], in0=bk2.t[:, :n], in1=ga.t[:, m, :n], op=ALU.mult),
                           reads=[bk2, ga], writes=[t2])
                    sch.op(dve, lambda: V.tensor_tensor(out=tb_.t[:, m, :n], in0=tm.t[:, m, :n], in1=t2.t[:, :n], op=ALU.add),
                           reads=[tm, t2], writes=[tb_])
                if more:
                    ld_g(ti + 1)
                for m in range(KC):
                    bk = sch.bank()
                    for k in range(KC):
                        sch.op(pe, lambda k=k: P.matmul(bk.t[:, :n], wout.t[:, k, m * 128:(m + 1) * 128], tb_.t[:, k, :n], start=(k == 0), stop=(k == KC - 1)),
                               reads=[wout, tb_], writes=[bk], inc=(k == KC - 1))
                    sch.op(dve, lambda: V.scalar_tensor_tensor(out=xt.t[:, m, :n], in0=bk.t[:, :n], scalar=GT.t[:, l, 1, m, j:j + 1], in1=xt.t[:, m, :n],
                                                               op0=ALU.mult, op1=ALU.add),
                           reads=[bk, GT, xt], writes=[xt])
                sch.dma("sp", XT[:, :, t0:t0 + n].rearrange("k p t -> p k t"), xt.t[:, :, :n], reads=[xt], writes=[xres[ti]])
                if more:
                    ld_x(ti + 1)
            sch.barrier()

    def final_phase():
        with ExitStack() as ps:
            xts = [sb(ps, f"fxt{q}", [128, KC, 512], F32) for q in range(2)]
            sq = sb(ps, "fsq", [128, KC, 512], BF16)
            rt = sb(ps, "frt", [128, 512], F32)
            ot = [sb(ps, f"fot{q}", [128, KC, 512], F32) for q in range(2)]
            for ti, (t0, n, j) in enumerate(tiles):
                if j == 1:
                    continue
                xt = xts[ti % 2]
                o = ot[ti % 2]
                sch.dma("sp", xt.t[:, :, :n], XT[:, :, t0:t0 + n].rearrange("k p t -> p k t"), reads=[xres[ti]], writes=[xt])
                rms_rstd((sq, rt), lambda c: xt.t[:, c, :n], KC, n, 1.0 / D, xt)
                for k in range(KC):
                    sch.op(dve, lambda k=k: V.scalar_tensor_tensor(out=o.t[:, k, :n], in0=xt.t[:, k, :n], scalar=gfin_s.t[:, k:k + 1], in1=rt.t[:, :n],
                                                                   op0=ALU.mult, op1=ALU.mult),
                           reads=[xt, gfin_s, rt], writes=[o])
                sch.dma("sp", outT[:, :, t0 - CT:t0 - CT + n].rearrange("k p t -> p k t"), o.t[:, :, :n], reads=[o], writes=[ROUT])
            sch.barrier()

    for l in range(L):
        ffn_phase(l, 0, first=(l == 0))
        inproj_phase(l)
        mlstm_phase(l)
        attn_phase(l)
        merge_phase(l)
        ffn_phase(l, 2, first=False)
    final_phase()
    es.close()
    build.ninst = sch.ninst
    return nc


def _prep_shared(inp, L, S):
    f = np.float32
    T = CT + S
    out = {}
    w_ada = np.asarray(inp["w_ada"], f)[:L]
    out["wada"] = np.ascontiguousarray(w_ada.reshape(L, KC, 128, 72, 128).transpose(0, 3, 2, 1, 4))
    out["bada"] = np.ascontiguousarray(np.asarray(inp["b_ada"], f)[:L].reshape(L, 72, 128).transpose(2, 0, 1))
    gn = np.stack([np.asarray(inp[k], f)[:L] for k in ("g_n1", "g_n2", "g_n3")], axis=1)
    out["gn"] = np.ascontiguousarray(gn.reshape(L, 3, KC, 128).transpose(3, 0, 1, 2))
    out["gfin"] = np.ascontiguousarray(np.asarray(inp["g_final"], f).reshape(KC, 128).T)
    for i, (ku, kd) in enumerate((("w_ff1_up", "w_ff1_dn"), ("w_ff2_up", "w_ff2_dn"))):
        wu = np.asarray(inp[ku], f)[:L]
        wu = wu.reshape(L, KC, 128, 2, NF, 128)
        out[f"wup{i}"] = np.ascontiguousarray(wu.transpose(0, 4, 2, 1, 3, 5))
        wd = np.asarray(inp[kd], f)[:L].reshape(L, NF, 128, D)
        out[f"wdn{i}"] = np.ascontiguousarray(wd.transpose(0, 2, 1, 3))
    w_in = np.asarray(inp["w_in"], f)[:L]
    o = 0
    offs = {}
    for name, n in (("m_q", 1024), ("m_k", 1024), ("m_v", 1024), ("m_o", 1024), ("m_gate", 16), ("a_cq", 384), ("a_ckv", 256), ("a_kr", 64), ("br_gate", 2048)):
        offs[name] = (o, n)
        o += n

    def grp(name):
        a, n = offs[name]
        return w_in[:, :, a:a + n]

    deint = np.concatenate([np.arange(0, 64, 2), np.arange(1, 64, 2)])
    swp = np.concatenate([deint[32:], deint[:32]])
    gates = grp("m_gate")
    grep = np.zeros((L, D, 4, 128), f)
    for d in range(2):
        for ki in range(2):
            for hd in range(4):
                for qd in range(2):
                    grep[:, :, d * 2 + ki, 32 * qd + hd] = gates[:, :, d * 8 + ki * 4 + hd]
    kr = grp("a_kr")
    kr2 = np.concatenate([kr[:, :, deint], kr[:, :, swp]], axis=2)
    cols = np.concatenate([grp("m_q"), grp("m_k"), grp("m_o"), grp("br_gate"), grp("a_cq"), grp("a_ckv"), grep.reshape(L, D, 512), kr2], axis=2)
    assert cols.shape[2] == NWF * 128
    out["wf"] = np.ascontiguousarray(cols.reshape(L, KC, 128, NWF, 128).transpose(0, 3, 2, 1, 4))
    out["wv"] = np.ascontiguousarray(grp("m_v").reshape(L, KC, 128, D).transpose(0, 2, 1, 3))
    wc = np.asarray(inp["w_conv"], f)[:L]
    out["convw"] = np.ascontiguousarray(wc.reshape(L, 3, 16, 128).transpose(3, 0, 2, 1))
    bgl = np.asarray(inp["b_gate"], f)[:L]
    bg = np.zeros((128, L, 4), f)
    for d in range(2):
        for ki in range(2):
            for hd in range(4):
                for qd in range(2):
                    bg[32 * qd + hd, :, d * 2 + ki] = bgl[:, d * 8 + ki * 4 + hd]
    out["bg"] = bg
    out["gmh"] = np.ascontiguousarray(np.asarray(inp["g_mh"], f)[:L].reshape(L, KC, 128).transpose(2, 0, 1))
    out["gqa"] = np.ascontiguousarray(np.asarray(inp["g_qa"], f)[:L].reshape(L, 3, 128).transpose(2, 0, 1))
    out["gkva"] = np.ascontiguousarray(np.asarray(inp["g_kva"], f)[:L].reshape(L, 2, 128).transpose(2, 0, 1))
    wuq = np.asarray(inp["w_uq"], f)[:L].reshape(L, 384, 8, 192)
    chunks = []
    for hd in range(8):
        chunks.append(wuq[:, :, hd, 0:128])
        r = wuq[:, :, hd, 128:192]
        chunks.append(np.concatenate([r[:, :, deint], r[:, :, swp]], axis=2))
    wuq2 = np.stack(chunks, axis=2)
    out["wuq"] = np.ascontiguousarray(wuq2.reshape(L, 3, 128, 16, 128).transpose(0, 2, 1, 3, 4))
    wukv = np.asarray(inp["w_ukv"], f)[:L].reshape(L, 256, 8, 256)
    out["wukvk"] = np.ascontiguousarray(wukv[:, :, :, 0:128].reshape(L, 2, 128, 8, 128).transpose(0, 2, 1, 3, 4))
    out["wukvv"] = np.ascontiguousarray(wukv[:, :, :, 128:256].reshape(L, 2, 128, 1024).transpose(0, 2, 1, 3))
    for k, kk in (("w_bm", "wbm"), ("w_ba", "wba"), ("w_out", "wout")):
        out[kk] = np.ascontiguousarray(np.asarray(inp[k], f)[:L].reshape(L, KC, 128, D).transpose(0, 2, 1, 3))
    inv = (10000.0 ** (-np.arange(0, 32, 2, dtype=np.float32) / 32)).astype(f)
    t = np.arange(S)
    row = (t // 64).astype(f)
    col = (t % 64).astype(f)
    ang = np.concatenate([row[:, None] * inv, col[:, None] * inv], axis=-1)
    cos = np.cos(ang).astype(f).T
    sin = np.sin(ang).astype(f).T
    rope = np.zeros((64, 2, T), f)
    rope[:, 0, :CT] = 1.0
    rope[0:32, 0, CT:] = cos
    rope[32:64, 0, CT:] = cos
    rope[0:32, 1, CT:] = -sin
    rope[32:64, 1, CT:] = sin
    out["rope"] = rope
    out["ident"] = np.eye(128, dtype=f)
    sel = np.zeros((128, 4, 128), f)
    for hd in range(4):
        sel[hd, hd, :] = 1.0
    out["sel"] = sel
    tri = np.zeros((64, 2, 64), f)
    s_ = np.arange(64)[:, None]
    j_ = np.arange(64)[None, :]
    tri[:, 0, :] = (s_ <= j_)
    tri[:, 1, :] = (s_ >= j_)
    out["tri"] = tri
    return out


def _run(inp, L, S, dbg=False):
    f = np.float32
    shared = _prep_shared(inp, L, S)
    x = np.asarray(inp["x"], f)
    ctx = np.asarray(inp["ctx"], f)
    c = np.asarray(inp["c"], f)
    cc = np.asarray(inp["c_ctx"], f)
    B = x.shape[0]
    T = CT + S
    in_maps = []
    for core in range(8):
        b = core % B
        cat = np.concatenate([ctx[b], x[b]], axis=0)
        m = dict(shared)
        m["xT"] = np.ascontiguousarray(cat.T.reshape(KC, 128, T))
        scT = np.stack([c[b].reshape(KC, 128).T, cc.reshape(KC, 128).T], axis=-1)
        m["scT"] = np.ascontiguousarray(scT)
        in_maps.append(m)
    nc = build(L, S, dbg)
    res = run_bass_kernel_spmd(nc, in_maps, core_ids=list(range(8)))
    outs = []
    for b in range(B):
        o = res.results[b]["outT"]
        outs.append(np.ascontiguousarray(o.reshape(D, S).T))
    out = np.stack(outs, axis=0).astype(f)
    if dbg:
        return out, res
    return out


def kernel(**inputs):
    return _run(inputs, 4, 4096)
```
